# Optimizing a Trainium2 kernel written in Bass

```python
import math
import jax, jax.numpy as jnp
from jax import lax
import numpy as np

D_MODEL = 1024
BATCH = 8
SEQ = 2048
DEPTH = 1
DEC_BATCH = 128
DEC_SEQ = 1
PAST_LEN = 16384
PAGE_SIZE = 128

RET_HEADS = 4
RET_DK = 128
RET_DV = 256
RET_QK = RET_HEADS * RET_DK
RET_V = RET_HEADS * RET_DV
GDN_HEADS = 8
GDN_DK = 128
GDN_DV = 128
GDN_QK = GDN_HEADS * GDN_DK
GDN_V = GDN_HEADS * GDN_DV
CONV_W = 4
CONV_CH = 2 * GDN_QK + GDN_V
CHUNK = 64
N_MEM = 256
X_HEADS = 4
X_HD = D_MODEL // X_HEADS
D_FF = ((8 * D_MODEL + 3 * 256 - 1) // (3 * 256)) * 256
ROPE_BASE = 10000.0
EPS = 1e-6
IN_SIZES = (RET_QK, RET_QK, RET_V, RET_V, CONV_CH, GDN_V, GDN_HEADS, GDN_HEADS, D_MODEL, D_MODEL)
D_IN = RET_QK * 2 + RET_V * 2 + CONV_CH + GDN_V + 2 * GDN_HEADS + 2 * D_MODEL

kernel_name = 'hybrid_retention_gdn_memxattn_step'


def _offsets(sizes):
    out, acc = [], 0
    for s in sizes[:-1]:
        acc += s
        out.append(acc)
    return out


def rmsnorm(x, g):
    xf = x.astype(jnp.float32)
    return xf * lax.rsqrt(jnp.mean(xf * xf, axis=-1, keepdims=True) + EPS) * g


def l2norm(x):
    return x * lax.rsqrt(jnp.sum(x * x, axis=-1, keepdims=True) + EPS)


def rotary(x, pos):
    half = x.shape[-1] // 2
    inv = ROPE_BASE ** (-jnp.arange(half, dtype=jnp.float32) / half)
    ang = pos.astype(jnp.float32)[:, None] * inv[None, :]
    cos, sin = jnp.cos(ang)[None, :, None, :], jnp.sin(ang)[None, :, None, :]
    x1, x2 = x[..., :half], x[..., half:]
    return jnp.concatenate([x1 * cos - x2 * sin, x1 * sin + x2 * cos], axis=-1)


def chunk_size(t):
    return t if t <= CHUNK else math.gcd(t, CHUNK)


def to_chunks(x, c):
    b, t = x.shape[:2]
    return jnp.moveaxis(x.reshape((b, t // c, c) + x.shape[2:]), 1, 0)


def from_chunks(o):
    n, b, c = o.shape[:3]
    return jnp.moveaxis(o, 0, 1).reshape((b, n * c) + o.shape[3:])


def retention_chunked(q, k, v, s0):
    h = q.shape[2]
    c = chunk_size(q.shape[1])
    log_g = jnp.log1p(-jnp.exp2(-5.0 - jnp.arange(h, dtype=jnp.float32)))
    idx = jnp.arange(c, dtype=jnp.float32)
    diff = idx[:, None] - idx[None, :]
    causal = diff >= 0
    intra_decay = jnp.where(causal[None], jnp.exp(jnp.maximum(diff, 0.0)[None] * log_g[:, None, None]), 0.0)
    q_decay = jnp.exp((idx + 1.0)[:, None] * log_g[None, :])
    k_decay = jnp.exp((c - 1.0 - idx)[:, None] * log_g[None, :])
    chunk_decay = jnp.exp(c * log_g)

    def step(s, xs):
        qc, kc, vc = xs
        scores = jnp.einsum('bihk,bjhk->bhij', qc, kc) * intra_decay
        o = jnp.einsum('bhij,bjhv->bihv', scores, vc) + jnp.einsum('bihk,bhkv->bihv', qc * q_decay[None, :, :, None], s)
        s = chunk_decay[None, :, None, None] * s + jnp.einsum('bjhk,bjhv->bhkv', kc * k_decay[None, :, :, None], vc)
        return s, o

    s, o = lax.scan(step, s0, (to_chunks(q, c), to_chunks(k, c), to_chunks(v, c)))
    return from_chunks(o), s


def gated_delta_chunked(q, k, v, beta, g, s0):
    dv = v.shape[-1]
    c = chunk_size(q.shape[1])
    idx = jnp.arange(c)
    incl = idx[:, None] >= idx[None, :]
    strict = idx[:, None] > idx[None, :]
    eye = jnp.eye(c, dtype=jnp.float32)

    def step(s, xs):
        qc, kc, vc, bc, gc = xs
        gcum = jnp.cumsum(gc, axis=1)
        gh = jnp.transpose(gcum, (0, 2, 1))
        dg = gh[:, :, :, None] - gh[:, :, None, :]
        gamma = jnp.where(incl, jnp.exp(jnp.where(incl, dg, 0.0)), 0.0)
        kk = jnp.einsum('bihk,bjhk->bhij', kc, kc)
        a = jnp.where(strict, kk * gamma * jnp.transpose(bc, (0, 2, 1))[..., None], 0.0)
        rhs = jnp.concatenate([vc * bc[..., None], kc * (bc * jnp.exp(gcum))[..., None]], axis=-1)
        rhs = jnp.transpose(rhs, (0, 2, 1, 3))
        sol = lax.linalg.triangular_solve(a + eye, rhs, left_side=True, lower=True, unit_diagonal=True)
        u, w = sol[..., :dv], sol[..., dv:]
        v_new = u - jnp.einsum('bhik,bhkv->bhiv', w, s)
        qk = jnp.where(incl, jnp.einsum('bihk,bjhk->bhij', qc, kc) * gamma, 0.0)
        qh = jnp.transpose(qc, (0, 2, 1, 3)) * jnp.exp(gh)[..., None]
        o = jnp.einsum('bhik,bhkv->bhiv', qh, s) + jnp.einsum('bhij,bhjv->bhiv', qk, v_new)
        g_last = gh[:, :, -1]
        kh = jnp.transpose(kc, (0, 2, 1, 3)) * jnp.exp(g_last[..., None] - gh)[..., None]
        s = jnp.exp(g_last)[..., None, None] * s + jnp.einsum('bhjk,bhjv->bhkv', kh, v_new)
        return s, jnp.transpose(o, (0, 2, 1, 3))

    xs = (to_chunks(q, c), to_chunks(k, c), to_chunks(v, c), to_chunks(beta, c), to_chunks(g, c))
    s, o = lax.scan(step, s0, xs)
    return from_chunks(o), s


def causal_conv(xc, buf, w):
    t = xc.shape[1]
    full = jnp.concatenate([buf.astype(jnp.float32), xc], axis=1)
    out = full[:, 0:t] * w[0]
    for i in range(1, CONV_W):
        out = out + full[:, i:i + t] * w[i]
    return jax.nn.silu(out), full[:, t:]


def mixer_block(h, s_ret, s_gdn, conv_buf, pos0, w_in, ret_gn_g, w_branch_a, gdn_conv_w, gdn_a_log,
                gdn_dt_bias, gdn_norm_g, w_branch_b, w_out):
    b, t, _ = h.shape
    f32 = jnp.float32
    proj = (h @ w_in).astype(f32)
    rq, rk, rv, rg, qkv, z, bb, aa, ga, gb = jnp.split(proj, _offsets(IN_SIZES), axis=-1)
    pos = pos0 + jnp.arange(t)
    rq = rotary(rq.reshape(b, t, RET_HEADS, RET_DK), pos)
    rk = rotary(rk.reshape(b, t, RET_HEADS, RET_DK), pos) * (RET_DK ** -0.5)
    rv = rv.reshape(b, t, RET_HEADS, RET_DV)
    o_r, s_ret_new = retention_chunked(rq, rk, rv, s_ret.astype(f32))
    mu = jnp.mean(o_r, axis=-1, keepdims=True)
    var = jnp.mean(jnp.square(o_r - mu), axis=-1, keepdims=True)
    o_r = ((o_r - mu) * lax.rsqrt(var + EPS)).reshape(b, t, RET_V) * ret_gn_g
    y_a = (jax.nn.silu(rg) * o_r) @ w_branch_a
    qkv, conv_new = causal_conv(qkv, conv_buf, gdn_conv_w.astype(f32))
    gq, gk, gv = jnp.split(qkv, [GDN_QK, 2 * GDN_QK], axis=-1)
    gq = l2norm(gq.reshape(b, t, GDN_HEADS, GDN_DK)) * (GDN_DK ** -0.5)
    gk = l2norm(gk.reshape(b, t, GDN_HEADS, GDN_DK))
    gv = gv.reshape(b, t, GDN_HEADS, GDN_DV)
    beta = jax.nn.sigmoid(bb)
    g = -jnp.exp(gdn_a_log.astype(f32)) * jax.nn.softplus(aa + gdn_dt_bias)
    o_g, s_gdn_new = gated_delta_chunked(gq, gk, gv, beta, g, s_gdn.astype(f32))
    o_g = rmsnorm(o_g, gdn_norm_g) * jax.nn.silu(z.reshape(b, t, GDN_HEADS, GDN_DV))
    y_b = o_g.reshape(b, t, GDN_V) @ w_branch_b
    merged = jax.nn.sigmoid(ga) * y_a + jax.nn.sigmoid(gb) * y_b
    return merged @ w_out, s_ret_new, s_gdn_new, conv_new


def mem_kv(mem, mem_norm_g, w_xk, w_xv):
    b, m, _ = mem.shape
    mn = rmsnorm(mem, mem_norm_g)
    return (mn @ w_xk).reshape(b, m, X_HEADS, X_HD), (mn @ w_xv).reshape(b, m, X_HEADS, X_HD)


def cross_attn(h, mk, mv, w_xq, w_xo):
    b, t, _ = h.shape
    q = (h @ w_xq).reshape(b, t, X_HEADS, X_HD).astype(jnp.float32)
    s = jnp.einsum('bthd,bmhd->bhtm', q, mk.astype(jnp.float32)) * (X_HD ** -0.5)
    p = jax.nn.softmax(s, axis=-1)
    o = jnp.einsum('bhtm,bmhd->bthd', p, mv.astype(jnp.float32)).reshape(b, t, D_MODEL)
    return o @ w_xo


def swiglu(h, w_gate, w_up, w_down):
    return (jax.nn.silu(h @ w_gate) * (h @ w_up)) @ w_down


def layer(x, s_ret, s_gdn, conv_buf, mk, mv, pos0, norm_mix_g, w_in, ret_gn_g, w_branch_a, gdn_conv_w,
          gdn_a_log, gdn_dt_bias, gdn_norm_g, w_branch_b, w_out, norm_x_g, w_xq, w_xo, norm_ffn_g,
          w_gate, w_up, w_down):
    y, s_ret, s_gdn, conv_buf = mixer_block(rmsnorm(x, norm_mix_g), s_ret, s_gdn, conv_buf, pos0, w_in,
                                            ret_gn_g, w_branch_a, gdn_conv_w, gdn_a_log, gdn_dt_bias,
                                            gdn_norm_g, w_branch_b, w_out)
    x = x + y
    x = x + cross_attn(rmsnorm(x, norm_x_g), mk, mv, w_xq, w_xo)
    x = x + swiglu(rmsnorm(x, norm_ffn_g), w_gate, w_up, w_down)
    return x, s_ret, s_gdn, conv_buf


def setup_inputs(seed: int = 0) -> dict:
    key = jax.random.key(seed)
    ks = jax.random.split(key, 40)
    f32 = jnp.float32
    L = DEPTH

    def nrm(k, shape, scale):
        return jax.random.normal(k, shape, f32) * scale

    def gain(k, shape):
        return 1.0 + 0.01 * jax.random.normal(k, shape, f32)

    dt = jnp.exp(jax.random.uniform(ks[0], (L, GDN_HEADS), f32, math.log(1e-3), math.log(1e-1)))
    dt_bias = dt + jnp.log(-jnp.expm1(-dt))
    a_log = jnp.log(jax.random.uniform(ks[1], (L, GDN_HEADS), f32, 1.0, 16.0))
    return {
        'x_prompt': nrm(ks[2], (BATCH, SEQ, D_MODEL), 1.0),
        'x_sample': nrm(ks[3], (DEC_BATCH, DEC_SEQ, D_MODEL), 1.0),
        'state_ret': nrm(ks[4], (L, DEC_BATCH, RET_HEADS, RET_DK, RET_DV), 0.1),
        'state_gdn': nrm(ks[5], (L, DEC_BATCH, GDN_HEADS, GDN_DK, GDN_DV), 0.1),
        'state_conv': nrm(ks[6], (L, DEC_BATCH, CONV_W - 1, CONV_CH), 1.0),
        'cache_mem_k': nrm(ks[7], (L, DEC_BATCH, N_MEM, X_HEADS, X_HD), 1.0),
        'cache_mem_v': nrm(ks[8], (L, DEC_BATCH, N_MEM, X_HEADS, X_HD), 1.0),
        'mem_prompt': nrm(ks[9], (BATCH, N_MEM, D_MODEL), 1.0),
        'norm_mix_g': gain(ks[10], (L, D_MODEL)),
        'w_in': nrm(ks[11], (L, D_MODEL, D_IN), D_MODEL ** -0.5),
        'ret_gn_g': gain(ks[12], (L, RET_V)),
        'w_branch_a': nrm(ks[13], (L, RET_V, D_MODEL), RET_V ** -0.5),
        'gdn_conv_w': nrm(ks[14], (L, CONV_W, CONV_CH), CONV_W ** -0.5),
        'gdn_a_log': a_log,
        'gdn_dt_bias': dt_bias,
        'gdn_norm_g': gain(ks[15], (L, GDN_DV)),
        'w_branch_b': nrm(ks[16], (L, GDN_V, D_MODEL), GDN_V ** -0.5),
        'w_out': nrm(ks[17], (L, D_MODEL, D_MODEL), D_MODEL ** -0.5),
        'norm_x_g': gain(ks[18], (L, D_MODEL)),
        'mem_norm_g': gain(ks[19], (L, D_MODEL)),
        'w_xq': nrm(ks[20], (L, D_MODEL, D_MODEL), D_MODEL ** -0.5),
        'w_xk': nrm(ks[21], (L, D_MODEL, D_MODEL), D_MODEL ** -0.5),
        'w_xv': nrm(ks[22], (L, D_MODEL, D_MODEL), D_MODEL ** -0.5),
        'w_xo': nrm(ks[23], (L, D_MODEL, D_MODEL), D_MODEL ** -0.5),
        'norm_ffn_g': gain(ks[24], (L, D_MODEL)),
        'w_gate': nrm(ks[25], (L, D_MODEL, D_FF), D_MODEL ** -0.5),
        'w_up': nrm(ks[26], (L, D_MODEL, D_FF), D_MODEL ** -0.5),
        'w_down': nrm(ks[27], (L, D_FF, D_MODEL), D_FF ** -0.5),
        'norm_final_g': gain(ks[28], (D_MODEL,)),
    }


def reference(x_prompt, x_sample, state_ret, state_gdn, state_conv, cache_mem_k, cache_mem_v, mem_prompt,
              norm_mix_g, w_in, ret_gn_g, w_branch_a, gdn_conv_w, gdn_a_log, gdn_dt_bias, gdn_norm_g,
              w_branch_b, w_out, norm_x_g, mem_norm_g, w_xq, w_xk, w_xv, w_xo, norm_ffn_g, w_gate, w_up,
              w_down, norm_final_g):
    f32 = jnp.float32
    bp = x_prompt.shape[0]
    xp, xs = x_prompt, x_sample
    sr_p_l, sg_p_l, sc_p_l, mk_p_l, mv_p_l, sr_s_l, sg_s_l, sc_s_l = [], [], [], [], [], [], [], []
    for l in range(DEPTH):
        lw = (norm_mix_g[l], w_in[l], ret_gn_g[l], w_branch_a[l], gdn_conv_w[l], gdn_a_log[l], gdn_dt_bias[l],
              gdn_norm_g[l], w_branch_b[l], w_out[l], norm_x_g[l], w_xq[l], w_xo[l], norm_ffn_g[l], w_gate[l],
              w_up[l], w_down[l])
        mk_p, mv_p = mem_kv(mem_prompt, mem_norm_g[l], w_xk[l], w_xv[l])
        xp, sr_p, sg_p, sc_p = layer(xp, jnp.zeros((bp, RET_HEADS, RET_DK, RET_DV), f32),
                                     jnp.zeros((bp, GDN_HEADS, GDN_DK, GDN_DV), f32),
                                     jnp.zeros((bp, CONV_W - 1, CONV_CH), f32), mk_p, mv_p, 0, *lw)
        xs, sr_s, sg_s, sc_s = layer(xs, state_ret[l], state_gdn[l], state_conv[l], cache_mem_k[l],
                                     cache_mem_v[l], PAST_LEN, *lw)
        sr_p_l.append(sr_p); sg_p_l.append(sg_p); sc_p_l.append(sc_p)
        mk_p_l.append(mk_p); mv_p_l.append(mv_p)
        sr_s_l.append(sr_s); sg_s_l.append(sg_s); sc_s_l.append(sc_s)
    y_prompt = rmsnorm(xp, norm_final_g)
    y_sample = rmsnorm(xs, norm_final_g)
    return (y_prompt, y_sample, jnp.stack(sr_p_l), jnp.stack(sg_p_l), jnp.stack(sc_p_l), jnp.stack(mk_p_l),
            jnp.stack(mv_p_l), jnp.stack(sr_s_l), jnp.stack(sg_s_l), jnp.stack(sc_s_l))
```

```python
from contextlib import ExitStack
import math
import numpy as np
import concourse.bass as bass
import concourse.mybir as mybir
from concourse.bass_utils import run_bass_kernel_spmd

F32 = mybir.dt.float32
BF16 = mybir.dt.bfloat16
AF = mybir.ActivationFunctionType
ALU = mybir.AluOpType

NCORES = 8
D = 1024
SEQ = 2048
NS = 16
PAST = 16384
RH, RDK, RDV = 4, 128, 256
GH, GDK, GDV = 8, 128, 128
CONV_CH = 3072
NMEM = 256
XH, XHD = 4, 256
DFF = 2816
DIN = 9232
EPS = 1e-6
TB = 256
NT = TB // 128
NBLK = SEQ // TB
ENGS = ("pe", "act", "dve", "pool", "sp")
EPOCH = 3000
NEG = -30000.0


class Sched:
    def __init__(self, nc, es):
        self.nc, self.es = nc, es
        self.eng_sems = {e: [] for e in ENGS}
        self.nsem = 0
        self.prev_final = []
        self.reset()

    def reset(self):
        self.ops = []
        self.last_w = {}
        self.readers = {}
        self.eng_count = {e: 0 for e in ENGS}
        self.eng_base = {e: len(self.eng_sems[e]) for e in ENGS}
        self.dma_sems = {}
        self.dma_cnt = {}
        self.bank_i = 0
        self.reserved = set()

    def _newsem(self, name):
        self.nsem += 1
        return self.es.enter_context(self.nc.semaphore(f"{name}_{self.nsem}"))

    def _token_compute(self, eng):
        i = self.eng_count[eng]
        self.eng_count[eng] += 1
        ep, k = divmod(i, EPOCH)
        ep += self.eng_base[eng]
        while len(self.eng_sems[eng]) <= ep:
            self.eng_sems[eng].append(self._newsem(f"s{eng}"))
        return (self.eng_sems[eng][ep], k + 1, 1)

    def _token_dma(self, key):
        if key not in self.dma_sems or self.dma_cnt[key] > 60000:
            self.dma_sems[key] = self._newsem("d")
            self.dma_cnt[key] = 0
        self.dma_cnt[key] += 16
        return (self.dma_sems[key], self.dma_cnt[key], 16)

    def bank(self):
        while True:
            b = self.bank_i
            self.bank_i = (self.bank_i + 1) % 8
            if b not in self.reserved:
                return b

    def op(self, eng, fn, reads=(), writes=(), dma=None):
        writes = list(writes) + [r for r in reads if r.startswith("pb") and r not in writes]
        deps = set()
        for r in reads:
            if r in self.last_w:
                deps.add(self.last_w[r])
        for w in writes:
            if w in self.last_w:
                deps.add(self.last_w[w])
            for rd in self.readers.get(w, ()):
                deps.add(rd)
        idx = len(self.ops)
        tok = self._token_dma(dma) if dma is not None else self._token_compute(eng)
        self.ops.append(dict(eng=eng, fn=fn, deps=deps, tok=tok, dma=dma is not None))
        for r in reads:
            self.readers.setdefault(r, []).append(idx)
        for w in writes:
            self.last_w[w] = idx
            self.readers[w] = []
        return idx

    def emit(self):
        nc, ops = self.nc, self.ops
        per = {e: [] for e in ENGS}
        for i, o in enumerate(ops):
            per[o["eng"]].append(i)
        final = {}
        for e in ENGS:
            for ep in range(self.eng_base[e], len(self.eng_sems[e])):
                n = min(EPOCH, self.eng_count[e] - (ep - self.eng_base[e]) * EPOCH)
                s = self.eng_sems[e][ep]
                final[id(s)] = (s, n)
        for o in ops:
            sem, val, _ = o["tok"]
            if final.get(id(sem), (None, 0))[1] < val:
                final[id(sem)] = (sem, val)
        prev_final = self.prev_final

        def run(eng_name, engine):
            waited = {}
            for (sem, val) in prev_final:
                engine.wait_ge(sem, val)
                waited[id(sem)] = val
            for i in per[eng_name]:
                o = ops[i]
                need = {}
                for d in o["deps"]:
                    od = ops[d]
                    if od["eng"] == "pe" and eng_name == "pe" and not od["dma"]:
                        continue
                    sem, val, _ = od["tok"]
                    k = id(sem)
                    if waited.get(k, 0) >= val:
                        continue
                    if k not in need or need[k][1] < val:
                        need[k] = (sem, val)
                for k, (sem, val) in need.items():
                    engine.wait_ge(sem, val)
                    waited[k] = val
                ins = o["fn"](engine)
                sem, val, inc = o["tok"]
                ins.then_inc(sem, inc)
            if eng_name == "sp":
                for k, (sem, val) in final.items():
                    if waited.get(k, 0) >= val:
                        continue
                    engine.wait_ge(sem, val)

        with nc.Block() as block:
            @block.tensor
            def _(e):
                run("pe", e)

            @block.scalar
            def _(e):
                run("act", e)

            @block.vector
            def _(e):
                run("dve", e)

            @block.gpsimd
            def _(e):
                run("pool", e)

            @block.sync
            def _(e):
                run("sp", e)
        self.prev_final = list(final.values())
        self.reset()


def _consts():
    c = {}
    idx = np.arange(128)
    c["c_ident"] = np.eye(128, dtype=np.float32)
    c["c_tri"] = (idx[:, None] <= idx[None, :]).astype(np.float32)
    c["c_mincl"] = np.where(idx[None, :] >= idx[:, None], 0.0, NEG).astype(np.float32)
    c["c_strict"] = (idx[None, :] > idx[:, None]).astype(np.float32)
    c["c_ones"] = np.ones((128, 128), np.float32)
    blk = lambda b: (idx[:, None] // b) == (idx[None, :] // b)
    c["c_m16"] = blk(16).astype(np.float32)
    c["c_moff"] = np.concatenate([-(blk(2 * b) & ~blk(b)).astype(np.float32) for b in (16, 32, 64)], axis=1)
    perm = np.zeros((128, 128), np.float32)
    perm[(idx + 64) % 128, idx] = 1.0
    c["c_perm"] = perm
    h = np.arange(RH, dtype=np.float64)
    log_g = np.log1p(-np.exp2(-5.0 - h))
    diff = idx[None, :] - idx[:, None]
    dt = np.where(diff[:, None, :] >= 0, np.exp(np.maximum(diff, 0)[:, None, :] * log_g[None, :, None]), 0.0)
    c["c_rdt"] = (dt * RDK ** -0.5).astype(np.float32).reshape(128, RH * 128)
    qd = np.exp((idx[None, :] + 1.0) * log_g[:, None])
    c["c_rqd"] = np.broadcast_to(qd.reshape(1, RH * 128), (128, RH * 128)).astype(np.float32).copy()
    kd = np.exp((127.0 - idx)[:, None] * log_g[None, :]) * RDK ** -0.5
    c["c_rkd"] = kd.astype(np.float32)
    gam = np.exp(log_g)
    half = RDK // 2
    inv = 10000.0 ** (-np.arange(half, dtype=np.float64) / half)
    pos = np.arange(SEQ, dtype=np.float64)
    ang = inv[:, None] * pos[None, :]
    ang = (inv.astype(np.float32)[:, None] * pos.astype(np.float32)[None, :]).astype(np.float64)
    cos, sin = np.cos(ang), np.sin(ang)
    c["c_cos"] = np.concatenate([cos, cos], 0).astype(np.float32)
    c["c_sin"] = np.concatenate([-sin, sin], 0).astype(np.float32)
    angs = (inv.astype(np.float32) * np.float32(PAST)).astype(np.float64)
    c["c_cs_s"] = np.stack([np.concatenate([np.cos(angs), np.cos(angs)]),
                            np.concatenate([-np.sin(angs), np.sin(angs)])]).astype(np.float32)
    return c, gam


_CONSTS, _GAM = _consts()

PJ_OFF = {"rq": (0, 0), "rk": (512, 512), "rv": (1024, 1024), "qkv": (3072, 2048), "ba": (7168, 5120)}
WC = 256
NQ = 1024 // WC
W_IN_TILES = []
for _k, _c0, _n in (("rq", 0, 512), ("rk", 512, 512), ("rv", 1024, 1024), ("rg", 2048, 1024), ("qkv", 3072, 3072),
                    ("z", 6144, 1024), ("ba", 7168, 16), ("ga", 7184, 1024), ("gb", 8208, 1024)):
    for _o in range(0, _n, WC):
        W_IN_TILES.append((_k, _c0 + _o, min(WC, _n - _o)))


def build_program(debug=False, nblk=NBLK, stage=99, kinds=None):
    nc = bass.Bass("TRN2", target_bir_lowering=False)
    di = lambda name, shape: nc.dram_tensor(name, list(shape), F32, kind="ExternalInput").ap()
    do = lambda name, shape: nc.dram_tensor(name, list(shape), F32, kind="ExternalOutput").ap()
    xp = di("xp", [SEQ, D]); memp = di("memp", [NMEM, D])
    xs = di("xs", [NS, D]); sret = di("sret", [NS, RH, RDK, RDV]); sgdn = di("sgdn", [NS, GH, GDK, GDV])
    sconv = di("sconv", [NS, 3, CONV_CH]); cmk = di("cmk", [NS, NMEM, D]); cmv = di("cmv", [NS, NMEM, D])
    w_in = di("w_in", [D, DIN]); w_a = di("w_a", [D, D]); w_b = di("w_b", [D, D]); w_out = di("w_out", [D, D])
    w_xq = di("w_xq", [D, D]); w_xk = di("w_xk", [D, D]); w_xv = di("w_xv", [D, D]); w_xo = di("w_xo", [D, D])
    w_gate = di("w_gate", [D, DFF]); w_up = di("w_up", [D, DFF]); w_down = di("w_down", [DFF, D])
    g_mix = di("g_mix", [D]); g_x = di("g_x", [D]); g_mem = di("g_mem", [D]); g_ffn = di("g_ffn", [D])
    g_fin = di("g_fin", [D]); g_gn = di("g_gn", [D]); g_gdn = di("g_gdn", [GDV])
    convw = di("convw", [4, CONV_CH]); a_log = di("a_log", [GH]); dt_bias = di("dt_bias", [GH])
    cst = {k: di(k, v.shape) for k, v in _CONSTS.items()}
    yp = do("yp", [SEQ, D]); ys = do("ys", [NS, D])
    srp = do("srp", [RH, RDK, RDV]); sgp = do("sgp", [GH, GDK, GDV]); scp = do("scp", [3, CONV_CH])
    mkp = do("mkp", [NMEM, D]); mvp = do("mvp", [NMEM, D])
    srs = do("srs", [NS, RH, RDK, RDV]); sgs = do("sgs", [NS, GH, GDK, GDV]); scs = do("scs", [NS, 3, CONV_CH])

    with ExitStack() as es:
        S = Sched(nc, es)
        sfx = [""]
        sbt = lambda name, shape, dt=BF16: cur_es[0].enter_context(nc.sbuf_tensor(name + sfx[0], list(shape), dt))
        cur_es = [es]
        PS = es.enter_context(nc.psum_tensor("PS", [128, 4096], F32))
        PSB = PS[:, :].bitcast(BF16)

        def pbank(b, n=512, off=0):
            return PS[:, b * 512 + off: b * 512 + off + n]

        def pbank_bf(b, n=1024, off=0):
            return PSB[:, b * 1024 + off: b * 1024 + off + n]

        def load_const(name, shape, src, dt=F32, eng="sp"):
            t = sbt(name, shape, dt)
            S.op(eng, lambda e: e.dma_start(out=t[:], in_=src), writes=[name], dma=name)
            return t
        ident_f = load_const("ident_f", [128, 128], cst["c_ident"])
        ident_b = load_const("ident_b", [128, 128], cst["c_ident"], BF16, "pool")
        ones_f = load_const("ones_f", [128, 128], cst["c_ones"])
        ones_b = load_const("ones_b", [128, 128], cst["c_ones"], BF16, "pool")
        perm_b = load_const("perm_b", [128, 128], cst["c_perm"], BF16, "pool")
        tri_f = load_const("tri_f", [128, 128], cst["c_tri"])
        mincl = load_const("mincl", [128, 128], cst["c_mincl"])
        strict = load_const("strict", [128, 128], cst["c_strict"])
        m16 = load_const("m16", [128, 128], cst["c_m16"], BF16, "pool")
        moff = load_const("moff", [128, 384], cst["c_moff"], BF16, "pool")
        rdt = load_const("rdt", [128, 512], cst["c_rdt"], BF16, "pool")
        rqd = load_const("rqd", [128, 512], cst["c_rqd"], BF16, "pool")
        rkd = load_const("rkd", [128, RH], cst["c_rkd"])
        with nc.allow_non_contiguous_dma(reason="small param column loads"):
            def col_load(name, src, n):
                t = sbt(name, [128, n], F32)
                S.op("sp", lambda e: e.dma_start(out=t[:], in_=src.rearrange("(k p) -> p k", p=128), allow_slow_non_contiguous=True),
                     writes=[name], dma=name)
                return t
            gmix_c = col_load("gmix_c", g_mix, 8)
            gx_c = col_load("gx_c", g_x, 8)
            gmem_c = col_load("gmem_c", g_mem, 8)
            gffn_c = col_load("gffn_c", g_ffn, 8)
            ggn_c = col_load("ggn_c", g_gn, 8)
            ggdn_c = col_load("ggdn_c", g_gdn, 1)
            cw = sbt("cw", [128, 4 * 24], F32)
            for i in range(4):
                S.op("sp", lambda e, i=i: e.dma_start(out=cw[:, i * 24:(i + 1) * 24], in_=convw[i].rearrange("(c p) -> p c", p=128),
                                                       allow_slow_non_contiguous=True),
                     writes=["cw"], dma="cw")
        alog_bc = sbt("alog_bc", [128, GH], F32)
        dtb_bc = sbt("dtb_bc", [128, GH], F32)
        S.op("sp", lambda e: e.dma_start(out=alog_bc[:], in_=a_log.rearrange("(o d) -> o d", o=1).partition_broadcast(128)),
             writes=["alog_bc"], dma="alog_bc")
        S.op("sp", lambda e: e.dma_start(out=dtb_bc[:], in_=dt_bias.rearrange("(o d) -> o d", o=1).partition_broadcast(128)),
             writes=["dtb_bc"], dma="dtb_bc")
        nega = sbt("nega", [128, GH], F32)
        S.op("act", lambda e: e.activation(out=nega[:], in_=alog_bc[:], func=AF.Exp), reads=["alog_bc"], writes=["nega"])
        S.op("dve", lambda e: e.tensor_scalar(out=nega[:], in0=nega[:], scalar1=-1.0, scalar2=None, op0=ALU.mult),
             reads=["nega"], writes=["nega"])
        ones128_b = sbt("ones128_b", [128, 128], BF16)
        S.op("dve", lambda e: e.tensor_scalar(out=ones128_b[:], in0=ones_f[:], scalar1=128.0, scalar2=None, op0=ALU.mult),
             reads=["ones_f"], writes=["ones128_b"])
        onesq_b = sbt("onesq_b", [128, 128], BF16)
        S.op("dve", lambda e: e.tensor_scalar(out=onesq_b[:], in0=ones_f[:], scalar1=1.0 / 256, scalar2=None, op0=ALU.mult),
             reads=["ones_f"], writes=["onesq_b"])

        NSLOT = 6
        wslots = [sbt(f"wslot{i}", [128, 8 * WC], BF16) for i in range(NSLOT)]
        wstate = dict(i=0)

        scratch = {}

        def wload(W, r0, nk, c0, ncols):
            s = wstate["i"] % NSLOT
            wstate["i"] += 1
            t = wslots[s]
            dst = t[:, 0:nk * ncols].rearrange("p (k c) -> p k c", c=ncols)
            key = (W.tensor.name, r0, nk, c0, ncols)
            if key not in scratch:
                sc = nc.dram_tensor(f"scr{len(scratch)}", [128, nk * ncols], BF16, kind="Internal").ap()
                scratch[key] = (sc, f"scr{len(scratch)}")
                sc, sres = scratch[key]
                src = W[r0:r0 + nk * 128, c0:c0 + ncols].rearrange("(k p) c -> p k c", p=128)
                S.op("pool", lambda e: e.dma_start(out=dst, in_=src), writes=[f"wslot{s}"], dma=f"wslot{s}")
                S.op("sp", lambda e: e.dma_start(out=sc, in_=t[:, 0:nk * ncols]), reads=[f"wslot{s}"], writes=[sres], dma=f"wst{s}")
            else:
                sc, sres = scratch[key]
                S.op("pool", lambda e: e.dma_start(out=t[:, 0:nk * ncols], in_=sc), reads=[sres], writes=[f"wslot{s}"], dma=f"wslot{s}")
            return t, f"wslot{s}"

        def wv(t, k, ncols, c0=0, n=None):
            n = ncols if n is None else n
            return t[:, k * ncols + c0: k * ncols + c0 + n]

        def rstd_from_ss(ss, n, eps, res):
            S.op("dve", lambda e: e.tensor_scalar(out=ss, in0=ss, scalar1=1.0 / n, scalar2=eps, op0=ALU.mult, op1=ALU.add),
                 reads=[res], writes=[res])
            S.op("act", lambda e: e.activation(out=ss, in_=ss, func=AF.Ln), reads=[res], writes=[res])
            S.op("act", lambda e: e.activation(out=ss, in_=ss, func=AF.Exp, scale=-0.5), reads=[res], writes=[res])

        xn = sbt("xn", [128, D], BF16)
        sqj = xn
        sscol = sbt("sscol", [128, 4], F32)

        def norm_transpose(xt, xres, gcol, gres, dstT, dres, ntok, tcol, ncols_total):
            S.op("act", lambda e: e.activation(out=sqj[0:ntok, :], in_=xt, func=AF.Square, accum_out=sscol[0:ntok, 0:1]),
                 reads=[xres], writes=["xn", "sscol"])
            rstd_from_ss(sscol[0:ntok, 0:1], D, EPS, "sscol")
            S.op("dve", lambda e: e.tensor_scalar(out=xn[0:ntok, :], in0=xt, scalar1=sscol[0:ntok, 0:1], scalar2=None, op0=ALU.mult),
                 reads=[xres, "sscol"], writes=["xn"])
            b = S.bank()

            def tr(e):
                ins = None
                for k in range(8):
                    ins = e.transpose(pbank_bf(b, ntok, k * 128)[:, :], xn[0:ntok, k * 128:(k + 1) * 128], ident_b[0:ntok, 0:ntok])
                return ins
            S.op("pe", tr, reads=["xn", "ident_b"], writes=[f"pb{b}"])
            src = pbank_bf(b, 1024).rearrange("p (k i) -> p k i", i=128)[:, :, 0:ntok]
            dst = dstT[:, :].rearrange("p (k t) -> p k t", t=ncols_total)[:, :, tcol:tcol + ntok]
            S.op("dve", lambda e: e.tensor_tensor(out=dst, in0=src, in1=gcol[:, 0:8].unsqueeze(2).to_broadcast([128, 8, ntok]), op=ALU.mult),
                 reads=[f"pb{b}", gres], writes=[dres])

        def proj_fm(W, r0, nk, c0, ncols, srcT, sres, src_cols, ntok, consume):
            t, wres = wload(W, r0, nk, c0, ncols)
            for j in range(ncols // 128):
                b = S.bank()

                def mm(e, j=j, b=b):
                    ins = None
                    for k in range(nk):
                        ins = e.matmul(pbank(b, ntok), lhsT=wv(t, k, ncols, j * 128, 128),
                                       rhs=srcT[:, k * src_cols: k * src_cols + ntok], start=(k == 0), stop=(k == nk - 1))
                    return ins
                S.op("pe", mm, reads=[wres, sres], writes=[f"pb{b}"])
                consume(j, b)

        def proj_tm(W, r0, nk, c0, ncols, srcT, sres, src_cols, ntiles, consume, tw=128):
            t, wres = wload(W, r0, nk, c0, ncols)
            for tau in range(ntiles):
                b = S.bank()

                def mm(e, tau=tau, b=b):
                    ins = None
                    for k in range(nk):
                        ins = e.matmul(pbank(b, ncols)[0:tw, :], lhsT=srcT[:, k * src_cols + tau * tw: k * src_cols + (tau + 1) * tw],
                                       rhs=wv(t, k, ncols), start=(k == 0), stop=(k == nk - 1))
                    return ins
                S.op("pe", mm, reads=[wres, sres], writes=[f"pb{b}"])
                consume(tau, b)


        def run_phase(is_sample):
            TBv = NS if is_sample else TB
            tw = NS if is_sample else 128
            NTv = TBv // tw
            nblk_v = 1 if is_sample else nblk
            esp = ExitStack()
            cur_es[0] = esp
            sfx[0] = "_s" if is_sample else "_p"
            XOR = [sbt(f"XOR{i_}", [128, NTv * D], F32) for i_ in range(2)]
            hnT = sbt("hnT", [128, 8 * TBv])
            RG = sbt("RG", [128, 8 * TBv])
            Z = sbt("Z", [128, 8 * TBv]); GA = sbt("GA", [128, 8 * TBv]); GB = sbt("GB", [128, 8 * TBv])
            NROT = 4
            _rot = {"tmpA": [sbt(f"tmpA{i}", [128, TBv], F32) for i in range(NROT)],
                    "tmpB": [sbt(f"tmpB{i}", [128, TBv], F32) for i in range(NROT)],
                    "tmpb": [sbt(f"tmpb{i}", [128, TBv], BF16) for i in range(NROT)],
                    "CACC": [sbt(f"CACC{i}", [128, TBv], F32) for i in range(NROT)]}
            _roti = {}

            def nxt(name):
                i = _roti.get(name, 0)
                _roti[name] = i + 1
                return _rot[name][i % NROT], f"{name}{i % NROT}"
            OG = sbt("OG", [128, 8 * TBv], F32)
            OR = XOR[1][:, 0:8 * TBv]
            ORg = sbt("ORg", [128, 8 * TBv]); OGg = sbt("OGg", [128, 8 * TBv])
            MRG = sbt("MRG", [128, 8 * TBv])
            if is_sample:
                FA = sbt("FA", [128, 22 * TBv])
                FA_AL = []
            else:
                QKVB = sbt("QKVB", [128, 24 * TBv])
                FA = QKVB[:, 0:22 * TBv]
                FA_AL = ["GQ", "GK", "GV"]
            if is_sample:
                PJ = sbt("PJ", [NS, 5136], F32)
                GNSs = sbt("GNSs", [128, 8 * TBv], F32)
                RNSs = sbt("RNSs", [128, 8 * TBv], F32)
                RNS = [(RNSs[:, 0:4 * TBv], "RNSa"), (RNSs[:, 4 * TBv:8 * TBv], "RNSb")]
                gfin_s = sbt("gfin_s", [128, D], F32)
            if not is_sample:
                RQ = sbt("RQ", [128, 4 * TBv]); RK = sbt("RK", [128, 4 * TBv]); RQd = sbt("RQd", [128, 4 * TBv])
                VtR = sbt("VtR", [128, NTv * 1024])
                GQ = QKVB[:, 0:8 * TBv]; GK = QKVB[:, 8 * TBv:16 * TBv]; GV = QKVB[:, 16 * TBv:24 * TBv]
                BA = sbt("BA", [128, NTv * 16], F32)
                _rot["CIN"] = [sbt(f"CIN{i}", [128, 3 + TBv], F32) for i in range(NROT)]
                HALO = sbt("HALO", [128, 24 * 3], F32)
                cosb = sbt("cosb", [128, TBv], F32); sinb = sbt("sinb", [128, TBv], F32)
                SR = sbt("SR", [128, RH * RDV], F32); SRb = sbt("SRb", [128, RH * RDV])
                SG = sbt("SG", [128, GH * GDV], F32); SGb = sbt("SGb", [128, GH * GDV])
                MKT = sbt("MKT", [128, 8 * NMEM]); MV = sbt("MV", [128, 2 * D])
                memx = OR[:, 0:D]
                mnT = OGg
                mo = OG[:, 0:512]

                S.op("dve", lambda e: e.memset(SR[:], 0.0), writes=["SR"])
                S.op("dve", lambda e: e.memset(SRb[:], 0.0), writes=["SRb"])
                S.op("dve", lambda e: e.memset(SG[:], 0.0), writes=["SG0", "SG1"])
                S.op("dve", lambda e: e.memset(SGb[:], 0.0), writes=["SGb0", "SGb1"])
                S.op("dve", lambda e: e.memset(HALO[:], 0.0), writes=["HALO"])

                for mt in range(2):
                    S.op("sp", lambda e, mt=mt: e.dma_start(out=memx[:], in_=memp[mt * 128:(mt + 1) * 128, :]), writes=["XOR1"], dma="memx")
                    norm_transpose(memx, "XOR1", gmem_c, "gmem_c", mnT, "OGg", 128, mt * 128, NMEM)
                for (W, outd, isk) in ((w_xk, mkp, True), (w_xv, mvp, False)):
                    for half in range(NQ):
                        def cons(tau, b, half=half, outd=outd, isk=isk):
                            S.op("act", lambda e: e.activation(out=mo[:, 0:WC], in_=pbank(b, WC), func=AF.Copy), reads=[f"pb{b}"], writes=["OG"])
                            if not isk:
                                S.op("dve", lambda e: e.tensor_copy(out=MV[:, tau * D + half * WC: tau * D + half * WC + WC], in_=pbank(b, WC)),
                                     reads=[f"pb{b}"], writes=["MV"])
                            S.op("sp", lambda e: e.dma_start(out=outd[tau * 128:(tau + 1) * 128, half * WC:(half + 1) * WC], in_=mo[:, 0:WC]),
                                 reads=["OG"], dma="mo_out")
                        proj_tm(W, 0, 8, half * WC, WC, mnT, "OGg", NMEM, 2, cons)
                for half in range(NQ):
                    def cons(j, b, half=half):
                        c = half * (WC // 128) + j
                        S.op("act", lambda e: e.activation(out=MKT[:, c * NMEM:(c + 1) * NMEM], in_=pbank(b, NMEM), func=AF.Copy),
                             reads=[f"pb{b}"], writes=["MKT"])
                    proj_fm(w_xk, 0, 8, half * WC, WC, mnT, "OGg", NMEM, NMEM, cons)

                gsmT = [sbt(f"gsm{t_}", [128, 64], F32) for t_ in range(NTv)]
                D1 = sbt("D1", [128, 1024], F32)
                GBm = sbt("GBm", [128, 1024], F32)
                G2 = GBm
                EGC = GBm
                GCTX = []
                for g_ in range(2):
                    GCTX.append((sbt(f"N0_{g_}", [128, 512], BF16), sbt(f"N0T_{g_}", [128, 512], BF16),
                                 sbt(f"WA_{g_}", [128, 1024], BF16), sbt(f"WB_{g_}", [128, 1024], BF16),
                                 [sbt(f"MO{l}_{g_}", [128, 512], BF16) for l in range(3)],
                                 [sbt(f"MOT{l}_{g_}", [128, 512], BF16) for l in range(2)],
                                 sbt(f"V1s_{g_}", [128, 512], BF16), sbt(f"V2s_{g_}", [128, 512], BF16)))
                TTf = sbt("TTf", [128, 1024], BF16)
                RNS = [(D1[:, 0:4 * TBv], "D1"), (GBm[:, 0:4 * TBv], "GBm")]
                PTg = sbt("PTg", [128, 1024], BF16)
                QTdT = [sbt(f"QTd{i_}", [128, 1024], BF16) for i_ in range(2)]
                KtG = sbt("KtG", [128, 1024], BF16)
                VtG = sbt("VtG", [128, 1024], BF16)
                rtl = sbt("rtl", [128, 1024], BF16)
                vnw = sbt("vnw", [128, 1024], BF16)
                PTr = sbt("PTr", [128, 512], BF16)
                KtR = sbt("KtR", [128, 512], BF16)
                identb8 = sbt("identb8", [128, 512], BF16)
                S.op("dve", lambda e: e.tensor_copy(out=identb8[:, :].rearrange("p (h i) -> p h i", i=128),
                                                     in_=ident_f[:, :].unsqueeze(1).to_broadcast([128, 4, 128])),
                     reads=["ident_f"], writes=["identb8"])

            if is_sample:
                AXX = mybir.AxisListType.X
                RES = 7
                cs = sbt("cs_s", [NS, 256], F32)
                S.op("sp", lambda e: e.dma_start(out=cs[:], in_=cst["c_cs_s"].rearrange("(o a) d -> o (a d)", o=1).partition_broadcast(NS)),
                     writes=["cs_s"], dma="cs_s")
                QR = sbt("QR", [NS, 512], F32); KR = sbt("KR", [NS, 512], F32); T1s = sbt("T1s", [NS, 512], F32)
                QKVs = sbt("QKVs", [NS, 3072], F32); QKn = sbt("QKn", [NS, 2048], F32)
                SCc = [sbt(f"SCc{i}", [NS, 3 * 512], F32) for i in range(2)]; CWc = [sbt(f"CWc{i}", [NS, 4 * 512], F32) for i in range(2)]
                ACC = sbt("ACC", [NS, 512], F32); TMPs = sbt("TMPs", [NS, 512], F32)
                gs = sbt("gs", [NS, 64], F32)
                BV = sbt("BV", [NS, 1024], F32); Rr = sbt("Rr", [NS, 1024], F32)
                KMr = sbt("KMr", [NS, 512], F32); KMg = sbt("KMg", [NS, 1024], F32)
                EGd = sbt("EGd", [NS, 128], F32); EGB = sbt("EGB", [128, 128], F32)
                qTr = sbt("qTr", [128, 64], BF16); qTg = sbt("qTg", [128, 128], BF16); kTg = sbt("kTg", [128, 128], F32)
                SRob = sbt("SRob", [128, 1024], BF16); SGob = sbt("SGob", [128, 1024], BF16); Vcb = [sbt("Vcb0", [128, 2048], BF16)] * 2
                ARENA = sbt("ARENA", [128, 8192], F32)
                SRin = [ARENA[:, i * 1024:(i + 1) * 1024] for i in range(2)]
                SGin = [ARENA[:, (2 + i) * 1024:(3 + i) * 1024] for i in range(2)]
                SRo = [ARENA[:, (4 + i) * 1024:(5 + i) * 1024] for i in range(2)]
                SGo = [ARENA[:, (6 + i) * 1024:(7 + i) * 1024] for i in range(2)]
                Kc = [ARENA[:, i * 2048:(i + 1) * 2048] for i in range(2)]
                Vc = [ARENA[:, (2 + i) * 2048:(3 + i) * 2048] for i in range(2)]
                ARENA_RES = [f"{n}{i}" for n in ("SRin", "SGin", "SRo", "SGo") for i in range(2)]
                QD = [sbt("QD0", [128, 1024], F32)] * 2; PR = sbt("PR", [128, 1024], F32)
                SCs = sbt("SCs", [128, 128], F32); Pm = sbt("Pm", [128, 128], BF16); RD = sbt("RD", [128, 64], F32)

                def rope(src0, dst, dres, scale):
                    x = PJ[0:NS, src0:src0 + 512].rearrange("p (h d) -> p h d", d=128)
                    d3 = dst[:, :].rearrange("p (h d) -> p h d", d=128)
                    t3 = T1s[:, :].rearrange("p (h d) -> p h d", d=128)
                    C = cs[:, 0:128].unsqueeze(1).to_broadcast([NS, 4, 128])
                    S.op("dve", lambda e: e.tensor_tensor(out=t3, in0=x, in1=C, op=ALU.mult), reads=["PJ", "cs_s"], writes=["T1s"])
                    S.op("dve", lambda e: e.tensor_tensor(out=d3[:, :, 0:64], in0=x[:, :, 64:128],
                                                          in1=cs[:, 128:192].unsqueeze(1).to_broadcast([NS, 4, 64]), op=ALU.mult),
                         reads=["PJ", "cs_s"], writes=[dres])
                    S.op("dve", lambda e: e.tensor_tensor(out=d3[:, :, 64:128], in0=x[:, :, 0:64],
                                                          in1=cs[:, 192:256].unsqueeze(1).to_broadcast([NS, 4, 64]), op=ALU.mult),
                         reads=["PJ", "cs_s"], writes=[dres])
                    S.op("dve", lambda e: e.scalar_tensor_tensor(out=dst[:, :], in0=dst[:, :], scalar=1.0, in1=T1s[:, :], op0=ALU.mult, op1=ALU.add),
                         reads=[dres, "T1s"], writes=[dres])
                    if scale != 1.0:
                        S.op("dve", lambda e: e.tensor_scalar(out=dst[:, :], in0=dst[:, :], scalar1=scale, scalar2=None, op0=ALU.mult),
                             reads=[dres], writes=[dres])

                def to_fm(src, sres, nh, dst, dres):
                    b = S.bank()

                    def tr(e):
                        ins = None
                        for h in range(nh):
                            ins = e.transpose(pbank(b, NS, h * NS), src[0:NS, h * 128:(h + 1) * 128], ident_f[0:NS, 0:NS])
                        return ins
                    S.op("pe", tr, reads=[sres, "ident_f"], writes=[f"pb{b}"])
                    S.op("act", lambda e: e.activation(out=dst[:, 0:nh * NS], in_=pbank(b, nh * NS), func=AF.Copy), reads=[f"pb{b}"], writes=[dres])

                def sample_mixers():
                    rope(0, QR, "QR", 1.0)
                    rope(512, KR, "KR", RDK ** -0.5)
                    to_fm(QR, "QR", 4, qTr, "qTr")
                    S.op("sp", lambda e: e.dma_start(out=scs[:, 0:2, :], in_=sconv[:, 1:3, :]), dma="scs_a")
                    S.op("sp", lambda e: e.dma_start(out=scs[:, 2, :], in_=PJ[0:NS, 2048:5120]), reads=["PJ"], dma="scs_b")
                    def conv_loads(cchunk):
                        c0 = cchunk * 512
                        pp = cchunk % 2
                        S.op("sp", lambda e: e.dma_start(out=SCc[pp][:, :].rearrange("p (i c) -> p i c", c=512), in_=sconv[:, :, c0:c0 + 512]),
                             writes=[f"SCc{pp}"], dma=f"SCc{pp}")
                        for i in range(4):
                            S.op("sp", lambda e, i=i: e.dma_start(out=CWc[pp][:, i * 512:(i + 1) * 512], in_=convw[i:i + 1, c0:c0 + 512].partition_broadcast(NS)),
                                 writes=[f"CWc{pp}"], dma=f"CWc{pp}")
                    conv_loads(0)
                    for cchunk in range(6):
                        c0 = cchunk * 512
                        pp = cchunk % 2
                        if cchunk + 1 < 6:
                            conv_loads(cchunk + 1)
                        SC_, CW_, rsc, rcw = SCc[pp], CWc[pp], f"SCc{pp}", f"CWc{pp}"
                        S.op("dve", lambda e, SC_=SC_, CW_=CW_: e.tensor_tensor(out=ACC[:, :], in0=SC_[:, 0:512], in1=CW_[:, 0:512], op=ALU.mult),
                             reads=[rsc, rcw], writes=["ACC"])
                        for i in range(1, 4):
                            src = SC_[:, i * 512:(i + 1) * 512] if i < 3 else PJ[0:NS, 2048 + c0: 2048 + c0 + 512]
                            S.op("dve", lambda e, src=src, i=i, CW_=CW_: e.tensor_tensor(out=TMPs[:, 0:512], in0=src, in1=CW_[:, i * 512:(i + 1) * 512], op=ALU.mult),
                                 reads=[rsc, rcw, "PJ"], writes=["TMPs"])
                            S.op("dve", lambda e: e.tensor_tensor(out=ACC[:, :], in0=ACC[:, :], in1=TMPs[:, 0:512], op=ALU.add), reads=["ACC", "TMPs"], writes=["ACC"])
                        S.op("act", lambda e, c0=c0: e.activation(out=QKVs[:, c0:c0 + 512], in_=ACC[:, :], func=AF.Silu), reads=["ACC"], writes=["QKVs"])
                    S.op("dve", lambda e: e.tensor_tensor(out=QKn[:, :], in0=QKVs[:, 0:2048], in1=QKVs[:, 0:2048], op=ALU.mult), reads=["QKVs"], writes=["QKn"])
                    S.op("dve", lambda e: e.tensor_reduce(out=gs[:, 32:48], in_=QKn[:, :].rearrange("p (h d) -> p h d", d=128), axis=AXX, op=ALU.add),
                         reads=["QKn"], writes=["gs"])
                    S.op("dve", lambda e: e.tensor_scalar(out=gs[:, 32:48], in0=gs[:, 32:48], scalar1=EPS, scalar2=None, op0=ALU.add), reads=["gs"], writes=["gs"])
                    S.op("act", lambda e: e.activation(out=gs[:, 32:48], in_=gs[:, 32:48], func=AF.Ln), reads=["gs"], writes=["gs"])
                    S.op("act", lambda e: e.activation(out=gs[:, 32:48], in_=gs[:, 32:48], func=AF.Exp, scale=-0.5), reads=["gs"], writes=["gs"])
                    S.op("dve", lambda e: e.tensor_scalar(out=gs[:, 32:40], in0=gs[:, 32:40], scalar1=GDK ** -0.5, scalar2=None, op0=ALU.mult), reads=["gs"], writes=["gs"])
                    S.op("dve", lambda e: e.tensor_tensor(out=QKn[:, :].rearrange("p (h d) -> p h d", d=128),
                                                          in0=QKVs[:, 0:2048].rearrange("p (h d) -> p h d", d=128),
                                                          in1=gs[:, 32:48].unsqueeze(2).to_broadcast([NS, 16, 128]), op=ALU.mult),
                         reads=["QKVs", "gs"], writes=["QKn"])
                    to_fm(QKn[:, 0:1024], "QKn", 8, qTg, "qTg")
                    to_fm(QKn[:, 1024:2048], "QKn", 8, kTg, "kTg")
                    ba = PJ[0:NS, 5120:5136]
                    S.op("act", lambda e: e.activation(out=gs[:, 0:8], in_=ba[:, 0:8], func=AF.Sigmoid), reads=["PJ", "gs"], writes=["gs"])
                    S.op("dve", lambda e: e.tensor_tensor(out=gs[:, 8:16], in0=ba[:, 8:16], in1=dtb_bc[0:NS, :], op=ALU.add), reads=["PJ", "dtb_bc", "gs"], writes=["gs"])
                    S.op("act", lambda e: e.activation(out=gs[:, 8:16], in_=gs[:, 8:16], func=AF.Exp), reads=["gs"], writes=["gs"])
                    S.op("dve", lambda e: e.tensor_scalar(out=gs[:, 8:16], in0=gs[:, 8:16], scalar1=1.0, scalar2=None, op0=ALU.add), reads=["gs"], writes=["gs"])
                    S.op("act", lambda e: e.activation(out=gs[:, 8:16], in_=gs[:, 8:16], func=AF.Ln), reads=["gs"], writes=["gs"])
                    S.op("dve", lambda e: e.tensor_tensor(out=gs[:, 8:16], in0=gs[:, 8:16], in1=nega[0:NS, :], op=ALU.mult), reads=["gs", "nega"], writes=["gs"])
                    S.op("act", lambda e: e.activation(out=gs[:, 16:24], in_=gs[:, 8:16], func=AF.Exp), reads=["gs"], writes=["gs"])
                    S.op("dve", lambda e: e.scalar_tensor_tensor(out=gs[:, 24:32], in0=gs[:, 16:24], scalar=-1.0, in1=gs[:, 0:8], op0=ALU.mult, op1=ALU.mult),
                         reads=["gs"], writes=["gs"])
                    S.op("dve", lambda e: e.tensor_tensor(out=BV[:, :].rearrange("p (h d) -> p h d", d=128),
                                                          in0=QKVs[:, 2048:3072].rearrange("p (h d) -> p h d", d=128),
                                                          in1=gs[:, 0:8].unsqueeze(2).to_broadcast([NS, 8, 128]), op=ALU.mult),
                         reads=["QKVs", "gs"], writes=["BV"])
                    S.op("dve", lambda e: e.tensor_tensor(out=EGd[:, :].rearrange("p (b h) -> p b h", h=8),
                                                          in0=ident_f[0:NS, 0:NS].unsqueeze(2).to_broadcast([NS, NS, 8]),
                                                          in1=gs[:, 16:24].unsqueeze(1).to_broadcast([NS, NS, 8]), op=ALU.mult),
                         reads=["ident_f", "gs"], writes=["EGd"])
                    bq = S.bank()
                    S.op("pe", lambda e: e.matmul(pbank(bq, 128), lhsT=ones_f[0:NS, :], rhs=EGd[:, :], start=True, stop=True),
                         reads=["ones_f", "EGd"], writes=[f"pb{bq}"])
                    S.op("act", lambda e: e.activation(out=EGB[:, :], in_=pbank(bq, 128), func=AF.Copy), reads=[f"pb{bq}"], writes=["EGB"])
                    S.reserved = {RES}

                    def loads(b):
                        p = b % 2
                        S.op("sp", lambda e: e.dma_start(out=SRin[p][:, :].rearrange("d (h v) -> d h v", v=RDV), in_=sret[b].rearrange("h d v -> d h v")),
                             writes=[f"SRin{p}"], dma=f"SRin{p}")
                        S.op("sp", lambda e: e.dma_start(out=SGin[p][:, :].rearrange("d (h v) -> d h v", v=GDV), in_=sgdn[b].rearrange("h d v -> d h v")),
                             writes=[f"SGin{p}"], dma=f"SGin{p}")
                    loads(0)
                    for b in range(NS):
                        p = b % 2
                        if b + 1 < NS:
                            loads(b + 1)
                        eb = ident_f[0:NS, b:b + 1]
                        S.op("dve", lambda e, eb=eb: e.tensor_scalar(out=KMr[:, :], in0=KR[:, :], scalar1=eb, scalar2=None, op0=ALU.mult), reads=["KR", "ident_f"], writes=["KMr"])
                        S.op("dve", lambda e, eb=eb: e.tensor_scalar(out=KMg[:, :], in0=QKn[:, 1024:2048], scalar1=eb, scalar2=None, op0=ALU.mult),
                             reads=["QKn", "ident_f"], writes=["KMg"])
                        def ret_chain(b=b, p=p):
                            yield
                            while S.bank_i % 2 != 0:
                                S.bank()
                            b0 = S.bank(); b1 = S.bank()
                            if b1 != b0 + 1:
                                while S.bank_i % 2 != 0:
                                    S.bank()
                                b0 = S.bank(); b1 = S.bank()

                            def mm_a(e, b0=b0):
                                ins = None
                                for h in range(RH):
                                    ins = e.matmul(PS[:, b0 * 512 + h * 256: b0 * 512 + (h + 1) * 256], lhsT=KMr[0:NS, h * 128:(h + 1) * 128],
                                                   rhs=PJ[0:NS, 1024 + h * 256: 1024 + (h + 1) * 256], start=True, stop=True)
                                return ins
                            S.op("pe", mm_a, reads=["KMr", "PJ"], writes=[f"pb{b0}", f"pb{b1}"])
                            for h in range(RH):
                                S.op("dve", lambda e, h=h, b0=b0, p=p: e.scalar_tensor_tensor(
                                    out=SRo[p][:, h * 256:(h + 1) * 256], in0=SRin[p][:, h * 256:(h + 1) * 256], scalar=float(_GAM[h]),
                                    in1=PS[:, b0 * 512 + h * 256: b0 * 512 + (h + 1) * 256], op0=ALU.mult, op1=ALU.add),
                                    reads=[f"SRin{p}", f"pb{b0}", f"pb{b1}"], writes=[f"SRo{p}"])

                            yield
                            S.op("act", lambda e, p=p: e.activation(out=SRob[:, :], in_=SRo[p][:, :], func=AF.Copy), reads=[f"SRo{p}"], writes=["SRob"])

                            def mm_b(e, b=b, p=p):
                                ins = None
                                for h in range(RH):
                                    for c in range(2):
                                        ins = e.matmul(pbank(RES, 1, (h * 2 + c) * NS + b), lhsT=SRob[:, h * 256 + c * 128: h * 256 + (c + 1) * 128],
                                                       rhs=qTr[:, h * NS + b: h * NS + b + 1], start=True, stop=True)
                                return ins
                            S.op("pe", mm_b, reads=["SRob", "qTr"], writes=[f"pb{RES}"])
                            S.op("act", lambda e, b=b, p=p: e.dma_start(out=srs[b].rearrange("h d v -> d h v"), in_=SRo[p][:, :].rearrange("d (h v) -> d h v", v=RDV)),
                                 reads=[f"SRo{p}"], dma=f"SRo{p}o")
                            yield
                        def gdn_chain(b=b, p=p):
                            yield
                            while S.bank_i % 2 != 0:
                                S.bank()
                            c0_ = S.bank(); c1_ = S.bank()
                            if c1_ != c0_ + 1:
                                while S.bank_i % 2 != 0:
                                    S.bank()
                                c0_ = S.bank(); c1_ = S.bank()

                            def mm_1(e, c0_=c0_, p=p):
                                ins = None
                                for h in range(GH):
                                    ins = e.matmul(PS[0:NS, c0_ * 512 + h * 128: c0_ * 512 + (h + 1) * 128], lhsT=kTg[:, h * NS:(h + 1) * NS],
                                                   rhs=SGin[p][:, h * 128:(h + 1) * 128], start=True, stop=True)
                                return ins
                            S.op("pe", mm_1, reads=["kTg", f"SGin{p}"], writes=[f"pb{c0_}", f"pb{c1_}"])
                            S.op("dve", lambda e, c0_=c0_: e.tensor_tensor(out=Rr[:, :].rearrange("p (h d) -> p h d", d=128),
                                                                           in0=PS[0:NS, c0_ * 512: c0_ * 512 + 1024].rearrange("p (h d) -> p h d", d=128),
                                                                           in1=gs[:, 24:32].unsqueeze(2).to_broadcast([NS, 8, 128]), op=ALU.mult),
                                 reads=[f"pb{c0_}", f"pb{c1_}", "gs"], writes=["Rr"])
                            S.op("dve", lambda e: e.tensor_tensor(out=Rr[:, :], in0=Rr[:, :], in1=BV[:, :], op=ALU.add), reads=["Rr", "BV"], writes=["Rr"])
                            yield
                            while S.bank_i % 2 != 0:
                                S.bank()
                            d0_ = S.bank(); d1_ = S.bank()
                            if d1_ != d0_ + 1:
                                while S.bank_i % 2 != 0:
                                    S.bank()
                                d0_ = S.bank(); d1_ = S.bank()

                            def mm_2(e, d0_=d0_):
                                ins = None
                                for h in range(GH):
                                    ins = e.matmul(PS[:, d0_ * 512 + h * 128: d0_ * 512 + (h + 1) * 128], lhsT=KMg[0:NS, h * 128:(h + 1) * 128],
                                                   rhs=Rr[0:NS, h * 128:(h + 1) * 128], start=True, stop=True)
                                return ins
                            S.op("pe", mm_2, reads=["KMg", "Rr"], writes=[f"pb{d0_}", f"pb{d1_}"])
                            S.op("dve", lambda e, b=b, p=p: e.tensor_tensor(out=SGo[p][:, :].rearrange("p (h d) -> p h d", d=128),
                                                                          in0=SGin[p][:, :].rearrange("p (h d) -> p h d", d=128),
                                                                          in1=EGB[:, b * 8:(b + 1) * 8].unsqueeze(2).to_broadcast([128, 8, 128]), op=ALU.mult),
                                 reads=[f"SGin{p}", "EGB"], writes=[f"SGo{p}"])
                            S.op("dve", lambda e, d0_=d0_, p=p: e.tensor_tensor(out=SGo[p][:, :], in0=SGo[p][:, :], in1=PS[:, d0_ * 512: d0_ * 512 + 1024], op=ALU.add),
                                 reads=[f"SGo{p}", f"pb{d0_}", f"pb{d1_}"], writes=[f"SGo{p}"])

                            yield
                            S.op("act", lambda e, p=p: e.activation(out=SGob[:, :], in_=SGo[p][:, :], func=AF.Copy), reads=[f"SGo{p}"], writes=["SGob"])

                            def mm_3(e, b=b, p=p):
                                ins = None
                                for h in range(GH):
                                    ins = e.matmul(pbank(RES, 1, 128 + h * NS + b), lhsT=SGob[:, h * 128:(h + 1) * 128],
                                                   rhs=qTg[:, h * NS + b: h * NS + b + 1], start=True, stop=True)
                                return ins
                            S.op("pe", mm_3, reads=["SGob", "qTg"], writes=[f"pb{RES}"])
                            S.op("act", lambda e, b=b, p=p: e.dma_start(out=sgs[b].rearrange("h d v -> d h v"), in_=SGo[p][:, :].rearrange("d (h v) -> d h v", v=GDV)),
                                 reads=[f"SGo{p}"], dma=f"SGo{p}o")
                            yield
                        _gens = [ret_chain(), gdn_chain()]
                        while _gens:
                            for _g in list(_gens):
                                try:
                                    next(_g)
                                except StopIteration:
                                    _gens.remove(_g)
                    S.op("act", lambda e: e.activation(out=OR[:, 0:128], in_=pbank(RES, 128), func=AF.Copy), reads=[f"pb{RES}"], writes=["XOR1"])
                    S.op("act", lambda e: e.activation(out=OG[:, 0:128], in_=pbank(RES, 128, 128), func=AF.Copy), reads=[f"pb{RES}"], writes=["OG"])
                    S.reserved = set()

                def sample_xattn(XQ, OX):
                    S.reserved = {RES}

                    def loads(b):
                        p = b % 2
                        S.op("sp", lambda e: e.dma_start(out=Kc[p][:, :].rearrange("m (t d) -> m t d", d=D), in_=cmk[b].rearrange("(t m) d -> m t d", m=128)),
                             writes=[f"Kc{p}"] + ARENA_RES, dma=f"Kc{p}")
                        S.op("sp", lambda e: e.dma_start(out=Vc[p][:, :].rearrange("m (t d) -> m t d", d=D), in_=cmv[b].rearrange("(t m) d -> m t d", m=128)),
                             writes=[f"Vc{p}"] + ARENA_RES, dma=f"Vc{p}")

                    qinfo = {}

                    def qstage(b):
                        p = b % 2
                        S.op("pool", lambda e: e.tensor_tensor(out=QD[p][:, :].rearrange("p (c d) -> p c d", d=128),
                                                               in0=ident_f[:, :].unsqueeze(1).to_broadcast([128, 8, 128]),
                                                               in1=XQ[:, :].rearrange("p (c t) -> p c t", t=NS)[:, :, b:b + 1].to_broadcast([128, 8, 128]), op=ALU.mult),
                             reads=["ident_f", "ORg"], writes=["QD0"])
                        while S.bank_i % 2 != 0:
                            S.bank()
                        b0 = S.bank(); b1 = S.bank()
                        if b1 != b0 + 1:
                            while S.bank_i % 2 != 0:
                                S.bank()
                            b0 = S.bank(); b1 = S.bank()

                        def mm_q(e):
                            e.matmul(pbank(b0), lhsT=ones_f[:], rhs=QD[p][:, 0:512], start=True, stop=True)
                            return e.matmul(pbank(b1), lhsT=ones_f[:], rhs=QD[p][:, 512:1024], start=True, stop=True)
                        S.op("pe", mm_q, reads=["ones_f", "QD0"], writes=[f"pb{b0}", f"pb{b1}"])
                        qinfo[b] = (b0, b1)
                    loads(0)
                    qstage(0)
                    for b in range(NS):
                        p = b % 2
                        if b + 1 < NS:
                            loads(b + 1)
                            qstage(b + 1)
                        b0, b1 = qinfo[b]
                        S.op("act", lambda e, p=p: e.activation(out=Vcb[p][:, :], in_=Vc[p][:, :], func=AF.Copy), reads=[f"Vc{p}"], writes=["Vcb0"])
                        for mt in range(2):
                            S.op("dve", lambda e, mt=mt, p=p, b0=b0: e.tensor_tensor(out=PR[:, :], in0=PS[:, b0 * 512: b0 * 512 + 1024],
                                                                                   in1=Kc[p][:, mt * D:(mt + 1) * D], op=ALU.mult),
                                 reads=[f"pb{b0}", f"pb{b1}", f"Kc{p}"], writes=["PR"])
                            S.op("dve", lambda e, mt=mt, b=b: e.tensor_reduce(out=SCs[:, b * 8 + mt * 4: b * 8 + mt * 4 + 4],
                                                                            in_=PR[:, :].rearrange("p (h d) -> p h d", d=XHD), axis=AXX, op=ALU.add),
                                 reads=["PR"], writes=["SCs"])
                        S.op("act", lambda e, b=b: e.activation(out=Pm[:, b * 8:(b + 1) * 8], in_=SCs[:, b * 8:(b + 1) * 8], func=AF.Exp, scale=XHD ** -0.5),
                             reads=["SCs"], writes=["Pm"])

                        def mm_o(e, b=b, p=p):
                            ins = None
                            for ch in range(8):
                                h = ch // 2
                                for mt in range(2):
                                    ins = e.matmul(pbank(RES, 1, ch * NS + b), lhsT=Vcb[p][:, mt * D + ch * 128: mt * D + (ch + 1) * 128],
                                                   rhs=Pm[:, b * 8 + mt * 4 + h: b * 8 + mt * 4 + h + 1], start=(mt == 0), stop=(mt == 1))
                            for mt in range(2):
                                ins = e.matmul(pbank(RES, 4, 128 + b * 4), lhsT=ones_b[:], rhs=Pm[:, b * 8 + mt * 4: b * 8 + mt * 4 + 4],
                                               start=(mt == 0), stop=(mt == 1))
                            return ins
                        S.op("pe", mm_o, reads=["Vcb0", "Pm", "ones_b"], writes=[f"pb{RES}"])
                    S.op("dve", lambda e: e.reciprocal(out=RD[:, :], in_=pbank(RES, 64, 128)), reads=[f"pb{RES}"], writes=["RD"])
                    for ch in range(8):
                        h = ch // 2
                        S.op("dve", lambda e, ch=ch, h=h: e.tensor_tensor(out=OX[:, ch * NS:(ch + 1) * NS], in0=pbank(RES, NS, ch * NS),
                                                                        in1=RD[:, :].rearrange("p (b h) -> p b h", h=4)[:, :, h], op=ALU.mult),
                             reads=[f"pb{RES}", "RD"], writes=["OGg"])
                    S.reserved = set()

            def load_x(blk):
                Xn = XOR[blk % 2]; xr = f"XOR{blk % 2}"
                t0 = blk * TBv
                if is_sample:
                    S.op("sp", lambda e: e.dma_start(out=Xn[0:NS, 0:D], in_=xs), writes=[xr], dma=xr)
                else:
                    S.op("sp", lambda e: e.dma_start(out=Xn[:, :].rearrange("p (a d) -> p a d", d=D),
                                                     in_=xp[t0:t0 + TBv, :].rearrange("(a p) d -> p a d", p=128)),
                         writes=[xr], dma=xr)

            def pre_norm(blk):
                Xn = XOR[blk % 2]; xr = f"XOR{blk % 2}"
                for tau in range(NTv):
                    norm_transpose(Xn[0:tw, tau * D:(tau + 1) * D], xr, gmix_c, "gmix_c", hnT, "hnT", tw, tau * tw, TBv)

            def load_tabs(blk):
                t0 = blk * TBv
                S.op("sp", lambda e: e.dma_start(out=cosb[:], in_=cst["c_cos"][:, t0:t0 + TBv]), writes=["cosb"], dma="cosb")
                S.op("sp", lambda e: e.dma_start(out=sinb[:], in_=cst["c_sin"][:, t0:t0 + TBv]), writes=["sinb"], dma="sinb")

            def do_block(blk):
                if stage < 1:
                    return
                Xb = XOR[blk % 2]; xres = f"XOR{blk % 2}"
                OR = XOR[(blk + 1) % 2][:, 0:8 * TBv]; or_res = f"XOR{(blk + 1) % 2}"
                t0 = blk * TBv
                if blk == 0:
                    load_x(0)
                if not is_sample:
                    load_tabs(blk)
                if blk == 0:
                    pre_norm(0)

                pend = []
                qkv_done = [False]
                LATE = ("ga", "gb")
                main_tiles = W_IN_TILES if is_sample else [t_ for t_ in W_IN_TILES if t_[0] not in LATE]
                late_tiles = [] if is_sample else [t_ for t_ in W_IN_TILES if t_[0] in LATE]
                for (kind, c0, ncols) in main_tiles:
                    if kinds is not None and kind not in kinds:
                        continue
                    if kind == "z" and not is_sample and pend is not None and (pend or not qkv_done[0]):
                        for p_ in pend:
                            p_()
                        del pend[:]
                        qkv_done[0] = True
                        for (ssb, sres, dstb, dres) in ((OR, or_res, GQ, "GQ"), (OG, "OG", GK, "GK")):
                            S.op("act", lambda e, ssb=ssb: e.activation(out=ssb[:, :], in_=ssb[:, :], func=AF.Ln), reads=[sres], writes=[sres])
                            S.op("act", lambda e, ssb=ssb: e.activation(out=ssb[:, :], in_=ssb[:, :], func=AF.Exp, scale=-0.5), reads=[sres], writes=[sres])
                            S.op("dve", lambda e, ssb=ssb, dstb=dstb: e.tensor_tensor(out=dstb[:, :], in0=dstb[:, :], in1=ssb[:, :], op=ALU.mult),
                                 reads=[sres, dres], writes=[dres, "FA"])
                    if is_sample and kind in PJ_OFF:
                        pj0 = PJ_OFF[kind][1] + (c0 - PJ_OFF[kind][0])

                        def cons(tau, b, pj0=pj0, ncols=ncols):
                            S.op("act", lambda e: e.activation(out=PJ[0:NS, pj0:pj0 + ncols], in_=pbank(b, ncols)[0:NS, :], func=AF.Copy),
                                 reads=[f"pb{b}"], writes=["PJ"])
                        proj_tm(w_in, 0, 8, c0, ncols, hnT, "hnT", TBv, 1, cons, tw=tw)
                        continue
                    if kind in ("rq", "rk"):
                        dstb = RQ if kind == "rq" else RK

                        def cons(j0, b, dstb=dstb, kind=kind, cb=(c0 - (0 if kind == "rq" else 512)) // 128):
                            j = cb + j0
                            tmpA, rA = nxt("tmpA"); tmpB, rB = nxt("tmpB"); tmpb, rb = nxt("tmpb")
                            S.op("act", lambda e: e.activation(out=tmpb[:], in_=pbank(b, TBv), func=AF.Copy), reads=[f"pb{b}"], writes=[rb])
                            b2 = S.bank()
                            S.op("pe", lambda e: e.matmul(pbank(b2, TBv), lhsT=perm_b[:], rhs=tmpb[:], start=True, stop=True),
                                 reads=[rb, "perm_b"], writes=[f"pb{b2}"])
                            S.op("dve", lambda e: e.tensor_tensor(out=tmpA[:], in0=pbank(b, TBv), in1=cosb[:], op=ALU.mult),
                                 reads=[f"pb{b}", "cosb"], writes=[rA])
                            S.op("dve", lambda e: e.tensor_tensor(out=tmpB[:], in0=pbank(b2, TBv), in1=sinb[:], op=ALU.mult),
                                 reads=[f"pb{b2}", "sinb"], writes=[rB])
                            S.op("dve", lambda e: e.tensor_tensor(out=dstb[:, j * TBv:(j + 1) * TBv], in0=tmpA[:], in1=tmpB[:], op=ALU.add),
                                 reads=[rA, rB], writes=[kind.upper()])
                            if kind == "rq":
                                S.op("dve", lambda e: e.tensor_tensor(
                                    out=RQd[:, j * TBv:(j + 1) * TBv].rearrange("p (a i) -> p a i", i=128),
                                    in0=RQ[:, j * TBv:(j + 1) * TBv].rearrange("p (a i) -> p a i", i=128),
                                    in1=rqd[:, j * 128:(j + 1) * 128].unsqueeze(1).to_broadcast([128, NTv, 128]), op=ALU.mult),
                                    reads=["RQ", "rqd"], writes=["RQd"])
                        proj_fm(w_in, 0, 8, c0, ncols, hnT, "hnT", TBv, TBv, cons)
                    elif kind == "rv":
                        hv = c0 - 1024

                        def cons(tau, b, hv=hv, ncols=ncols):
                            S.op("act", lambda e: e.activation(out=VtR[:, tau * 1024 + hv: tau * 1024 + hv + ncols], in_=pbank(b, ncols), func=AF.Copy),
                                 reads=[f"pb{b}"], writes=["VtR"])
                        proj_tm(w_in, 0, 8, c0, ncols, hnT, "hnT", TBv, NTv, cons)
                    elif kind in ("rg", "z", "ga", "gb"):
                        base = {"rg": 2048, "z": 6144, "ga": 7184, "gb": 8208}[kind]
                        dstb = {"rg": RG, "z": Z, "ga": GA, "gb": GB}[kind]
                        fn = AF.Silu if kind in ("rg", "z") else AF.Tanh
                        fsc = 1.0 if kind in ("rg", "z") else 0.5
                        cb = (c0 - base) // 128

                        def cons(j, b, dstb=dstb, fn=fn, cb=cb, kind=kind, fsc=fsc):
                            S.op("act", lambda e: e.activation(out=dstb[:, (cb + j) * TBv:(cb + j + 1) * TBv], in_=pbank(b, TBv), func=fn, scale=fsc),
                                 reads=[f"pb{b}"], writes=[kind.upper()])
                        proj_fm(w_in, 0, 8, c0, ncols, hnT, "hnT", TBv, TBv, cons)
                    elif kind == "ba":
                        def cons(tau, b):
                            S.op("dve", lambda e: e.tensor_copy(out=BA[:, tau * 16:(tau + 1) * 16], in_=pbank(b, 16)), reads=[f"pb{b}"], writes=["BA"])
                        proj_tm(w_in, 0, 8, c0, ncols, hnT, "hnT", TBv, NTv, cons)
                    else:
                        cb = (c0 - 3072) // 128

                        def cons(j, b, cb=cb, blk=blk):
                            cc = cb + j
                            tmpb, rb = nxt("tmpb")
                            CACC, rCA = nxt("CACC"); CIN, rCI = nxt("CIN")
                            while len(pend) > 1:
                                pend.pop(0)()
                            S.op("act", lambda e: e.activation(out=CIN[:, 0:3], in_=HALO[:, cc * 3:cc * 3 + 3], func=AF.Copy), reads=["HALO"], writes=[rCI])
                            S.op("act", lambda e: e.activation(out=CIN[:, 3:3 + TBv], in_=pbank(b, TBv), func=AF.Copy), reads=[f"pb{b}", rCI], writes=[rCI])
                            S.op("act", lambda e: e.activation(out=HALO[:, cc * 3:cc * 3 + 3], in_=CIN[:, TBv:TBv + 3], func=AF.Copy), reads=[rCI], writes=["HALO"])
                            S.op("dve", lambda e: e.tensor_scalar(out=CACC[:], in0=CIN[:, 0:TBv], scalar1=cw[:, cc:cc + 1], scalar2=None, op0=ALU.mult),
                                 reads=[rCI, "cw"], writes=[rCA])
                            for i in range(1, 4):
                                S.op("dve", lambda e, i=i: e.scalar_tensor_tensor(out=CACC[:], in0=CIN[:, i:i + TBv], scalar=cw[:, i * 24 + cc:i * 24 + cc + 1],
                                                                                 in1=CACC[:], op0=ALU.mult, op1=ALU.add),
                                     reads=[rCI, "cw", rCA], writes=[rCA])

                            def stage_b():
                                if cc >= 16:
                                    S.op("act", lambda e: e.activation(out=GV[:, (cc - 16) * TBv:(cc - 15) * TBv], in_=CACC[:], func=AF.Silu), reads=[rCA], writes=["GV", "FA"])
                                    return
                                dstb, dres, hh = (GQ, "GQ", cc) if cc < 8 else (GK, "GK", cc - 8)
                                ssb, sres = (OR, or_res) if cc < 8 else (OG, "OG")
                                dsl = dstb[:, hh * TBv:(hh + 1) * TBv]
                                S.op("act", lambda e: e.activation(out=dsl, in_=CACC[:], func=AF.Silu), reads=[rCA], writes=[dres, "FA"])
                                S.op("act", lambda e: e.activation(out=tmpb[:], in_=dsl, func=AF.Square), reads=[dres], writes=[rb])
                                b2 = S.bank()
                                lh = ones128_b if cc < 8 else ones_b
                                S.op("pe", lambda e: e.matmul(pbank(b2, TBv), lhsT=lh[:], rhs=tmpb[:], start=True, stop=True),
                                     reads=[rb, "ones_b", "ones128_b"], writes=[f"pb{b2}"])
                                eps = EPS * 128 if cc < 8 else EPS
                                S.op("dve", lambda e: e.tensor_scalar(out=ssb[:, hh * TBv:(hh + 1) * TBv], in0=pbank(b2, TBv), scalar1=eps, scalar2=None, op0=ALU.add),
                                     reads=[f"pb{b2}"], writes=[sres])
                            pend.append(stage_b)
                        proj_fm(w_in, 0, 8, c0, ncols, hnT, "hnT", TBv, TBv, cons)
                if blk == nblk_v - 1 and not is_sample:
                    for i in range(3):
                        S.op("sp", lambda e, i=i: e.dma_start(out=scp[i].rearrange("(c p) -> p c", p=128),
                                                               in_=HALO[:, :].rearrange("p (c i) -> p c i", i=3)[:, :, i],
                                                               allow_slow_non_contiguous=True),
                             reads=["HALO"], dma="scp")

                def gdn_small(tau):
                    gsm = gsmT[tau]; gres = f"gsm{tau}"
                    ba = BA[:, tau * 16:(tau + 1) * 16]
                    S.op("act", lambda e, ba=ba: e.activation(out=gsm[:, 0:8], in_=ba[:, 0:8], func=AF.Sigmoid), reads=["BA"], writes=[gres])
                    S.op("dve", lambda e, ba=ba: e.tensor_tensor(out=gsm[:, 8:16], in0=ba[:, 8:16], in1=dtb_bc[:], op=ALU.add), reads=["BA", "dtb_bc", gres], writes=[gres])
                    S.op("act", lambda e: e.activation(out=gsm[:, 8:16], in_=gsm[:, 8:16], func=AF.Exp), reads=[gres], writes=[gres])
                    S.op("dve", lambda e: e.tensor_scalar(out=gsm[:, 8:16], in0=gsm[:, 8:16], scalar1=1.0, scalar2=None, op0=ALU.add), reads=[gres], writes=[gres])
                    S.op("act", lambda e: e.activation(out=gsm[:, 8:16], in_=gsm[:, 8:16], func=AF.Ln), reads=[gres], writes=[gres])
                    S.op("dve", lambda e: e.tensor_tensor(out=gsm[:, 8:16], in0=gsm[:, 8:16], in1=nega[:], op=ALU.mult), reads=[gres, "nega"], writes=[gres])
                    bq = S.bank()
                    S.op("pe", lambda e, bq=bq: e.matmul(pbank(bq, 8), lhsT=tri_f[:], rhs=gsm[:, 8:16], start=True, stop=True),
                         reads=["tri_f", gres], writes=[f"pb{bq}"])
                    S.op("dve", lambda e, bq=bq: e.tensor_copy(out=gsm[:, 16:24], in_=pbank(bq, 8)), reads=[f"pb{bq}", gres], writes=[gres])
                    S.op("act", lambda e: e.activation(out=gsm[:, 24:32], in_=gsm[:, 16:24], func=AF.Exp), reads=[gres], writes=[gres])
                    S.op("dve", lambda e: e.tensor_scalar(out=gsm[:, 24:32], in0=gsm[:, 24:32], scalar1=-1.0, scalar2=None, op0=ALU.mult), reads=[gres], writes=[gres])
                for tau in (range(NTv) if not is_sample else []):
                    gdn_small(tau)

                if stage < 2:
                    return
                if is_sample:
                    sample_mixers()
                def ret_gen():
                    for tau in (range(NTv) if not is_sample else []):
                        tc0 = tau * 128
                        yield
                        b = S.bank()

                        def mm_sc(e, b=b, tc0=tc0):
                            ins = None
                            for h in range(RH):
                                ins = e.matmul(pbank(b, 128, h * 128), lhsT=RK[:, h * TBv + tc0: h * TBv + tc0 + 128],
                                               rhs=RQ[:, h * TBv + tc0: h * TBv + tc0 + 128], start=True, stop=True)
                            return ins
                        S.op("pe", mm_sc, reads=["RK", "RQ"], writes=[f"pb{b}"])
                        S.op("dve", lambda e, b=b: e.tensor_tensor(out=PTr[:], in0=pbank(b), in1=rdt[:], op=ALU.mult),
                             reads=[f"pb{b}", "rdt"], writes=["PTr"])
                        yield
                        b3 = S.bank()

                        def tr_k(e, b3=b3, tc0=tc0):
                            ins = None
                            for h in range(RH):
                                ins = e.transpose(pbank_bf(b3, 128, h * 128), RK[:, h * TBv + tc0: h * TBv + tc0 + 128], ident_b[:])
                            return ins
                        S.op("pe", tr_k, reads=["RK", "ident_b"], writes=[f"pb{b3}"])
                        S.op("dve", lambda e, b3=b3: e.tensor_tensor(out=KtR[:, :].rearrange("p (h d) -> p h d", d=128),
                                                                   in0=pbank_bf(b3, 512).rearrange("p (h d) -> p h d", d=128),
                                                                   in1=rkd[:, :].unsqueeze(2).to_broadcast([128, RH, 128]), op=ALU.mult),
                             reads=[f"pb{b3}", "rkd"], writes=["KtR"])
                        for hp in range(2):
                            yield
                            b2 = S.bank()

                            def mm_o(e, b2=b2, hp=hp, tau=tau, tc0=tc0):
                                ins = None
                                for hh in range(2):
                                    h = hp * 2 + hh
                                    for c in range(2):
                                        o = pbank(b2, 128, (hh * 2 + c) * 128)
                                        e.matmul(o, lhsT=SRb[:, h * RDV + c * 128: h * RDV + (c + 1) * 128],
                                                 rhs=RQd[:, h * TBv + tc0: h * TBv + tc0 + 128], start=True, stop=False)
                                        ins = e.matmul(o, lhsT=VtR[:, tau * 1024 + h * RDV + c * 128: tau * 1024 + h * RDV + (c + 1) * 128],
                                                       rhs=PTr[:, h * 128:(h + 1) * 128], start=False, stop=True)
                                return ins
                            S.op("pe", mm_o, reads=["SRb", "RQd", "VtR", "PTr"], writes=[f"pb{b2}"])
                            dst = OR[:, hp * 4 * TBv:(hp + 1) * 4 * TBv].rearrange("p (c t) -> p c t", t=TBv)[:, :, tc0:tc0 + 128]
                            S.op("act", lambda e, b2=b2, dst=dst: e.activation(out=dst, in_=pbank(b2).rearrange("p (c t) -> p c t", t=128), func=AF.Copy),
                                 reads=[f"pb{b2}"], writes=[or_res])
                        for hp in range(2):
                            yield
                            b4 = S.bank()

                            def mm_s(e, b4=b4, hp=hp, tau=tau):
                                ins = None
                                for hh in range(2):
                                    h = hp * 2 + hh
                                    ins = e.matmul(pbank(b4, 256, hh * 256), lhsT=KtR[:, h * 128:(h + 1) * 128],
                                                   rhs=VtR[:, tau * 1024 + h * RDV: tau * 1024 + (h + 1) * RDV], start=True, stop=True)
                                return ins
                            S.op("pe", mm_s, reads=["KtR", "VtR"], writes=[f"pb{b4}"])
                            for hh in range(2):
                                h = hp * 2 + hh
                                S.op("dve", lambda e, b4=b4, hh=hh, h=h: e.scalar_tensor_tensor(
                                    out=SR[:, h * RDV:(h + 1) * RDV], in0=SR[:, h * RDV:(h + 1) * RDV], scalar=float(_GAM[h] ** 128),
                                    in1=pbank(b4, 256, hh * 256), op0=ALU.mult, op1=ALU.add),
                                    reads=[f"pb{b4}", "SR"], writes=["SR"])
                            S.op("act", lambda e, hp=hp: e.activation(out=SRb[:, hp * 512:(hp + 1) * 512], in_=SR[:, hp * 512:(hp + 1) * 512], func=AF.Copy),
                                 reads=["SR"], writes=["SRb"])
                    yield
                _retg = ret_gen()

                if stage < 3:
                    return
                def gdn_tile(tau):
                    gsm = gsmT[tau]; gres = f"gsm{tau}"
                    tc0 = tau * 128
                    QTd = QTdT[tau % 2]; qres = f"QTd{tau % 2}"
                    def setup():
                        yield
                        S.op("dve", lambda e: e.tensor_tensor(out=G2[:, :].rearrange("p (h i) -> p h i", i=128),
                                                              in0=tri_f[:, :].unsqueeze(1).to_broadcast([128, 8, 128]),
                                                              in1=gsm[:, 8:16].unsqueeze(2).to_broadcast([128, 8, 128]), op=ALU.mult),
                             reads=["tri_f", gres], writes=["GBm"])
                        yield
                        while S.bank_i % 2 != 0:
                            S.bank()
                        bg = S.bank(); bg2 = S.bank()

                        def mm_g(e, bg=bg, bg2=bg2):
                            e.matmul(pbank(bg), lhsT=ones_f[:], rhs=G2[:, 0:512], start=True, stop=True)
                            return e.matmul(pbank(bg2), lhsT=ones_f[:], rhs=G2[:, 512:1024], start=True, stop=True)
                        S.op("pe", mm_g, reads=["ones_f", "GBm"], writes=[f"pb{bg}", f"pb{bg2}"])
                        gbc = PS[:, bg * 512: bg * 512 + 1024]
                        S.op("dve", lambda e, gbc=gbc: e.tensor_tensor(out=D1[:, :].rearrange("p (h i) -> p h i", i=128),
                                                                       in0=gbc.rearrange("p (h i) -> p h i", i=128),
                                                                       in1=gsm[:, 16:24].unsqueeze(2).to_broadcast([128, 8, 128]), op=ALU.subtract),
                             reads=[f"pb{bg}", f"pb{bg2}", gres], writes=["D1"])
                        S.op("act", lambda e, gbc=gbc: e.activation(out=EGC[:], in_=gbc, func=AF.Exp), reads=[f"pb{bg}", f"pb{bg2}"], writes=["GBm"])
                        yield
                        S.op("dve", lambda e, tc0=tc0: e.tensor_tensor(out=QTd[:, :].rearrange("p (h i) -> p h i", i=128),
                                                                       in0=GQ[:, :].rearrange("p (h t) -> p h t", t=TBv)[:, :, tc0:tc0 + 128],
                                                                       in1=EGC[:, :].rearrange("p (h i) -> p h i", i=128), op=ALU.mult),
                             reads=["GQ", "GBm"], writes=[qres])
                        yield
                        S.op("act", lambda e: e.activation(out=gsm[:, 40:48], in_=EGC[:, :].rearrange("p (h i) -> p h i", i=128)[:, :, 127], func=AF.Copy),
                             reads=["GBm", gres], writes=[gres])
                        yield
                        S.op("act", lambda e: e.activation(out=gsm[:, 32:40], in_=D1[:, :].rearrange("p (h i) -> p h i", i=128)[:, :, 127], func=AF.Exp),
                             reads=["D1", gres], writes=[gres])
                        yield
                        S.op("dve", lambda e: e.tensor_tensor(out=D1[:, :].rearrange("p (h i) -> p h i", i=128),
                                                              in0=D1[:, :].rearrange("p (h i) -> p h i", i=128),
                                                              in1=mincl[:, :].unsqueeze(1).to_broadcast([128, 8, 128]), op=ALU.add),
                             reads=["D1", "mincl"], writes=["D1"])
                        yield
                        S.op("act", lambda e: e.activation(out=D1[:], in_=D1[:], func=AF.Exp), reads=["D1"], writes=["D1"])
                        yield
                        S.op("dve", lambda e: e.tensor_tensor(out=GBm[:, :].rearrange("p (h i) -> p h i", i=128),
                                                              in0=D1[:, :].rearrange("p (h i) -> p h i", i=128),
                                                              in1=strict[:, :].unsqueeze(1).to_broadcast([128, 8, 128]), op=ALU.mult),
                             reads=["D1", "strict"], writes=["GBm"])
                        yield
                        S.op("dve", lambda e: e.scalar_tensor_tensor(out=GBm[:, :].rearrange("p (h i) -> p h i", i=128),
                                                                     in0=GBm[:, :].rearrange("p (h i) -> p h i", i=128), scalar=-1.0,
                                                                     in1=gsm[:, 0:8].unsqueeze(2).to_broadcast([128, 8, 128]), op0=ALU.mult, op1=ALU.mult),
                             reads=["GBm", gres], writes=["GBm"])
                        yield
                    def chain(hg):
                        N0, N0T, WA, WB, MO, MOT, V1s, V2s = GCTX[hg]
                        rN0, rN0T, rWA, rWB, rV1, rV2 = (f"{n}_{hg}" for n in ("N0", "N0T", "WA", "WB", "V1s", "V2s"))
                        yield
                        bk = S.bank(); bqk = S.bank(); bt = S.bank()

                        def mm_kk(e, hg=hg, bk=bk, bqk=bqk, bt=bt, tc0=tc0):
                            ins = None
                            for hh in range(4):
                                h = hg * 4 + hh
                                ks = GK[:, h * TBv + tc0: h * TBv + tc0 + 128]
                                e.matmul(pbank(bk, 128, hh * 128), lhsT=ks, rhs=ks, start=True, stop=True)
                                e.matmul(pbank(bqk, 128, hh * 128), lhsT=ks, rhs=GQ[:, h * TBv + tc0: h * TBv + tc0 + 128], start=True, stop=True)
                                e.transpose(pbank_bf(bt, 128, hh * 128), ks, ident_b[:])
                                ins = e.transpose(pbank_bf(bt, 128, 512 + hh * 128), GV[:, h * TBv + tc0: h * TBv + tc0 + 128], ident_b[:])
                            return ins
                        S.op("pe", mm_kk, reads=["GK", "GQ", "GV", "ident_b"], writes=[f"pb{bk}", f"pb{bqk}", f"pb{bt}"])
                        sl = slice(hg * 512, (hg + 1) * 512)
                        S.op("dve", lambda e, bk=bk, sl=sl: e.tensor_tensor(out=N0[:, :], in0=pbank(bk), in1=GBm[:, sl], op=ALU.mult),
                             reads=[f"pb{bk}", "GBm"], writes=[rN0])
                        S.op("dve", lambda e, bqk=bqk, sl=sl: e.tensor_tensor(out=PTg[:, sl], in0=pbank(bqk), in1=D1[:, sl], op=ALU.mult),
                             reads=[f"pb{bqk}", "D1"], writes=[f"PTg{hg}"])
                        S.op("dve", lambda e, bt=bt, sl=sl, hg=hg: e.tensor_tensor(out=KtG[:, sl].rearrange("p (h d) -> p h d", d=128),
                                                                                 in0=pbank_bf(bt, 512).rearrange("p (h d) -> p h d", d=128),
                                                                                 in1=gsm[:, 32 + hg * 4:36 + hg * 4].unsqueeze(2).to_broadcast([128, 4, 128]), op=ALU.mult),
                             reads=[f"pb{bt}", gres], writes=[f"KtG{hg}"])
                        S.op("act", lambda e, bt=bt, sl=sl: e.activation(out=VtG[:, sl], in_=pbank_bf(bt, 512, 512), func=AF.Copy),
                             reads=[f"pb{bt}"], writes=[f"VtG{hg}"])
                        yield
                        btr = S.bank()

                        def tr_n(e, btr=btr):
                            ins = None
                            for hh in range(4):
                                ins = e.transpose(pbank_bf(btr, 128, hh * 128), N0[:, hh * 128:(hh + 1) * 128], ident_b[:])
                            return ins
                        S.op("pe", tr_n, reads=[rN0, "ident_b"], writes=[f"pb{btr}"])
                        S.op("act", lambda e, btr=btr: e.activation(out=N0T[:, :], in_=pbank_bf(btr, 512), func=AF.Copy),
                             reads=[f"pb{btr}"], writes=[rN0T])
                        v3 = lambda t: t[:, :].rearrange("p (h i) -> p h i", i=128)
                        wav = WA[:, :].rearrange("p (h a i) -> p h a i", a=2, i=128)
                        wbv = WB[:, :].rearrange("p (h a i) -> p h a i", a=2, i=128)
                        m16b = m16[:, :].unsqueeze(1).to_broadcast([128, 4, 128])
                        idb4 = identb8[:, 0:512].rearrange("p (h i) -> p h i", i=128)
                        S.op("dve", lambda e, wav=wav: e.tensor_tensor(out=wav[:, :, 0, :], in0=v3(N0), in1=m16b, op=ALU.mult), reads=[rN0, "m16"], writes=[rWA])
                        S.op("dve", lambda e, wav=wav: e.tensor_tensor(out=wav[:, :, 1, :], in0=wav[:, :, 0, :], in1=idb4, op=ALU.add), reads=[rWA, "identb8"], writes=[rWA])
                        S.op("dve", lambda e, wbv=wbv: e.tensor_tensor(out=wbv[:, :, 0, :], in0=v3(N0T), in1=m16b, op=ALU.mult), reads=[rN0T, "m16"], writes=[rWB])
                        S.op("dve", lambda e, wbv=wbv: e.tensor_tensor(out=wbv[:, :, 1, :], in0=wbv[:, :, 0, :], in1=idb4, op=ALU.add), reads=[rWB, "identb8"], writes=[rWB])
                        for l in range(3):
                            mk = moff[:, l * 128:(l + 1) * 128].unsqueeze(1).to_broadcast([128, 4, 128])
                            S.op("pool", lambda e, l=l, mk=mk: e.tensor_tensor(out=v3(MO[l]), in0=v3(N0T), in1=mk, op=ALU.mult), reads=[rN0T, "moff"], writes=[f"MO{l}_{hg}"])
                            if l < 2:
                                S.op("pool", lambda e, l=l, mk=mk: e.tensor_tensor(out=v3(MOT[l]), in0=v3(N0), in1=mk, op=ALU.mult), reads=[rN0, "moff"], writes=[f"MOT{l}_{hg}"])
                        for lvl in range(4):
                            yield
                            while S.bank_i % 2 != 0:
                                S.bank()
                            ba0 = S.bank(); ba1 = S.bank(); bb0 = S.bank(); bb1 = S.bank()

                            def mm_l(e, lvl=lvl, ba0=ba0, bb0=bb0):
                                ins = None
                                for hh in range(4):
                                    oa = PS[:, ba0 * 512 + hh * 256: ba0 * 512 + hh * 256 + 256]
                                    ob = PS[:, bb0 * 512 + hh * 256: bb0 * 512 + hh * 256 + 256]
                                    wa = WA[:, hh * 256: hh * 256 + 256]
                                    wb = WB[:, hh * 256: hh * 256 + 256]
                                    lo, hi = (0, 128) if lvl == 0 else ((0, 256) if lvl < 3 else (128, 256))
                                    e.matmul(oa[:, lo:hi], lhsT=wb[:, 0:128], rhs=wa[:, lo:hi], start=True, stop=True)
                                    ins = e.matmul(ob[:, lo:hi], lhsT=wa[:, 0:128], rhs=wb[:, lo:hi], start=True, stop=True)
                                return ins
                            S.op("pe", mm_l, reads=[rWA, rWB], writes=[f"pb{ba0}", f"pb{ba1}", f"pb{bb0}", f"pb{bb1}"])
                            pva = PS[:, ba0 * 512: ba0 * 512 + 1024].rearrange("p (h a i) -> p h a i", a=2, i=128)
                            pvb = PS[:, bb0 * 512: bb0 * 512 + 1024].rearrange("p (h a i) -> p h a i", a=2, i=128)
                            if lvl >= 1:
                                S.op("dve", lambda e, pva=pva, wav=wav: e.tensor_tensor(out=wav[:, :, 1, :], in0=pva[:, :, 1, :], in1=wav[:, :, 1, :], op=ALU.add),
                                     reads=[f"pb{ba0}", f"pb{ba1}", rWA], writes=[rWA])
                                S.op("dve", lambda e, pvb=pvb, wbv=wbv: e.tensor_tensor(out=wbv[:, :, 1, :], in0=pvb[:, :, 1, :], in1=wbv[:, :, 1, :], op=ALU.add),
                                     reads=[f"pb{bb0}", f"pb{bb1}", rWB], writes=[rWB])
                            if lvl < 3:
                                S.op("act", lambda e, pva=pva, wav=wav: e.activation(out=wav[:, :, 0, :], in_=pva[:, :, 0, :], func=AF.Copy),
                                     reads=[f"pb{ba0}", f"pb{ba1}", rWA], writes=[rWA])
                                S.op("act", lambda e, pvb=pvb, wbv=wbv: e.activation(out=wbv[:, :, 0, :], in_=pvb[:, :, 0, :], func=AF.Copy),
                                     reads=[f"pb{bb0}", f"pb{bb1}", rWB], writes=[rWB])
                        for l in range(3):
                            last = (l == 2)
                            yield
                            b1_ = S.bank(); b2_ = None if last else S.bank()

                            def mm_v(e, l=l, b1_=b1_, b2_=b2_, last=last):
                                ins = None
                                for hh in range(4):
                                    ins = e.matmul(pbank(b1_, 128, hh * 128), lhsT=MO[l][:, hh * 128:(hh + 1) * 128], rhs=WA[:, hh * 256 + 128: hh * 256 + 256],
                                                   start=True, stop=True)
                                    if not last:
                                        ins = e.matmul(pbank(b2_, 128, hh * 128), lhsT=MOT[l][:, hh * 128:(hh + 1) * 128], rhs=WB[:, hh * 256 + 128: hh * 256 + 256],
                                                       start=True, stop=True)
                                return ins
                            S.op("pe", mm_v, reads=[rWA, rWB, f"MO{l}_{hg}"] + ([] if last else [f"MOT{l}_{hg}"]),
                                 writes=[f"pb{b1_}"] + ([] if last else [f"pb{b2_}"]))
                            S.op("act", lambda e, b1_=b1_: e.activation(out=V1s[:, :], in_=pbank(b1_), func=AF.Copy), reads=[f"pb{b1_}"], writes=[rV1])
                            if not last:
                                S.op("act", lambda e, b2_=b2_: e.activation(out=V2s[:, :], in_=pbank(b2_), func=AF.Copy), reads=[f"pb{b2_}"], writes=[rV2])
                            yield
                            b3_ = S.bank(); b4_ = None if last else S.bank()

                            def mm_u(e, b3_=b3_, b4_=b4_, last=last):
                                ins = None
                                for hh in range(4):
                                    ins = e.matmul(pbank(b3_, 128, hh * 128), lhsT=WB[:, hh * 256 + 128: hh * 256 + 256], rhs=V1s[:, hh * 128:(hh + 1) * 128],
                                                   start=True, stop=True)
                                    if not last:
                                        ins = e.matmul(pbank(b4_, 128, hh * 128), lhsT=WA[:, hh * 256 + 128: hh * 256 + 256], rhs=V2s[:, hh * 128:(hh + 1) * 128],
                                                       start=True, stop=True)
                                return ins
                            S.op("pe", mm_u, reads=[rWA, rWB, rV1] + ([] if last else [rV2]),
                                 writes=[f"pb{b3_}"] + ([] if last else [f"pb{b4_}"]))
                            if last:
                                S.op("dve", lambda e, b3_=b3_, wav=wav, sl=sl: e.tensor_tensor(out=TTf[:, sl].rearrange("p (h i) -> p h i", i=128),
                                                                                           in0=wav[:, :, 1, :], in1=pbank(b3_).rearrange("p (h i) -> p h i", i=128), op=ALU.subtract),
                                     reads=[f"pb{b3_}", rWA], writes=[f"TTf{hg}"])
                            else:
                                S.op("dve", lambda e, b3_=b3_, wav=wav: e.tensor_tensor(out=wav[:, :, 1, :], in0=wav[:, :, 1, :],
                                                                                      in1=pbank(b3_).rearrange("p (h i) -> p h i", i=128), op=ALU.subtract),
                                     reads=[f"pb{b3_}", rWA], writes=[rWA])
                                S.op("dve", lambda e, b4_=b4_, wbv=wbv: e.tensor_tensor(out=wbv[:, :, 1, :], in0=wbv[:, :, 1, :],
                                                                                      in1=pbank(b4_).rearrange("p (h i) -> p h i", i=128), op=ALU.subtract),
                                     reads=[f"pb{b4_}", rWB], writes=[rWB])
                    def rec(hg):
                        sl = slice(hg * 512, (hg + 1) * 512)
                        yield
                        b1 = S.bank()

                        def mm_ks(e, b1=b1, hg=hg, tc0=tc0):
                            ins = None
                            for hh in range(4):
                                h = hg * 4 + hh
                                ins = e.matmul(pbank(b1, 128, hh * 128), lhsT=GK[:, h * TBv + tc0: h * TBv + tc0 + 128],
                                               rhs=SGb[:, h * 128:(h + 1) * 128], start=True, stop=True)
                            return ins
                        S.op("pe", mm_ks, reads=["GK", f"SGb{hg}"], writes=[f"pb{b1}"])
                        S.op("dve", lambda e, b1=b1, sl=sl, hg=hg: e.tensor_tensor(out=rtl[:, sl].rearrange("p (h d) -> p h d", d=128),
                                                                                 in0=pbank(b1).rearrange("p (h d) -> p h d", d=128),
                                                                                 in1=gsm[:, 24 + hg * 4:28 + hg * 4].unsqueeze(2).to_broadcast([128, 4, 128]), op=ALU.mult),
                             reads=[f"pb{b1}", gres], writes=[f"rtl{hg}"])
                        S.op("dve", lambda e, sl=sl: e.tensor_tensor(out=rtl[:, sl], in0=rtl[:, sl], in1=VtG[:, sl], op=ALU.add),
                             reads=[f"rtl{hg}", f"VtG{hg}"], writes=[f"rtl{hg}"])
                        yield
                        b2 = S.bank()

                        def mm_vn(e, b2=b2, hg=hg):
                            ins = None
                            for hh in range(4):
                                h = hg * 4 + hh
                                ins = e.matmul(pbank(b2, 128, hh * 128), lhsT=TTf[:, h * 128:(h + 1) * 128],
                                               rhs=rtl[:, h * 128:(h + 1) * 128], start=True, stop=True)
                            return ins
                        S.op("pe", mm_vn, reads=[f"TTf{hg}", f"rtl{hg}"], writes=[f"pb{b2}"])
                        S.op("dve", lambda e, b2=b2, sl=sl, hg=hg: e.tensor_tensor(out=vnw[:, sl].rearrange("p (h d) -> p h d", d=128),
                                                                                 in0=pbank(b2).rearrange("p (h d) -> p h d", d=128),
                                                                                 in1=gsm[:, hg * 4:hg * 4 + 4].unsqueeze(2).to_broadcast([128, 4, 128]), op=ALU.mult),
                             reads=[f"pb{b2}", gres], writes=[f"vnw{hg}"])
                        yield
                        b3 = S.bank(); b4 = S.bank()

                        def mm_o(e, b3=b3, b4=b4, hg=hg):
                            ins = None
                            for hh in range(4):
                                h = hg * 4 + hh
                                o = pbank(b3, 128, hh * 128)
                                e.matmul(o, lhsT=SGb[:, h * 128:(h + 1) * 128], rhs=QTd[:, h * 128:(h + 1) * 128], start=True, stop=False)
                                e.matmul(o, lhsT=vnw[:, h * 128:(h + 1) * 128], rhs=PTg[:, h * 128:(h + 1) * 128], start=False, stop=True)
                                ins = e.matmul(pbank(b4, 128, hh * 128), lhsT=KtG[:, h * 128:(h + 1) * 128], rhs=vnw[:, h * 128:(h + 1) * 128],
                                               start=True, stop=True)
                            return ins
                        S.op("pe", mm_o, reads=[f"SGb{hg}", qres, f"vnw{hg}", f"PTg{hg}", f"KtG{hg}"], writes=[f"pb{b3}", f"pb{b4}"])
                        dst = OG[:, hg * 4 * TBv:(hg + 1) * 4 * TBv].rearrange("p (h t) -> p h t", t=TBv)[:, :, tc0:tc0 + 128]
                        S.op("act", lambda e, b3=b3, dst=dst: e.activation(out=dst, in_=pbank(b3).rearrange("p (h t) -> p h t", t=128), func=AF.Copy),
                             reads=[f"pb{b3}"], writes=["OG"])
                        S.op("dve", lambda e, sl=sl, hg=hg: e.tensor_tensor(out=SG[:, sl].rearrange("p (h d) -> p h d", d=128),
                                                                     in0=SG[:, sl].rearrange("p (h d) -> p h d", d=128),
                                                                     in1=gsm[:, 40 + hg * 4:44 + hg * 4].unsqueeze(2).to_broadcast([128, 4, 128]), op=ALU.mult),
                             reads=[f"SG{hg}", gres], writes=[f"SG{hg}"])
                        S.op("dve", lambda e, b4=b4, sl=sl: e.tensor_tensor(out=SG[:, sl], in0=SG[:, sl], in1=pbank(b4), op=ALU.add),
                             reads=[f"SG{hg}", f"pb{b4}"], writes=[f"SG{hg}"])
                        S.op("act", lambda e, sl=sl: e.activation(out=SGb[:, sl], in_=SG[:, sl], func=AF.Copy), reads=[f"SG{hg}"], writes=[f"SGb{hg}"])
                    return setup, chain, rec
                def _drive(gens):
                    gens = list(gens)
                    while gens:
                        for _g in list(gens):
                            try:
                                next(_g)
                            except StopIteration:
                                gens.remove(_g)
                _tiles = [gdn_tile(tau) for tau in (range(NTv) if not is_sample else [])]
                if _tiles:
                    _drive([_tiles[0][0]()])
                for tau in range(len(_tiles)):
                    _st, _ch, _rc = _tiles[tau]
                    _drive([_ch(0), _ch(1), _retg])
                    _drive(([_tiles[tau + 1][0]()] if tau + 1 < len(_tiles) else []) + [_rc(0), _rc(1)])
                _drive([_retg])
                if blk == nblk_v - 1 and not is_sample:
                    S.op("sp", lambda e: e.dma_start(out=sgp.rearrange("h d v -> d h v"), in_=SG[:, :].rearrange("p (h v) -> p h v", v=GDV)),
                         reads=["SG0", "SG1"], dma="sgp")
                if blk == nblk_v - 1 and not is_sample:
                    S.op("sp", lambda e: e.dma_start(out=srp.rearrange("h d v -> d h v"), in_=SR[:, :].rearrange("p (h v) -> p h v", v=RDV)),
                         reads=["SR"], dma="srp")
                def _g_rms():
                    yield
                    S.op("act", lambda e: e.activation(out=MRG[:, :], in_=OG[:, :], func=AF.Square), reads=["OG"], writes=["MRG"])
                    yield
                    while S.bank_i % 4 != 0:
                        S.bank()
                    rb_ = [S.bank() for _ in range(4)]
                    assert rb_[3] == rb_[0] + 3

                    def mm_rn(e):
                        ins = None
                        for h in range(GH):
                            ins = e.matmul(PS[:, rb_[0] * 512 + h * TBv: rb_[0] * 512 + (h + 1) * TBv], lhsT=ones_b[:], rhs=MRG[:, h * TBv:(h + 1) * TBv],
                                           start=True, stop=True)
                        return ins
                    rbr = [f"pb{b}" for b in rb_]
                    S.op("pe", mm_rn, reads=["MRG", "ones_b"], writes=rbr)
                    T4r = 4 * TBv
                    for hf in range(2):
                        rs_, rr_ = RNS[hf]
                        S.op("dve", lambda e, hf=hf, rs_=rs_: e.tensor_scalar(out=rs_, in0=PS[:, rb_[0] * 512 + hf * T4r: rb_[0] * 512 + (hf + 1) * T4r],
                                                                             scalar1=1.0 / GDV, scalar2=EPS, op0=ALU.mult, op1=ALU.add), reads=rbr, writes=[rr_])
                    for hf in range(2):
                        rs_, rr_ = RNS[hf]
                        yield
                        S.op("act", lambda e, rs_=rs_: e.activation(out=rs_, in_=rs_, func=AF.Ln), reads=[rr_], writes=[rr_])
                        yield
                        S.op("act", lambda e, rs_=rs_: e.activation(out=rs_, in_=rs_, func=AF.Exp, scale=-0.5), reads=[rr_], writes=[rr_])
                        yield
                        S.op("dve", lambda e, hf=hf, rs_=rs_: e.scalar_tensor_tensor(out=rs_, in0=OG[:, hf * T4r:(hf + 1) * T4r], scalar=ggdn_c[:, 0:1], in1=rs_,
                                                                                    op0=ALU.mult, op1=ALU.mult), reads=["OG", rr_, "ggdn_c"], writes=[rr_])
                        yield
                        S.op("dve", lambda e, hf=hf, rs_=rs_: e.tensor_tensor(out=OGg[:, hf * T4r:(hf + 1) * T4r], in0=rs_, in1=Z[:, hf * T4r:(hf + 1) * T4r], op=ALU.mult),
                             reads=[rr_, "Z"], writes=["OGg"])
                    yield
                def _g_gn():
                    T4 = 4 * TBv
                    if is_sample:
                        GNS, gns_r = GNSs, "GNSs"
                    else:
                        GNS, gns_r = QKVB[:, 8 * TBv:24 * TBv].bitcast(F32), "FA"
                    OBF = FA[:, 0:8 * TBv]
                    OSQ, osq_r = (hnT, "hnT") if is_sample else (VtR, "VtR")
                    yield
                    S.op("act", lambda e: e.activation(out=OBF, in_=OR[:, :], func=AF.Copy), reads=[or_res], writes=["FA"] + FA_AL)
                    yield
                    S.op("act", lambda e: e.activation(out=OSQ[:, :], in_=OR[:, :], func=AF.Square), reads=[or_res], writes=[osq_r])
                    yield
                    while S.bank_i % 4 != 0:
                        S.bank()
                    gb_ = [S.bank() for _ in range(4)]
                    assert gb_[3] == gb_[0] + 3

                    def mm_gn(e):
                        ins = None
                        for h in range(RH):
                            om = PS[:, gb_[0] * 512 + h * TBv: gb_[0] * 512 + (h + 1) * TBv]
                            oq = PS[:, gb_[0] * 512 + T4 + h * TBv: gb_[0] * 512 + T4 + (h + 1) * TBv]
                            for c in range(2):
                                ch = h * 2 + c
                                e.matmul(om, lhsT=onesq_b[:], rhs=OBF[:, ch * TBv:(ch + 1) * TBv], start=(c == 0), stop=(c == 1))
                            for c in range(2):
                                ch = h * 2 + c
                                ins = e.matmul(oq, lhsT=onesq_b[:], rhs=OSQ[:, ch * TBv:(ch + 1) * TBv], start=(c == 0), stop=(c == 1))
                        return ins
                    S.op("pe", mm_gn, reads=["FA", osq_r, "onesq_b"], writes=[f"pb{b}" for b in gb_])
                    pmean = PS[:, gb_[0] * 512: gb_[0] * 512 + T4]
                    pmsq = PS[:, gb_[0] * 512 + T4: gb_[0] * 512 + 2 * T4]
                    gbr = [f"pb{b}" for b in gb_]
                    S.op("act", lambda e: e.activation(out=GNS[:, 0:T4], in_=pmean, func=AF.Copy), reads=gbr, writes=[gns_r])
                    S.op("dve", lambda e: e.tensor_tensor(out=GNS[:, T4:2 * T4], in0=GNS[:, 0:T4], in1=GNS[:, 0:T4], op=ALU.mult), reads=[gns_r], writes=[gns_r])
                    S.op("dve", lambda e: e.scalar_tensor_tensor(out=GNS[:, T4:2 * T4], in0=GNS[:, T4:2 * T4], scalar=-1.0, in1=pmsq, op0=ALU.mult, op1=ALU.add),
                         reads=[gns_r] + gbr, writes=[gns_r])
                    yield
                    S.op("dve", lambda e: e.tensor_scalar(out=GNS[:, T4:2 * T4], in0=GNS[:, T4:2 * T4], scalar1=EPS, scalar2=None, op0=ALU.add), reads=[gns_r], writes=[gns_r])
                    yield
                    S.op("act", lambda e: e.activation(out=GNS[:, T4:2 * T4], in_=GNS[:, T4:2 * T4], func=AF.Ln), reads=[gns_r], writes=[gns_r])
                    yield
                    S.op("act", lambda e: e.activation(out=GNS[:, T4:2 * T4], in_=GNS[:, T4:2 * T4], func=AF.Exp, scale=-0.5), reads=[gns_r], writes=[gns_r])
                    or4 = OR[:, :].rearrange("p (h c t) -> p h c t", c=2, t=TBv)
                    yield
                    S.op("dve", lambda e: e.tensor_tensor(out=or4, in0=or4,
                                                          in1=GNS[:, 0:T4].rearrange("p (h t) -> p h t", t=TBv).unsqueeze(2).to_broadcast([128, RH, 2, TBv]), op=ALU.subtract),
                         reads=[or_res, gns_r], writes=[or_res])
                    yield
                    S.op("dve", lambda e: e.tensor_tensor(out=or4, in0=or4,
                                                          in1=GNS[:, T4:2 * T4].rearrange("p (h t) -> p h t", t=TBv).unsqueeze(2).to_broadcast([128, RH, 2, TBv]), op=ALU.mult),
                         reads=[or_res, gns_r], writes=[or_res])
                    or3 = OR[:, :].rearrange("p (c t) -> p c t", t=TBv)
                    yield
                    S.op("dve", lambda e: e.tensor_tensor(out=or3, in0=or3, in1=ggn_c[:, 0:8].unsqueeze(2).to_broadcast([128, 8, TBv]), op=ALU.mult),
                         reads=[or_res, "ggn_c"], writes=[or_res])
                    yield
                    S.op("dve", lambda e: e.tensor_tensor(out=ORg[:, :], in0=OR[:, :], in1=RG[:, :], op=ALU.mult), reads=[or_res, "RG"], writes=["ORg"])


                    yield
                def late_proj():
                    for (kind, c0, ncols) in late_tiles:
                        base = {"ga": 7184, "gb": 8208}[kind]
                        dstb = {"ga": GA, "gb": GB}[kind]
                        cb = (c0 - base) // 128
                        t, wres = wload(w_in, 0, 8, c0, ncols)
                        for j in range(ncols // 128):
                            yield
                            b = S.bank()

                            def mm(e, j=j, b=b, t=t, ncols=ncols):
                                ins = None
                                for k in range(8):
                                    ins = e.matmul(pbank(b, TBv), lhsT=wv(t, k, ncols, j * 128, 128), rhs=hnT[:, k * TBv:(k + 1) * TBv], start=(k == 0), stop=(k == 7))
                                return ins
                            S.op("pe", mm, reads=[wres, "hnT"], writes=[f"pb{b}"])
                            S.op("act", lambda e, b=b, j=j, dstb=dstb, cb=cb: e.activation(out=dstb[:, (cb + j) * TBv:(cb + j + 1) * TBv], in_=pbank(b, TBv), func=AF.Tanh, scale=0.5),
                                 reads=[f"pb{b}"], writes=[kind.upper()])
                    yield
                _drive([_g_rms(), _g_gn(), late_proj()])

                if blk + 1 < nblk_v:
                    load_x(blk + 1)
                if stage < 4:
                    return
                for half in range(NQ):
                    ta, ra = wload(w_a, 0, 8, half * WC, WC)
                    tb_, rb = wload(w_b, 0, 8, half * WC, WC)
                    def _br(j, half=half, ta=ta, tb_=tb_, ra=ra, rb_w=rb):
                        tmpA, rA = nxt("tmpA"); tmpB, rB = nxt("tmpB")
                        ch = half * (WC // 128) + j
                        b1 = S.bank(); b2 = S.bank()

                        def mm_br(e):
                            ins = None
                            for k in range(8):
                                e.matmul(pbank(b1, TBv), lhsT=wv(ta, k, WC, j * 128, 128), rhs=ORg[:, k * TBv:(k + 1) * TBv], start=(k == 0), stop=(k == 7))
                            for k in range(8):
                                ins = e.matmul(pbank(b2, TBv), lhsT=wv(tb_, k, WC, j * 128, 128), rhs=OGg[:, k * TBv:(k + 1) * TBv], start=(k == 0), stop=(k == 7))
                            return ins
                        S.op("pe", mm_br, reads=[ra, rb_w, "ORg", "OGg"], writes=[f"pb{b1}", f"pb{b2}"])
                        S.op("dve", lambda e: e.scalar_tensor_tensor(out=tmpA[:], in0=GA[:, ch * TBv:(ch + 1) * TBv], scalar=1.0, in1=pbank(b1, TBv),
                                                                     op0=ALU.add, op1=ALU.mult), reads=[f"pb{b1}", "GA"], writes=[rA])
                        S.op("dve", lambda e: e.scalar_tensor_tensor(out=tmpB[:], in0=GB[:, ch * TBv:(ch + 1) * TBv], scalar=1.0, in1=pbank(b2, TBv),
                                                                     op0=ALU.add, op1=ALU.mult), reads=[f"pb{b2}", "GB"], writes=[rB])
                        S.op("dve", lambda e: e.tensor_tensor(out=MRG[:, ch * TBv:(ch + 1) * TBv], in0=tmpA[:], in1=tmpB[:], op=ALU.add),
                             reads=[rA, rB], writes=["MRG"])
                    for j in range(WC // 128):
                        _br(j)

                def resid_cons(half, sc=None):
                    def cons(tau, b):
                        xs_ = Xb[0:tw, tau * D + half * WC: tau * D + half * WC + WC]
                        if sc is None:
                            S.op("dve", lambda e: e.tensor_tensor(out=xs_, in0=xs_, in1=pbank(b, WC)[0:tw, :], op=ALU.add), reads=[f"pb{b}", xres], writes=[xres])
                        else:
                            S.op("dve", lambda e: e.scalar_tensor_tensor(out=xs_, in0=pbank(b, WC)[0:tw, :], scalar=sc, in1=xs_, op0=ALU.mult, op1=ALU.add),
                                 reads=[f"pb{b}", xres], writes=[xres])
                    return cons
                for half in range(NQ):
                    proj_tm(w_out, 0, 8, half * WC, WC, MRG, "MRG", TBv, NTv, resid_cons(half, 0.5), tw=tw)

                if stage < 5:
                    return
                for tau in range(NTv):
                    norm_transpose(Xb[0:tw, tau * D:(tau + 1) * D], xres, gx_c, "gx_c", hnT, "hnT", tw, tau * tw, TBv)
                XQ = ORg
                OX = OGg
                for half in range(NQ):
                    def cons(j, b, half=half):
                        c = half * (WC // 128) + j
                        S.op("act", lambda e: e.activation(out=XQ[:, c * TBv:(c + 1) * TBv], in_=pbank(b, TBv), func=AF.Copy), reads=[f"pb{b}"], writes=["ORg"])
                    proj_fm(w_xq, 0, 8, half * WC, WC, hnT, "hnT", TBv, TBv, cons)
                ET = MRG
                if is_sample:
                    sample_xattn(XQ, OX)
                def _xh(h):
                    tmpB, rB = nxt("tmpB")
                    eo = (h % 2) * 2 * TBv
                    eres = f"MRGs{h % 2}"
                    for mt in range(2):
                        yield
                        b = S.bank()

                        def mm_s(e, b=b, mt=mt):
                            ins = None
                            for c in range(2):
                                ch = h * 2 + c
                                ins = e.matmul(pbank(b, TBv), lhsT=MKT[:, ch * NMEM + mt * 128: ch * NMEM + (mt + 1) * 128],
                                               rhs=XQ[:, ch * TBv:(ch + 1) * TBv], start=(c == 0), stop=(c == 1))
                            return ins
                        S.op("pe", mm_s, reads=["MKT", "ORg"], writes=[f"pb{b}"])
                        S.op("act", lambda e, b=b, mt=mt: e.activation(out=ET[:, eo + mt * TBv: eo + (mt + 1) * TBv], in_=pbank(b, TBv), func=AF.Exp, scale=XHD ** -0.5),
                             reads=[f"pb{b}"], writes=[eres])
                    yield
                    bd = S.bank()

                    def mm_d(e):
                        e.matmul(pbank(bd, TBv), lhsT=ones_b[:], rhs=ET[:, eo:eo + TBv], start=True, stop=False)
                        return e.matmul(pbank(bd, TBv), lhsT=ones_b[:], rhs=ET[:, eo + TBv: eo + 2 * TBv], start=False, stop=True)
                    S.op("pe", mm_d, reads=[eres, "ones_b"], writes=[f"pb{bd}"])
                    S.op("dve", lambda e: e.reciprocal(out=tmpB[:], in_=pbank(bd, TBv)), reads=[f"pb{bd}"], writes=[rB])
                    for c in range(2):
                        ch = h * 2 + c
                        yield
                        b = S.bank()

                        def mm_o(e, b=b, ch=ch):
                            e.matmul(pbank(b, TBv), lhsT=MV[:, ch * 128:(ch + 1) * 128], rhs=ET[:, eo:eo + TBv], start=True, stop=False)
                            return e.matmul(pbank(b, TBv), lhsT=MV[:, D + ch * 128: D + (ch + 1) * 128], rhs=ET[:, eo + TBv: eo + 2 * TBv], start=False, stop=True)
                        S.op("pe", mm_o, reads=["MV", eres], writes=[f"pb{b}"])
                        S.op("dve", lambda e, b=b, ch=ch: e.tensor_tensor(out=OX[:, ch * TBv:(ch + 1) * TBv], in0=pbank(b, TBv), in1=tmpB[:], op=ALU.mult),
                             reads=[f"pb{b}", rB], writes=["OGg"])
                if not is_sample:
                    _drive([_xh(0), _xh(1)])
                    _drive([_xh(2), _xh(3)])
                for half in range(NQ):
                    proj_tm(w_xo, 0, 8, half * WC, WC, OX, "OGg", TBv, NTv, resid_cons(half), tw=tw)

                if stage < 6:
                    return
                for tau in range(NTv):
                    norm_transpose(Xb[0:tw, tau * D:(tau + 1) * D], xres, gffn_c, "gffn_c", hnT, "hnT", tw, tau * tw, TBv)
                for c0 in range(0, DFF, WC):
                    ncols = min(WC, DFF - c0)
                    tg, rg_ = wload(w_gate, 0, 8, c0, ncols)
                    tu, ru = wload(w_up, 0, 8, c0, ncols)
                    def _ff(j, c0=c0, ncols=ncols, tg=tg, tu=tu, rg_=rg_, ru=ru):
                        tmpA, rA = nxt("tmpA")
                        ch = c0 // 128 + j
                        b1 = S.bank(); b2 = S.bank()

                        def mm_f(e):
                            ins = None
                            for k in range(8):
                                e.matmul(pbank(b1, TBv), lhsT=wv(tg, k, ncols, j * 128, 128), rhs=hnT[:, k * TBv:(k + 1) * TBv], start=(k == 0), stop=(k == 7))
                            for k in range(8):
                                ins = e.matmul(pbank(b2, TBv), lhsT=wv(tu, k, ncols, j * 128, 128), rhs=hnT[:, k * TBv:(k + 1) * TBv], start=(k == 0), stop=(k == 7))
                            return ins
                        S.op("pe", mm_f, reads=[rg_, ru, "hnT"], writes=[f"pb{b1}", f"pb{b2}"])
                        S.op("act", lambda e: e.activation(out=tmpA[:], in_=pbank(b1, TBv), func=AF.Silu), reads=[f"pb{b1}"], writes=[rA])
                        S.op("dve", lambda e: e.tensor_tensor(out=FA[:, ch * TBv:(ch + 1) * TBv], in0=tmpA[:], in1=pbank(b2, TBv), op=ALU.mult),
                             reads=[rA, f"pb{b2}"], writes=["FA"] + FA_AL)
                    for j in range(ncols // 128):
                        _ff(j)
                if blk + 1 < nblk_v:
                    pre_norm(blk + 1)
                for half in range(NQ):
                    tiles = [wload(w_down, g * 1024, (8 if g < 2 else 6), half * WC, WC) for g in range(3)]
                    for tau in range(NTv):
                        b = S.bank()

                        def mm_d(e, b=b, tau=tau, tiles=tiles):
                            ins = None
                            for g in range(3):
                                nk = 8 if g < 2 else 6
                                for k in range(nk):
                                    kk = g * 8 + k
                                    ins = e.matmul(pbank(b, WC)[0:tw, :], lhsT=FA[:, kk * TBv + tau * tw: kk * TBv + (tau + 1) * tw], rhs=wv(tiles[g][0], k, WC),
                                                   start=(kk == 0), stop=(kk == 21))
                            return ins
                        S.op("pe", mm_d, reads=[t[1] for t in tiles] + ["FA"], writes=[f"pb{b}"])
                        resid_cons(half)(tau, b)
                if is_sample:
                    gfin_bc, gfin_r = gfin_s, "gfin_s"
                else:
                    gfin_bc, gfin_r = D1, "D1"
                S.op("sp", lambda e, gfin_bc=gfin_bc: e.dma_start(out=gfin_bc[:, 0:D], in_=g_fin.rearrange("(o d) -> o d", o=1).partition_broadcast(128)),
                     writes=[gfin_r], dma="gfin_ld")
                for tau in range(NTv):
                    xt = Xb[0:tw, tau * D:(tau + 1) * D]
                    S.op("act", lambda e, xt=xt: e.activation(out=sqj[0:tw, :], in_=xt, func=AF.Square, accum_out=sscol[0:tw, 0:1]), reads=[xres], writes=["xn", "sscol"])
                    rstd_from_ss(sscol[0:tw, 0:1], D, EPS, "sscol")
                    S.op("dve", lambda e, xt=xt: e.scalar_tensor_tensor(out=xt, in0=xt, scalar=sscol[0:tw, 0:1], in1=gfin_bc[0:tw, 0:D], op0=ALU.mult, op1=ALU.mult),
                         reads=[xres, "sscol", gfin_r], writes=[xres])
                if is_sample:
                    S.op("sp", lambda e, Xb=Xb: e.dma_start(out=ys, in_=Xb[0:NS, 0:D]), reads=[xres], dma=xres + "o")
                else:
                    S.op("sp", lambda e, Xb=Xb, t0=t0: e.dma_start(out=yp[t0:t0 + TBv, :].rearrange("(a p) d -> p a d", p=128),
                                                                 in_=Xb[:, :].rearrange("p (a d) -> p a d", d=D)),
                         reads=[xres], dma=xres + "o")
            for blk in range(nblk_v):
                do_block(blk)
            S.emit()
            esp.close()
            cur_es[0] = es

        run_phase(False)
        if stage >= 7:
            run_phase(True)
    return nc


_CACHE = {}


def _prep_inputs(inputs):
    f = lambda a: np.ascontiguousarray(np.asarray(a, dtype=np.float32))
    common = dict(
        w_in=f(inputs["w_in"][0]), w_a=f(inputs["w_branch_a"][0]), w_b=f(inputs["w_branch_b"][0]), w_out=f(inputs["w_out"][0]),
        w_xq=f(inputs["w_xq"][0]), w_xk=f(inputs["w_xk"][0]), w_xv=f(inputs["w_xv"][0]), w_xo=f(inputs["w_xo"][0]),
        w_gate=f(inputs["w_gate"][0]), w_up=f(inputs["w_up"][0]), w_down=f(inputs["w_down"][0]),
        g_mix=f(inputs["norm_mix_g"][0]), g_x=f(inputs["norm_x_g"][0]), g_mem=f(inputs["mem_norm_g"][0]),
        g_ffn=f(inputs["norm_ffn_g"][0]), g_fin=f(inputs["norm_final_g"]), g_gn=f(inputs["ret_gn_g"][0]),
        g_gdn=f(inputs["gdn_norm_g"][0]), convw=f(inputs["gdn_conv_w"][0]), a_log=f(inputs["gdn_a_log"][0]),
        dt_bias=f(inputs["gdn_dt_bias"][0]))
    common.update(_CONSTS)
    maps = []
    for c in range(NCORES):
        m = dict(common)
        sl = slice(c * NS, (c + 1) * NS)
        m["xp"] = f(inputs["x_prompt"][c]); m["memp"] = f(inputs["mem_prompt"][c])
        m["xs"] = f(inputs["x_sample"][sl, 0]); m["sret"] = f(inputs["state_ret"][0, sl]); m["sgdn"] = f(inputs["state_gdn"][0, sl])
        m["sconv"] = f(inputs["state_conv"][0, sl])
        m["cmk"] = f(inputs["cache_mem_k"][0, sl]).reshape(NS, NMEM, D); m["cmv"] = f(inputs["cache_mem_v"][0, sl]).reshape(NS, NMEM, D)
        maps.append(m)
    return maps


def kernel(**inputs):
    if "nc" not in _CACHE:
        _CACHE["nc"] = build_program()
    nc = _CACHE["nc"]
    maps = _prep_inputs(inputs)
    res = run_bass_kernel_spmd(nc, maps, core_ids=list(range(NCORES)))
    R = res.results
    st = lambda k: np.stack([np.asarray(r[k], dtype=np.float32) for r in R])
    cat = lambda k: np.concatenate([np.asarray(r[k], dtype=np.float32) for r in R], axis=0)
    y_prompt = st("yp")
    y_sample = cat("ys").reshape(NCORES * NS, 1, D)
    return (y_prompt, y_sample, st("srp")[None], st("sgp")[None], st("scp")[None],
            st("mkp").reshape(1, NCORES, NMEM, XH, XHD), st("mvp").reshape(1, NCORES, NMEM, XH, XHD),
            cat("srs")[None], cat("sgs")[None], cat("scs")[None])
```

```python
from contextlib import ExitStack
import math
import numpy as np
import concourse.bass as bass
import concourse.mybir as mybir
from concourse.bass_utils import run_bass_kernel_spmd

F32 = mybir.dt.float32
BF16 = mybir.dt.bfloat16
AF = mybir.ActivationFunctionType
ALU = mybir.AluOpType

NCORES = 8
D = 1024
SEQ = 2048
NS = 16
PAST = 16384
RH, RDK, RDV = 4, 128, 256
GH, GDK, GDV = 8, 128, 128
CONV_CH = 3072
NMEM = 256
XH, XHD = 4, 256
DFF = 2816
DIN = 9232
EPS = 1e-6
TB = 256
NT = TB // 128
NBLK = SEQ // TB
ENGS = ("pe", "act", "dve", "pool", "sp")
EPOCH = 3000
NEG = -30000.0


class Sched:
    def __init__(self, nc, es):
        self.nc, self.es = nc, es
        self.eng_sems = {e: [] for e in ENGS}
        self.nsem = 0
        self.prev_final = []
        self.reset()

    def reset(self):
        self.ops = []
        self.last_w = {}
        self.readers = {}
        self.eng_count = {e: 0 for e in ENGS}
        self.eng_base = {e: len(self.eng_sems[e]) for e in ENGS}
        self.dma_sems = {}
        self.dma_cnt = {}
        self.bank_i = 0
        self.reserved = set()

    def _newsem(self, name):
        self.nsem += 1
        return self.es.enter_context(self.nc.semaphore(f"{name}_{self.nsem}"))

    def _token_compute(self, eng):
        i = self.eng_count[eng]
        self.eng_count[eng] += 1
        ep, k = divmod(i, EPOCH)
        ep += self.eng_base[eng]
        while len(self.eng_sems[eng]) <= ep:
            self.eng_sems[eng].append(self._newsem(f"s{eng}"))
        return (self.eng_sems[eng][ep], k + 1, 1)

    def _token_dma(self, key):
        if key not in self.dma_sems or self.dma_cnt[key] > 60000:
            self.dma_sems[key] = self._newsem("d")
            self.dma_cnt[key] = 0
        self.dma_cnt[key] += 16
        return (self.dma_sems[key], self.dma_cnt[key], 16)

    def bank(self):
        while True:
            b = self.bank_i
            self.bank_i = (self.bank_i + 1) % 8
            if b not in self.reserved:
                return b

    def op(self, eng, fn, reads=(), writes=(), dma=None):
        writes = list(writes) + [r for r in reads if r.startswith("pb") and r not in writes]
        deps = set()
        for r in reads:
            if r in self.last_w:
                deps.add(self.last_w[r])
        for w in writes:
            if w in self.last_w:
                deps.add(self.last_w[w])
            for rd in self.readers.get(w, ()):
                deps.add(rd)
        idx = len(self.ops)
        tok = self._token_dma(dma) if dma is not None else self._token_compute(eng)
        self.ops.append(dict(eng=eng, fn=fn, deps=deps, tok=tok, dma=dma is not None))
        for r in reads:
            self.readers.setdefault(r, []).append(idx)
        for w in writes:
            self.last_w[w] = idx
            self.readers[w] = []
        return idx

    def emit(self):
        nc, ops = self.nc, self.ops
        per = {e: [] for e in ENGS}
        for i, o in enumerate(ops):
            per[o["eng"]].append(i)
        final = {}
        for e in ENGS:
            for ep in range(self.eng_base[e], len(self.eng_sems[e])):
                n = min(EPOCH, self.eng_count[e] - (ep - self.eng_base[e]) * EPOCH)
                s = self.eng_sems[e][ep]
                final[id(s)] = (s, n)
        for o in ops:
            sem, val, _ = o["tok"]
            if final.get(id(sem), (None, 0))[1] < val:
                final[id(sem)] = (sem, val)
        prev_final = self.prev_final

        def run(eng_name, engine):
            waited = {}
            for (sem, val) in prev_final:
                engine.wait_ge(sem, val)
                waited[id(sem)] = val
            for i in per[eng_name]:
                o = ops[i]
                need = {}
                for d in o["deps"]:
                    od = ops[d]
                    if od["eng"] == "pe" and eng_name == "pe" and not od["dma"]:
                        continue
                    sem, val, _ = od["tok"]
                    k = id(sem)
                    if waited.get(k, 0) >= val:
                        continue
                    if k not in need or need[k][1] < val:
                        need[k] = (sem, val)
                for k, (sem, val) in need.items():
                    engine.wait_ge(sem, val)
                    waited[k] = val
                ins = o["fn"](engine)
                sem, val, inc = o["tok"]
                ins.then_inc(sem, inc)
            if eng_name == "sp":
                for k, (sem, val) in final.items():
                    if waited.get(k, 0) >= val:
                        continue
                    engine.wait_ge(sem, val)

        with nc.Block() as block:
            @block.tensor
            def _(e):
                run("pe", e)

            @block.scalar
            def _(e):
                run("act", e)

            @block.vector
            def _(e):
                run("dve", e)

            @block.gpsimd
            def _(e):
                run("pool", e)

            @block.sync
            def _(e):
                run("sp", e)
        self.prev_final = list(final.values())
        self.reset()


def _consts():
    c = {}
    idx = np.arange(128)
    c["c_ident"] = np.eye(128, dtype=np.float32)
    c["c_tri"] = (idx[:, None] <= idx[None, :]).astype(np.float32)
    c["c_mincl"] = np.where(idx[None, :] >= idx[:, None], 0.0, NEG).astype(np.float32)
    c["c_strict"] = (idx[None, :] > idx[:, None]).astype(np.float32)
    c["c_ones"] = np.ones((128, 128), np.float32)
    blk = lambda b: (idx[:, None] // b) == (idx[None, :] // b)
    c["c_m16"] = blk(16).astype(np.float32)
    c["c_moff"] = np.concatenate([-(blk(2 * b) & ~blk(b)).astype(np.float32) for b in (16, 32, 64)], axis=1)
    perm = np.zeros((128, 128), np.float32)
    perm[(idx + 64) % 128, idx] = 1.0
    c["c_perm"] = perm
    h = np.arange(RH, dtype=np.float64)
    log_g = np.log1p(-np.exp2(-5.0 - h))
    diff = idx[None, :] - idx[:, None]
    dt = np.where(diff[:, None, :] >= 0, np.exp(np.maximum(diff, 0)[:, None, :] * log_g[None, :, None]), 0.0)
    c["c_rdt"] = (dt * RDK ** -0.5).astype(np.float32).reshape(128, RH * 128)
    qd = np.exp((idx[None, :] + 1.0) * log_g[:, None])
    c["c_rqd"] = np.broadcast_to(qd.reshape(1, RH * 128), (128, RH * 128)).astype(np.float32).copy()
    kd = np.exp((127.0 - idx)[:, None] * log_g[None, :]) * RDK ** -0.5
    c["c_rkd"] = kd.astype(np.float32)
    gam = np.exp(log_g)
    half = RDK // 2
    inv = 10000.0 ** (-np.arange(half, dtype=np.float64) / half)
    pos = np.arange(SEQ, dtype=np.float64)
    ang = inv[:, None] * pos[None, :]
    ang = (inv.astype(np.float32)[:, None] * pos.astype(np.float32)[None, :]).astype(np.float64)
    cos, sin = np.cos(ang), np.sin(ang)
    c["c_cos"] = np.concatenate([cos, cos], 0).astype(np.float32)
    c["c_sin"] = np.concatenate([-sin, sin], 0).astype(np.float32)
    angs = (inv.astype(np.float32) * np.float32(PAST)).astype(np.float64)
    c["c_cs_s"] = np.stack([np.concatenate([np.cos(angs), np.cos(angs)]),
                            np.concatenate([-np.sin(angs), np.sin(angs)])]).astype(np.float32)
    return c, gam


_CONSTS, _GAM = _consts()

PJ_OFF = {"rq": (0, 0), "rk": (512, 512), "rv": (1024, 1024), "qkv": (3072, 2048), "ba": (7168, 5120)}
WC = 256
NQ = 1024 // WC
W_IN_TILES = []
for _k, _c0, _n in (("rq", 0, 512), ("rk", 512, 512), ("rv", 1024, 1024), ("rg", 2048, 1024), ("qkv", 3072, 3072),
                    ("z", 6144, 1024), ("ba", 7168, 16), ("ga", 7184, 1024), ("gb", 8208, 1024)):
    for _o in range(0, _n, WC):
        W_IN_TILES.append((_k, _c0 + _o, min(WC, _n - _o)))


def build_program(debug=False, nblk=NBLK, stage=99, kinds=None):
    nc = bass.Bass("TRN2", target_bir_lowering=False)
    di = lambda name, shape: nc.dram_tensor(name, list(shape), F32, kind="ExternalInput").ap()
    do = lambda name, shape: nc.dram_tensor(name, list(shape), F32, kind="ExternalOutput").ap()
    xp = di("xp", [SEQ, D]); memp = di("memp", [NMEM, D])
    xs = di("xs", [NS, D]); sret = di("sret", [NS, RH, RDK, RDV]); sgdn = di("sgdn", [NS, GH, GDK, GDV])
    sconv = di("sconv", [NS, 3, CONV_CH]); cmk = di("cmk", [NS, NMEM, D]); cmv = di("cmv", [NS, NMEM, D])
    w_in = di("w_in", [D, DIN]); w_a = di("w_a", [D, D]); w_b = di("w_b", [D, D]); w_out = di("w_out", [D, D])
    w_xq = di("w_xq", [D, D]); w_xk = di("w_xk", [D, D]); w_xv = di("w_xv", [D, D]); w_xo = di("w_xo", [D, D])
    w_gate = di("w_gate", [D, DFF]); w_up = di("w_up", [D, DFF]); w_down = di("w_down", [DFF, D])
    g_mix = di("g_mix", [D]); g_x = di("g_x", [D]); g_mem = di("g_mem", [D]); g_ffn = di("g_ffn", [D])
    g_fin = di("g_fin", [D]); g_gn = di("g_gn", [D]); g_gdn = di("g_gdn", [GDV])
    convw = di("convw", [4, CONV_CH]); a_log = di("a_log", [GH]); dt_bias = di("dt_bias", [GH])
    cst = {k: di(k, v.shape) for k, v in _CONSTS.items()}
    yp = do("yp", [SEQ, D]); ys = do("ys", [NS, D])
    srp = do("srp", [RH, RDK, RDV]); sgp = do("sgp", [GH, GDK, GDV]); scp = do("scp", [3, CONV_CH])
    mkp = do("mkp", [NMEM, D]); mvp = do("mvp", [NMEM, D])
    srs = do("srs", [NS, RH, RDK, RDV]); sgs = do("sgs", [NS, GH, GDK, GDV]); scs = do("scs", [NS, 3, CONV_CH])

    with ExitStack() as es:
        S = Sched(nc, es)
        sfx = [""]
        sbt = lambda name, shape, dt=BF16: cur_es[0].enter_context(nc.sbuf_tensor(name + sfx[0], list(shape), dt))
        cur_es = [es]
        PS = es.enter_context(nc.psum_tensor("PS", [128, 4096], F32))
        PSB = PS[:, :].bitcast(BF16)

        def pbank(b, n=512, off=0):
            return PS[:, b * 512 + off: b * 512 + off + n]

        def pbank_bf(b, n=1024, off=0):
            return PSB[:, b * 1024 + off: b * 1024 + off + n]

        def load_const(name, shape, src, dt=F32, eng="sp"):
            t = sbt(name, shape, dt)
            S.op(eng, lambda e: e.dma_start(out=t[:], in_=src), writes=[name], dma=name)
            return t
        ident_f = load_const("ident_f", [128, 128], cst["c_ident"])
        ident_b = load_const("ident_b", [128, 128], cst["c_ident"], BF16, "pool")
        ones_f = load_const("ones_f", [128, 128], cst["c_ones"])
        ones_b = load_const("ones_b", [128, 128], cst["c_ones"], BF16, "pool")
        perm_b = load_const("perm_b", [128, 128], cst["c_perm"], BF16, "pool")
        tri_f = load_const("tri_f", [128, 128], cst["c_tri"])
        mincl = load_const("mincl", [128, 128], cst["c_mincl"])
        strict = load_const("strict", [128, 128], cst["c_strict"])
        m16 = load_const("m16", [128, 128], cst["c_m16"], BF16, "pool")
        moff = load_const("moff", [128, 384], cst["c_moff"], BF16, "pool")
        rdt = load_const("rdt", [128, 512], cst["c_rdt"], BF16, "pool")
        rqd = load_const("rqd", [128, 512], cst["c_rqd"], BF16, "pool")
        rkd = load_const("rkd", [128, RH], cst["c_rkd"])
        with nc.allow_non_contiguous_dma(reason="small param column loads"):
            def col_load(name, src, n):
                t = sbt(name, [128, n], F32)
                S.op("sp", lambda e: e.dma_start(out=t[:], in_=src.rearrange("(k p) -> p k", p=128), allow_slow_non_contiguous=True),
                     writes=[name], dma=name)
                return t
            gmix_c = col_load("gmix_c", g_mix, 8)
            gx_c = col_load("gx_c", g_x, 8)
            gmem_c = col_load("gmem_c", g_mem, 8)
            gffn_c = col_load("gffn_c", g_ffn, 8)
            ggn_c = col_load("ggn_c", g_gn, 8)
            ggdn_c = col_load("ggdn_c", g_gdn, 1)
            cw = sbt("cw", [128, 4 * 24], F32)
            for i in range(4):
                S.op("sp", lambda e, i=i: e.dma_start(out=cw[:, i * 24:(i + 1) * 24], in_=convw[i].rearrange("(c p) -> p c", p=128),
                                                       allow_slow_non_contiguous=True),
                     writes=["cw"], dma="cw")
        alog_bc = sbt("alog_bc", [128, GH], F32)
        dtb_bc = sbt("dtb_bc", [128, GH], F32)
        S.op("sp", lambda e: e.dma_start(out=alog_bc[:], in_=a_log.rearrange("(o d) -> o d", o=1).partition_broadcast(128)),
             writes=["alog_bc"], dma="alog_bc")
        S.op("sp", lambda e: e.dma_start(out=dtb_bc[:], in_=dt_bias.rearrange("(o d) -> o d", o=1).partition_broadcast(128)),
             writes=["dtb_bc"], dma="dtb_bc")
        nega = sbt("nega", [128, GH], F32)
        S.op("act", lambda e: e.activation(out=nega[:], in_=alog_bc[:], func=AF.Exp), reads=["alog_bc"], writes=["nega"])
        S.op("dve", lambda e: e.tensor_scalar(out=nega[:], in0=nega[:], scalar1=-1.0, scalar2=None, op0=ALU.mult),
             reads=["nega"], writes=["nega"])
        ones128_b = sbt("ones128_b", [128, 128], BF16)
        S.op("dve", lambda e: e.tensor_scalar(out=ones128_b[:], in0=ones_f[:], scalar1=128.0, scalar2=None, op0=ALU.mult),
             reads=["ones_f"], writes=["ones128_b"])
        onesq_b = sbt("onesq_b", [128, 128], BF16)
        S.op("dve", lambda e: e.tensor_scalar(out=onesq_b[:], in0=ones_f[:], scalar1=1.0 / 256, scalar2=None, op0=ALU.mult),
             reads=["ones_f"], writes=["onesq_b"])

        NSLOT = 6
        wslots = [sbt(f"wslot{i}", [128, 8 * WC], BF16) for i in range(NSLOT)]
        wstate = dict(i=0)

        scratch = {}

        def wload(W, r0, nk, c0, ncols):
            s = wstate["i"] % NSLOT
            wstate["i"] += 1
            t = wslots[s]
            dst = t[:, 0:nk * ncols].rearrange("p (k c) -> p k c", c=ncols)
            key = (W.tensor.name, r0, nk, c0, ncols)
            if key not in scratch:
                sc = nc.dram_tensor(f"scr{len(scratch)}", [128, nk * ncols], BF16, kind="Internal").ap()
                scratch[key] = (sc, f"scr{len(scratch)}")
                sc, sres = scratch[key]
                src = W[r0:r0 + nk * 128, c0:c0 + ncols].rearrange("(k p) c -> p k c", p=128)
                S.op("pool", lambda e: e.dma_start(out=dst, in_=src), writes=[f"wslot{s}"], dma=f"wslot{s}")
                S.op("sp", lambda e: e.dma_start(out=sc, in_=t[:, 0:nk * ncols]), reads=[f"wslot{s}"], writes=[sres], dma=f"wst{s}")
            else:
                sc, sres = scratch[key]
                S.op("pool", lambda e: e.dma_start(out=t[:, 0:nk * ncols], in_=sc), reads=[sres], writes=[f"wslot{s}"], dma=f"wslot{s}")
            return t, f"wslot{s}"

        def wv(t, k, ncols, c0=0, n=None):
            n = ncols if n is None else n
            return t[:, k * ncols + c0: k * ncols + c0 + n]

        def rstd_from_ss(ss, n, eps, res):
            S.op("dve", lambda e: e.tensor_scalar(out=ss, in0=ss, scalar1=1.0 / n, scalar2=eps, op0=ALU.mult, op1=ALU.add),
                 reads=[res], writes=[res])
            S.op("act", lambda e: e.activation(out=ss, in_=ss, func=AF.Ln), reads=[res], writes=[res])
            S.op("act", lambda e: e.activation(out=ss, in_=ss, func=AF.Exp, scale=-0.5), reads=[res], writes=[res])

        xn = sbt("xn", [128, D], BF16)
        sqj = xn
        sscol = sbt("sscol", [128, 4], F32)

        def norm_transpose(xt, xres, gcol, gres, dstT, dres, ntok, tcol, ncols_total):
            S.op("act", lambda e: e.activation(out=sqj[0:ntok, :], in_=xt, func=AF.Square, accum_out=sscol[0:ntok, 0:1]),
                 reads=[xres], writes=["xn", "sscol"])
            rstd_from_ss(sscol[0:ntok, 0:1], D, EPS, "sscol")
            S.op("dve", lambda e: e.tensor_scalar(out=xn[0:ntok, :], in0=xt, scalar1=sscol[0:ntok, 0:1], scalar2=None, op0=ALU.mult),
                 reads=[xres, "sscol"], writes=["xn"])
            b = S.bank()

            def tr(e):
                ins = None
                for k in range(8):
                    ins = e.transpose(pbank_bf(b, ntok, k * 128)[:, :], xn[0:ntok, k * 128:(k + 1) * 128], ident_b[0:ntok, 0:ntok])
                return ins
            S.op("pe", tr, reads=["xn", "ident_b"], writes=[f"pb{b}"])
            src = pbank_bf(b, 1024).rearrange("p (k i) -> p k i", i=128)[:, :, 0:ntok]
            dst = dstT[:, :].rearrange("p (k t) -> p k t", t=ncols_total)[:, :, tcol:tcol + ntok]
            S.op("dve", lambda e: e.tensor_tensor(out=dst, in0=src, in1=gcol[:, 0:8].unsqueeze(2).to_broadcast([128, 8, ntok]), op=ALU.mult),
                 reads=[f"pb{b}", gres], writes=[dres])

        def proj_fm(W, r0, nk, c0, ncols, srcT, sres, src_cols, ntok, consume):
            t, wres = wload(W, r0, nk, c0, ncols)
            for j in range(ncols // 128):
                b = S.bank()

                def mm(e, j=j, b=b):
                    ins = None
                    for k in range(nk):
                        ins = e.matmul(pbank(b, ntok), lhsT=wv(t, k, ncols, j * 128, 128),
                                       rhs=srcT[:, k * src_cols: k * src_cols + ntok], start=(k == 0), stop=(k == nk - 1))
                    return ins
                S.op("pe", mm, reads=[wres, sres], writes=[f"pb{b}"])
                consume(j, b)

        def proj_tm(W, r0, nk, c0, ncols, srcT, sres, src_cols, ntiles, consume, tw=128):
            t, wres = wload(W, r0, nk, c0, ncols)
            for tau in range(ntiles):
                b = S.bank()

                def mm(e, tau=tau, b=b):
                    ins = None
                    for k in range(nk):
                        ins = e.matmul(pbank(b, ncols)[0:tw, :], lhsT=srcT[:, k * src_cols + tau * tw: k * src_cols + (tau + 1) * tw],
                                       rhs=wv(t, k, ncols), start=(k == 0), stop=(k == nk - 1))
                    return ins
                S.op("pe", mm, reads=[wres, sres], writes=[f"pb{b}"])
                consume(tau, b)


        def run_phase(is_sample):
            TBv = NS if is_sample else TB
            tw = NS if is_sample else 128
            NTv = TBv // tw
            nblk_v = 1 if is_sample else nblk
            esp = ExitStack()
            cur_es[0] = esp
            sfx[0] = "_s" if is_sample else "_p"
            XOR = [sbt(f"XOR{i_}", [128, NTv * D], F32) for i_ in range(2)]
            hnT = sbt("hnT", [128, 8 * TBv])
            RG = sbt("RG", [128, 8 * TBv])
            Z = sbt("Z", [128, 8 * TBv]); GA = sbt("GA", [128, 8 * TBv]); GB = sbt("GB", [128, 8 * TBv])
            NROT = 4
            _rot = {"tmpA": [sbt(f"tmpA{i}", [128, TBv], F32) for i in range(NROT)],
                    "tmpB": [sbt(f"tmpB{i}", [128, TBv], F32) for i in range(NROT)],
                    "tmpb": [sbt(f"tmpb{i}", [128, TBv], BF16) for i in range(NROT)],
                    "CACC": [sbt(f"CACC{i}", [128, TBv], F32) for i in range(NROT)]}
            _roti = {}

            def nxt(name):
                i = _roti.get(name, 0)
                _roti[name] = i + 1
                return _rot[name][i % NROT], f"{name}{i % NROT}"
            OG = sbt("OG", [128, 8 * TBv], F32)
            OR = XOR[1][:, 0:8 * TBv]
            ORg = sbt("ORg", [128, 8 * TBv]); OGg = sbt("OGg", [128, 8 * TBv])
            MRG = sbt("MRG", [128, 8 * TBv])
            if is_sample:
                FA = sbt("FA", [128, 22 * TBv])
                FA_AL = []
            else:
                QKVB = sbt("QKVB", [128, 24 * TBv])
                FA = QKVB[:, 0:22 * TBv]
                FA_AL = ["GQ", "GK", "GV"]
            if is_sample:
                PJ = sbt("PJ", [NS, 5136], F32)
                GNSs = sbt("GNSs", [128, 8 * TBv], F32)
                RNSs = sbt("RNSs", [128, 8 * TBv], F32)
                RNS = [(RNSs[:, 0:4 * TBv], "RNSa"), (RNSs[:, 4 * TBv:8 * TBv], "RNSb")]
                gfin_s = sbt("gfin_s", [128, D], F32)
            if not is_sample:
                RQ = sbt("RQ", [128, 4 * TBv]); RK = sbt("RK", [128, 4 * TBv]); RQd = sbt("RQd", [128, 4 * TBv])
                VtR = sbt("VtR", [128, NTv * 1024])
                GQ = QKVB[:, 0:8 * TBv]; GK = QKVB[:, 8 * TBv:16 * TBv]; GV = QKVB[:, 16 * TBv:24 * TBv]
                BA = sbt("BA", [128, NTv * 16], F32)
                _rot["CIN"] = [sbt(f"CIN{i}", [128, 3 + TBv], F32) for i in range(NROT)]
                HALO = sbt("HALO", [128, 24 * 3], F32)
                cosb = sbt("cosb", [128, TBv], F32); sinb = sbt("sinb", [128, TBv], F32)
                SR = sbt("SR", [128, RH * RDV], F32); SRb = sbt("SRb", [128, RH * RDV])
                SG = sbt("SG", [128, GH * GDV], F32); SGb = sbt("SGb", [128, GH * GDV])
                MKT = sbt("MKT", [128, 8 * NMEM]); MV = sbt("MV", [128, 2 * D])
                memx = OR[:, 0:D]
                mnT = OGg
                mo = OG[:, 0:512]

                S.op("dve", lambda e: e.memset(SR[:], 0.0), writes=["SR"])
                S.op("dve", lambda e: e.memset(SRb[:], 0.0), writes=["SRb"])
                S.op("dve", lambda e: e.memset(SG[:], 0.0), writes=["SG0", "SG1"])
                S.op("dve", lambda e: e.memset(SGb[:], 0.0), writes=["SGb0", "SGb1"])
                S.op("dve", lambda e: e.memset(HALO[:], 0.0), writes=["HALO"])

                for mt in range(2):
                    S.op("sp", lambda e, mt=mt: e.dma_start(out=memx[:], in_=memp[mt * 128:(mt + 1) * 128, :]), writes=["XOR1"], dma="memx")
                    norm_transpose(memx, "XOR1", gmem_c, "gmem_c", mnT, "OGg", 128, mt * 128, NMEM)
                for (W, outd, isk) in ((w_xk, mkp, True), (w_xv, mvp, False)):
                    for half in range(NQ):
                        def cons(tau, b, half=half, outd=outd, isk=isk):
                            S.op("act", lambda e: e.activation(out=mo[:, 0:WC], in_=pbank(b, WC), func=AF.Copy), reads=[f"pb{b}"], writes=["OG"])
                            if not isk:
                                S.op("dve", lambda e: e.tensor_copy(out=MV[:, tau * D + half * WC: tau * D + half * WC + WC], in_=pbank(b, WC)),
                                     reads=[f"pb{b}"], writes=["MV"])
                            S.op("sp", lambda e: e.dma_start(out=outd[tau * 128:(tau + 1) * 128, half * WC:(half + 1) * WC], in_=mo[:, 0:WC]),
                                 reads=["OG"], dma="mo_out")
                        proj_tm(W, 0, 8, half * WC, WC, mnT, "OGg", NMEM, 2, cons)
                for half in range(NQ):
                    def cons(j, b, half=half):
                        c = half * (WC // 128) + j
                        S.op("act", lambda e: e.activation(out=MKT[:, c * NMEM:(c + 1) * NMEM], in_=pbank(b, NMEM), func=AF.Copy),
                             reads=[f"pb{b}"], writes=["MKT"])
                    proj_fm(w_xk, 0, 8, half * WC, WC, mnT, "OGg", NMEM, NMEM, cons)

                gsmT = [sbt(f"gsm{t_}", [128, 64], F32) for t_ in range(NTv)]
                D1 = sbt("D1", [128, 1024], F32)
                GBm = sbt("GBm", [128, 1024], F32)
                G2 = GBm
                EGC = GBm
                GCTX = []
                for g_ in range(2):
                    GCTX.append((sbt(f"N0_{g_}", [128, 512], BF16), sbt(f"N0T_{g_}", [128, 512], BF16),
                                 sbt(f"WA_{g_}", [128, 1024], BF16), sbt(f"WB_{g_}", [128, 1024], BF16),
                                 [sbt(f"MO{l}_{g_}", [128, 512], BF16) for l in range(3)],
                                 [sbt(f"MOT{l}_{g_}", [128, 512], BF16) for l in range(2)],
                                 sbt(f"V1s_{g_}", [128, 512], BF16), sbt(f"V2s_{g_}", [128, 512], BF16)))
                TTf = sbt("TTf", [128, 1024], BF16)
                RNS = [(D1[:, 0:4 * TBv], "D1"), (GBm[:, 0:4 * TBv], "GBm")]
                PTg = sbt("PTg", [128, 1024], BF16)
                QTdT = [sbt(f"QTd{i_}", [128, 1024], BF16) for i_ in range(2)]
                KtG = sbt("KtG", [128, 1024], BF16)
                VtG = sbt("VtG", [128, 1024], BF16)
                rtl = sbt("rtl", [128, 1024], BF16)
                vnw = sbt("vnw", [128, 1024], BF16)
                PTr = sbt("PTr", [128, 512], BF16)
                KtR = sbt("KtR", [128, 512], BF16)
                identb8 = sbt("identb8", [128, 512], BF16)
                S.op("dve", lambda e: e.tensor_copy(out=identb8[:, :].rearrange("p (h i) -> p h i", i=128),
                                                     in_=ident_f[:, :].unsqueeze(1).to_broadcast([128, 4, 128])),
                     reads=["ident_f"], writes=["identb8"])

            if is_sample:
                AXX = mybir.AxisListType.X
                RES = 7
                cs = sbt("cs_s", [NS, 256], F32)
                S.op("sp", lambda e: e.dma_start(out=cs[:], in_=cst["c_cs_s"].rearrange("(o a) d -> o (a d)", o=1).partition_broadcast(NS)),
                     writes=["cs_s"], dma="cs_s")
                QR = sbt("QR", [NS, 512], F32); KR = sbt("KR", [NS, 512], F32); T1s = sbt("T1s", [NS, 512], F32)
                QKVs = sbt("QKVs", [NS, 3072], F32); QKn = sbt("QKn", [NS, 2048], F32)
                SCc = [sbt(f"SCc{i}", [NS, 3 * 512], F32) for i in range(2)]; CWc = [sbt(f"CWc{i}", [NS, 4 * 512], F32) for i in range(2)]
                ACC = sbt("ACC", [NS, 512], F32); TMPs = sbt("TMPs", [NS, 512], F32)
                gs = sbt("gs", [NS, 64], F32)
                BV = sbt("BV", [NS, 1024], F32); Rr = sbt("Rr", [NS, 1024], F32)
                KMr = sbt("KMr", [NS, 512], F32); KMg = sbt("KMg", [NS, 1024], F32)
                EGd = sbt("EGd", [NS, 128], F32); EGB = sbt("EGB", [128, 128], F32)
                qTr = sbt("qTr", [128, 64], BF16); qTg = sbt("qTg", [128, 128], BF16); kTg = sbt("kTg", [128, 128], F32)
                SRob = sbt("SRob", [128, 1024], BF16); SGob = sbt("SGob", [128, 1024], BF16); Vcb = [sbt("Vcb0", [128, 2048], BF16)] * 2
                ARENA = sbt("ARENA", [128, 8192], F32)
                SRin = [ARENA[:, i * 1024:(i + 1) * 1024] for i in range(2)]
                SGin = [ARENA[:, (2 + i) * 1024:(3 + i) * 1024] for i in range(2)]
                SRo = [ARENA[:, (4 + i) * 1024:(5 + i) * 1024] for i in range(2)]
                SGo = [ARENA[:, (6 + i) * 1024:(7 + i) * 1024] for i in range(2)]
                Kc = [ARENA[:, i * 2048:(i + 1) * 2048] for i in range(2)]
                Vc = [ARENA[:, (2 + i) * 2048:(3 + i) * 2048] for i in range(2)]
                ARENA_RES = [f"{n}{i}" for n in ("SRin", "SGin", "SRo", "SGo") for i in range(2)]
                QD = [sbt("QD0", [128, 1024], F32)] * 2; PR = sbt("PR", [128, 1024], F32)
                SCs = sbt("SCs", [128, 128], F32); Pm = sbt("Pm", [128, 128], BF16); RD = sbt("RD", [128, 64], F32)

                def rope(src0, dst, dres, scale):
                    x = PJ[0:NS, src0:src0 + 512].rearrange("p (h d) -> p h d", d=128)
                    d3 = dst[:, :].rearrange("p (h d) -> p h d", d=128)
                    t3 = T1s[:, :].rearrange("p (h d) -> p h d", d=128)
                    C = cs[:, 0:128].unsqueeze(1).to_broadcast([NS, 4, 128])
                    S.op("dve", lambda e: e.tensor_tensor(out=t3, in0=x, in1=C, op=ALU.mult), reads=["PJ", "cs_s"], writes=["T1s"])
                    S.op("dve", lambda e: e.tensor_tensor(out=d3[:, :, 0:64], in0=x[:, :, 64:128],
                                                          in1=cs[:, 128:192].unsqueeze(1).to_broadcast([NS, 4, 64]), op=ALU.mult),
                         reads=["PJ", "cs_s"], writes=[dres])
                    S.op("dve", lambda e: e.tensor_tensor(out=d3[:, :, 64:128], in0=x[:, :, 0:64],
                                                          in1=cs[:, 192:256].unsqueeze(1).to_broadcast([NS, 4, 64]), op=ALU.mult),
                         reads=["PJ", "cs_s"], writes=[dres])
                    S.op("dve", lambda e: e.scalar_tensor_tensor(out=dst[:, :], in0=dst[:, :], scalar=1.0, in1=T1s[:, :], op0=ALU.mult, op1=ALU.add),
                         reads=[dres, "T1s"], writes=[dres])
                    if scale != 1.0:
                        S.op("dve", lambda e: e.tensor_scalar(out=dst[:, :], in0=dst[:, :], scalar1=scale, scalar2=None, op0=ALU.mult),
                             reads=[dres], writes=[dres])

                def to_fm(src, sres, nh, dst, dres):
                    b = S.bank()

                    def tr(e):
                        ins = None
                        for h in range(nh):
                            ins = e.transpose(pbank(b, NS, h * NS), src[0:NS, h * 128:(h + 1) * 128], ident_f[0:NS, 0:NS])
                        return ins
                    S.op("pe", tr, reads=[sres, "ident_f"], writes=[f"pb{b}"])
                    S.op("act", lambda e: e.activation(out=dst[:, 0:nh * NS], in_=pbank(b, nh * NS), func=AF.Copy), reads=[f"pb{b}"], writes=[dres])

                def sample_mixers():
                    rope(0, QR, "QR", 1.0)
                    rope(512, KR, "KR", RDK ** -0.5)
                    to_fm(QR, "QR", 4, qTr, "qTr")
                    S.op("sp", lambda e: e.dma_start(out=scs[:, 0:2, :], in_=sconv[:, 1:3, :]), dma="scs_a")
                    S.op("sp", lambda e: e.dma_start(out=scs[:, 2, :], in_=PJ[0:NS, 2048:5120]), reads=["PJ"], dma="scs_b")
                    def conv_loads(cchunk):
                        c0 = cchunk * 512
                        pp = cchunk % 2
                        S.op("sp", lambda e: e.dma_start(out=SCc[pp][:, :].rearrange("p (i c) -> p i c", c=512), in_=sconv[:, :, c0:c0 + 512]),
                             writes=[f"SCc{pp}"], dma=f"SCc{pp}")
                        for i in range(4):
                            S.op("sp", lambda e, i=i: e.dma_start(out=CWc[pp][:, i * 512:(i + 1) * 512], in_=convw[i:i + 1, c0:c0 + 512].partition_broadcast(NS)),
                                 writes=[f"CWc{pp}"], dma=f"CWc{pp}")
                    conv_loads(0)
                    for cchunk in range(6):
                        c0 = cchunk * 512
                        pp = cchunk % 2
                        if cchunk + 1 < 6:
                            conv_loads(cchunk + 1)
                        SC_, CW_, rsc, rcw = SCc[pp], CWc[pp], f"SCc{pp}", f"CWc{pp}"
                        S.op("dve", lambda e, SC_=SC_, CW_=CW_: e.tensor_tensor(out=ACC[:, :], in0=SC_[:, 0:512], in1=CW_[:, 0:512], op=ALU.mult),
                             reads=[rsc, rcw], writes=["ACC"])
                        for i in range(1, 4):
                            src = SC_[:, i * 512:(i + 1) * 512] if i < 3 else PJ[0:NS, 2048 + c0: 2048 + c0 + 512]
                            S.op("dve", lambda e, src=src, i=i, CW_=CW_: e.tensor_tensor(out=TMPs[:, 0:512], in0=src, in1=CW_[:, i * 512:(i + 1) * 512], op=ALU.mult),
                                 reads=[rsc, rcw, "PJ"], writes=["TMPs"])
                            S.op("dve", lambda e: e.tensor_tensor(out=ACC[:, :], in0=ACC[:, :], in1=TMPs[:, 0:512], op=ALU.add), reads=["ACC", "TMPs"], writes=["ACC"])
                        S.op("act", lambda e, c0=c0: e.activation(out=QKVs[:, c0:c0 + 512], in_=ACC[:, :], func=AF.Silu), reads=["ACC"], writes=["QKVs"])
                    S.op("dve", lambda e: e.tensor_tensor(out=QKn[:, :], in0=QKVs[:, 0:2048], in1=QKVs[:, 0:2048], op=ALU.mult), reads=["QKVs"], writes=["QKn"])
                    S.op("dve", lambda e: e.tensor_reduce(out=gs[:, 32:48], in_=QKn[:, :].rearrange("p (h d) -> p h d", d=128), axis=AXX, op=ALU.add),
                         reads=["QKn"], writes=["gs"])
                    S.op("dve", lambda e: e.tensor_scalar(out=gs[:, 32:48], in0=gs[:, 32:48], scalar1=EPS, scalar2=None, op0=ALU.add), reads=["gs"], writes=["gs"])
                    S.op("act", lambda e: e.activation(out=gs[:, 32:48], in_=gs[:, 32:48], func=AF.Ln), reads=["gs"], writes=["gs"])
                    S.op("act", lambda e: e.activation(out=gs[:, 32:48], in_=gs[:, 32:48], func=AF.Exp, scale=-0.5), reads=["gs"], writes=["gs"])
                    S.op("dve", lambda e: e.tensor_scalar(out=gs[:, 32:40], in0=gs[:, 32:40], scalar1=GDK ** -0.5, scalar2=None, op0=ALU.mult), reads=["gs"], writes=["gs"])
                    S.op("dve", lambda e: e.tensor_tensor(out=QKn[:, :].rearrange("p (h d) -> p h d", d=128),
                                                          in0=QKVs[:, 0:2048].rearrange("p (h d) -> p h d", d=128),
                                                          in1=gs[:, 32:48].unsqueeze(2).to_broadcast([NS, 16, 128]), op=ALU.mult),
                         reads=["QKVs", "gs"], writes=["QKn"])
                    to_fm(QKn[:, 0:1024], "QKn", 8, qTg, "qTg")
                    to_fm(QKn[:, 1024:2048], "QKn", 8, kTg, "kTg")
                    ba = PJ[0:NS, 5120:5136]
                    S.op("act", lambda e: e.activation(out=gs[:, 0:8], in_=ba[:, 0:8], func=AF.Sigmoid), reads=["PJ", "gs"], writes=["gs"])
                    S.op("dve", lambda e: e.tensor_tensor(out=gs[:, 8:16], in0=ba[:, 8:16], in1=dtb_bc[0:NS, :], op=ALU.add), reads=["PJ", "dtb_bc", "gs"], writes=["gs"])
                    S.op("act", lambda e: e.activation(out=gs[:, 8:16], in_=gs[:, 8:16], func=AF.Exp), reads=["gs"], writes=["gs"])
                    S.op("dve", lambda e: e.tensor_scalar(out=gs[:, 8:16], in0=gs[:, 8:16], scalar1=1.0, scalar2=None, op0=ALU.add), reads=["gs"], writes=["gs"])
                    S.op("act", lambda e: e.activation(out=gs[:, 8:16], in_=gs[:, 8:16], func=AF.Ln), reads=["gs"], writes=["gs"])
                    S.op("dve", lambda e: e.tensor_tensor(out=gs[:, 8:16], in0=gs[:, 8:16], in1=nega[0:NS, :], op=ALU.mult), reads=["gs", "nega"], writes=["gs"])
                    S.op("act", lambda e: e.activation(out=gs[:, 16:24], in_=gs[:, 8:16], func=AF.Exp), reads=["gs"], writes=["gs"])
                    S.op("dve", lambda e: e.scalar_tensor_tensor(out=gs[:, 24:32], in0=gs[:, 16:24], scalar=-1.0, in1=gs[:, 0:8], op0=ALU.mult, op1=ALU.mult),
                         reads=["gs"], writes=["gs"])
                    S.op("dve", lambda e: e.tensor_tensor(out=BV[:, :].rearrange("p (h d) -> p h d", d=128),
                                                          in0=QKVs[:, 2048:3072].rearrange("p (h d) -> p h d", d=128),
                                                          in1=gs[:, 0:8].unsqueeze(2).to_broadcast([NS, 8, 128]), op=ALU.mult),
                         reads=["QKVs", "gs"], writes=["BV"])
                    S.op("dve", lambda e: e.tensor_tensor(out=EGd[:, :].rearrange("p (b h) -> p b h", h=8),
                                                          in0=ident_f[0:NS, 0:NS].unsqueeze(2).to_broadcast([NS, NS, 8]),
                                                          in1=gs[:, 16:24].unsqueeze(1).to_broadcast([NS, NS, 8]), op=ALU.mult),
                         reads=["ident_f", "gs"], writes=["EGd"])
                    bq = S.bank()
                    S.op("pe", lambda e: e.matmul(pbank(bq, 128), lhsT=ones_f[0:NS, :], rhs=EGd[:, :], start=True, stop=True),
                         reads=["ones_f", "EGd"], writes=[f"pb{bq}"])
                    S.op("act", lambda e: e.activation(out=EGB[:, :], in_=pbank(bq, 128), func=AF.Copy), reads=[f"pb{bq}"], writes=["EGB"])
                    S.reserved = {RES}

                    def loads(b):
                        p = b % 2
                        S.op("sp", lambda e: e.dma_start(out=SRin[p][:, :].rearrange("d (h v) -> d h v", v=RDV), in_=sret[b].rearrange("h d v -> d h v")),
                             writes=[f"SRin{p}"], dma=f"SRin{p}")
                        S.op("sp", lambda e: e.dma_start(out=SGin[p][:, :].rearrange("d (h v) -> d h v", v=GDV), in_=sgdn[b].rearrange("h d v -> d h v")),
                             writes=[f"SGin{p}"], dma=f"SGin{p}")
                    loads(0)
                    for b in range(NS):
                        p = b % 2
                        if b + 1 < NS:
                            loads(b + 1)
                        eb = ident_f[0:NS, b:b + 1]
                        S.op("dve", lambda e, eb=eb: e.tensor_scalar(out=KMr[:, :], in0=KR[:, :], scalar1=eb, scalar2=None, op0=ALU.mult), reads=["KR", "ident_f"], writes=["KMr"])
                        S.op("dve", lambda e, eb=eb: e.tensor_scalar(out=KMg[:, :], in0=QKn[:, 1024:2048], scalar1=eb, scalar2=None, op0=ALU.mult),
                             reads=["QKn", "ident_f"], writes=["KMg"])
                        def ret_chain(b=b, p=p):
                            yield
                            while S.bank_i % 2 != 0:
                                S.bank()
                            b0 = S.bank(); b1 = S.bank()
                            if b1 != b0 + 1:
                                while S.bank_i % 2 != 0:
                                    S.bank()
                                b0 = S.bank(); b1 = S.bank()

                            def mm_a(e, b0=b0):
                                ins = None
                                for h in range(RH):
                                    ins = e.matmul(PS[:, b0 * 512 + h * 256: b0 * 512 + (h + 1) * 256], lhsT=KMr[0:NS, h * 128:(h + 1) * 128],
                                                   rhs=PJ[0:NS, 1024 + h * 256: 1024 + (h + 1) * 256], start=True, stop=True)
                                return ins
                            S.op("pe", mm_a, reads=["KMr", "PJ"], writes=[f"pb{b0}", f"pb{b1}"])
                            for h in range(RH):
                                S.op("dve", lambda e, h=h, b0=b0, p=p: e.scalar_tensor_tensor(
                                    out=SRo[p][:, h * 256:(h + 1) * 256], in0=SRin[p][:, h * 256:(h + 1) * 256], scalar=float(_GAM[h]),
                                    in1=PS[:, b0 * 512 + h * 256: b0 * 512 + (h + 1) * 256], op0=ALU.mult, op1=ALU.add),
                                    reads=[f"SRin{p}", f"pb{b0}", f"pb{b1}"], writes=[f"SRo{p}"])

                            yield
                            S.op("act", lambda e, p=p: e.activation(out=SRob[:, :], in_=SRo[p][:, :], func=AF.Copy), reads=[f"SRo{p}"], writes=["SRob"])

                            def mm_b(e, b=b, p=p):
                                ins = None
                                for h in range(RH):
                                    for c in range(2):
                                        ins = e.matmul(pbank(RES, 1, (h * 2 + c) * NS + b), lhsT=SRob[:, h * 256 + c * 128: h * 256 + (c + 1) * 128],
                                                       rhs=qTr[:, h * NS + b: h * NS + b + 1], start=True, stop=True)
                                return ins
                            S.op("pe", mm_b, reads=["SRob", "qTr"], writes=[f"pb{RES}"])
                            S.op("act", lambda e, b=b, p=p: e.dma_start(out=srs[b].rearrange("h d v -> d h v"), in_=SRo[p][:, :].rearrange("d (h v) -> d h v", v=RDV)),
                                 reads=[f"SRo{p}"], dma=f"SRo{p}o")
                            yield
                        def gdn_chain(b=b, p=p):
                            yield
                            while S.bank_i % 2 != 0:
                                S.bank()
                            c0_ = S.bank(); c1_ = S.bank()
                            if c1_ != c0_ + 1:
                                while S.bank_i % 2 != 0:
                                    S.bank()
                                c0_ = S.bank(); c1_ = S.bank()

                            def mm_1(e, c0_=c0_, p=p):
                                ins = None
                                for h in range(GH):
                                    ins = e.matmul(PS[0:NS, c0_ * 512 + h * 128: c0_ * 512 + (h + 1) * 128], lhsT=kTg[:, h * NS:(h + 1) * NS],
                                                   rhs=SGin[p][:, h * 128:(h + 1) * 128], start=True, stop=True)
                                return ins
                            S.op("pe", mm_1, reads=["kTg", f"SGin{p}"], writes=[f"pb{c0_}", f"pb{c1_}"])
                            S.op("dve", lambda e, c0_=c0_: e.tensor_tensor(out=Rr[:, :].rearrange("p (h d) -> p h d", d=128),
                                                                           in0=PS[0:NS, c0_ * 512: c0_ * 512 + 1024].rearrange("p (h d) -> p h d", d=128),
                                                                           in1=gs[:, 24:32].unsqueeze(2).to_broadcast([NS, 8, 128]), op=ALU.mult),
                                 reads=[f"pb{c0_}", f"pb{c1_}", "gs"], writes=["Rr"])
                            S.op("dve", lambda e: e.tensor_tensor(out=Rr[:, :], in0=Rr[:, :], in1=BV[:, :], op=ALU.add), reads=["Rr", "BV"], writes=["Rr"])
                            yield
                            while S.bank_i % 2 != 0:
                                S.bank()
                            d0_ = S.bank(); d1_ = S.bank()
                            if d1_ != d0_ + 1:
                                while S.bank_i % 2 != 0:
                                    S.bank()
                                d0_ = S.bank(); d1_ = S.bank()

                            def mm_2(e, d0_=d0_):
                                ins = None
                                for h in range(GH):
                                    ins = e.matmul(PS[:, d0_ * 512 + h * 128: d0_ * 512 + (h + 1) * 128], lhsT=KMg[0:NS, h * 128:(h + 1) * 128],
                                                   rhs=Rr[0:NS, h * 128:(h + 1) * 128], start=True, stop=True)
                                return ins
                            S.op("pe", mm_2, reads=["KMg", "Rr"], writes=[f"pb{d0_}", f"pb{d1_}"])
                            S.op("dve", lambda e, b=b, p=p: e.tensor_tensor(out=SGo[p][:, :].rearrange("p (h d) -> p h d", d=128),
                                                                          in0=SGin[p][:, :].rearrange("p (h d) -> p h d", d=128),
                                                                          in1=EGB[:, b * 8:(b + 1) * 8].unsqueeze(2).to_broadcast([128, 8, 128]), op=ALU.mult),
                                 reads=[f"SGin{p}", "EGB"], writes=[f"SGo{p}"])
                            S.op("dve", lambda e, d0_=d0_, p=p: e.tensor_tensor(out=SGo[p][:, :], in0=SGo[p][:, :], in1=PS[:, d0_ * 512: d0_ * 512 + 1024], op=ALU.add),
                                 reads=[f"SGo{p}", f"pb{d0_}", f"pb{d1_}"], writes=[f"SGo{p}"])

                            yield
                            S.op("act", lambda e, p=p: e.activation(out=SGob[:, :], in_=SGo[p][:, :], func=AF.Copy), reads=[f"SGo{p}"], writes=["SGob"])

                            def mm_3(e, b=b, p=p):
                                ins = None
                                for h in range(GH):
                                    ins = e.matmul(pbank(RES, 1, 128 + h * NS + b), lhsT=SGob[:, h * 128:(h + 1) * 128],
                                                   rhs=qTg[:, h * NS + b: h * NS + b + 1], start=True, stop=True)
                                return ins
                            S.op("pe", mm_3, reads=["SGob", "qTg"], writes=[f"pb{RES}"])
                            S.op("act", lambda e, b=b, p=p: e.dma_start(out=sgs[b].rearrange("h d v -> d h v"), in_=SGo[p][:, :].rearrange("d (h v) -> d h v", v=GDV)),
                                 reads=[f"SGo{p}"], dma=f"SGo{p}o")
                            yield
                        _gens = [ret_chain(), gdn_chain()]
                        while _gens:
                            for _g in list(_gens):
                                try:
                                    next(_g)
                                except StopIteration:
                                    _gens.remove(_g)
                    S.op("act", lambda e: e.activation(out=OR[:, 0:128], in_=pbank(RES, 128), func=AF.Copy), reads=[f"pb{RES}"], writes=["XOR1"])
                    S.op("act", lambda e: e.activation(out=OG[:, 0:128], in_=pbank(RES, 128, 128), func=AF.Copy), reads=[f"pb{RES}"], writes=["OG"])
                    S.reserved = set()

                def sample_xattn(XQ, OX):
                    S.reserved = {RES}

                    def loads(b):
                        p = b % 2
                        S.op("sp", lambda e: e.dma_start(out=Kc[p][:, :].rearrange("m (t d) -> m t d", d=D), in_=cmk[b].rearrange("(t m) d -> m t d", m=128)),
                             writes=[f"Kc{p}"] + ARENA_RES, dma=f"Kc{p}")
                        S.op("sp", lambda e: e.dma_start(out=Vc[p][:, :].rearrange("m (t d) -> m t d", d=D), in_=cmv[b].rearrange("(t m) d -> m t d", m=128)),
                             writes=[f"Vc{p}"] + ARENA_RES, dma=f"Vc{p}")

                    qinfo = {}

                    def qstage(b):
                        p = b % 2
                        S.op("pool", lambda e: e.tensor_tensor(out=QD[p][:, :].rearrange("p (c d) -> p c d", d=128),
                                                               in0=ident_f[:, :].unsqueeze(1).to_broadcast([128, 8, 128]),
                                                               in1=XQ[:, :].rearrange("p (c t) -> p c t", t=NS)[:, :, b:b + 1].to_broadcast([128, 8, 128]), op=ALU.mult),
                             reads=["ident_f", "ORg"], writes=["QD0"])
                        while S.bank_i % 2 != 0:
                            S.bank()
                        b0 = S.bank(); b1 = S.bank()
                        if b1 != b0 + 1:
                            while S.bank_i % 2 != 0:
                                S.bank()
                            b0 = S.bank(); b1 = S.bank()

                        def mm_q(e):
                            e.matmul(pbank(b0), lhsT=ones_f[:], rhs=QD[p][:, 0:512], start=True, stop=True)
                            return e.matmul(pbank(b1), lhsT=ones_f[:], rhs=QD[p][:, 512:1024], start=True, stop=True)
                        S.op("pe", mm_q, reads=["ones_f", "QD0"], writes=[f"pb{b0}", f"pb{b1}"])
                        qinfo[b] = (b0, b1)
                    loads(0)
                    qstage(0)
                    for b in range(NS):
                        p = b % 2
                        if b + 1 < NS:
                            loads(b + 1)
                            qstage(b + 1)
                        b0, b1 = qinfo[b]
                        S.op("act", lambda e, p=p: e.activation(out=Vcb[p][:, :], in_=Vc[p][:, :], func=AF.Copy), reads=[f"Vc{p}"], writes=["Vcb0"])
                        for mt in range(2):
                            S.op("dve", lambda e, mt=mt, p=p, b0=b0: e.tensor_tensor(out=PR[:, :], in0=PS[:, b0 * 512: b0 * 512 + 1024],
                                                                                   in1=Kc[p][:, mt * D:(mt + 1) * D], op=ALU.mult),
                                 reads=[f"pb{b0}", f"pb{b1}", f"Kc{p}"], writes=["PR"])
                            S.op("dve", lambda e, mt=mt, b=b: e.tensor_reduce(out=SCs[:, b * 8 + mt * 4: b * 8 + mt * 4 + 4],
                                                                            in_=PR[:, :].rearrange("p (h d) -> p h d", d=XHD), axis=AXX, op=ALU.add),
                                 reads=["PR"], writes=["SCs"])
                        S.op("act", lambda e, b=b: e.activation(out=Pm[:, b * 8:(b + 1) * 8], in_=SCs[:, b * 8:(b + 1) * 8], func=AF.Exp, scale=XHD ** -0.5),
                             reads=["SCs"], writes=["Pm"])

                        def mm_o(e, b=b, p=p):
                            ins = None
                            for ch in range(8):
                                h = ch // 2
                                for mt in range(2):
                                    ins = e.matmul(pbank(RES, 1, ch * NS + b), lhsT=Vcb[p][:, mt * D + ch * 128: mt * D + (ch + 1) * 128],
                                                   rhs=Pm[:, b * 8 + mt * 4 + h: b * 8 + mt * 4 + h + 1], start=(mt == 0), stop=(mt == 1))
                            for mt in range(2):
                                ins = e.matmul(pbank(RES, 4, 128 + b * 4), lhsT=ones_b[:], rhs=Pm[:, b * 8 + mt * 4: b * 8 + mt * 4 + 4],
                                               start=(mt == 0), stop=(mt == 1))
                            return ins
                        S.op("pe", mm_o, reads=["Vcb0", "Pm", "ones_b"], writes=[f"pb{RES}"])
                    S.op("dve", lambda e: e.reciprocal(out=RD[:, :], in_=pbank(RES, 64, 128)), reads=[f"pb{RES}"], writes=["RD"])
                    for ch in range(8):
                        h = ch // 2
                        S.op("dve", lambda e, ch=ch, h=h: e.tensor_tensor(out=OX[:, ch * NS:(ch + 1) * NS], in0=pbank(RES, NS, ch * NS),
                                                                        in1=RD[:, :].rearrange("p (b h) -> p b h", h=4)[:, :, h], op=ALU.mult),
                             reads=[f"pb{RES}", "RD"], writes=["OGg"])
                    S.reserved = set()

            def load_x(blk):
                Xn = XOR[blk % 2]; xr = f"XOR{blk % 2}"
                t0 = blk * TBv
                if is_sample:
                    S.op("sp", lambda e: e.dma_start(out=Xn[0:NS, 0:D], in_=xs), writes=[xr], dma=xr)
                else:
                    S.op("sp", lambda e: e.dma_start(out=Xn[:, :].rearrange("p (a d) -> p a d", d=D),
                                                     in_=xp[t0:t0 + TBv, :].rearrange("(a p) d -> p a d", p=128)),
                         writes=[xr], dma=xr)

            def pre_norm(blk):
                Xn = XOR[blk % 2]; xr = f"XOR{blk % 2}"
                for tau in range(NTv):
                    norm_transpose(Xn[0:tw, tau * D:(tau + 1) * D], xr, gmix_c, "gmix_c", hnT, "hnT", tw, tau * tw, TBv)

            def load_tabs(blk):
                t0 = blk * TBv
                S.op("sp", lambda e: e.dma_start(out=cosb[:], in_=cst["c_cos"][:, t0:t0 + TBv]), writes=["cosb"], dma="cosb")
                S.op("sp", lambda e: e.dma_start(out=sinb[:], in_=cst["c_sin"][:, t0:t0 + TBv]), writes=["sinb"], dma="sinb")

            def do_block(blk):
                if stage < 1:
                    return
                Xb = XOR[blk % 2]; xres = f"XOR{blk % 2}"
                OR = XOR[(blk + 1) % 2][:, 0:8 * TBv]; or_res = f"XOR{(blk + 1) % 2}"
                t0 = blk * TBv
                if blk == 0:
                    load_x(0)
                if not is_sample and blk == 0:
                    load_tabs(0)
                if blk == 0:
                    pre_norm(0)

                pend = []
                qkv_done = [False]
                LATE = ("ga", "gb")
                main_tiles = W_IN_TILES if is_sample else [t_ for t_ in W_IN_TILES if t_[0] not in LATE]
                late_tiles = [] if is_sample else [t_ for t_ in W_IN_TILES if t_[0] in LATE]
                for (kind, c0, ncols) in main_tiles:
                    if kinds is not None and kind not in kinds:
                        continue
                    if kind == "z" and not is_sample and pend is not None and (pend or not qkv_done[0]):
                        for p_ in pend:
                            p_()
                        del pend[:]
                        qkv_done[0] = True
                        for (ssb, sres, dstb, dres) in ((OR, or_res, GQ, "GQ"), (OG, "OG", GK, "GK")):
                            S.op("act", lambda e, ssb=ssb: e.activation(out=ssb[:, :], in_=ssb[:, :], func=AF.Ln), reads=[sres], writes=[sres])
                            S.op("act", lambda e, ssb=ssb: e.activation(out=ssb[:, :], in_=ssb[:, :], func=AF.Exp, scale=-0.5), reads=[sres], writes=[sres])
                            S.op("dve", lambda e, ssb=ssb, dstb=dstb: e.tensor_tensor(out=dstb[:, :], in0=dstb[:, :], in1=ssb[:, :], op=ALU.mult),
                                 reads=[sres, dres], writes=[dres, "FA"])
                    if is_sample and kind in PJ_OFF:
                        pj0 = PJ_OFF[kind][1] + (c0 - PJ_OFF[kind][0])

                        def cons(tau, b, pj0=pj0, ncols=ncols):
                            S.op("act", lambda e: e.activation(out=PJ[0:NS, pj0:pj0 + ncols], in_=pbank(b, ncols)[0:NS, :], func=AF.Copy),
                                 reads=[f"pb{b}"], writes=["PJ"])
                        proj_tm(w_in, 0, 8, c0, ncols, hnT, "hnT", TBv, 1, cons, tw=tw)
                        continue
                    if kind in ("rq", "rk"):
                        dstb = RQ if kind == "rq" else RK

                        def cons(j0, b, dstb=dstb, kind=kind, cb=(c0 - (0 if kind == "rq" else 512)) // 128):
                            j = cb + j0
                            tmpA, rA = nxt("tmpA"); tmpB, rB = nxt("tmpB"); tmpb, rb = nxt("tmpb")
                            S.op("act", lambda e: e.activation(out=tmpb[:], in_=pbank(b, TBv), func=AF.Copy), reads=[f"pb{b}"], writes=[rb])
                            b2 = S.bank()
                            S.op("pe", lambda e: e.matmul(pbank(b2, TBv), lhsT=perm_b[:], rhs=tmpb[:], start=True, stop=True),
                                 reads=[rb, "perm_b"], writes=[f"pb{b2}"])
                            S.op("dve", lambda e: e.tensor_tensor(out=tmpA[:], in0=pbank(b, TBv), in1=cosb[:], op=ALU.mult),
                                 reads=[f"pb{b}", "cosb"], writes=[rA])
                            S.op("dve", lambda e: e.tensor_tensor(out=tmpB[:], in0=pbank(b2, TBv), in1=sinb[:], op=ALU.mult),
                                 reads=[f"pb{b2}", "sinb"], writes=[rB])
                            S.op("dve", lambda e: e.tensor_tensor(out=dstb[:, j * TBv:(j + 1) * TBv], in0=tmpA[:], in1=tmpB[:], op=ALU.add),
                                 reads=[rA, rB], writes=[kind.upper()])
                            if kind == "rq":
                                S.op("dve", lambda e: e.tensor_tensor(
                                    out=RQd[:, j * TBv:(j + 1) * TBv].rearrange("p (a i) -> p a i", i=128),
                                    in0=RQ[:, j * TBv:(j + 1) * TBv].rearrange("p (a i) -> p a i", i=128),
                                    in1=rqd[:, j * 128:(j + 1) * 128].unsqueeze(1).to_broadcast([128, NTv, 128]), op=ALU.mult),
                                    reads=["RQ", "rqd"], writes=["RQd"])
                        proj_fm(w_in, 0, 8, c0, ncols, hnT, "hnT", TBv, TBv, cons)
                    elif kind == "rv":
                        hv = c0 - 1024

                        def cons(tau, b, hv=hv, ncols=ncols):
                            S.op("act", lambda e: e.activation(out=VtR[:, tau * 1024 + hv: tau * 1024 + hv + ncols], in_=pbank(b, ncols), func=AF.Copy),
                                 reads=[f"pb{b}"], writes=["VtR"])
                        proj_tm(w_in, 0, 8, c0, ncols, hnT, "hnT", TBv, NTv, cons)
                    elif kind in ("rg", "z", "ga", "gb"):
                        base = {"rg": 2048, "z": 6144, "ga": 7184, "gb": 8208}[kind]
                        dstb = {"rg": RG, "z": Z, "ga": GA, "gb": GB}[kind]
                        fn = AF.Silu if kind in ("rg", "z") else AF.Tanh
                        fsc = 1.0 if kind in ("rg", "z") else 0.5
                        cb = (c0 - base) // 128

                        def cons(j, b, dstb=dstb, fn=fn, cb=cb, kind=kind, fsc=fsc):
                            S.op("act", lambda e: e.activation(out=dstb[:, (cb + j) * TBv:(cb + j + 1) * TBv], in_=pbank(b, TBv), func=fn, scale=fsc),
                                 reads=[f"pb{b}"], writes=[kind.upper()])
                        proj_fm(w_in, 0, 8, c0, ncols, hnT, "hnT", TBv, TBv, cons)
                    elif kind == "ba":
                        def cons(tau, b):
                            S.op("dve", lambda e: e.tensor_copy(out=BA[:, tau * 16:(tau + 1) * 16], in_=pbank(b, 16)), reads=[f"pb{b}"], writes=["BA"])
                        proj_tm(w_in, 0, 8, c0, ncols, hnT, "hnT", TBv, NTv, cons)
                    else:
                        cb = (c0 - 3072) // 128

                        def cons(j, b, cb=cb, blk=blk):
                            cc = cb + j
                            tmpb, rb = nxt("tmpb")
                            CACC, rCA = nxt("CACC"); CIN, rCI = nxt("CIN")
                            while len(pend) > 1:
                                pend.pop(0)()
                            S.op("act", lambda e: e.activation(out=CIN[:, 0:3], in_=HALO[:, cc * 3:cc * 3 + 3], func=AF.Copy), reads=["HALO"], writes=[rCI])
                            S.op("act", lambda e: e.activation(out=CIN[:, 3:3 + TBv], in_=pbank(b, TBv), func=AF.Copy), reads=[f"pb{b}", rCI], writes=[rCI])
                            S.op("act", lambda e: e.activation(out=HALO[:, cc * 3:cc * 3 + 3], in_=CIN[:, TBv:TBv + 3], func=AF.Copy), reads=[rCI], writes=["HALO"])
                            S.op("dve", lambda e: e.tensor_scalar(out=CACC[:], in0=CIN[:, 0:TBv], scalar1=cw[:, cc:cc + 1], scalar2=None, op0=ALU.mult),
                                 reads=[rCI, "cw"], writes=[rCA])
                            for i in range(1, 4):
                                S.op("dve", lambda e, i=i: e.scalar_tensor_tensor(out=CACC[:], in0=CIN[:, i:i + TBv], scalar=cw[:, i * 24 + cc:i * 24 + cc + 1],
                                                                                 in1=CACC[:], op0=ALU.mult, op1=ALU.add),
                                     reads=[rCI, "cw", rCA], writes=[rCA])

                            def stage_b():
                                if cc >= 16:
                                    S.op("act", lambda e: e.activation(out=GV[:, (cc - 16) * TBv:(cc - 15) * TBv], in_=CACC[:], func=AF.Silu), reads=[rCA], writes=["GV", "FA"])
                                    return
                                dstb, dres, hh = (GQ, "GQ", cc) if cc < 8 else (GK, "GK", cc - 8)
                                ssb, sres = (OR, or_res) if cc < 8 else (OG, "OG")
                                dsl = dstb[:, hh * TBv:(hh + 1) * TBv]
                                S.op("act", lambda e: e.activation(out=dsl, in_=CACC[:], func=AF.Silu), reads=[rCA], writes=[dres, "FA"])
                                S.op("act", lambda e: e.activation(out=tmpb[:], in_=dsl, func=AF.Square), reads=[dres], writes=[rb])
                                b2 = S.bank()
                                lh = ones128_b if cc < 8 else ones_b
                                S.op("pe", lambda e: e.matmul(pbank(b2, TBv), lhsT=lh[:], rhs=tmpb[:], start=True, stop=True),
                                     reads=[rb, "ones_b", "ones128_b"], writes=[f"pb{b2}"])
                                eps = EPS * 128 if cc < 8 else EPS
                                S.op("dve", lambda e: e.tensor_scalar(out=ssb[:, hh * TBv:(hh + 1) * TBv], in0=pbank(b2, TBv), scalar1=eps, scalar2=None, op0=ALU.add),
                                     reads=[f"pb{b2}"], writes=[sres])
                            pend.append(stage_b)
                        proj_fm(w_in, 0, 8, c0, ncols, hnT, "hnT", TBv, TBv, cons)
                if blk == nblk_v - 1 and not is_sample:
                    for i in range(3):
                        S.op("sp", lambda e, i=i: e.dma_start(out=scp[i].rearrange("(c p) -> p c", p=128),
                                                               in_=HALO[:, :].rearrange("p (c i) -> p c i", i=3)[:, :, i],
                                                               allow_slow_non_contiguous=True),
                             reads=["HALO"], dma="scp")

                def gdn_small(tau):
                    gsm = gsmT[tau]; gres = f"gsm{tau}"
                    ba = BA[:, tau * 16:(tau + 1) * 16]
                    S.op("act", lambda e, ba=ba: e.activation(out=gsm[:, 0:8], in_=ba[:, 0:8], func=AF.Sigmoid), reads=["BA"], writes=[gres])
                    S.op("dve", lambda e, ba=ba: e.tensor_tensor(out=gsm[:, 8:16], in0=ba[:, 8:16], in1=dtb_bc[:], op=ALU.add), reads=["BA", "dtb_bc", gres], writes=[gres])
                    S.op("act", lambda e: e.activation(out=gsm[:, 8:16], in_=gsm[:, 8:16], func=AF.Exp), reads=[gres], writes=[gres])
                    S.op("dve", lambda e: e.tensor_scalar(out=gsm[:, 8:16], in0=gsm[:, 8:16], scalar1=1.0, scalar2=None, op0=ALU.add), reads=[gres], writes=[gres])
                    S.op("act", lambda e: e.activation(out=gsm[:, 8:16], in_=gsm[:, 8:16], func=AF.Ln), reads=[gres], writes=[gres])
                    S.op("dve", lambda e: e.tensor_tensor(out=gsm[:, 8:16], in0=gsm[:, 8:16], in1=nega[:], op=ALU.mult), reads=[gres, "nega"], writes=[gres])
                    bq = S.bank()
                    S.op("pe", lambda e, bq=bq: e.matmul(pbank(bq, 8), lhsT=tri_f[:], rhs=gsm[:, 8:16], start=True, stop=True),
                         reads=["tri_f", gres], writes=[f"pb{bq}"])
                    S.op("dve", lambda e, bq=bq: e.tensor_copy(out=gsm[:, 16:24], in_=pbank(bq, 8)), reads=[f"pb{bq}", gres], writes=[gres])
                    S.op("act", lambda e: e.activation(out=gsm[:, 24:32], in_=gsm[:, 16:24], func=AF.Exp), reads=[gres], writes=[gres])
                    S.op("dve", lambda e: e.tensor_scalar(out=gsm[:, 24:32], in0=gsm[:, 24:32], scalar1=-1.0, scalar2=None, op0=ALU.mult), reads=[gres], writes=[gres])
                for tau in (range(NTv) if not is_sample else []):
                    gdn_small(tau)

                if stage < 2:
                    return
                if is_sample:
                    sample_mixers()
                def ret_gen():
                    for tau in (range(NTv) if not is_sample else []):
                        tc0 = tau * 128
                        yield
                        b = S.bank()

                        def mm_sc(e, b=b, tc0=tc0):
                            ins = None
                            for h in range(RH):
                                ins = e.matmul(pbank(b, 128, h * 128), lhsT=RK[:, h * TBv + tc0: h * TBv + tc0 + 128],
                                               rhs=RQ[:, h * TBv + tc0: h * TBv + tc0 + 128], start=True, stop=True)
                            return ins
                        S.op("pe", mm_sc, reads=["RK", "RQ"], writes=[f"pb{b}"])
                        S.op("dve", lambda e, b=b: e.tensor_tensor(out=PTr[:], in0=pbank(b), in1=rdt[:], op=ALU.mult),
                             reads=[f"pb{b}", "rdt"], writes=["PTr"])
                        yield
                        b3 = S.bank()

                        def tr_k(e, b3=b3, tc0=tc0):
                            ins = None
                            for h in range(RH):
                                ins = e.transpose(pbank_bf(b3, 128, h * 128), RK[:, h * TBv + tc0: h * TBv + tc0 + 128], ident_b[:])
                            return ins
                        S.op("pe", tr_k, reads=["RK", "ident_b"], writes=[f"pb{b3}"])
                        S.op("dve", lambda e, b3=b3: e.tensor_tensor(out=KtR[:, :].rearrange("p (h d) -> p h d", d=128),
                                                                   in0=pbank_bf(b3, 512).rearrange("p (h d) -> p h d", d=128),
                                                                   in1=rkd[:, :].unsqueeze(2).to_broadcast([128, RH, 128]), op=ALU.mult),
                             reads=[f"pb{b3}", "rkd"], writes=["KtR"])
                        for hp in range(2):
                            yield
                            b2 = S.bank()

                            def mm_o(e, b2=b2, hp=hp, tau=tau, tc0=tc0):
                                ins = None
                                for hh in range(2):
                                    h = hp * 2 + hh
                                    for c in range(2):
                                        o = pbank(b2, 128, (hh * 2 + c) * 128)
                                        e.matmul(o, lhsT=SRb[:, h * RDV + c * 128: h * RDV + (c + 1) * 128],
                                                 rhs=RQd[:, h * TBv + tc0: h * TBv + tc0 + 128], start=True, stop=False)
                                        ins = e.matmul(o, lhsT=VtR[:, tau * 1024 + h * RDV + c * 128: tau * 1024 + h * RDV + (c + 1) * 128],
                                                       rhs=PTr[:, h * 128:(h + 1) * 128], start=False, stop=True)
                                return ins
                            S.op("pe", mm_o, reads=["SRb", "RQd", "VtR", "PTr"], writes=[f"pb{b2}"])
                            dst = OR[:, hp * 4 * TBv:(hp + 1) * 4 * TBv].rearrange("p (c t) -> p c t", t=TBv)[:, :, tc0:tc0 + 128]
                            S.op("act", lambda e, b2=b2, dst=dst: e.activation(out=dst, in_=pbank(b2).rearrange("p (c t) -> p c t", t=128), func=AF.Copy),
                                 reads=[f"pb{b2}"], writes=[or_res])
                        for hp in range(2):
                            yield
                            b4 = S.bank()

                            def mm_s(e, b4=b4, hp=hp, tau=tau):
                                ins = None
                                for hh in range(2):
                                    h = hp * 2 + hh
                                    ins = e.matmul(pbank(b4, 256, hh * 256), lhsT=KtR[:, h * 128:(h + 1) * 128],
                                                   rhs=VtR[:, tau * 1024 + h * RDV: tau * 1024 + (h + 1) * RDV], start=True, stop=True)
                                return ins
                            S.op("pe", mm_s, reads=["KtR", "VtR"], writes=[f"pb{b4}"])
                            for hh in range(2):
                                h = hp * 2 + hh
                                S.op("dve", lambda e, b4=b4, hh=hh, h=h: e.scalar_tensor_tensor(
                                    out=SR[:, h * RDV:(h + 1) * RDV], in0=SR[:, h * RDV:(h + 1) * RDV], scalar=float(_GAM[h] ** 128),
                                    in1=pbank(b4, 256, hh * 256), op0=ALU.mult, op1=ALU.add),
                                    reads=[f"pb{b4}", "SR"], writes=["SR"])
                            S.op("act", lambda e, hp=hp: e.activation(out=SRb[:, hp * 512:(hp + 1) * 512], in_=SR[:, hp * 512:(hp + 1) * 512], func=AF.Copy),
                                 reads=["SR"], writes=["SRb"])
                    yield
                _retg = ret_gen()

                if stage < 3:
                    return
                def gdn_tile(tau):
                    gsm = gsmT[tau]; gres = f"gsm{tau}"
                    tc0 = tau * 128
                    QTd = QTdT[tau % 2]; qres = f"QTd{tau % 2}"
                    def setup():
                        yield
                        S.op("dve", lambda e: e.tensor_tensor(out=G2[:, :].rearrange("p (h i) -> p h i", i=128),
                                                              in0=tri_f[:, :].unsqueeze(1).to_broadcast([128, 8, 128]),
                                                              in1=gsm[:, 8:16].unsqueeze(2).to_broadcast([128, 8, 128]), op=ALU.mult),
                             reads=["tri_f", gres], writes=["GBm"])
                        yield
                        while S.bank_i % 2 != 0:
                            S.bank()
                        bg = S.bank(); bg2 = S.bank()

                        def mm_g(e, bg=bg, bg2=bg2):
                            e.matmul(pbank(bg), lhsT=ones_f[:], rhs=G2[:, 0:512], start=True, stop=True)
                            return e.matmul(pbank(bg2), lhsT=ones_f[:], rhs=G2[:, 512:1024], start=True, stop=True)
                        S.op("pe", mm_g, reads=["ones_f", "GBm"], writes=[f"pb{bg}", f"pb{bg2}"])
                        gbc = PS[:, bg * 512: bg * 512 + 1024]
                        S.op("dve", lambda e, gbc=gbc: e.tensor_tensor(out=D1[:, :].rearrange("p (h i) -> p h i", i=128),
                                                                       in0=gbc.rearrange("p (h i) -> p h i", i=128),
                                                                       in1=gsm[:, 16:24].unsqueeze(2).to_broadcast([128, 8, 128]), op=ALU.subtract),
                             reads=[f"pb{bg}", f"pb{bg2}", gres], writes=["D1"])
                        S.op("act", lambda e, gbc=gbc: e.activation(out=EGC[:], in_=gbc, func=AF.Exp), reads=[f"pb{bg}", f"pb{bg2}"], writes=["GBm"])
                        yield
                        S.op("dve", lambda e, tc0=tc0: e.tensor_tensor(out=QTd[:, :].rearrange("p (h i) -> p h i", i=128),
                                                                       in0=GQ[:, :].rearrange("p (h t) -> p h t", t=TBv)[:, :, tc0:tc0 + 128],
                                                                       in1=EGC[:, :].rearrange("p (h i) -> p h i", i=128), op=ALU.mult),
                             reads=["GQ", "GBm"], writes=[qres])
                        yield
                        S.op("act", lambda e: e.activation(out=gsm[:, 40:48], in_=EGC[:, :].rearrange("p (h i) -> p h i", i=128)[:, :, 127], func=AF.Copy),
                             reads=["GBm", gres], writes=[gres])
                        yield
                        S.op("act", lambda e: e.activation(out=gsm[:, 32:40], in_=D1[:, :].rearrange("p (h i) -> p h i", i=128)[:, :, 127], func=AF.Exp),
                             reads=["D1", gres], writes=[gres])
                        yield
                        S.op("dve", lambda e: e.tensor_tensor(out=D1[:, :].rearrange("p (h i) -> p h i", i=128),
                                                              in0=D1[:, :].rearrange("p (h i) -> p h i", i=128),
                                                              in1=mincl[:, :].unsqueeze(1).to_broadcast([128, 8, 128]), op=ALU.add),
                             reads=["D1", "mincl"], writes=["D1"])
                        yield
                        S.op("act", lambda e: e.activation(out=D1[:], in_=D1[:], func=AF.Exp), reads=["D1"], writes=["D1"])
                        yield
                        S.op("dve", lambda e: e.tensor_tensor(out=GBm[:, :].rearrange("p (h i) -> p h i", i=128),
                                                              in0=D1[:, :].rearrange("p (h i) -> p h i", i=128),
                                                              in1=strict[:, :].unsqueeze(1).to_broadcast([128, 8, 128]), op=ALU.mult),
                             reads=["D1", "strict"], writes=["GBm"])
                        yield
                        S.op("dve", lambda e: e.scalar_tensor_tensor(out=GBm[:, :].rearrange("p (h i) -> p h i", i=128),
                                                                     in0=GBm[:, :].rearrange("p (h i) -> p h i", i=128), scalar=-1.0,
                                                                     in1=gsm[:, 0:8].unsqueeze(2).to_broadcast([128, 8, 128]), op0=ALU.mult, op1=ALU.mult),
                             reads=["GBm", gres], writes=["GBm"])
                        yield
                    def chain(hg):
                        N0, N0T, WA, WB, MO, MOT, V1s, V2s = GCTX[hg]
                        rN0, rN0T, rWA, rWB, rV1, rV2 = (f"{n}_{hg}" for n in ("N0", "N0T", "WA", "WB", "V1s", "V2s"))
                        yield
                        bk = S.bank(); bqk = S.bank(); bt = S.bank()

                        def mm_kk(e, hg=hg, bk=bk, bqk=bqk, bt=bt, tc0=tc0):
                            ins = None
                            for hh in range(4):
                                h = hg * 4 + hh
                                ks = GK[:, h * TBv + tc0: h * TBv + tc0 + 128]
                                e.matmul(pbank(bk, 128, hh * 128), lhsT=ks, rhs=ks, start=True, stop=True)
                                e.matmul(pbank(bqk, 128, hh * 128), lhsT=ks, rhs=GQ[:, h * TBv + tc0: h * TBv + tc0 + 128], start=True, stop=True)
                                e.transpose(pbank_bf(bt, 128, hh * 128), ks, ident_b[:])
                                ins = e.transpose(pbank_bf(bt, 128, 512 + hh * 128), GV[:, h * TBv + tc0: h * TBv + tc0 + 128], ident_b[:])
                            return ins
                        S.op("pe", mm_kk, reads=["GK", "GQ", "GV", "ident_b"], writes=[f"pb{bk}", f"pb{bqk}", f"pb{bt}"])
                        sl = slice(hg * 512, (hg + 1) * 512)
                        S.op("dve", lambda e, bk=bk, sl=sl: e.tensor_tensor(out=N0[:, :], in0=pbank(bk), in1=GBm[:, sl], op=ALU.mult),
                             reads=[f"pb{bk}", "GBm"], writes=[rN0])
                        S.op("dve", lambda e, bqk=bqk, sl=sl: e.tensor_tensor(out=PTg[:, sl], in0=pbank(bqk), in1=D1[:, sl], op=ALU.mult),
                             reads=[f"pb{bqk}", "D1"], writes=[f"PTg{hg}"])
                        S.op("dve", lambda e, bt=bt, sl=sl, hg=hg: e.tensor_tensor(out=KtG[:, sl].rearrange("p (h d) -> p h d", d=128),
                                                                                 in0=pbank_bf(bt, 512).rearrange("p (h d) -> p h d", d=128),
                                                                                 in1=gsm[:, 32 + hg * 4:36 + hg * 4].unsqueeze(2).to_broadcast([128, 4, 128]), op=ALU.mult),
                             reads=[f"pb{bt}", gres], writes=[f"KtG{hg}"])
                        S.op("act", lambda e, bt=bt, sl=sl: e.activation(out=VtG[:, sl], in_=pbank_bf(bt, 512, 512), func=AF.Copy),
                             reads=[f"pb{bt}"], writes=[f"VtG{hg}"])
                        yield
                        btr = S.bank()

                        def tr_n(e, btr=btr):
                            ins = None
                            for hh in range(4):
                                ins = e.transpose(pbank_bf(btr, 128, hh * 128), N0[:, hh * 128:(hh + 1) * 128], ident_b[:])
                            return ins
                        S.op("pe", tr_n, reads=[rN0, "ident_b"], writes=[f"pb{btr}"])
                        S.op("act", lambda e, btr=btr: e.activation(out=N0T[:, :], in_=pbank_bf(btr, 512), func=AF.Copy),
                             reads=[f"pb{btr}"], writes=[rN0T])
                        v3 = lambda t: t[:, :].rearrange("p (h i) -> p h i", i=128)
                        wav = WA[:, :].rearrange("p (h a i) -> p h a i", a=2, i=128)
                        wbv = WB[:, :].rearrange("p (h a i) -> p h a i", a=2, i=128)
                        m16b = m16[:, :].unsqueeze(1).to_broadcast([128, 4, 128])
                        idb4 = identb8[:, 0:512].rearrange("p (h i) -> p h i", i=128)
                        S.op("dve", lambda e, wav=wav: e.tensor_tensor(out=wav[:, :, 0, :], in0=v3(N0), in1=m16b, op=ALU.mult), reads=[rN0, "m16"], writes=[rWA])
                        S.op("dve", lambda e, wav=wav: e.tensor_tensor(out=wav[:, :, 1, :], in0=wav[:, :, 0, :], in1=idb4, op=ALU.add), reads=[rWA, "identb8"], writes=[rWA])
                        S.op("dve", lambda e, wbv=wbv: e.tensor_tensor(out=wbv[:, :, 0, :], in0=v3(N0T), in1=m16b, op=ALU.mult), reads=[rN0T, "m16"], writes=[rWB])
                        S.op("dve", lambda e, wbv=wbv: e.tensor_tensor(out=wbv[:, :, 1, :], in0=wbv[:, :, 0, :], in1=idb4, op=ALU.add), reads=[rWB, "identb8"], writes=[rWB])
                        for l in range(3):
                            mk = moff[:, l * 128:(l + 1) * 128].unsqueeze(1).to_broadcast([128, 4, 128])
                            S.op("pool", lambda e, l=l, mk=mk: e.tensor_tensor(out=v3(MO[l]), in0=v3(N0T), in1=mk, op=ALU.mult), reads=[rN0T, "moff"], writes=[f"MO{l}_{hg}"])
                            if l < 2:
                                S.op("pool", lambda e, l=l, mk=mk: e.tensor_tensor(out=v3(MOT[l]), in0=v3(N0), in1=mk, op=ALU.mult), reads=[rN0, "moff"], writes=[f"MOT{l}_{hg}"])
                        for lvl in range(4):
                            yield
                            while S.bank_i % 2 != 0:
                                S.bank()
                            ba0 = S.bank(); ba1 = S.bank(); bb0 = S.bank(); bb1 = S.bank()

                            def mm_l(e, lvl=lvl, ba0=ba0, bb0=bb0):
                                ins = None
                                for hh in range(4):
                                    oa = PS[:, ba0 * 512 + hh * 256: ba0 * 512 + hh * 256 + 256]
                                    ob = PS[:, bb0 * 512 + hh * 256: bb0 * 512 + hh * 256 + 256]
                                    wa = WA[:, hh * 256: hh * 256 + 256]
                                    wb = WB[:, hh * 256: hh * 256 + 256]
                                    lo, hi = (0, 128) if lvl == 0 else ((0, 256) if lvl < 3 else (128, 256))
                                    e.matmul(oa[:, lo:hi], lhsT=wb[:, 0:128], rhs=wa[:, lo:hi], start=True, stop=True)
                                    ins = e.matmul(ob[:, lo:hi], lhsT=wa[:, 0:128], rhs=wb[:, lo:hi], start=True, stop=True)
                                return ins
                            S.op("pe", mm_l, reads=[rWA, rWB], writes=[f"pb{ba0}", f"pb{ba1}", f"pb{bb0}", f"pb{bb1}"])
                            pva = PS[:, ba0 * 512: ba0 * 512 + 1024].rearrange("p (h a i) -> p h a i", a=2, i=128)
                            pvb = PS[:, bb0 * 512: bb0 * 512 + 1024].rearrange("p (h a i) -> p h a i", a=2, i=128)
                            if lvl >= 1:
                                S.op("dve", lambda e, pva=pva, wav=wav: e.tensor_tensor(out=wav[:, :, 1, :], in0=pva[:, :, 1, :], in1=wav[:, :, 1, :], op=ALU.add),
                                     reads=[f"pb{ba0}", f"pb{ba1}", rWA], writes=[rWA])
                                S.op("dve", lambda e, pvb=pvb, wbv=wbv: e.tensor_tensor(out=wbv[:, :, 1, :], in0=pvb[:, :, 1, :], in1=wbv[:, :, 1, :], op=ALU.add),
                                     reads=[f"pb{bb0}", f"pb{bb1}", rWB], writes=[rWB])
                            if lvl < 3:
                                S.op("act", lambda e, pva=pva, wav=wav: e.activation(out=wav[:, :, 0, :], in_=pva[:, :, 0, :], func=AF.Copy),
                                     reads=[f"pb{ba0}", f"pb{ba1}", rWA], writes=[rWA])
                                S.op("act", lambda e, pvb=pvb, wbv=wbv: e.activation(out=wbv[:, :, 0, :], in_=pvb[:, :, 0, :], func=AF.Copy),
                                     reads=[f"pb{bb0}", f"pb{bb1}", rWB], writes=[rWB])
                        for l in range(3):
                            last = (l == 2)
                            yield
                            b1_ = S.bank(); b2_ = None if last else S.bank()

                            def mm_v(e, l=l, b1_=b1_, b2_=b2_, last=last):
                                ins = None
                                for hh in range(4):
                                    ins = e.matmul(pbank(b1_, 128, hh * 128), lhsT=MO[l][:, hh * 128:(hh + 1) * 128], rhs=WA[:, hh * 256 + 128: hh * 256 + 256],
                                                   start=True, stop=True)
                                    if not last:
                                        ins = e.matmul(pbank(b2_, 128, hh * 128), lhsT=MOT[l][:, hh * 128:(hh + 1) * 128], rhs=WB[:, hh * 256 + 128: hh * 256 + 256],
                                                       start=True, stop=True)
                                return ins
                            S.op("pe", mm_v, reads=[rWA, rWB, f"MO{l}_{hg}"] + ([] if last else [f"MOT{l}_{hg}"]),
                                 writes=[f"pb{b1_}"] + ([] if last else [f"pb{b2_}"]))
                            S.op("act", lambda e, b1_=b1_: e.activation(out=V1s[:, :], in_=pbank(b1_), func=AF.Copy), reads=[f"pb{b1_}"], writes=[rV1])
                            if not last:
                                S.op("act", lambda e, b2_=b2_: e.activation(out=V2s[:, :], in_=pbank(b2_), func=AF.Copy), reads=[f"pb{b2_}"], writes=[rV2])
                            yield
                            b3_ = S.bank(); b4_ = None if last else S.bank()

                            def mm_u(e, b3_=b3_, b4_=b4_, last=last):
                                ins = None
                                for hh in range(4):
                                    ins = e.matmul(pbank(b3_, 128, hh * 128), lhsT=WB[:, hh * 256 + 128: hh * 256 + 256], rhs=V1s[:, hh * 128:(hh + 1) * 128],
                                                   start=True, stop=True)
                                    if not last:
                                        ins = e.matmul(pbank(b4_, 128, hh * 128), lhsT=WA[:, hh * 256 + 128: hh * 256 + 256], rhs=V2s[:, hh * 128:(hh + 1) * 128],
                                                       start=True, stop=True)
                                return ins
                            S.op("pe", mm_u, reads=[rWA, rWB, rV1] + ([] if last else [rV2]),
                                 writes=[f"pb{b3_}"] + ([] if last else [f"pb{b4_}"]))
                            if last:
                                S.op("dve", lambda e, b3_=b3_, wav=wav, sl=sl: e.tensor_tensor(out=TTf[:, sl].rearrange("p (h i) -> p h i", i=128),
                                                                                           in0=wav[:, :, 1, :], in1=pbank(b3_).rearrange("p (h i) -> p h i", i=128), op=ALU.subtract),
                                     reads=[f"pb{b3_}", rWA], writes=[f"TTf{hg}"])
                            else:
                                S.op("dve", lambda e, b3_=b3_, wav=wav: e.tensor_tensor(out=wav[:, :, 1, :], in0=wav[:, :, 1, :],
                                                                                      in1=pbank(b3_).rearrange("p (h i) -> p h i", i=128), op=ALU.subtract),
                                     reads=[f"pb{b3_}", rWA], writes=[rWA])
                                S.op("dve", lambda e, b4_=b4_, wbv=wbv: e.tensor_tensor(out=wbv[:, :, 1, :], in0=wbv[:, :, 1, :],
                                                                                      in1=pbank(b4_).rearrange("p (h i) -> p h i", i=128), op=ALU.subtract),
                                     reads=[f"pb{b4_}", rWB], writes=[rWB])
                    def rec(hg):
                        sl = slice(hg * 512, (hg + 1) * 512)
                        yield
                        b1 = S.bank()

                        def mm_ks(e, b1=b1, hg=hg, tc0=tc0):
                            ins = None
                            for hh in range(4):
                                h = hg * 4 + hh
                                ins = e.matmul(pbank(b1, 128, hh * 128), lhsT=GK[:, h * TBv + tc0: h * TBv + tc0 + 128],
                                               rhs=SGb[:, h * 128:(h + 1) * 128], start=True, stop=True)
                            return ins
                        S.op("pe", mm_ks, reads=["GK", f"SGb{hg}"], writes=[f"pb{b1}"])
                        S.op("dve", lambda e, b1=b1, sl=sl, hg=hg: e.tensor_tensor(out=rtl[:, sl].rearrange("p (h d) -> p h d", d=128),
                                                                                 in0=pbank(b1).rearrange("p (h d) -> p h d", d=128),
                                                                                 in1=gsm[:, 24 + hg * 4:28 + hg * 4].unsqueeze(2).to_broadcast([128, 4, 128]), op=ALU.mult),
                             reads=[f"pb{b1}", gres], writes=[f"rtl{hg}"])
                        S.op("dve", lambda e, sl=sl: e.tensor_tensor(out=rtl[:, sl], in0=rtl[:, sl], in1=VtG[:, sl], op=ALU.add),
                             reads=[f"rtl{hg}", f"VtG{hg}"], writes=[f"rtl{hg}"])
                        yield
                        b2 = S.bank()

                        def mm_vn(e, b2=b2, hg=hg):
                            ins = None
                            for hh in range(4):
                                h = hg * 4 + hh
                                ins = e.matmul(pbank(b2, 128, hh * 128), lhsT=TTf[:, h * 128:(h + 1) * 128],
                                               rhs=rtl[:, h * 128:(h + 1) * 128], start=True, stop=True)
                            return ins
                        S.op("pe", mm_vn, reads=[f"TTf{hg}", f"rtl{hg}"], writes=[f"pb{b2}"])
                        S.op("dve", lambda e, b2=b2, sl=sl, hg=hg: e.tensor_tensor(out=vnw[:, sl].rearrange("p (h d) -> p h d", d=128),
                                                                                 in0=pbank(b2).rearrange("p (h d) -> p h d", d=128),
                                                                                 in1=gsm[:, hg * 4:hg * 4 + 4].unsqueeze(2).to_broadcast([128, 4, 128]), op=ALU.mult),
                             reads=[f"pb{b2}", gres], writes=[f"vnw{hg}"])
                        yield
                        b3 = S.bank(); b4 = S.bank()

                        def mm_o(e, b3=b3, b4=b4, hg=hg):
                            ins = None
                            for hh in range(4):
                                h = hg * 4 + hh
                                o = pbank(b3, 128, hh * 128)
                                e.matmul(o, lhsT=SGb[:, h * 128:(h + 1) * 128], rhs=QTd[:, h * 128:(h + 1) * 128], start=True, stop=False)
                                e.matmul(o, lhsT=vnw[:, h * 128:(h + 1) * 128], rhs=PTg[:, h * 128:(h + 1) * 128], start=False, stop=True)
                                ins = e.matmul(pbank(b4, 128, hh * 128), lhsT=KtG[:, h * 128:(h + 1) * 128], rhs=vnw[:, h * 128:(h + 1) * 128],
                                               start=True, stop=True)
                            return ins
                        S.op("pe", mm_o, reads=[f"SGb{hg}", qres, f"vnw{hg}", f"PTg{hg}", f"KtG{hg}"], writes=[f"pb{b3}", f"pb{b4}"])
                        dst = OG[:, hg * 4 * TBv:(hg + 1) * 4 * TBv].rearrange("p (h t) -> p h t", t=TBv)[:, :, tc0:tc0 + 128]
                        S.op("act", lambda e, b3=b3, dst=dst: e.activation(out=dst, in_=pbank(b3).rearrange("p (h t) -> p h t", t=128), func=AF.Copy),
                             reads=[f"pb{b3}"], writes=["OG"])
                        S.op("dve", lambda e, sl=sl, hg=hg: e.tensor_tensor(out=SG[:, sl].rearrange("p (h d) -> p h d", d=128),
                                                                     in0=SG[:, sl].rearrange("p (h d) -> p h d", d=128),
                                                                     in1=gsm[:, 40 + hg * 4:44 + hg * 4].unsqueeze(2).to_broadcast([128, 4, 128]), op=ALU.mult),
                             reads=[f"SG{hg}", gres], writes=[f"SG{hg}"])
                        S.op("dve", lambda e, b4=b4, sl=sl: e.tensor_tensor(out=SG[:, sl], in0=SG[:, sl], in1=pbank(b4), op=ALU.add),
                             reads=[f"SG{hg}", f"pb{b4}"], writes=[f"SG{hg}"])
                        S.op("act", lambda e, sl=sl: e.activation(out=SGb[:, sl], in_=SG[:, sl], func=AF.Copy), reads=[f"SG{hg}"], writes=[f"SGb{hg}"])
                    return setup, chain, rec
                def _drive(gens):
                    gens = list(gens)
                    while gens:
                        for _g in list(gens):
                            try:
                                next(_g)
                            except StopIteration:
                                gens.remove(_g)
                _tiles = [gdn_tile(tau) for tau in (range(NTv) if not is_sample else [])]
                if _tiles:
                    _drive([_tiles[0][0]()])
                for tau in range(len(_tiles)):
                    _st, _ch, _rc = _tiles[tau]
                    _drive([_ch(0), _ch(1), _retg])
                    _drive(([_tiles[tau + 1][0]()] if tau + 1 < len(_tiles) else []) + [_rc(0), _rc(1)])
                _drive([_retg])
                if blk == nblk_v - 1 and not is_sample:
                    S.op("sp", lambda e: e.dma_start(out=sgp.rearrange("h d v -> d h v"), in_=SG[:, :].rearrange("p (h v) -> p h v", v=GDV)),
                         reads=["SG0", "SG1"], dma="sgp")
                if blk == nblk_v - 1 and not is_sample:
                    S.op("sp", lambda e: e.dma_start(out=srp.rearrange("h d v -> d h v"), in_=SR[:, :].rearrange("p (h v) -> p h v", v=RDV)),
                         reads=["SR"], dma="srp")
                def _g_rms():
                    yield
                    S.op("act", lambda e: e.activation(out=MRG[:, :], in_=OG[:, :], func=AF.Square), reads=["OG"], writes=["MRG"])
                    yield
                    while S.bank_i % 4 != 0:
                        S.bank()
                    rb_ = [S.bank() for _ in range(4)]
                    assert rb_[3] == rb_[0] + 3

                    def mm_rn(e):
                        ins = None
                        for h in range(GH):
                            ins = e.matmul(PS[:, rb_[0] * 512 + h * TBv: rb_[0] * 512 + (h + 1) * TBv], lhsT=ones_b[:], rhs=MRG[:, h * TBv:(h + 1) * TBv],
                                           start=True, stop=True)
                        return ins
                    rbr = [f"pb{b}" for b in rb_]
                    S.op("pe", mm_rn, reads=["MRG", "ones_b"], writes=rbr)
                    T4r = 4 * TBv
                    for hf in range(2):
                        rs_, rr_ = RNS[hf]
                        S.op("dve", lambda e, hf=hf, rs_=rs_: e.tensor_scalar(out=rs_, in0=PS[:, rb_[0] * 512 + hf * T4r: rb_[0] * 512 + (hf + 1) * T4r],
                                                                             scalar1=1.0 / GDV, scalar2=EPS, op0=ALU.mult, op1=ALU.add), reads=rbr, writes=[rr_])
                    for hf in range(2):
                        rs_, rr_ = RNS[hf]
                        yield
                        S.op("act", lambda e, rs_=rs_: e.activation(out=rs_, in_=rs_, func=AF.Ln), reads=[rr_], writes=[rr_])
                        yield
                        S.op("act", lambda e, rs_=rs_: e.activation(out=rs_, in_=rs_, func=AF.Exp, scale=-0.5), reads=[rr_], writes=[rr_])
                        yield
                        S.op("dve", lambda e, hf=hf, rs_=rs_: e.scalar_tensor_tensor(out=rs_, in0=OG[:, hf * T4r:(hf + 1) * T4r], scalar=ggdn_c[:, 0:1], in1=rs_,
                                                                                    op0=ALU.mult, op1=ALU.mult), reads=["OG", rr_, "ggdn_c"], writes=[rr_])
                        yield
                        S.op("dve", lambda e, hf=hf, rs_=rs_: e.tensor_tensor(out=OGg[:, hf * T4r:(hf + 1) * T4r], in0=rs_, in1=Z[:, hf * T4r:(hf + 1) * T4r], op=ALU.mult),
                             reads=[rr_, "Z"], writes=["OGg"])
                    yield
                def _g_gn():
                    T4 = 4 * TBv
                    if is_sample:
                        GNS, gns_r = GNSs, "GNSs"
                    else:
                        GNS, gns_r = QKVB[:, 8 * TBv:24 * TBv].bitcast(F32), "FA"
                    OBF = FA[:, 0:8 * TBv]
                    OSQ, osq_r = (hnT, "hnT") if is_sample else (VtR, "VtR")
                    yield
                    S.op("act", lambda e: e.activation(out=OBF, in_=OR[:, :], func=AF.Copy), reads=[or_res], writes=["FA"] + FA_AL)
                    yield
                    S.op("act", lambda e: e.activation(out=OSQ[:, :], in_=OR[:, :], func=AF.Square), reads=[or_res], writes=[osq_r])
                    yield
                    while S.bank_i % 4 != 0:
                        S.bank()
                    gb_ = [S.bank() for _ in range(4)]
                    assert gb_[3] == gb_[0] + 3

                    def mm_gn(e):
                        ins = None
                        for h in range(RH):
                            om = PS[:, gb_[0] * 512 + h * TBv: gb_[0] * 512 + (h + 1) * TBv]
                            oq = PS[:, gb_[0] * 512 + T4 + h * TBv: gb_[0] * 512 + T4 + (h + 1) * TBv]
                            for c in range(2):
                                ch = h * 2 + c
                                e.matmul(om, lhsT=onesq_b[:], rhs=OBF[:, ch * TBv:(ch + 1) * TBv], start=(c == 0), stop=(c == 1))
                            for c in range(2):
                                ch = h * 2 + c
                                ins = e.matmul(oq, lhsT=onesq_b[:], rhs=OSQ[:, ch * TBv:(ch + 1) * TBv], start=(c == 0), stop=(c == 1))
                        return ins
                    S.op("pe", mm_gn, reads=["FA", osq_r, "onesq_b"], writes=[f"pb{b}" for b in gb_])
                    pmean = PS[:, gb_[0] * 512: gb_[0] * 512 + T4]
                    pmsq = PS[:, gb_[0] * 512 + T4: gb_[0] * 512 + 2 * T4]
                    gbr = [f"pb{b}" for b in gb_]
                    S.op("act", lambda e: e.activation(out=GNS[:, 0:T4], in_=pmean, func=AF.Copy), reads=gbr, writes=[gns_r])
                    S.op("dve", lambda e: e.tensor_tensor(out=GNS[:, T4:2 * T4], in0=GNS[:, 0:T4], in1=GNS[:, 0:T4], op=ALU.mult), reads=[gns_r], writes=[gns_r])
                    S.op("dve", lambda e: e.scalar_tensor_tensor(out=GNS[:, T4:2 * T4], in0=GNS[:, T4:2 * T4], scalar=-1.0, in1=pmsq, op0=ALU.mult, op1=ALU.add),
                         reads=[gns_r] + gbr, writes=[gns_r])
                    yield
                    S.op("dve", lambda e: e.tensor_scalar(out=GNS[:, T4:2 * T4], in0=GNS[:, T4:2 * T4], scalar1=EPS, scalar2=None, op0=ALU.add), reads=[gns_r], writes=[gns_r])
                    yield
                    S.op("act", lambda e: e.activation(out=GNS[:, T4:2 * T4], in_=GNS[:, T4:2 * T4], func=AF.Ln), reads=[gns_r], writes=[gns_r])
                    yield
                    S.op("act", lambda e: e.activation(out=GNS[:, T4:2 * T4], in_=GNS[:, T4:2 * T4], func=AF.Exp, scale=-0.5), reads=[gns_r], writes=[gns_r])
                    or4 = OR[:, :].rearrange("p (h c t) -> p h c t", c=2, t=TBv)
                    yield
                    S.op("dve", lambda e: e.tensor_tensor(out=or4, in0=or4,
                                                          in1=GNS[:, 0:T4].rearrange("p (h t) -> p h t", t=TBv).unsqueeze(2).to_broadcast([128, RH, 2, TBv]), op=ALU.subtract),
                         reads=[or_res, gns_r], writes=[or_res])
                    yield
                    S.op("dve", lambda e: e.tensor_tensor(out=or4, in0=or4,
                                                          in1=GNS[:, T4:2 * T4].rearrange("p (h t) -> p h t", t=TBv).unsqueeze(2).to_broadcast([128, RH, 2, TBv]), op=ALU.mult),
                         reads=[or_res, gns_r], writes=[or_res])
                    or3 = OR[:, :].rearrange("p (c t) -> p c t", t=TBv)
                    yield
                    S.op("dve", lambda e: e.tensor_tensor(out=or3, in0=or3, in1=ggn_c[:, 0:8].unsqueeze(2).to_broadcast([128, 8, TBv]), op=ALU.mult),
                         reads=[or_res, "ggn_c"], writes=[or_res])
                    yield
                    S.op("dve", lambda e: e.tensor_tensor(out=ORg[:, :], in0=OR[:, :], in1=RG[:, :], op=ALU.mult), reads=[or_res, "RG"], writes=["ORg"])


                    yield
                def late_proj():
                    for (kind, c0, ncols) in late_tiles:
                        base = {"ga": 7184, "gb": 8208}[kind]
                        dstb = {"ga": GA, "gb": GB}[kind]
                        cb = (c0 - base) // 128
                        t, wres = wload(w_in, 0, 8, c0, ncols)
                        for j in range(ncols // 128):
                            yield
                            b = S.bank()

                            def mm(e, j=j, b=b, t=t, ncols=ncols):
                                ins = None
                                for k in range(8):
                                    ins = e.matmul(pbank(b, TBv), lhsT=wv(t, k, ncols, j * 128, 128), rhs=hnT[:, k * TBv:(k + 1) * TBv], start=(k == 0), stop=(k == 7))
                                return ins
                            S.op("pe", mm, reads=[wres, "hnT"], writes=[f"pb{b}"])
                            S.op("act", lambda e, b=b, j=j, dstb=dstb, cb=cb: e.activation(out=dstb[:, (cb + j) * TBv:(cb + j + 1) * TBv], in_=pbank(b, TBv), func=AF.Tanh, scale=0.5),
                                 reads=[f"pb{b}"], writes=[kind.upper()])
                    yield
                _drive([_g_rms(), _g_gn(), late_proj()])

                if blk + 1 < nblk_v:
                    load_x(blk + 1)
                    if not is_sample:
                        load_tabs(blk + 1)
                if stage < 4:
                    return
                for half in range(NQ):
                    ta, ra = wload(w_a, 0, 8, half * WC, WC)
                    tb_, rb = wload(w_b, 0, 8, half * WC, WC)
                    def _br(j, half=half, ta=ta, tb_=tb_, ra=ra, rb_w=rb):
                        tmpA, rA = nxt("tmpA"); tmpB, rB = nxt("tmpB")
                        ch = half * (WC // 128) + j
                        b1 = S.bank(); b2 = S.bank()

                        def mm_br(e):
                            ins = None
                            for k in range(8):
                                e.matmul(pbank(b1, TBv), lhsT=wv(ta, k, WC, j * 128, 128), rhs=ORg[:, k * TBv:(k + 1) * TBv], start=(k == 0), stop=(k == 7))
                            for k in range(8):
                                ins = e.matmul(pbank(b2, TBv), lhsT=wv(tb_, k, WC, j * 128, 128), rhs=OGg[:, k * TBv:(k + 1) * TBv], start=(k == 0), stop=(k == 7))
                            return ins
                        S.op("pe", mm_br, reads=[ra, rb_w, "ORg", "OGg"], writes=[f"pb{b1}", f"pb{b2}"])
                        S.op("dve", lambda e: e.scalar_tensor_tensor(out=tmpA[:], in0=GA[:, ch * TBv:(ch + 1) * TBv], scalar=1.0, in1=pbank(b1, TBv),
                                                                     op0=ALU.add, op1=ALU.mult), reads=[f"pb{b1}", "GA"], writes=[rA])
                        S.op("dve", lambda e: e.scalar_tensor_tensor(out=tmpB[:], in0=GB[:, ch * TBv:(ch + 1) * TBv], scalar=1.0, in1=pbank(b2, TBv),
                                                                     op0=ALU.add, op1=ALU.mult), reads=[f"pb{b2}", "GB"], writes=[rB])
                        S.op("dve", lambda e: e.tensor_tensor(out=MRG[:, ch * TBv:(ch + 1) * TBv], in0=tmpA[:], in1=tmpB[:], op=ALU.add),
                             reads=[rA, rB], writes=["MRG"])
                    for j in range(WC // 128):
                        _br(j)

                def resid_cons(half, sc=None):
                    def cons(tau, b):
                        xs_ = Xb[0:tw, tau * D + half * WC: tau * D + half * WC + WC]
                        if sc is None:
                            S.op("dve", lambda e: e.tensor_tensor(out=xs_, in0=xs_, in1=pbank(b, WC)[0:tw, :], op=ALU.add), reads=[f"pb{b}", xres], writes=[xres])
                        else:
                            S.op("dve", lambda e: e.scalar_tensor_tensor(out=xs_, in0=pbank(b, WC)[0:tw, :], scalar=sc, in1=xs_, op0=ALU.mult, op1=ALU.add),
                                 reads=[f"pb{b}", xres], writes=[xres])
                    return cons
                for half in range(NQ):
                    proj_tm(w_out, 0, 8, half * WC, WC, MRG, "MRG", TBv, NTv, resid_cons(half, 0.5), tw=tw)

                if stage < 5:
                    return
                for tau in range(NTv):
                    norm_transpose(Xb[0:tw, tau * D:(tau + 1) * D], xres, gx_c, "gx_c", hnT, "hnT", tw, tau * tw, TBv)
                XQ = ORg
                OX = OGg
                for half in range(NQ):
                    def cons(j, b, half=half):
                        c = half * (WC // 128) + j
                        S.op("act", lambda e: e.activation(out=XQ[:, c * TBv:(c + 1) * TBv], in_=pbank(b, TBv), func=AF.Copy), reads=[f"pb{b}"], writes=["ORg"])
                    proj_fm(w_xq, 0, 8, half * WC, WC, hnT, "hnT", TBv, TBv, cons)
                ET = MRG
                if is_sample:
                    sample_xattn(XQ, OX)
                def _xh(h):
                    tmpB, rB = nxt("tmpB")
                    eo = (h % 2) * 2 * TBv
                    eres = f"MRGs{h % 2}"
                    for mt in range(2):
                        yield
                        b = S.bank()

                        def mm_s(e, b=b, mt=mt):
                            ins = None
                            for c in range(2):
                                ch = h * 2 + c
                                ins = e.matmul(pbank(b, TBv), lhsT=MKT[:, ch * NMEM + mt * 128: ch * NMEM + (mt + 1) * 128],
                                               rhs=XQ[:, ch * TBv:(ch + 1) * TBv], start=(c == 0), stop=(c == 1))
                            return ins
                        S.op("pe", mm_s, reads=["MKT", "ORg"], writes=[f"pb{b}"])
                        S.op("act", lambda e, b=b, mt=mt: e.activation(out=ET[:, eo + mt * TBv: eo + (mt + 1) * TBv], in_=pbank(b, TBv), func=AF.Exp, scale=XHD ** -0.5),
                             reads=[f"pb{b}"], writes=[eres])
                    yield
                    bd = S.bank()

                    def mm_d(e):
                        e.matmul(pbank(bd, TBv), lhsT=ones_b[:], rhs=ET[:, eo:eo + TBv], start=True, stop=False)
                        return e.matmul(pbank(bd, TBv), lhsT=ones_b[:], rhs=ET[:, eo + TBv: eo + 2 * TBv], start=False, stop=True)
                    S.op("pe", mm_d, reads=[eres, "ones_b"], writes=[f"pb{bd}"])
                    S.op("dve", lambda e: e.reciprocal(out=tmpB[:], in_=pbank(bd, TBv)), reads=[f"pb{bd}"], writes=[rB])
                    for c in range(2):
                        ch = h * 2 + c
                        yield
                        b = S.bank()

                        def mm_o(e, b=b, ch=ch):
                            e.matmul(pbank(b, TBv), lhsT=MV[:, ch * 128:(ch + 1) * 128], rhs=ET[:, eo:eo + TBv], start=True, stop=False)
                            return e.matmul(pbank(b, TBv), lhsT=MV[:, D + ch * 128: D + (ch + 1) * 128], rhs=ET[:, eo + TBv: eo + 2 * TBv], start=False, stop=True)
                        S.op("pe", mm_o, reads=["MV", eres], writes=[f"pb{b}"])
                        S.op("dve", lambda e, b=b, ch=ch: e.tensor_tensor(out=OX[:, ch * TBv:(ch + 1) * TBv], in0=pbank(b, TBv), in1=tmpB[:], op=ALU.mult),
                             reads=[f"pb{b}", rB], writes=["OGg"])
                if not is_sample:
                    _drive([_xh(0), _xh(1)])
                    _drive([_xh(2), _xh(3)])
                for half in range(NQ):
                    proj_tm(w_xo, 0, 8, half * WC, WC, OX, "OGg", TBv, NTv, resid_cons(half), tw=tw)

                if stage < 6:
                    return
                for tau in range(NTv):
                    norm_transpose(Xb[0:tw, tau * D:(tau + 1) * D], xres, gffn_c, "gffn_c", hnT, "hnT", tw, tau * tw, TBv)
                for c0 in range(0, DFF, WC):
                    ncols = min(WC, DFF - c0)
                    tg, rg_ = wload(w_gate, 0, 8, c0, ncols)
                    tu, ru = wload(w_up, 0, 8, c0, ncols)
                    def _ff(j, c0=c0, ncols=ncols, tg=tg, tu=tu, rg_=rg_, ru=ru):
                        tmpA, rA = nxt("tmpA")
                        ch = c0 // 128 + j
                        b1 = S.bank(); b2 = S.bank()

                        def mm_f(e):
                            ins = None
                            for k in range(8):
                                e.matmul(pbank(b1, TBv), lhsT=wv(tg, k, ncols, j * 128, 128), rhs=hnT[:, k * TBv:(k + 1) * TBv], start=(k == 0), stop=(k == 7))
                            for k in range(8):
                                ins = e.matmul(pbank(b2, TBv), lhsT=wv(tu, k, ncols, j * 128, 128), rhs=hnT[:, k * TBv:(k + 1) * TBv], start=(k == 0), stop=(k == 7))
                            return ins
                        S.op("pe", mm_f, reads=[rg_, ru, "hnT"], writes=[f"pb{b1}", f"pb{b2}"])
                        S.op("act", lambda e: e.activation(out=tmpA[:], in_=pbank(b1, TBv), func=AF.Silu), reads=[f"pb{b1}"], writes=[rA])
                        S.op("dve", lambda e: e.tensor_tensor(out=FA[:, ch * TBv:(ch + 1) * TBv], in0=tmpA[:], in1=pbank(b2, TBv), op=ALU.mult),
                             reads=[rA, f"pb{b2}"], writes=["FA"] + FA_AL)
                    for j in range(ncols // 128):
                        _ff(j)
                if blk + 1 < nblk_v:
                    pre_norm(blk + 1)
                for half in range(NQ):
                    tiles = [wload(w_down, g * 1024, (8 if g < 2 else 6), half * WC, WC) for g in range(3)]
                    for tau in range(NTv):
                        b = S.bank()

                        def mm_d(e, b=b, tau=tau, tiles=tiles):
                            ins = None
                            for g in range(3):
                                nk = 8 if g < 2 else 6
                                for k in range(nk):
                                    kk = g * 8 + k
                                    ins = e.matmul(pbank(b, WC)[0:tw, :], lhsT=FA[:, kk * TBv + tau * tw: kk * TBv + (tau + 1) * tw], rhs=wv(tiles[g][0], k, WC),
                                                   start=(kk == 0), stop=(kk == 21))
                            return ins
                        S.op("pe", mm_d, reads=[t[1] for t in tiles] + ["FA"], writes=[f"pb{b}"])
                        resid_cons(half)(tau, b)
                if is_sample:
                    gfin_bc, gfin_r = gfin_s, "gfin_s"
                else:
                    gfin_bc, gfin_r = D1, "D1"
                S.op("sp", lambda e, gfin_bc=gfin_bc: e.dma_start(out=gfin_bc[:, 0:D], in_=g_fin.rearrange("(o d) -> o d", o=1).partition_broadcast(128)),
                     writes=[gfin_r], dma="gfin_ld")
                for tau in range(NTv):
                    xt = Xb[0:tw, tau * D:(tau + 1) * D]
                    S.op("act", lambda e, xt=xt: e.activation(out=sqj[0:tw, :], in_=xt, func=AF.Square, accum_out=sscol[0:tw, 0:1]), reads=[xres], writes=["xn", "sscol"])
                    rstd_from_ss(sscol[0:tw, 0:1], D, EPS, "sscol")
                    S.op("dve", lambda e, xt=xt: e.scalar_tensor_tensor(out=xt, in0=xt, scalar=sscol[0:tw, 0:1], in1=gfin_bc[0:tw, 0:D], op0=ALU.mult, op1=ALU.mult),
                         reads=[xres, "sscol", gfin_r], writes=[xres])
                if is_sample:
                    S.op("sp", lambda e, Xb=Xb: e.dma_start(out=ys, in_=Xb[0:NS, 0:D]), reads=[xres], dma=xres + "o")
                else:
                    S.op("sp", lambda e, Xb=Xb, t0=t0: e.dma_start(out=yp[t0:t0 + TBv, :].rearrange("(a p) d -> p a d", p=128),
                                                                 in_=Xb[:, :].rearrange("p (a d) -> p a d", d=D)),
                         reads=[xres], dma=xres + "o")
            for blk in range(nblk_v):
                do_block(blk)
            S.emit()
            esp.close()
            cur_es[0] = es

        run_phase(False)
        if stage >= 7:
            run_phase(True)
    return nc


_CACHE = {}


def _prep_inputs(inputs):
    f = lambda a: np.ascontiguousarray(np.asarray(a, dtype=np.float32))
    common = dict(
        w_in=f(inputs["w_in"][0]), w_a=f(inputs["w_branch_a"][0]), w_b=f(inputs["w_branch_b"][0]), w_out=f(inputs["w_out"][0]),
        w_xq=f(inputs["w_xq"][0]), w_xk=f(inputs["w_xk"][0]), w_xv=f(inputs["w_xv"][0]), w_xo=f(inputs["w_xo"][0]),
        w_gate=f(inputs["w_gate"][0]), w_up=f(inputs["w_up"][0]), w_down=f(inputs["w_down"][0]),
        g_mix=f(inputs["norm_mix_g"][0]), g_x=f(inputs["norm_x_g"][0]), g_mem=f(inputs["mem_norm_g"][0]),
        g_ffn=f(inputs["norm_ffn_g"][0]), g_fin=f(inputs["norm_final_g"]), g_gn=f(inputs["ret_gn_g"][0]),
        g_gdn=f(inputs["gdn_norm_g"][0]), convw=f(inputs["gdn_conv_w"][0]), a_log=f(inputs["gdn_a_log"][0]),
        dt_bias=f(inputs["gdn_dt_bias"][0]))
    common.update(_CONSTS)
    maps = []
    for c in range(NCORES):
        m = dict(common)
        sl = slice(c * NS, (c + 1) * NS)
        m["xp"] = f(inputs["x_prompt"][c]); m["memp"] = f(inputs["mem_prompt"][c])
        m["xs"] = f(inputs["x_sample"][sl, 0]); m["sret"] = f(inputs["state_ret"][0, sl]); m["sgdn"] = f(inputs["state_gdn"][0, sl])
        m["sconv"] = f(inputs["state_conv"][0, sl])
        m["cmk"] = f(inputs["cache_mem_k"][0, sl]).reshape(NS, NMEM, D); m["cmv"] = f(inputs["cache_mem_v"][0, sl]).reshape(NS, NMEM, D)
        maps.append(m)
    return maps


def kernel(**inputs):
    if "nc" not in _CACHE:
        _CACHE["nc"] = build_program()
    nc = _CACHE["nc"]
    maps = _prep_inputs(inputs)
    res = run_bass_kernel_spmd(nc, maps, core_ids=list(range(NCORES)))
    R = res.results
    st = lambda k: np.stack([np.asarray(r[k], dtype=np.float32) for r in R])
    cat = lambda k: np.concatenate([np.asarray(r[k], dtype=np.float32) for r in R], axis=0)
    y_prompt = st("yp")
    y_sample = cat("ys").reshape(NCORES * NS, 1, D)
    return (y_prompt, y_sample, st("srp")[None], st("sgp")[None], st("scp")[None],
            st("mkp").reshape(1, NCORES, NMEM, XH, XHD), st("mvp").reshape(1, NCORES, NMEM, XH, XHD),
            cat("srs")[None], cat("sgs")[None], cat("scs")[None])
```

```python
from contextlib import ExitStack
import math
import numpy as np
import concourse.bass as bass
import concourse.mybir as mybir
from concourse.bass_utils import run_bass_kernel_spmd

F32 = mybir.dt.float32
BF16 = mybir.dt.bfloat16
AF = mybir.ActivationFunctionType
ALU = mybir.AluOpType

NCORES = 8
D = 1024
SEQ = 2048
NS = 16
PAST = 16384
RH, RDK, RDV = 4, 128, 256
GH, GDK, GDV = 8, 128, 128
CONV_CH = 3072
NMEM = 256
XH, XHD = 4, 256
DFF = 2816
DIN = 9232
EPS = 1e-6
TB = 256
NT = TB // 128
NBLK = SEQ // TB
ENGS = ("pe", "act", "dve", "pool", "sp")
EPOCH = 3000
NEG = -30000.0


class Sched:
    def __init__(self, nc, es):
        self.nc, self.es = nc, es
        self.eng_sems = {e: [] for e in ENGS}
        self.nsem = 0
        self.prev_final = []
        self.reset()

    def reset(self):
        self.ops = []
        self.last_w = {}
        self.readers = {}
        self.eng_count = {e: 0 for e in ENGS}
        self.eng_base = {e: len(self.eng_sems[e]) for e in ENGS}
        self.dma_sems = {}
        self.dma_cnt = {}
        self.bank_i = 0
        self.reserved = set()

    def _newsem(self, name):
        self.nsem += 1
        return self.es.enter_context(self.nc.semaphore(f"{name}_{self.nsem}"))

    def _token_compute(self, eng):
        i = self.eng_count[eng]
        self.eng_count[eng] += 1
        ep, k = divmod(i, EPOCH)
        ep += self.eng_base[eng]
        while len(self.eng_sems[eng]) <= ep:
            self.eng_sems[eng].append(self._newsem(f"s{eng}"))
        return (self.eng_sems[eng][ep], k + 1, 1)

    def _token_dma(self, key):
        if key not in self.dma_sems or self.dma_cnt[key] > 60000:
            self.dma_sems[key] = self._newsem("d")
            self.dma_cnt[key] = 0
        self.dma_cnt[key] += 16
        return (self.dma_sems[key], self.dma_cnt[key], 16)

    def bank(self):
        while True:
            b = self.bank_i
            self.bank_i = (self.bank_i + 1) % 8
            if b not in self.reserved:
                return b

    def op(self, eng, fn, reads=(), writes=(), dma=None):
        writes = list(writes) + [r for r in reads if r.startswith("pb") and r not in writes]
        deps = set()
        for r in reads:
            if r in self.last_w:
                deps.add(self.last_w[r])
        for w in writes:
            if w in self.last_w:
                deps.add(self.last_w[w])
            for rd in self.readers.get(w, ()):
                deps.add(rd)
        idx = len(self.ops)
        tok = self._token_dma(dma) if dma is not None else self._token_compute(eng)
        self.ops.append(dict(eng=eng, fn=fn, deps=deps, tok=tok, dma=dma is not None))
        for r in reads:
            self.readers.setdefault(r, []).append(idx)
        for w in writes:
            self.last_w[w] = idx
            self.readers[w] = []
        return idx

    def emit(self):
        nc, ops = self.nc, self.ops
        per = {e: [] for e in ENGS}
        for i, o in enumerate(ops):
            per[o["eng"]].append(i)
        final = {}
        for e in ENGS:
            for ep in range(self.eng_base[e], len(self.eng_sems[e])):
                n = min(EPOCH, self.eng_count[e] - (ep - self.eng_base[e]) * EPOCH)
                s = self.eng_sems[e][ep]
                final[id(s)] = (s, n)
        for o in ops:
            sem, val, _ = o["tok"]
            if final.get(id(sem), (None, 0))[1] < val:
                final[id(sem)] = (sem, val)
        prev_final = self.prev_final

        def run(eng_name, engine):
            waited = {}
            for (sem, val) in prev_final:
                engine.wait_ge(sem, val)
                waited[id(sem)] = val
            for i in per[eng_name]:
                o = ops[i]
                need = {}
                for d in o["deps"]:
                    od = ops[d]
                    if od["eng"] == "pe" and eng_name == "pe" and not od["dma"]:
                        continue
                    sem, val, _ = od["tok"]
                    k = id(sem)
                    if waited.get(k, 0) >= val:
                        continue
                    if k not in need or need[k][1] < val:
                        need[k] = (sem, val)
                for k, (sem, val) in need.items():
                    engine.wait_ge(sem, val)
                    waited[k] = val
                ins = o["fn"](engine)
                sem, val, inc = o["tok"]
                ins.then_inc(sem, inc)
            if eng_name == "sp":
                for k, (sem, val) in final.items():
                    if waited.get(k, 0) >= val:
                        continue
                    engine.wait_ge(sem, val)

        with nc.Block() as block:
            @block.tensor
            def _(e):
                run("pe", e)

            @block.scalar
            def _(e):
                run("act", e)

            @block.vector
            def _(e):
                run("dve", e)

            @block.gpsimd
            def _(e):
                run("pool", e)

            @block.sync
            def _(e):
                run("sp", e)
        self.prev_final = list(final.values())
        self.reset()


def _consts():
    c = {}
    idx = np.arange(128)
    c["c_ident"] = np.eye(128, dtype=np.float32)
    c["c_tri"] = (idx[:, None] <= idx[None, :]).astype(np.float32)
    c["c_mincl"] = np.where(idx[None, :] >= idx[:, None], 0.0, NEG).astype(np.float32)
    c["c_strict"] = (idx[None, :] > idx[:, None]).astype(np.float32)
    c["c_ones"] = np.ones((128, 128), np.float32)
    blk = lambda b: (idx[:, None] // b) == (idx[None, :] // b)
    c["c_m16"] = blk(16).astype(np.float32)
    c["c_moff"] = np.concatenate([-(blk(2 * b) & ~blk(b)).astype(np.float32) for b in (16, 32, 64)], axis=1)
    perm = np.zeros((128, 128), np.float32)
    perm[(idx + 64) % 128, idx] = 1.0
    c["c_perm"] = perm
    h = np.arange(RH, dtype=np.float64)
    log_g = np.log1p(-np.exp2(-5.0 - h))
    diff = idx[None, :] - idx[:, None]
    dt = np.where(diff[:, None, :] >= 0, np.exp(np.maximum(diff, 0)[:, None, :] * log_g[None, :, None]), 0.0)
    c["c_rdt"] = (dt * RDK ** -0.5).astype(np.float32).reshape(128, RH * 128)
    qd = np.exp((idx[None, :] + 1.0) * log_g[:, None])
    c["c_rqd"] = np.broadcast_to(qd.reshape(1, RH * 128), (128, RH * 128)).astype(np.float32).copy()
    kd = np.exp((127.0 - idx)[:, None] * log_g[None, :]) * RDK ** -0.5
    c["c_rkd"] = kd.astype(np.float32)
    gam = np.exp(log_g)
    half = RDK // 2
    inv = 10000.0 ** (-np.arange(half, dtype=np.float64) / half)
    pos = np.arange(SEQ, dtype=np.float64)
    ang = inv[:, None] * pos[None, :]
    ang = (inv.astype(np.float32)[:, None] * pos.astype(np.float32)[None, :]).astype(np.float64)
    cos, sin = np.cos(ang), np.sin(ang)
    c["c_cos"] = np.concatenate([cos, cos], 0).astype(np.float32)
    c["c_sin"] = np.concatenate([-sin, sin], 0).astype(np.float32)
    angs = (inv.astype(np.float32) * np.float32(PAST)).astype(np.float64)
    c["c_cs_s"] = np.stack([np.concatenate([np.cos(angs), np.cos(angs)]),
                            np.concatenate([-np.sin(angs), np.sin(angs)])]).astype(np.float32)
    return c, gam


_CONSTS, _GAM = _consts()

PJ_OFF = {"rq": (0, 0), "rk": (512, 512), "rv": (1024, 1024), "qkv": (3072, 2048), "ba": (7168, 5120)}
WC = 256
NQ = 1024 // WC
W_IN_TILES = []
for _k, _c0, _n in (("rq", 0, 512), ("rk", 512, 512), ("rv", 1024, 1024), ("rg", 2048, 1024), ("qkv", 3072, 3072),
                    ("z", 6144, 1024), ("ba", 7168, 16), ("ga", 7184, 1024), ("gb", 8208, 1024)):
    for _o in range(0, _n, WC):
        W_IN_TILES.append((_k, _c0 + _o, min(WC, _n - _o)))


def build_program(debug=False, nblk=NBLK, stage=99, kinds=None):
    nc = bass.Bass("TRN2", target_bir_lowering=False)
    di = lambda name, shape: nc.dram_tensor(name, list(shape), F32, kind="ExternalInput").ap()
    do = lambda name, shape: nc.dram_tensor(name, list(shape), F32, kind="ExternalOutput").ap()
    xp = di("xp", [SEQ, D]); memp = di("memp", [NMEM, D])
    xs = di("xs", [NS, D]); sret = di("sret", [NS, RH, RDK, RDV]); sgdn = di("sgdn", [NS, GH, GDK, GDV])
    sconv = di("sconv", [NS, 3, CONV_CH]); cmk = di("cmk", [NS, NMEM, D]); cmv = di("cmv", [NS, NMEM, D])
    w_in = di("w_in", [D, DIN]); w_a = di("w_a", [D, D]); w_b = di("w_b", [D, D]); w_out = di("w_out", [D, D])
    w_xq = di("w_xq", [D, D]); w_xk = di("w_xk", [D, D]); w_xv = di("w_xv", [D, D]); w_xo = di("w_xo", [D, D])
    w_gate = di("w_gate", [D, DFF]); w_up = di("w_up", [D, DFF]); w_down = di("w_down", [DFF, D])
    g_mix = di("g_mix", [D]); g_x = di("g_x", [D]); g_mem = di("g_mem", [D]); g_ffn = di("g_ffn", [D])
    g_fin = di("g_fin", [D]); g_gn = di("g_gn", [D]); g_gdn = di("g_gdn", [GDV])
    convw = di("convw", [4, CONV_CH]); a_log = di("a_log", [GH]); dt_bias = di("dt_bias", [GH])
    cst = {k: di(k, v.shape) for k, v in _CONSTS.items()}
    yp = do("yp", [SEQ, D]); ys = do("ys", [NS, D])
    srp = do("srp", [RH, RDK, RDV]); sgp = do("sgp", [GH, GDK, GDV]); scp = do("scp", [3, CONV_CH])
    mkp = do("mkp", [NMEM, D]); mvp = do("mvp", [NMEM, D])
    srs = do("srs", [NS, RH, RDK, RDV]); sgs = do("sgs", [NS, GH, GDK, GDV]); scs = do("scs", [NS, 3, CONV_CH])

    with ExitStack() as es:
        S = Sched(nc, es)
        sfx = [""]
        sbt = lambda name, shape, dt=BF16: cur_es[0].enter_context(nc.sbuf_tensor(name + sfx[0], list(shape), dt))
        cur_es = [es]
        PS = es.enter_context(nc.psum_tensor("PS", [128, 4096], F32))
        PSB = PS[:, :].bitcast(BF16)

        def pbank(b, n=512, off=0):
            return PS[:, b * 512 + off: b * 512 + off + n]

        def pbank_bf(b, n=1024, off=0):
            return PSB[:, b * 1024 + off: b * 1024 + off + n]

        def load_const(name, shape, src, dt=F32, eng="sp"):
            t = sbt(name, shape, dt)
            S.op(eng, lambda e: e.dma_start(out=t[:], in_=src), writes=[name], dma=name)
            return t
        ident_f = load_const("ident_f", [128, 128], cst["c_ident"])
        ident_b = load_const("ident_b", [128, 128], cst["c_ident"], BF16, "pool")
        ones_f = load_const("ones_f", [128, 128], cst["c_ones"])
        ones_b = load_const("ones_b", [128, 128], cst["c_ones"], BF16, "pool")
        perm_b = load_const("perm_b", [128, 128], cst["c_perm"], BF16, "pool")
        tri_f = load_const("tri_f", [128, 128], cst["c_tri"])
        mincl = load_const("mincl", [128, 128], cst["c_mincl"])
        strict = load_const("strict", [128, 128], cst["c_strict"])
        m16 = load_const("m16", [128, 128], cst["c_m16"], BF16, "pool")
        moff = load_const("moff", [128, 384], cst["c_moff"], BF16, "pool")
        rdt = load_const("rdt", [128, 512], cst["c_rdt"], BF16, "pool")
        rqd = load_const("rqd", [128, 512], cst["c_rqd"], BF16, "pool")
        rkd = load_const("rkd", [128, RH], cst["c_rkd"])
        with nc.allow_non_contiguous_dma(reason="small param column loads"):
            def col_load(name, src, n):
                t = sbt(name, [128, n], F32)
                S.op("sp", lambda e: e.dma_start(out=t[:], in_=src.rearrange("(k p) -> p k", p=128), allow_slow_non_contiguous=True),
                     writes=[name], dma=name)
                return t
            gmix_c = col_load("gmix_c", g_mix, 8)
            gx_c = col_load("gx_c", g_x, 8)
            gmem_c = col_load("gmem_c", g_mem, 8)
            gffn_c = col_load("gffn_c", g_ffn, 8)
            ggn_c = col_load("ggn_c", g_gn, 8)
            ggdn_c = col_load("ggdn_c", g_gdn, 1)
            cw = sbt("cw", [128, 4 * 24], F32)
            for i in range(4):
                S.op("sp", lambda e, i=i: e.dma_start(out=cw[:, i * 24:(i + 1) * 24], in_=convw[i].rearrange("(c p) -> p c", p=128),
                                                       allow_slow_non_contiguous=True),
                     writes=["cw"], dma="cw")
        alog_bc = sbt("alog_bc", [128, GH], F32)
        dtb_bc = sbt("dtb_bc", [128, GH], F32)
        S.op("sp", lambda e: e.dma_start(out=alog_bc[:], in_=a_log.rearrange("(o d) -> o d", o=1).partition_broadcast(128)),
             writes=["alog_bc"], dma="alog_bc")
        S.op("sp", lambda e: e.dma_start(out=dtb_bc[:], in_=dt_bias.rearrange("(o d) -> o d", o=1).partition_broadcast(128)),
             writes=["dtb_bc"], dma="dtb_bc")
        nega = sbt("nega", [128, GH], F32)
        S.op("act", lambda e: e.activation(out=nega[:], in_=alog_bc[:], func=AF.Exp), reads=["alog_bc"], writes=["nega"])
        S.op("dve", lambda e: e.tensor_scalar(out=nega[:], in0=nega[:], scalar1=-1.0, scalar2=None, op0=ALU.mult),
             reads=["nega"], writes=["nega"])
        ones128_b = sbt("ones128_b", [128, 128], BF16)
        S.op("dve", lambda e: e.tensor_scalar(out=ones128_b[:], in0=ones_f[:], scalar1=128.0, scalar2=None, op0=ALU.mult),
             reads=["ones_f"], writes=["ones128_b"])
        onesq_b = sbt("onesq_b", [128, 128], BF16)
        S.op("dve", lambda e: e.tensor_scalar(out=onesq_b[:], in0=ones_f[:], scalar1=1.0 / 256, scalar2=None, op0=ALU.mult),
             reads=["ones_f"], writes=["onesq_b"])

        NSLOT = 6
        wslots = [sbt(f"wslot{i}", [128, 8 * WC], BF16) for i in range(NSLOT)]
        wstate = dict(i=0)

        scratch = {}

        def wload(W, r0, nk, c0, ncols):
            s = wstate["i"] % NSLOT
            wstate["i"] += 1
            t = wslots[s]
            dst = t[:, 0:nk * ncols].rearrange("p (k c) -> p k c", c=ncols)
            key = (W.tensor.name, r0, nk, c0, ncols)
            if key not in scratch:
                sc = nc.dram_tensor(f"scr{len(scratch)}", [128, nk * ncols], BF16, kind="Internal").ap()
                scratch[key] = (sc, f"scr{len(scratch)}")
                sc, sres = scratch[key]
                src = W[r0:r0 + nk * 128, c0:c0 + ncols].rearrange("(k p) c -> p k c", p=128)
                S.op("pool", lambda e: e.dma_start(out=dst, in_=src), writes=[f"wslot{s}"], dma=f"wslot{s}")
                S.op("sp", lambda e: e.dma_start(out=sc, in_=t[:, 0:nk * ncols]), reads=[f"wslot{s}"], writes=[sres], dma=f"wst{s}")
            else:
                sc, sres = scratch[key]
                S.op("pool", lambda e: e.dma_start(out=t[:, 0:nk * ncols], in_=sc), reads=[sres], writes=[f"wslot{s}"], dma=f"wslot{s}")
            return t, f"wslot{s}"

        def wv(t, k, ncols, c0=0, n=None):
            n = ncols if n is None else n
            return t[:, k * ncols + c0: k * ncols + c0 + n]

        def rstd_from_ss(ss, n, eps, res):
            S.op("dve", lambda e: e.tensor_scalar(out=ss, in0=ss, scalar1=1.0 / n, scalar2=eps, op0=ALU.mult, op1=ALU.add),
                 reads=[res], writes=[res])
            S.op("act", lambda e: e.activation(out=ss, in_=ss, func=AF.Ln), reads=[res], writes=[res])
            S.op("act", lambda e: e.activation(out=ss, in_=ss, func=AF.Exp, scale=-0.5), reads=[res], writes=[res])

        xn = sbt("xn", [128, D], BF16)
        sqj = xn
        sscol = sbt("sscol", [128, 4], F32)

        def norm_transpose(xt, xres, gcol, gres, dstT, dres, ntok, tcol, ncols_total):
            S.op("act", lambda e: e.activation(out=sqj[0:ntok, :], in_=xt, func=AF.Square, accum_out=sscol[0:ntok, 0:1]),
                 reads=[xres], writes=["xn", "sscol"])
            rstd_from_ss(sscol[0:ntok, 0:1], D, EPS, "sscol")
            S.op("dve", lambda e: e.tensor_scalar(out=xn[0:ntok, :], in0=xt, scalar1=sscol[0:ntok, 0:1], scalar2=None, op0=ALU.mult),
                 reads=[xres, "sscol"], writes=["xn"])
            b = S.bank()

            def tr(e):
                ins = None
                for k in range(8):
                    ins = e.transpose(pbank_bf(b, ntok, k * 128)[:, :], xn[0:ntok, k * 128:(k + 1) * 128], ident_b[0:ntok, 0:ntok])
                return ins
            S.op("pe", tr, reads=["xn", "ident_b"], writes=[f"pb{b}"])
            src = pbank_bf(b, 1024).rearrange("p (k i) -> p k i", i=128)[:, :, 0:ntok]
            dst = dstT[:, :].rearrange("p (k t) -> p k t", t=ncols_total)[:, :, tcol:tcol + ntok]
            S.op("dve", lambda e: e.tensor_tensor(out=dst, in0=src, in1=gcol[:, 0:8].unsqueeze(2).to_broadcast([128, 8, ntok]), op=ALU.mult),
                 reads=[f"pb{b}", gres], writes=[dres])

        def proj_fm(W, r0, nk, c0, ncols, srcT, sres, src_cols, ntok, consume):
            t, wres = wload(W, r0, nk, c0, ncols)
            for j in range(ncols // 128):
                b = S.bank()

                def mm(e, j=j, b=b):
                    ins = None
                    for k in range(nk):
                        ins = e.matmul(pbank(b, ntok), lhsT=wv(t, k, ncols, j * 128, 128),
                                       rhs=srcT[:, k * src_cols: k * src_cols + ntok], start=(k == 0), stop=(k == nk - 1))
                    return ins
                S.op("pe", mm, reads=[wres, sres], writes=[f"pb{b}"])
                consume(j, b)

        def proj_tm(W, r0, nk, c0, ncols, srcT, sres, src_cols, ntiles, consume, tw=128):
            t, wres = wload(W, r0, nk, c0, ncols)
            for tau in range(ntiles):
                b = S.bank()

                def mm(e, tau=tau, b=b):
                    ins = None
                    for k in range(nk):
                        ins = e.matmul(pbank(b, ncols)[0:tw, :], lhsT=srcT[:, k * src_cols + tau * tw: k * src_cols + (tau + 1) * tw],
                                       rhs=wv(t, k, ncols), start=(k == 0), stop=(k == nk - 1))
                    return ins
                S.op("pe", mm, reads=[wres, sres], writes=[f"pb{b}"])
                consume(tau, b)


        def run_phase(is_sample):
            TBv = NS if is_sample else TB
            tw = NS if is_sample else 128
            NTv = TBv // tw
            nblk_v = 1 if is_sample else nblk
            esp = ExitStack()
            cur_es[0] = esp
            sfx[0] = "_s" if is_sample else "_p"
            XOR = [sbt(f"XOR{i_}", [128, NTv * D], F32) for i_ in range(2)]
            hnT = sbt("hnT", [128, 8 * TBv])
            RG = sbt("RG", [128, 8 * TBv])
            Z = sbt("Z", [128, 8 * TBv]); GA = sbt("GA", [128, 8 * TBv]); GB = sbt("GB", [128, 8 * TBv])
            NROT = 4
            _rot = {"tmpA": [sbt(f"tmpA{i}", [128, TBv], F32) for i in range(NROT)],
                    "tmpB": [sbt(f"tmpB{i}", [128, TBv], F32) for i in range(NROT)],
                    "tmpb": [sbt(f"tmpb{i}", [128, TBv], BF16) for i in range(NROT)],
                    "CACC": [sbt(f"CACC{i}", [128, TBv], F32) for i in range(NROT)]}
            _roti = {}

            def nxt(name):
                i = _roti.get(name, 0)
                _roti[name] = i + 1
                return _rot[name][i % NROT], f"{name}{i % NROT}"
            OG = sbt("OG", [128, 8 * TBv], F32)
            OR = XOR[1][:, 0:8 * TBv]
            ORg = sbt("ORg", [128, 8 * TBv]); OGg = sbt("OGg", [128, 8 * TBv])
            MRG = sbt("MRG", [128, 8 * TBv])
            if is_sample:
                FA = sbt("FA", [128, 22 * TBv])
                FA_AL = []
            else:
                QKVB = sbt("QKVB", [128, 24 * TBv])
                FA = QKVB[:, 0:22 * TBv]
                FA_AL = ["GQ", "GK", "GV"]
            if is_sample:
                PJ = sbt("PJ", [NS, 5136], F32)
                GNSs = sbt("GNSs", [128, 8 * TBv], F32)
                RNSs = sbt("RNSs", [128, 8 * TBv], F32)
                RNS = [(RNSs[:, 0:4 * TBv], "RNSa"), (RNSs[:, 4 * TBv:8 * TBv], "RNSb")]
                gfin_s = sbt("gfin_s", [128, D], F32)
            if not is_sample:
                RQ = sbt("RQ", [128, 4 * TBv]); RK = sbt("RK", [128, 4 * TBv]); RQd = sbt("RQd", [128, 4 * TBv])
                VtR = sbt("VtR", [128, NTv * 1024])
                GQ = QKVB[:, 0:8 * TBv]; GK = QKVB[:, 8 * TBv:16 * TBv]; GV = QKVB[:, 16 * TBv:24 * TBv]
                BA = sbt("BA", [128, NTv * 16], F32)
                _rot["CIN"] = [sbt(f"CIN{i}", [128, 3 + TBv], F32) for i in range(NROT)]
                HALO = sbt("HALO", [128, 24 * 3], F32)
                cosb = sbt("cosb", [128, TBv], F32); sinb = sbt("sinb", [128, TBv], F32)
                SR = sbt("SR", [128, RH * RDV], F32); SRb = sbt("SRb", [128, RH * RDV])
                SG = sbt("SG", [128, GH * GDV], F32); SGb = sbt("SGb", [128, GH * GDV])
                MKT = sbt("MKT", [128, 8 * NMEM]); MV = sbt("MV", [128, 2 * D])
                memx = OR[:, 0:D]
                mnT = OGg
                mo = OG[:, 0:512]

                S.op("dve", lambda e: e.memset(SR[:], 0.0), writes=["SR"])
                S.op("dve", lambda e: e.memset(SRb[:], 0.0), writes=["SRb"])
                S.op("dve", lambda e: e.memset(SG[:], 0.0), writes=["SG0", "SG1"])
                S.op("dve", lambda e: e.memset(SGb[:], 0.0), writes=["SGb0", "SGb1"])
                S.op("dve", lambda e: e.memset(HALO[:], 0.0), writes=["HALO"])

                for mt in range(2):
                    S.op("sp", lambda e, mt=mt: e.dma_start(out=memx[:], in_=memp[mt * 128:(mt + 1) * 128, :]), writes=["XOR1"], dma="memx")
                    norm_transpose(memx, "XOR1", gmem_c, "gmem_c", mnT, "OGg", 128, mt * 128, NMEM)
                for (W, outd, isk) in ((w_xk, mkp, True), (w_xv, mvp, False)):
                    for half in range(NQ):
                        def cons(tau, b, half=half, outd=outd, isk=isk):
                            S.op("act", lambda e: e.activation(out=mo[:, 0:WC], in_=pbank(b, WC), func=AF.Copy), reads=[f"pb{b}"], writes=["OG"])
                            if not isk:
                                S.op("dve", lambda e: e.tensor_copy(out=MV[:, tau * D + half * WC: tau * D + half * WC + WC], in_=pbank(b, WC)),
                                     reads=[f"pb{b}"], writes=["MV"])
                            S.op("sp", lambda e: e.dma_start(out=outd[tau * 128:(tau + 1) * 128, half * WC:(half + 1) * WC], in_=mo[:, 0:WC]),
                                 reads=["OG"], dma="mo_out")
                        proj_tm(W, 0, 8, half * WC, WC, mnT, "OGg", NMEM, 2, cons)
                for half in range(NQ):
                    def cons(j, b, half=half):
                        c = half * (WC // 128) + j
                        S.op("act", lambda e: e.activation(out=MKT[:, c * NMEM:(c + 1) * NMEM], in_=pbank(b, NMEM), func=AF.Copy),
                             reads=[f"pb{b}"], writes=["MKT"])
                    proj_fm(w_xk, 0, 8, half * WC, WC, mnT, "OGg", NMEM, NMEM, cons)

                gsmT = [sbt(f"gsm{t_}", [128, 64], F32) for t_ in range(NTv)]
                D1 = sbt("D1", [128, 1024], F32)
                GBm = sbt("GBm", [128, 1024], F32)
                G2 = GBm
                EGC = GBm
                GCTX = []
                for g_ in range(2):
                    GCTX.append((sbt(f"N0_{g_}", [128, 512], BF16), sbt(f"N0T_{g_}", [128, 512], BF16),
                                 sbt(f"WA_{g_}", [128, 1024], BF16), sbt(f"WB_{g_}", [128, 1024], BF16),
                                 [sbt(f"MO{l}_{g_}", [128, 512], BF16) for l in range(3)],
                                 [sbt(f"MOT{l}_{g_}", [128, 512], BF16) for l in range(2)],
                                 sbt(f"V1s_{g_}", [128, 512], BF16), sbt(f"V2s_{g_}", [128, 512], BF16)))
                TTf = sbt("TTf", [128, 1024], BF16)
                RNS = [(D1[:, 0:4 * TBv], "D1"), (GBm[:, 0:4 * TBv], "GBm")]
                PTg = sbt("PTg", [128, 1024], BF16)
                QTdT = [sbt(f"QTd{i_}", [128, 1024], BF16) for i_ in range(2)]
                KtG = sbt("KtG", [128, 1024], BF16)
                VtG = sbt("VtG", [128, 1024], BF16)
                rtl = sbt("rtl", [128, 1024], BF16)
                vnw = sbt("vnw", [128, 1024], BF16)
                PTr = sbt("PTr", [128, 512], BF16)
                KtR = sbt("KtR", [128, 512], BF16)
                identb8 = sbt("identb8", [128, 512], BF16)
                S.op("dve", lambda e: e.tensor_copy(out=identb8[:, :].rearrange("p (h i) -> p h i", i=128),
                                                     in_=ident_f[:, :].unsqueeze(1).to_broadcast([128, 4, 128])),
                     reads=["ident_f"], writes=["identb8"])

            if is_sample:
                AXX = mybir.AxisListType.X
                RES = 7
                cs = sbt("cs_s", [NS, 256], F32)
                S.op("sp", lambda e: e.dma_start(out=cs[:], in_=cst["c_cs_s"].rearrange("(o a) d -> o (a d)", o=1).partition_broadcast(NS)),
                     writes=["cs_s"], dma="cs_s")
                QR = sbt("QR", [NS, 512], F32); KR = sbt("KR", [NS, 512], F32); T1s = sbt("T1s", [NS, 512], F32)
                QKVs = sbt("QKVs", [NS, 3072], F32); QKn = sbt("QKn", [NS, 2048], F32)
                SCc = [sbt(f"SCc{i}", [NS, 3 * 512], F32) for i in range(2)]; CWc = [sbt(f"CWc{i}", [NS, 4 * 512], F32) for i in range(2)]
                ACC = sbt("ACC", [NS, 512], F32); TMPs = sbt("TMPs", [NS, 512], F32)
                gs = sbt("gs", [NS, 64], F32)
                BV = sbt("BV", [NS, 1024], F32); Rr = sbt("Rr", [NS, 1024], F32)
                KMr = sbt("KMr", [NS, 512], F32); KMg = sbt("KMg", [NS, 1024], F32)
                EGd = sbt("EGd", [NS, 128], F32); EGB = sbt("EGB", [128, 128], F32)
                qTr = sbt("qTr", [128, 64], BF16); qTg = sbt("qTg", [128, 128], BF16); kTg = sbt("kTg", [128, 128], F32)
                SRob = sbt("SRob", [128, 1024], BF16); SGob = sbt("SGob", [128, 1024], BF16); Vcb = [sbt("Vcb0", [128, 2048], BF16)] * 2
                ARENA = sbt("ARENA", [128, 8192], F32)
                SRin = [ARENA[:, i * 1024:(i + 1) * 1024] for i in range(2)]
                SGin = [ARENA[:, (2 + i) * 1024:(3 + i) * 1024] for i in range(2)]
                SRo = [ARENA[:, (4 + i) * 1024:(5 + i) * 1024] for i in range(2)]
                SGo = [ARENA[:, (6 + i) * 1024:(7 + i) * 1024] for i in range(2)]
                Kc = [ARENA[:, i * 2048:(i + 1) * 2048] for i in range(2)]
                Vc = [ARENA[:, (2 + i) * 2048:(3 + i) * 2048] for i in range(2)]
                ARENA_RES = [f"{n}{i}" for n in ("SRin", "SGin", "SRo", "SGo") for i in range(2)]
                QD = [sbt("QD0", [128, 1024], F32)] * 2; PR = sbt("PR", [128, 1024], F32)
                SCs = sbt("SCs", [128, 128], F32); Pm = sbt("Pm", [128, 128], BF16); RD = sbt("RD", [128, 64], F32)

                def rope(src0, dst, dres, scale):
                    x = PJ[0:NS, src0:src0 + 512].rearrange("p (h d) -> p h d", d=128)
                    d3 = dst[:, :].rearrange("p (h d) -> p h d", d=128)
                    t3 = T1s[:, :].rearrange("p (h d) -> p h d", d=128)
                    C = cs[:, 0:128].unsqueeze(1).to_broadcast([NS, 4, 128])
                    S.op("dve", lambda e: e.tensor_tensor(out=t3, in0=x, in1=C, op=ALU.mult), reads=["PJ", "cs_s"], writes=["T1s"])
                    S.op("dve", lambda e: e.tensor_tensor(out=d3[:, :, 0:64], in0=x[:, :, 64:128],
                                                          in1=cs[:, 128:192].unsqueeze(1).to_broadcast([NS, 4, 64]), op=ALU.mult),
                         reads=["PJ", "cs_s"], writes=[dres])
                    S.op("dve", lambda e: e.tensor_tensor(out=d3[:, :, 64:128], in0=x[:, :, 0:64],
                                                          in1=cs[:, 192:256].unsqueeze(1).to_broadcast([NS, 4, 64]), op=ALU.mult),
                         reads=["PJ", "cs_s"], writes=[dres])
                    S.op("dve", lambda e: e.scalar_tensor_tensor(out=dst[:, :], in0=dst[:, :], scalar=1.0, in1=T1s[:, :], op0=ALU.mult, op1=ALU.add),
                         reads=[dres, "T1s"], writes=[dres])
                    if scale != 1.0:
                        S.op("dve", lambda e: e.tensor_scalar(out=dst[:, :], in0=dst[:, :], scalar1=scale, scalar2=None, op0=ALU.mult),
                             reads=[dres], writes=[dres])

                def to_fm(src, sres, nh, dst, dres):
                    b = S.bank()

                    def tr(e):
                        ins = None
                        for h in range(nh):
                            ins = e.transpose(pbank(b, NS, h * NS), src[0:NS, h * 128:(h + 1) * 128], ident_f[0:NS, 0:NS])
                        return ins
                    S.op("pe", tr, reads=[sres, "ident_f"], writes=[f"pb{b}"])
                    S.op("act", lambda e: e.activation(out=dst[:, 0:nh * NS], in_=pbank(b, nh * NS), func=AF.Copy), reads=[f"pb{b}"], writes=[dres])

                def sample_mixers():
                    rope(0, QR, "QR", 1.0)
                    rope(512, KR, "KR", RDK ** -0.5)
                    to_fm(QR, "QR", 4, qTr, "qTr")
                    S.op("sp", lambda e: e.dma_start(out=scs[:, 0:2, :], in_=sconv[:, 1:3, :]), dma="scs_a")
                    S.op("sp", lambda e: e.dma_start(out=scs[:, 2, :], in_=PJ[0:NS, 2048:5120]), reads=["PJ"], dma="scs_b")
                    def conv_loads(cchunk):
                        c0 = cchunk * 512
                        pp = cchunk % 2
                        S.op("sp", lambda e: e.dma_start(out=SCc[pp][:, :].rearrange("p (i c) -> p i c", c=512), in_=sconv[:, :, c0:c0 + 512]),
                             writes=[f"SCc{pp}"], dma=f"SCc{pp}")
                        for i in range(4):
                            S.op("sp", lambda e, i=i: e.dma_start(out=CWc[pp][:, i * 512:(i + 1) * 512], in_=convw[i:i + 1, c0:c0 + 512].partition_broadcast(NS)),
                                 writes=[f"CWc{pp}"], dma=f"CWc{pp}")
                    conv_loads(0)
                    for cchunk in range(6):
                        c0 = cchunk * 512
                        pp = cchunk % 2
                        if cchunk + 1 < 6:
                            conv_loads(cchunk + 1)
                        SC_, CW_, rsc, rcw = SCc[pp], CWc[pp], f"SCc{pp}", f"CWc{pp}"
                        S.op("dve", lambda e, SC_=SC_, CW_=CW_: e.tensor_tensor(out=ACC[:, :], in0=SC_[:, 0:512], in1=CW_[:, 0:512], op=ALU.mult),
                             reads=[rsc, rcw], writes=["ACC"])
                        for i in range(1, 4):
                            src = SC_[:, i * 512:(i + 1) * 512] if i < 3 else PJ[0:NS, 2048 + c0: 2048 + c0 + 512]
                            S.op("dve", lambda e, src=src, i=i, CW_=CW_: e.tensor_tensor(out=TMPs[:, 0:512], in0=src, in1=CW_[:, i * 512:(i + 1) * 512], op=ALU.mult),
                                 reads=[rsc, rcw, "PJ"], writes=["TMPs"])
                            S.op("dve", lambda e: e.tensor_tensor(out=ACC[:, :], in0=ACC[:, :], in1=TMPs[:, 0:512], op=ALU.add), reads=["ACC", "TMPs"], writes=["ACC"])
                        S.op("act", lambda e, c0=c0: e.activation(out=QKVs[:, c0:c0 + 512], in_=ACC[:, :], func=AF.Silu), reads=["ACC"], writes=["QKVs"])
                    S.op("dve", lambda e: e.tensor_tensor(out=QKn[:, :], in0=QKVs[:, 0:2048], in1=QKVs[:, 0:2048], op=ALU.mult), reads=["QKVs"], writes=["QKn"])
                    S.op("dve", lambda e: e.tensor_reduce(out=gs[:, 32:48], in_=QKn[:, :].rearrange("p (h d) -> p h d", d=128), axis=AXX, op=ALU.add),
                         reads=["QKn"], writes=["gs"])
                    S.op("dve", lambda e: e.tensor_scalar(out=gs[:, 32:48], in0=gs[:, 32:48], scalar1=EPS, scalar2=None, op0=ALU.add), reads=["gs"], writes=["gs"])
                    S.op("act", lambda e: e.activation(out=gs[:, 32:48], in_=gs[:, 32:48], func=AF.Ln), reads=["gs"], writes=["gs"])
                    S.op("act", lambda e: e.activation(out=gs[:, 32:48], in_=gs[:, 32:48], func=AF.Exp, scale=-0.5), reads=["gs"], writes=["gs"])
                    S.op("dve", lambda e: e.tensor_scalar(out=gs[:, 32:40], in0=gs[:, 32:40], scalar1=GDK ** -0.5, scalar2=None, op0=ALU.mult), reads=["gs"], writes=["gs"])
                    S.op("dve", lambda e: e.tensor_tensor(out=QKn[:, :].rearrange("p (h d) -> p h d", d=128),
                                                          in0=QKVs[:, 0:2048].rearrange("p (h d) -> p h d", d=128),
                                                          in1=gs[:, 32:48].unsqueeze(2).to_broadcast([NS, 16, 128]), op=ALU.mult),
                         reads=["QKVs", "gs"], writes=["QKn"])
                    to_fm(QKn[:, 0:1024], "QKn", 8, qTg, "qTg")
                    to_fm(QKn[:, 1024:2048], "QKn", 8, kTg, "kTg")
                    ba = PJ[0:NS, 5120:5136]
                    S.op("act", lambda e: e.activation(out=gs[:, 0:8], in_=ba[:, 0:8], func=AF.Sigmoid), reads=["PJ", "gs"], writes=["gs"])
                    S.op("dve", lambda e: e.tensor_tensor(out=gs[:, 8:16], in0=ba[:, 8:16], in1=dtb_bc[0:NS, :], op=ALU.add), reads=["PJ", "dtb_bc", "gs"], writes=["gs"])
                    S.op("act", lambda e: e.activation(out=gs[:, 8:16], in_=gs[:, 8:16], func=AF.Exp), reads=["gs"], writes=["gs"])
                    S.op("dve", lambda e: e.tensor_scalar(out=gs[:, 8:16], in0=gs[:, 8:16], scalar1=1.0, scalar2=None, op0=ALU.add), reads=["gs"], writes=["gs"])
                    S.op("act", lambda e: e.activation(out=gs[:, 8:16], in_=gs[:, 8:16], func=AF.Ln), reads=["gs"], writes=["gs"])
                    S.op("dve", lambda e: e.tensor_tensor(out=gs[:, 8:16], in0=gs[:, 8:16], in1=nega[0:NS, :], op=ALU.mult), reads=["gs", "nega"], writes=["gs"])
                    S.op("act", lambda e: e.activation(out=gs[:, 16:24], in_=gs[:, 8:16], func=AF.Exp), reads=["gs"], writes=["gs"])
                    S.op("dve", lambda e: e.scalar_tensor_tensor(out=gs[:, 24:32], in0=gs[:, 16:24], scalar=-1.0, in1=gs[:, 0:8], op0=ALU.mult, op1=ALU.mult),
                         reads=["gs"], writes=["gs"])
                    S.op("dve", lambda e: e.tensor_tensor(out=BV[:, :].rearrange("p (h d) -> p h d", d=128),
                                                          in0=QKVs[:, 2048:3072].rearrange("p (h d) -> p h d", d=128),
                                                          in1=gs[:, 0:8].unsqueeze(2).to_broadcast([NS, 8, 128]), op=ALU.mult),
                         reads=["QKVs", "gs"], writes=["BV"])
                    S.op("dve", lambda e: e.tensor_tensor(out=EGd[:, :].rearrange("p (b h) -> p b h", h=8),
                                                          in0=ident_f[0:NS, 0:NS].unsqueeze(2).to_broadcast([NS, NS, 8]),
                                                          in1=gs[:, 16:24].unsqueeze(1).to_broadcast([NS, NS, 8]), op=ALU.mult),
                         reads=["ident_f", "gs"], writes=["EGd"])
                    bq = S.bank()
                    S.op("pe", lambda e: e.matmul(pbank(bq, 128), lhsT=ones_f[0:NS, :], rhs=EGd[:, :], start=True, stop=True),
                         reads=["ones_f", "EGd"], writes=[f"pb{bq}"])
                    S.op("act", lambda e: e.activation(out=EGB[:, :], in_=pbank(bq, 128), func=AF.Copy), reads=[f"pb{bq}"], writes=["EGB"])
                    S.reserved = {RES}

                    def loads(b):
                        p = b % 2
                        S.op("sp", lambda e: e.dma_start(out=SRin[p][:, :].rearrange("d (h v) -> d h v", v=RDV), in_=sret[b].rearrange("h d v -> d h v")),
                             writes=[f"SRin{p}"], dma=f"SRin{p}")
                        S.op("sp", lambda e: e.dma_start(out=SGin[p][:, :].rearrange("d (h v) -> d h v", v=GDV), in_=sgdn[b].rearrange("h d v -> d h v")),
                             writes=[f"SGin{p}"], dma=f"SGin{p}")
                    loads(0)
                    for b in range(NS):
                        p = b % 2
                        if b + 1 < NS:
                            loads(b + 1)
                        eb = ident_f[0:NS, b:b + 1]
                        S.op("dve", lambda e, eb=eb: e.tensor_scalar(out=KMr[:, :], in0=KR[:, :], scalar1=eb, scalar2=None, op0=ALU.mult), reads=["KR", "ident_f"], writes=["KMr"])
                        S.op("dve", lambda e, eb=eb: e.tensor_scalar(out=KMg[:, :], in0=QKn[:, 1024:2048], scalar1=eb, scalar2=None, op0=ALU.mult),
                             reads=["QKn", "ident_f"], writes=["KMg"])
                        def ret_chain(b=b, p=p):
                            yield
                            while S.bank_i % 2 != 0:
                                S.bank()
                            b0 = S.bank(); b1 = S.bank()
                            if b1 != b0 + 1:
                                while S.bank_i % 2 != 0:
                                    S.bank()
                                b0 = S.bank(); b1 = S.bank()

                            def mm_a(e, b0=b0):
                                ins = None
                                for h in range(RH):
                                    ins = e.matmul(PS[:, b0 * 512 + h * 256: b0 * 512 + (h + 1) * 256], lhsT=KMr[0:NS, h * 128:(h + 1) * 128],
                                                   rhs=PJ[0:NS, 1024 + h * 256: 1024 + (h + 1) * 256], start=True, stop=True)
                                return ins
                            S.op("pe", mm_a, reads=["KMr", "PJ"], writes=[f"pb{b0}", f"pb{b1}"])
                            for h in range(RH):
                                S.op("dve", lambda e, h=h, b0=b0, p=p: e.scalar_tensor_tensor(
                                    out=SRo[p][:, h * 256:(h + 1) * 256], in0=SRin[p][:, h * 256:(h + 1) * 256], scalar=float(_GAM[h]),
                                    in1=PS[:, b0 * 512 + h * 256: b0 * 512 + (h + 1) * 256], op0=ALU.mult, op1=ALU.add),
                                    reads=[f"SRin{p}", f"pb{b0}", f"pb{b1}"], writes=[f"SRo{p}"])

                            yield
                            S.op("act", lambda e, p=p: e.activation(out=SRob[:, :], in_=SRo[p][:, :], func=AF.Copy), reads=[f"SRo{p}"], writes=["SRob"])

                            def mm_b(e, b=b, p=p):
                                ins = None
                                for h in range(RH):
                                    for c in range(2):
                                        ins = e.matmul(pbank(RES, 1, (h * 2 + c) * NS + b), lhsT=SRob[:, h * 256 + c * 128: h * 256 + (c + 1) * 128],
                                                       rhs=qTr[:, h * NS + b: h * NS + b + 1], start=True, stop=True)
                                return ins
                            S.op("pe", mm_b, reads=["SRob", "qTr"], writes=[f"pb{RES}"])
                            S.op("act", lambda e, b=b, p=p: e.dma_start(out=srs[b].rearrange("h d v -> d h v"), in_=SRo[p][:, :].rearrange("d (h v) -> d h v", v=RDV)),
                                 reads=[f"SRo{p}"], dma=f"SRo{p}o")
                            yield
                        def gdn_chain(b=b, p=p):
                            yield
                            while S.bank_i % 2 != 0:
                                S.bank()
                            c0_ = S.bank(); c1_ = S.bank()
                            if c1_ != c0_ + 1:
                                while S.bank_i % 2 != 0:
                                    S.bank()
                                c0_ = S.bank(); c1_ = S.bank()

                            def mm_1(e, c0_=c0_, p=p):
                                ins = None
                                for h in range(GH):
                                    ins = e.matmul(PS[0:NS, c0_ * 512 + h * 128: c0_ * 512 + (h + 1) * 128], lhsT=kTg[:, h * NS:(h + 1) * NS],
                                                   rhs=SGin[p][:, h * 128:(h + 1) * 128], start=True, stop=True)
                                return ins
                            S.op("pe", mm_1, reads=["kTg", f"SGin{p}"], writes=[f"pb{c0_}", f"pb{c1_}"])
                            S.op("dve", lambda e, c0_=c0_: e.tensor_tensor(out=Rr[:, :].rearrange("p (h d) -> p h d", d=128),
                                                                           in0=PS[0:NS, c0_ * 512: c0_ * 512 + 1024].rearrange("p (h d) -> p h d", d=128),
                                                                           in1=gs[:, 24:32].unsqueeze(2).to_broadcast([NS, 8, 128]), op=ALU.mult),
                                 reads=[f"pb{c0_}", f"pb{c1_}", "gs"], writes=["Rr"])
                            S.op("dve", lambda e: e.tensor_tensor(out=Rr[:, :], in0=Rr[:, :], in1=BV[:, :], op=ALU.add), reads=["Rr", "BV"], writes=["Rr"])
                            yield
                            while S.bank_i % 2 != 0:
                                S.bank()
                            d0_ = S.bank(); d1_ = S.bank()
                            if d1_ != d0_ + 1:
                                while S.bank_i % 2 != 0:
                                    S.bank()
                                d0_ = S.bank(); d1_ = S.bank()

                            def mm_2(e, d0_=d0_):
                                ins = None
                                for h in range(GH):
                                    ins = e.matmul(PS[:, d0_ * 512 + h * 128: d0_ * 512 + (h + 1) * 128], lhsT=KMg[0:NS, h * 128:(h + 1) * 128],
                                                   rhs=Rr[0:NS, h * 128:(h + 1) * 128], start=True, stop=True)
                                return ins
                            S.op("pe", mm_2, reads=["KMg", "Rr"], writes=[f"pb{d0_}", f"pb{d1_}"])
                            S.op("dve", lambda e, b=b, p=p: e.tensor_tensor(out=SGo[p][:, :].rearrange("p (h d) -> p h d", d=128),
                                                                          in0=SGin[p][:, :].rearrange("p (h d) -> p h d", d=128),
                                                                          in1=EGB[:, b * 8:(b + 1) * 8].unsqueeze(2).to_broadcast([128, 8, 128]), op=ALU.mult),
                                 reads=[f"SGin{p}", "EGB"], writes=[f"SGo{p}"])
                            S.op("dve", lambda e, d0_=d0_, p=p: e.tensor_tensor(out=SGo[p][:, :], in0=SGo[p][:, :], in1=PS[:, d0_ * 512: d0_ * 512 + 1024], op=ALU.add),
                                 reads=[f"SGo{p}", f"pb{d0_}", f"pb{d1_}"], writes=[f"SGo{p}"])

                            yield
                            S.op("act", lambda e, p=p: e.activation(out=SGob[:, :], in_=SGo[p][:, :], func=AF.Copy), reads=[f"SGo{p}"], writes=["SGob"])

                            def mm_3(e, b=b, p=p):
                                ins = None
                                for h in range(GH):
                                    ins = e.matmul(pbank(RES, 1, 128 + h * NS + b), lhsT=SGob[:, h * 128:(h + 1) * 128],
                                                   rhs=qTg[:, h * NS + b: h * NS + b + 1], start=True, stop=True)
                                return ins
                            S.op("pe", mm_3, reads=["SGob", "qTg"], writes=[f"pb{RES}"])
                            S.op("act", lambda e, b=b, p=p: e.dma_start(out=sgs[b].rearrange("h d v -> d h v"), in_=SGo[p][:, :].rearrange("d (h v) -> d h v", v=GDV)),
                                 reads=[f"SGo{p}"], dma=f"SGo{p}o")
                            yield
                        _gens = [ret_chain(), gdn_chain()]
                        while _gens:
                            for _g in list(_gens):
                                try:
                                    next(_g)
                                except StopIteration:
                                    _gens.remove(_g)
                    S.op("act", lambda e: e.activation(out=OR[:, 0:128], in_=pbank(RES, 128), func=AF.Copy), reads=[f"pb{RES}"], writes=["XOR1"])
                    S.op("act", lambda e: e.activation(out=OG[:, 0:128], in_=pbank(RES, 128, 128), func=AF.Copy), reads=[f"pb{RES}"], writes=["OG"])
                    S.reserved = set()

                def sample_xattn(XQ, OX):
                    S.reserved = {RES}

                    def loads(b):
                        p = b % 2
                        S.op("sp", lambda e: e.dma_start(out=Kc[p][:, :].rearrange("m (t d) -> m t d", d=D), in_=cmk[b].rearrange("(t m) d -> m t d", m=128)),
                             writes=[f"Kc{p}"] + ARENA_RES, dma=f"Kc{p}")
                        S.op("sp", lambda e: e.dma_start(out=Vc[p][:, :].rearrange("m (t d) -> m t d", d=D), in_=cmv[b].rearrange("(t m) d -> m t d", m=128)),
                             writes=[f"Vc{p}"] + ARENA_RES, dma=f"Vc{p}")

                    qinfo = {}

                    def qstage(b):
                        p = b % 2
                        S.op("pool", lambda e: e.tensor_tensor(out=QD[p][:, :].rearrange("p (c d) -> p c d", d=128),
                                                               in0=ident_f[:, :].unsqueeze(1).to_broadcast([128, 8, 128]),
                                                               in1=XQ[:, :].rearrange("p (c t) -> p c t", t=NS)[:, :, b:b + 1].to_broadcast([128, 8, 128]), op=ALU.mult),
                             reads=["ident_f", "ORg"], writes=["QD0"])
                        while S.bank_i % 2 != 0:
                            S.bank()
                        b0 = S.bank(); b1 = S.bank()
                        if b1 != b0 + 1:
                            while S.bank_i % 2 != 0:
                                S.bank()
                            b0 = S.bank(); b1 = S.bank()

                        def mm_q(e):
                            e.matmul(pbank(b0), lhsT=ones_f[:], rhs=QD[p][:, 0:512], start=True, stop=True)
                            return e.matmul(pbank(b1), lhsT=ones_f[:], rhs=QD[p][:, 512:1024], start=True, stop=True)
                        S.op("pe", mm_q, reads=["ones_f", "QD0"], writes=[f"pb{b0}", f"pb{b1}"])
                        qinfo[b] = (b0, b1)
                    loads(0)
                    qstage(0)
                    for b in range(NS):
                        p = b % 2
                        if b + 1 < NS:
                            loads(b + 1)
                            qstage(b + 1)
                        b0, b1 = qinfo[b]
                        S.op("act", lambda e, p=p: e.activation(out=Vcb[p][:, :], in_=Vc[p][:, :], func=AF.Copy), reads=[f"Vc{p}"], writes=["Vcb0"])
                        for mt in range(2):
                            S.op("dve", lambda e, mt=mt, p=p, b0=b0: e.tensor_tensor(out=PR[:, :], in0=PS[:, b0 * 512: b0 * 512 + 1024],
                                                                                   in1=Kc[p][:, mt * D:(mt + 1) * D], op=ALU.mult),
                                 reads=[f"pb{b0}", f"pb{b1}", f"Kc{p}"], writes=["PR"])
                            S.op("dve", lambda e, mt=mt, b=b: e.tensor_reduce(out=SCs[:, b * 8 + mt * 4: b * 8 + mt * 4 + 4],
                                                                            in_=PR[:, :].rearrange("p (h d) -> p h d", d=XHD), axis=AXX, op=ALU.add),
                                 reads=["PR"], writes=["SCs"])
                        S.op("act", lambda e, b=b: e.activation(out=Pm[:, b * 8:(b + 1) * 8], in_=SCs[:, b * 8:(b + 1) * 8], func=AF.Exp, scale=XHD ** -0.5),
                             reads=["SCs"], writes=["Pm"])

                        def mm_o(e, b=b, p=p):
                            ins = None
                            for ch in range(8):
                                h = ch // 2
                                for mt in range(2):
                                    ins = e.matmul(pbank(RES, 1, ch * NS + b), lhsT=Vcb[p][:, mt * D + ch * 128: mt * D + (ch + 1) * 128],
                                                   rhs=Pm[:, b * 8 + mt * 4 + h: b * 8 + mt * 4 + h + 1], start=(mt == 0), stop=(mt == 1))
                            for mt in range(2):
                                ins = e.matmul(pbank(RES, 4, 128 + b * 4), lhsT=ones_b[:], rhs=Pm[:, b * 8 + mt * 4: b * 8 + mt * 4 + 4],
                                               start=(mt == 0), stop=(mt == 1))
                            return ins
                        S.op("pe", mm_o, reads=["Vcb0", "Pm", "ones_b"], writes=[f"pb{RES}"])
                    S.op("dve", lambda e: e.reciprocal(out=RD[:, :], in_=pbank(RES, 64, 128)), reads=[f"pb{RES}"], writes=["RD"])
                    for ch in range(8):
                        h = ch // 2
                        S.op("dve", lambda e, ch=ch, h=h: e.tensor_tensor(out=OX[:, ch * NS:(ch + 1) * NS], in0=pbank(RES, NS, ch * NS),
                                                                        in1=RD[:, :].rearrange("p (b h) -> p b h", h=4)[:, :, h], op=ALU.mult),
                             reads=[f"pb{RES}", "RD"], writes=["OGg"])
                    S.reserved = set()

            def load_x(blk):
                Xn = XOR[blk % 2]; xr = f"XOR{blk % 2}"
                t0 = blk * TBv
                if is_sample:
                    S.op("sp", lambda e: e.dma_start(out=Xn[0:NS, 0:D], in_=xs), writes=[xr], dma=xr)
                else:
                    S.op("sp", lambda e: e.dma_start(out=Xn[:, :].rearrange("p (a d) -> p a d", d=D),
                                                     in_=xp[t0:t0 + TBv, :].rearrange("(a p) d -> p a d", p=128)),
                         writes=[xr], dma=xr)

            def pre_norm(blk):
                Xn = XOR[blk % 2]; xr = f"XOR{blk % 2}"
                for tau in range(NTv):
                    norm_transpose(Xn[0:tw, tau * D:(tau + 1) * D], xr, gmix_c, "gmix_c", hnT, "hnT", tw, tau * tw, TBv)

            def load_tabs(blk):
                t0 = blk * TBv
                S.op("sp", lambda e: e.dma_start(out=cosb[:], in_=cst["c_cos"][:, t0:t0 + TBv]), writes=["cosb"], dma="cosb")
                S.op("sp", lambda e: e.dma_start(out=sinb[:], in_=cst["c_sin"][:, t0:t0 + TBv]), writes=["sinb"], dma="sinb")

            def do_block(blk):
                if stage < 1:
                    return
                Xb = XOR[blk % 2]; xres = f"XOR{blk % 2}"
                OR = XOR[(blk + 1) % 2][:, 0:8 * TBv]; or_res = f"XOR{(blk + 1) % 2}"
                t0 = blk * TBv
                if blk == 0:
                    load_x(0)
                if not is_sample:
                    load_tabs(blk)
                if blk == 0:
                    pre_norm(0)

                pend = []
                qkv_done = [False]
                LATE = ("ga", "gb")
                main_tiles = W_IN_TILES if is_sample else [t_ for t_ in W_IN_TILES if t_[0] not in LATE]
                late_tiles = [] if is_sample else [t_ for t_ in W_IN_TILES if t_[0] in LATE]
                for (kind, c0, ncols) in main_tiles:
                    if kinds is not None and kind not in kinds:
                        continue
                    if kind == "z" and not is_sample and pend is not None and (pend or not qkv_done[0]):
                        for p_ in pend:
                            p_()
                        del pend[:]
                        qkv_done[0] = True
                        for (ssb, sres, dstb, dres) in ((OR, or_res, GQ, "GQ"), (OG, "OG", GK, "GK")):
                            S.op("act", lambda e, ssb=ssb: e.activation(out=ssb[:, :], in_=ssb[:, :], func=AF.Ln), reads=[sres], writes=[sres])
                            S.op("act", lambda e, ssb=ssb: e.activation(out=ssb[:, :], in_=ssb[:, :], func=AF.Exp, scale=-0.5), reads=[sres], writes=[sres])
                            S.op("dve", lambda e, ssb=ssb, dstb=dstb: e.tensor_tensor(out=dstb[:, :], in0=dstb[:, :], in1=ssb[:, :], op=ALU.mult),
                                 reads=[sres, dres], writes=[dres, "FA"])
                    if is_sample and kind in PJ_OFF:
                        pj0 = PJ_OFF[kind][1] + (c0 - PJ_OFF[kind][0])

                        def cons(tau, b, pj0=pj0, ncols=ncols):
                            S.op("act", lambda e: e.activation(out=PJ[0:NS, pj0:pj0 + ncols], in_=pbank(b, ncols)[0:NS, :], func=AF.Copy),
                                 reads=[f"pb{b}"], writes=["PJ"])
                        proj_tm(w_in, 0, 8, c0, ncols, hnT, "hnT", TBv, 1, cons, tw=tw)
                        continue
                    if kind in ("rq", "rk"):
                        dstb = RQ if kind == "rq" else RK

                        def cons(j0, b, dstb=dstb, kind=kind, cb=(c0 - (0 if kind == "rq" else 512)) // 128):
                            j = cb + j0
                            tmpA, rA = nxt("tmpA"); tmpB, rB = nxt("tmpB"); tmpb, rb = nxt("tmpb")
                            S.op("act", lambda e: e.activation(out=tmpb[:], in_=pbank(b, TBv), func=AF.Copy), reads=[f"pb{b}"], writes=[rb])
                            b2 = S.bank()
                            S.op("pe", lambda e: e.matmul(pbank(b2, TBv), lhsT=perm_b[:], rhs=tmpb[:], start=True, stop=True),
                                 reads=[rb, "perm_b"], writes=[f"pb{b2}"])
                            S.op("dve", lambda e: e.tensor_tensor(out=tmpA[:], in0=pbank(b, TBv), in1=cosb[:], op=ALU.mult),
                                 reads=[f"pb{b}", "cosb"], writes=[rA])
                            S.op("dve", lambda e: e.tensor_tensor(out=tmpB[:], in0=pbank(b2, TBv), in1=sinb[:], op=ALU.mult),
                                 reads=[f"pb{b2}", "sinb"], writes=[rB])
                            S.op("dve", lambda e: e.tensor_tensor(out=dstb[:, j * TBv:(j + 1) * TBv], in0=tmpA[:], in1=tmpB[:], op=ALU.add),
                                 reads=[rA, rB], writes=[kind.upper()])
                            if kind == "rq":
                                S.op("dve", lambda e: e.tensor_tensor(
                                    out=RQd[:, j * TBv:(j + 1) * TBv].rearrange("p (a i) -> p a i", i=128),
                                    in0=RQ[:, j * TBv:(j + 1) * TBv].rearrange("p (a i) -> p a i", i=128),
                                    in1=rqd[:, j * 128:(j + 1) * 128].unsqueeze(1).to_broadcast([128, NTv, 128]), op=ALU.mult),
                                    reads=["RQ", "rqd"], writes=["RQd"])
                        proj_fm(w_in, 0, 8, c0, ncols, hnT, "hnT", TBv, TBv, cons)
                    elif kind == "rv":
                        hv = c0 - 1024

                        def cons(tau, b, hv=hv, ncols=ncols):
                            S.op("act", lambda e: e.activation(out=VtR[:, tau * 1024 + hv: tau * 1024 + hv + ncols], in_=pbank(b, ncols), func=AF.Copy),
                                 reads=[f"pb{b}"], writes=["VtR"])
                        proj_tm(w_in, 0, 8, c0, ncols, hnT, "hnT", TBv, NTv, cons)
                    elif kind in ("rg", "z", "ga", "gb"):
                        base = {"rg": 2048, "z": 6144, "ga": 7184, "gb": 8208}[kind]
                        dstb = {"rg": RG, "z": Z, "ga": GA, "gb": GB}[kind]
                        fn = AF.Silu if kind in ("rg", "z") else AF.Tanh
                        fsc = 1.0 if kind in ("rg", "z") else 0.5
                        cb = (c0 - base) // 128

                        def cons(j, b, dstb=dstb, fn=fn, cb=cb, kind=kind, fsc=fsc):
                            S.op("act", lambda e: e.activation(out=dstb[:, (cb + j) * TBv:(cb + j + 1) * TBv], in_=pbank(b, TBv), func=fn, scale=fsc),
                                 reads=[f"pb{b}"], writes=[kind.upper()])
                        proj_fm(w_in, 0, 8, c0, ncols, hnT, "hnT", TBv, TBv, cons)
                    elif kind == "ba":
                        def cons(tau, b):
                            S.op("dve", lambda e: e.tensor_copy(out=BA[:, tau * 16:(tau + 1) * 16], in_=pbank(b, 16)), reads=[f"pb{b}"], writes=["BA"])
                        proj_tm(w_in, 0, 8, c0, ncols, hnT, "hnT", TBv, NTv, cons)
                    else:
                        cb = (c0 - 3072) // 128

                        def cons(j, b, cb=cb, blk=blk):
                            cc = cb + j
                            tmpb, rb = nxt("tmpb")
                            CACC, rCA = nxt("CACC"); CIN, rCI = nxt("CIN")
                            while len(pend) > 1:
                                pend.pop(0)()
                            S.op("act", lambda e: e.activation(out=CIN[:, 0:3], in_=HALO[:, cc * 3:cc * 3 + 3], func=AF.Copy), reads=["HALO"], writes=[rCI])
                            S.op("act", lambda e: e.activation(out=CIN[:, 3:3 + TBv], in_=pbank(b, TBv), func=AF.Copy), reads=[f"pb{b}", rCI], writes=[rCI])
                            S.op("act", lambda e: e.activation(out=HALO[:, cc * 3:cc * 3 + 3], in_=CIN[:, TBv:TBv + 3], func=AF.Copy), reads=[rCI], writes=["HALO"])
                            S.op("dve", lambda e: e.tensor_scalar(out=CACC[:], in0=CIN[:, 0:TBv], scalar1=cw[:, cc:cc + 1], scalar2=None, op0=ALU.mult),
                                 reads=[rCI, "cw"], writes=[rCA])
                            for i in range(1, 4):
                                S.op("dve", lambda e, i=i: e.scalar_tensor_tensor(out=CACC[:], in0=CIN[:, i:i + TBv], scalar=cw[:, i * 24 + cc:i * 24 + cc + 1],
                                                                                 in1=CACC[:], op0=ALU.mult, op1=ALU.add),
                                     reads=[rCI, "cw", rCA], writes=[rCA])

                            def stage_b():
                                if cc >= 16:
                                    S.op("act", lambda e: e.activation(out=GV[:, (cc - 16) * TBv:(cc - 15) * TBv], in_=CACC[:], func=AF.Silu), reads=[rCA], writes=["GV", "FA"])
                                    return
                                dstb, dres, hh = (GQ, "GQ", cc) if cc < 8 else (GK, "GK", cc - 8)
                                ssb, sres = (OR, or_res) if cc < 8 else (OG, "OG")
                                dsl = dstb[:, hh * TBv:(hh + 1) * TBv]
                                S.op("act", lambda e: e.activation(out=dsl, in_=CACC[:], func=AF.Silu), reads=[rCA], writes=[dres, "FA"])
                                S.op("act", lambda e: e.activation(out=tmpb[:], in_=dsl, func=AF.Square), reads=[dres], writes=[rb])
                                b2 = S.bank()
                                lh = ones128_b if cc < 8 else ones_b
                                S.op("pe", lambda e: e.matmul(pbank(b2, TBv), lhsT=lh[:], rhs=tmpb[:], start=True, stop=True),
                                     reads=[rb, "ones_b", "ones128_b"], writes=[f"pb{b2}"])
                                eps = EPS * 128 if cc < 8 else EPS
                                S.op("dve", lambda e: e.tensor_scalar(out=ssb[:, hh * TBv:(hh + 1) * TBv], in0=pbank(b2, TBv), scalar1=eps, scalar2=None, op0=ALU.add),
                                     reads=[f"pb{b2}"], writes=[sres])
                            pend.append(stage_b)
                        proj_fm(w_in, 0, 8, c0, ncols, hnT, "hnT", TBv, TBv, cons)
                if blk == nblk_v - 1 and not is_sample:
                    for i in range(3):
                        S.op("sp", lambda e, i=i: e.dma_start(out=scp[i].rearrange("(c p) -> p c", p=128),
                                                               in_=HALO[:, :].rearrange("p (c i) -> p c i", i=3)[:, :, i],
                                                               allow_slow_non_contiguous=True),
                             reads=["HALO"], dma="scp")

                def gdn_small(tau):
                    gsm = gsmT[tau]; gres = f"gsm{tau}"
                    ba = BA[:, tau * 16:(tau + 1) * 16]
                    S.op("act", lambda e, ba=ba: e.activation(out=gsm[:, 0:8], in_=ba[:, 0:8], func=AF.Tanh, scale=0.5), reads=["BA"], writes=[gres])
                    S.op("dve", lambda e: e.tensor_scalar(out=gsm[:, 0:8], in0=gsm[:, 0:8], scalar1=0.5, scalar2=0.5, op0=ALU.mult, op1=ALU.add),
                         reads=[gres], writes=[gres])
                    S.op("dve", lambda e, ba=ba: e.tensor_tensor(out=gsm[:, 8:16], in0=ba[:, 8:16], in1=dtb_bc[:], op=ALU.add), reads=["BA", "dtb_bc", gres], writes=[gres])
                    S.op("act", lambda e: e.activation(out=gsm[:, 8:16], in_=gsm[:, 8:16], func=AF.Exp), reads=[gres], writes=[gres])
                    S.op("dve", lambda e: e.tensor_scalar(out=gsm[:, 8:16], in0=gsm[:, 8:16], scalar1=1.0, scalar2=None, op0=ALU.add), reads=[gres], writes=[gres])
                    S.op("act", lambda e: e.activation(out=gsm[:, 8:16], in_=gsm[:, 8:16], func=AF.Ln), reads=[gres], writes=[gres])
                    S.op("dve", lambda e: e.tensor_tensor(out=gsm[:, 8:16], in0=gsm[:, 8:16], in1=nega[:], op=ALU.mult), reads=[gres, "nega"], writes=[gres])
                    bq = S.bank()
                    S.op("pe", lambda e, bq=bq: e.matmul(pbank(bq, 8), lhsT=tri_f[:], rhs=gsm[:, 8:16], start=True, stop=True),
                         reads=["tri_f", gres], writes=[f"pb{bq}"])
                    S.op("dve", lambda e, bq=bq: e.tensor_copy(out=gsm[:, 16:24], in_=pbank(bq, 8)), reads=[f"pb{bq}", gres], writes=[gres])
                    S.op("act", lambda e: e.activation(out=gsm[:, 24:32], in_=gsm[:, 16:24], func=AF.Exp), reads=[gres], writes=[gres])
                    S.op("dve", lambda e: e.tensor_scalar(out=gsm[:, 24:32], in0=gsm[:, 24:32], scalar1=-1.0, scalar2=None, op0=ALU.mult), reads=[gres], writes=[gres])
                for tau in (range(NTv) if not is_sample else []):
                    gdn_small(tau)

                if stage < 2:
                    return
                if is_sample:
                    sample_mixers()
                def ret_gen():
                    for tau in (range(NTv) if not is_sample else []):
                        tc0 = tau * 128
                        yield
                        b = S.bank()

                        def mm_sc(e, b=b, tc0=tc0):
                            ins = None
                            for h in range(RH):
                                ins = e.matmul(pbank(b, 128, h * 128), lhsT=RK[:, h * TBv + tc0: h * TBv + tc0 + 128],
                                               rhs=RQ[:, h * TBv + tc0: h * TBv + tc0 + 128], start=True, stop=True)
                            return ins
                        S.op("pe", mm_sc, reads=["RK", "RQ"], writes=[f"pb{b}"])
                        S.op("dve", lambda e, b=b: e.tensor_tensor(out=PTr[:], in0=pbank(b), in1=rdt[:], op=ALU.mult),
                             reads=[f"pb{b}", "rdt"], writes=["PTr"])
                        yield
                        b3 = S.bank()

                        def tr_k(e, b3=b3, tc0=tc0):
                            ins = None
                            for h in range(RH):
                                ins = e.transpose(pbank_bf(b3, 128, h * 128), RK[:, h * TBv + tc0: h * TBv + tc0 + 128], ident_b[:])
                            return ins
                        S.op("pe", tr_k, reads=["RK", "ident_b"], writes=[f"pb{b3}"])
                        S.op("dve", lambda e, b3=b3: e.tensor_tensor(out=KtR[:, :].rearrange("p (h d) -> p h d", d=128),
                                                                   in0=pbank_bf(b3, 512).rearrange("p (h d) -> p h d", d=128),
                                                                   in1=rkd[:, :].unsqueeze(2).to_broadcast([128, RH, 128]), op=ALU.mult),
                             reads=[f"pb{b3}", "rkd"], writes=["KtR"])
                        for hp in range(2):
                            yield
                            b2 = S.bank()

                            def mm_o(e, b2=b2, hp=hp, tau=tau, tc0=tc0):
                                ins = None
                                for hh in range(2):
                                    h = hp * 2 + hh
                                    for c in range(2):
                                        o = pbank(b2, 128, (hh * 2 + c) * 128)
                                        e.matmul(o, lhsT=SRb[:, h * RDV + c * 128: h * RDV + (c + 1) * 128],
                                                 rhs=RQd[:, h * TBv + tc0: h * TBv + tc0 + 128], start=True, stop=False)
                                        ins = e.matmul(o, lhsT=VtR[:, tau * 1024 + h * RDV + c * 128: tau * 1024 + h * RDV + (c + 1) * 128],
                                                       rhs=PTr[:, h * 128:(h + 1) * 128], start=False, stop=True)
                                return ins
                            S.op("pe", mm_o, reads=["SRb", "RQd", "VtR", "PTr"], writes=[f"pb{b2}"])
                            dst = OR[:, hp * 4 * TBv:(hp + 1) * 4 * TBv].rearrange("p (c t) -> p c t", t=TBv)[:, :, tc0:tc0 + 128]
                            S.op("act", lambda e, b2=b2, dst=dst: e.activation(out=dst, in_=pbank(b2).rearrange("p (c t) -> p c t", t=128), func=AF.Copy),
                                 reads=[f"pb{b2}"], writes=[or_res])
                        for hp in range(2):
                            yield
                            b4 = S.bank()

                            def mm_s(e, b4=b4, hp=hp, tau=tau):
                                ins = None
                                for hh in range(2):
                                    h = hp * 2 + hh
                                    ins = e.matmul(pbank(b4, 256, hh * 256), lhsT=KtR[:, h * 128:(h + 1) * 128],
                                                   rhs=VtR[:, tau * 1024 + h * RDV: tau * 1024 + (h + 1) * RDV], start=True, stop=True)
                                return ins
                            S.op("pe", mm_s, reads=["KtR", "VtR"], writes=[f"pb{b4}"])
                            for hh in range(2):
                                h = hp * 2 + hh
                                S.op("dve", lambda e, b4=b4, hh=hh, h=h: e.scalar_tensor_tensor(
                                    out=SR[:, h * RDV:(h + 1) * RDV], in0=SR[:, h * RDV:(h + 1) * RDV], scalar=float(_GAM[h] ** 128),
                                    in1=pbank(b4, 256, hh * 256), op0=ALU.mult, op1=ALU.add),
                                    reads=[f"pb{b4}", "SR"], writes=["SR"])
                            S.op("act", lambda e, hp=hp: e.activation(out=SRb[:, hp * 512:(hp + 1) * 512], in_=SR[:, hp * 512:(hp + 1) * 512], func=AF.Copy),
                                 reads=["SR"], writes=["SRb"])
                    yield
                _retg = ret_gen()

                if stage < 3:
                    return
                def gdn_tile(tau):
                    gsm = gsmT[tau]; gres = f"gsm{tau}"
                    tc0 = tau * 128
                    QTd = QTdT[tau % 2]; qres = f"QTd{tau % 2}"
                    def setup():
                        yield
                        S.op("dve", lambda e: e.tensor_tensor(out=G2[:, :].rearrange("p (h i) -> p h i", i=128),
                                                              in0=tri_f[:, :].unsqueeze(1).to_broadcast([128, 8, 128]),
                                                              in1=gsm[:, 8:16].unsqueeze(2).to_broadcast([128, 8, 128]), op=ALU.mult),
                             reads=["tri_f", gres], writes=["GBm"])
                        yield
                        while S.bank_i % 2 != 0:
                            S.bank()
                        bg = S.bank(); bg2 = S.bank()

                        def mm_g(e, bg=bg, bg2=bg2):
                            e.matmul(pbank(bg), lhsT=ones_f[:], rhs=G2[:, 0:512], start=True, stop=True)
                            return e.matmul(pbank(bg2), lhsT=ones_f[:], rhs=G2[:, 512:1024], start=True, stop=True)
                        S.op("pe", mm_g, reads=["ones_f", "GBm"], writes=[f"pb{bg}", f"pb{bg2}"])
                        gbc = PS[:, bg * 512: bg * 512 + 1024]
                        S.op("dve", lambda e, gbc=gbc: e.tensor_tensor(out=D1[:, :].rearrange("p (h i) -> p h i", i=128),
                                                                       in0=gbc.rearrange("p (h i) -> p h i", i=128),
                                                                       in1=gsm[:, 16:24].unsqueeze(2).to_broadcast([128, 8, 128]), op=ALU.subtract),
                             reads=[f"pb{bg}", f"pb{bg2}", gres], writes=["D1"])
                        S.op("act", lambda e, gbc=gbc: e.activation(out=EGC[:], in_=gbc, func=AF.Exp), reads=[f"pb{bg}", f"pb{bg2}"], writes=["GBm"])
                        yield
                        S.op("dve", lambda e, tc0=tc0: e.tensor_tensor(out=QTd[:, :].rearrange("p (h i) -> p h i", i=128),
                                                                       in0=GQ[:, :].rearrange("p (h t) -> p h t", t=TBv)[:, :, tc0:tc0 + 128],
                                                                       in1=EGC[:, :].rearrange("p (h i) -> p h i", i=128), op=ALU.mult),
                             reads=["GQ", "GBm"], writes=[qres])
                        yield
                        S.op("act", lambda e: e.activation(out=gsm[:, 40:48], in_=EGC[:, :].rearrange("p (h i) -> p h i", i=128)[:, :, 127], func=AF.Copy),
                             reads=["GBm", gres], writes=[gres])
                        yield
                        S.op("act", lambda e: e.activation(out=gsm[:, 32:40], in_=D1[:, :].rearrange("p (h i) -> p h i", i=128)[:, :, 127], func=AF.Exp),
                             reads=["D1", gres], writes=[gres])
                        yield
                        S.op("dve", lambda e: e.tensor_tensor(out=D1[:, :].rearrange("p (h i) -> p h i", i=128),
                                                              in0=D1[:, :].rearrange("p (h i) -> p h i", i=128),
                                                              in1=mincl[:, :].unsqueeze(1).to_broadcast([128, 8, 128]), op=ALU.add),
                             reads=["D1", "mincl"], writes=["D1"])
                        yield
                        S.op("act", lambda e: e.activation(out=D1[:], in_=D1[:], func=AF.Exp), reads=["D1"], writes=["D1"])
                        yield
                        S.op("dve", lambda e: e.tensor_tensor(out=GBm[:, :].rearrange("p (h i) -> p h i", i=128),
                                                              in0=D1[:, :].rearrange("p (h i) -> p h i", i=128),
                                                              in1=strict[:, :].unsqueeze(1).to_broadcast([128, 8, 128]), op=ALU.mult),
                             reads=["D1", "strict"], writes=["GBm"])
                        yield
                        S.op("dve", lambda e: e.scalar_tensor_tensor(out=GBm[:, :].rearrange("p (h i) -> p h i", i=128),
                                                                     in0=GBm[:, :].rearrange("p (h i) -> p h i", i=128), scalar=-1.0,
                                                                     in1=gsm[:, 0:8].unsqueeze(2).to_broadcast([128, 8, 128]), op0=ALU.mult, op1=ALU.mult),
                             reads=["GBm", gres], writes=["GBm"])
                        yield
                    def chain(hg):
                        N0, N0T, WA, WB, MO, MOT, V1s, V2s = GCTX[hg]
                        rN0, rN0T, rWA, rWB, rV1, rV2 = (f"{n}_{hg}" for n in ("N0", "N0T", "WA", "WB", "V1s", "V2s"))
                        yield
                        bk = S.bank(); bqk = S.bank(); bt = S.bank()

                        def mm_kk(e, hg=hg, bk=bk, bqk=bqk, bt=bt, tc0=tc0):
                            ins = None
                            for hh in range(4):
                                h = hg * 4 + hh
                                ks = GK[:, h * TBv + tc0: h * TBv + tc0 + 128]
                                e.matmul(pbank(bk, 128, hh * 128), lhsT=ks, rhs=ks, start=True, stop=True)
                                e.matmul(pbank(bqk, 128, hh * 128), lhsT=ks, rhs=GQ[:, h * TBv + tc0: h * TBv + tc0 + 128], start=True, stop=True)
                                e.transpose(pbank_bf(bt, 128, hh * 128), ks, ident_b[:])
                                ins = e.transpose(pbank_bf(bt, 128, 512 + hh * 128), GV[:, h * TBv + tc0: h * TBv + tc0 + 128], ident_b[:])
                            return ins
                        S.op("pe", mm_kk, reads=["GK", "GQ", "GV", "ident_b"], writes=[f"pb{bk}", f"pb{bqk}", f"pb{bt}"])
                        sl = slice(hg * 512, (hg + 1) * 512)
                        S.op("dve", lambda e, bk=bk, sl=sl: e.tensor_tensor(out=N0[:, :], in0=pbank(bk), in1=GBm[:, sl], op=ALU.mult),
                             reads=[f"pb{bk}", "GBm"], writes=[rN0])
                        S.op("dve", lambda e, bqk=bqk, sl=sl: e.tensor_tensor(out=PTg[:, sl], in0=pbank(bqk), in1=D1[:, sl], op=ALU.mult),
                             reads=[f"pb{bqk}", "D1"], writes=[f"PTg{hg}"])
                        S.op("dve", lambda e, bt=bt, sl=sl, hg=hg: e.tensor_tensor(out=KtG[:, sl].rearrange("p (h d) -> p h d", d=128),
                                                                                 in0=pbank_bf(bt, 512).rearrange("p (h d) -> p h d", d=128),
                                                                                 in1=gsm[:, 32 + hg * 4:36 + hg * 4].unsqueeze(2).to_broadcast([128, 4, 128]), op=ALU.mult),
                             reads=[f"pb{bt}", gres], writes=[f"KtG{hg}"])
                        S.op("act", lambda e, bt=bt, sl=sl: e.activation(out=VtG[:, sl], in_=pbank_bf(bt, 512, 512), func=AF.Copy),
                             reads=[f"pb{bt}"], writes=[f"VtG{hg}"])
                        yield
                        btr = S.bank()

                        def tr_n(e, btr=btr):
                            ins = None
                            for hh in range(4):
                                ins = e.transpose(pbank_bf(btr, 128, hh * 128), N0[:, hh * 128:(hh + 1) * 128], ident_b[:])
                            return ins
                        S.op("pe", tr_n, reads=[rN0, "ident_b"], writes=[f"pb{btr}"])
                        S.op("act", lambda e, btr=btr: e.activation(out=N0T[:, :], in_=pbank_bf(btr, 512), func=AF.Copy),
                             reads=[f"pb{btr}"], writes=[rN0T])
                        v3 = lambda t: t[:, :].rearrange("p (h i) -> p h i", i=128)
                        wav = WA[:, :].rearrange("p (h a i) -> p h a i", a=2, i=128)
                        wbv = WB[:, :].rearrange("p (h a i) -> p h a i", a=2, i=128)
                        m16b = m16[:, :].unsqueeze(1).to_broadcast([128, 4, 128])
                        idb4 = identb8[:, 0:512].rearrange("p (h i) -> p h i", i=128)
                        S.op("dve", lambda e, wav=wav: e.tensor_tensor(out=wav[:, :, 0, :], in0=v3(N0), in1=m16b, op=ALU.mult), reads=[rN0, "m16"], writes=[rWA])
                        S.op("dve", lambda e, wav=wav: e.tensor_tensor(out=wav[:, :, 1, :], in0=wav[:, :, 0, :], in1=idb4, op=ALU.add), reads=[rWA, "identb8"], writes=[rWA])
                        S.op("dve", lambda e, wbv=wbv: e.tensor_tensor(out=wbv[:, :, 0, :], in0=v3(N0T), in1=m16b, op=ALU.mult), reads=[rN0T, "m16"], writes=[rWB])
                        S.op("dve", lambda e, wbv=wbv: e.tensor_tensor(out=wbv[:, :, 1, :], in0=wbv[:, :, 0, :], in1=idb4, op=ALU.add), reads=[rWB, "identb8"], writes=[rWB])
                        for l in range(3):
                            mk = moff[:, l * 128:(l + 1) * 128].unsqueeze(1).to_broadcast([128, 4, 128])
                            S.op("pool", lambda e, l=l, mk=mk: e.tensor_tensor(out=v3(MO[l]), in0=v3(N0T), in1=mk, op=ALU.mult), reads=[rN0T, "moff"], writes=[f"MO{l}_{hg}"])
                            if l < 2:
                                S.op("pool", lambda e, l=l, mk=mk: e.tensor_tensor(out=v3(MOT[l]), in0=v3(N0), in1=mk, op=ALU.mult), reads=[rN0, "moff"], writes=[f"MOT{l}_{hg}"])
                        for lvl in range(4):
                            yield
                            while S.bank_i % 2 != 0:
                                S.bank()
                            ba0 = S.bank(); ba1 = S.bank(); bb0 = S.bank(); bb1 = S.bank()

                            def mm_l(e, lvl=lvl, ba0=ba0, bb0=bb0):
                                ins = None
                                for hh in range(4):
                                    oa = PS[:, ba0 * 512 + hh * 256: ba0 * 512 + hh * 256 + 256]
                                    ob = PS[:, bb0 * 512 + hh * 256: bb0 * 512 + hh * 256 + 256]
                                    wa = WA[:, hh * 256: hh * 256 + 256]
                                    wb = WB[:, hh * 256: hh * 256 + 256]
                                    lo, hi = (0, 128) if lvl == 0 else ((0, 256) if lvl < 3 else (128, 256))
                                    e.matmul(oa[:, lo:hi], lhsT=wb[:, 0:128], rhs=wa[:, lo:hi], start=True, stop=True)
                                    ins = e.matmul(ob[:, lo:hi], lhsT=wa[:, 0:128], rhs=wb[:, lo:hi], start=True, stop=True)
                                return ins
                            S.op("pe", mm_l, reads=[rWA, rWB], writes=[f"pb{ba0}", f"pb{ba1}", f"pb{bb0}", f"pb{bb1}"])
                            pva = PS[:, ba0 * 512: ba0 * 512 + 1024].rearrange("p (h a i) -> p h a i", a=2, i=128)
                            pvb = PS[:, bb0 * 512: bb0 * 512 + 1024].rearrange("p (h a i) -> p h a i", a=2, i=128)
                            if lvl >= 1:
                                S.op("dve", lambda e, pva=pva, wav=wav: e.tensor_tensor(out=wav[:, :, 1, :], in0=pva[:, :, 1, :], in1=wav[:, :, 1, :], op=ALU.add),
                                     reads=[f"pb{ba0}", f"pb{ba1}", rWA], writes=[rWA])
                                S.op("dve", lambda e, pvb=pvb, wbv=wbv: e.tensor_tensor(out=wbv[:, :, 1, :], in0=pvb[:, :, 1, :], in1=wbv[:, :, 1, :], op=ALU.add),
                                     reads=[f"pb{bb0}", f"pb{bb1}", rWB], writes=[rWB])
                            if lvl < 3:
                                S.op("act", lambda e, pva=pva, wav=wav: e.activation(out=wav[:, :, 0, :], in_=pva[:, :, 0, :], func=AF.Copy),
                                     reads=[f"pb{ba0}", f"pb{ba1}", rWA], writes=[rWA])
                                S.op("act", lambda e, pvb=pvb, wbv=wbv: e.activation(out=wbv[:, :, 0, :], in_=pvb[:, :, 0, :], func=AF.Copy),
                                     reads=[f"pb{bb0}", f"pb{bb1}", rWB], writes=[rWB])
                        for l in range(3):
                            last = (l == 2)
                            yield
                            b1_ = S.bank(); b2_ = None if last else S.bank()

                            def mm_v(e, l=l, b1_=b1_, b2_=b2_, last=last):
                                ins = None
                                for hh in range(4):
                                    ins = e.matmul(pbank(b1_, 128, hh * 128), lhsT=MO[l][:, hh * 128:(hh + 1) * 128], rhs=WA[:, hh * 256 + 128: hh * 256 + 256],
                                                   start=True, stop=True)
                                    if not last:
                                        ins = e.matmul(pbank(b2_, 128, hh * 128), lhsT=MOT[l][:, hh * 128:(hh + 1) * 128], rhs=WB[:, hh * 256 + 128: hh * 256 + 256],
                                                       start=True, stop=True)
                                return ins
                            S.op("pe", mm_v, reads=[rWA, rWB, f"MO{l}_{hg}"] + ([] if last else [f"MOT{l}_{hg}"]),
                                 writes=[f"pb{b1_}"] + ([] if last else [f"pb{b2_}"]))
                            S.op("act", lambda e, b1_=b1_: e.activation(out=V1s[:, :], in_=pbank(b1_), func=AF.Copy), reads=[f"pb{b1_}"], writes=[rV1])
                            if not last:
                                S.op("act", lambda e, b2_=b2_: e.activation(out=V2s[:, :], in_=pbank(b2_), func=AF.Copy), reads=[f"pb{b2_}"], writes=[rV2])
                            yield
                            b3_ = S.bank(); b4_ = None if last else S.bank()

                            def mm_u(e, b3_=b3_, b4_=b4_, last=last):
                                ins = None
                                for hh in range(4):
                                    ins = e.matmul(pbank(b3_, 128, hh * 128), lhsT=WB[:, hh * 256 + 128: hh * 256 + 256], rhs=V1s[:, hh * 128:(hh + 1) * 128],
                                                   start=True, stop=True)
                                    if not last:
                                        ins = e.matmul(pbank(b4_, 128, hh * 128), lhsT=WA[:, hh * 256 + 128: hh * 256 + 256], rhs=V2s[:, hh * 128:(hh + 1) * 128],
                                                       start=True, stop=True)
                                return ins
                            S.op("pe", mm_u, reads=[rWA, rWB, rV1] + ([] if last else [rV2]),
                                 writes=[f"pb{b3_}"] + ([] if last else [f"pb{b4_}"]))
                            if last:
                                S.op("dve", lambda e, b3_=b3_, wav=wav, sl=sl: e.tensor_tensor(out=TTf[:, sl].rearrange("p (h i) -> p h i", i=128),
                                                                                           in0=wav[:, :, 1, :], in1=pbank(b3_).rearrange("p (h i) -> p h i", i=128), op=ALU.subtract),
                                     reads=[f"pb{b3_}", rWA], writes=[f"TTf{hg}"])
                            else:
                                S.op("dve", lambda e, b3_=b3_, wav=wav: e.tensor_tensor(out=wav[:, :, 1, :], in0=wav[:, :, 1, :],
                                                                                      in1=pbank(b3_).rearrange("p (h i) -> p h i", i=128), op=ALU.subtract),
                                     reads=[f"pb{b3_}", rWA], writes=[rWA])
                                S.op("dve", lambda e, b4_=b4_, wbv=wbv: e.tensor_tensor(out=wbv[:, :, 1, :], in0=wbv[:, :, 1, :],
                                                                                      in1=pbank(b4_).rearrange("p (h i) -> p h i", i=128), op=ALU.subtract),
                                     reads=[f"pb{b4_}", rWB], writes=[rWB])
                    def rec(hg):
                        sl = slice(hg * 512, (hg + 1) * 512)
                        yield
                        b1 = S.bank()

                        def mm_ks(e, b1=b1, hg=hg, tc0=tc0):
                            ins = None
                            for hh in range(4):
                                h = hg * 4 + hh
                                ins = e.matmul(pbank(b1, 128, hh * 128), lhsT=GK[:, h * TBv + tc0: h * TBv + tc0 + 128],
                                               rhs=SGb[:, h * 128:(h + 1) * 128], start=True, stop=True)
                            return ins
                        S.op("pe", mm_ks, reads=["GK", f"SGb{hg}"], writes=[f"pb{b1}"])
                        S.op("dve", lambda e, b1=b1, sl=sl, hg=hg: e.tensor_tensor(out=rtl[:, sl].rearrange("p (h d) -> p h d", d=128),
                                                                                 in0=pbank(b1).rearrange("p (h d) -> p h d", d=128),
                                                                                 in1=gsm[:, 24 + hg * 4:28 + hg * 4].unsqueeze(2).to_broadcast([128, 4, 128]), op=ALU.mult),
                             reads=[f"pb{b1}", gres], writes=[f"rtl{hg}"])
                        S.op("dve", lambda e, sl=sl: e.tensor_tensor(out=rtl[:, sl], in0=rtl[:, sl], in1=VtG[:, sl], op=ALU.add),
                             reads=[f"rtl{hg}", f"VtG{hg}"], writes=[f"rtl{hg}"])
                        yield
                        b2 = S.bank()

                        def mm_vn(e, b2=b2, hg=hg):
                            ins = None
                            for hh in range(4):
                                h = hg * 4 + hh
                                ins = e.matmul(pbank(b2, 128, hh * 128), lhsT=TTf[:, h * 128:(h + 1) * 128],
                                               rhs=rtl[:, h * 128:(h + 1) * 128], start=True, stop=True)
                            return ins
                        S.op("pe", mm_vn, reads=[f"TTf{hg}", f"rtl{hg}"], writes=[f"pb{b2}"])
                        S.op("dve", lambda e, b2=b2, sl=sl, hg=hg: e.tensor_tensor(out=vnw[:, sl].rearrange("p (h d) -> p h d", d=128),
                                                                                 in0=pbank(b2).rearrange("p (h d) -> p h d", d=128),
                                                                                 in1=gsm[:, hg * 4:hg * 4 + 4].unsqueeze(2).to_broadcast([128, 4, 128]), op=ALU.mult),
                             reads=[f"pb{b2}", gres], writes=[f"vnw{hg}"])
                        yield
                        b3 = S.bank(); b4 = S.bank()

                        def mm_o(e, b3=b3, b4=b4, hg=hg):
                            ins = None
                            for hh in range(4):
                                h = hg * 4 + hh
                                o = pbank(b3, 128, hh * 128)
                                e.matmul(o, lhsT=SGb[:, h * 128:(h + 1) * 128], rhs=QTd[:, h * 128:(h + 1) * 128], start=True, stop=False)
                                e.matmul(o, lhsT=vnw[:, h * 128:(h + 1) * 128], rhs=PTg[:, h * 128:(h + 1) * 128], start=False, stop=True)
                                ins = e.matmul(pbank(b4, 128, hh * 128), lhsT=KtG[:, h * 128:(h + 1) * 128], rhs=vnw[:, h * 128:(h + 1) * 128],
                                               start=True, stop=True)
                            return ins
                        S.op("pe", mm_o, reads=[f"SGb{hg}", qres, f"vnw{hg}", f"PTg{hg}", f"KtG{hg}"], writes=[f"pb{b3}", f"pb{b4}"])
                        dst = OG[:, hg * 4 * TBv:(hg + 1) * 4 * TBv].rearrange("p (h t) -> p h t", t=TBv)[:, :, tc0:tc0 + 128]
                        S.op("act", lambda e, b3=b3, dst=dst: e.activation(out=dst, in_=pbank(b3).rearrange("p (h t) -> p h t", t=128), func=AF.Copy),
                             reads=[f"pb{b3}"], writes=["OG"])
                        S.op("dve", lambda e, sl=sl, hg=hg: e.tensor_tensor(out=SG[:, sl].rearrange("p (h d) -> p h d", d=128),
                                                                     in0=SG[:, sl].rearrange("p (h d) -> p h d", d=128),
                                                                     in1=gsm[:, 40 + hg * 4:44 + hg * 4].unsqueeze(2).to_broadcast([128, 4, 128]), op=ALU.mult),
                             reads=[f"SG{hg}", gres], writes=[f"SG{hg}"])
                        S.op("dve", lambda e, b4=b4, sl=sl: e.tensor_tensor(out=SG[:, sl], in0=SG[:, sl], in1=pbank(b4), op=ALU.add),
                             reads=[f"SG{hg}", f"pb{b4}"], writes=[f"SG{hg}"])
                        S.op("act", lambda e, sl=sl: e.activation(out=SGb[:, sl], in_=SG[:, sl], func=AF.Copy), reads=[f"SG{hg}"], writes=[f"SGb{hg}"])
                    return setup, chain, rec
                def _drive(gens):
                    gens = list(gens)
                    while gens:
                        for _g in list(gens):
                            try:
                                next(_g)
                            except StopIteration:
                                gens.remove(_g)
                _tiles = [gdn_tile(tau) for tau in (range(NTv) if not is_sample else [])]
                if _tiles:
                    _drive([_tiles[0][0]()])
                for tau in range(len(_tiles)):
                    _st, _ch, _rc = _tiles[tau]
                    _drive([_ch(0), _ch(1), _retg])
                    _drive(([_tiles[tau + 1][0]()] if tau + 1 < len(_tiles) else []) + [_rc(0), _rc(1)])
                _drive([_retg])
                if blk == nblk_v - 1 and not is_sample:
                    S.op("sp", lambda e: e.dma_start(out=sgp.rearrange("h d v -> d h v"), in_=SG[:, :].rearrange("p (h v) -> p h v", v=GDV)),
                         reads=["SG0", "SG1"], dma="sgp")
                if blk == nblk_v - 1 and not is_sample:
                    S.op("sp", lambda e: e.dma_start(out=srp.rearrange("h d v -> d h v"), in_=SR[:, :].rearrange("p (h v) -> p h v", v=RDV)),
                         reads=["SR"], dma="srp")
                def _g_rms():
                    yield
                    S.op("act", lambda e: e.activation(out=MRG[:, :], in_=OG[:, :], func=AF.Square), reads=["OG"], writes=["MRG"])
                    yield
                    while S.bank_i % 4 != 0:
                        S.bank()
                    rb_ = [S.bank() for _ in range(4)]
                    assert rb_[3] == rb_[0] + 3

                    def mm_rn(e):
                        ins = None
                        for h in range(GH):
                            ins = e.matmul(PS[:, rb_[0] * 512 + h * TBv: rb_[0] * 512 + (h + 1) * TBv], lhsT=ones_b[:], rhs=MRG[:, h * TBv:(h + 1) * TBv],
                                           start=True, stop=True)
                        return ins
                    rbr = [f"pb{b}" for b in rb_]
                    S.op("pe", mm_rn, reads=["MRG", "ones_b"], writes=rbr)
                    T4r = 4 * TBv
                    for hf in range(2):
                        rs_, rr_ = RNS[hf]
                        S.op("dve", lambda e, hf=hf, rs_=rs_: e.tensor_scalar(out=rs_, in0=PS[:, rb_[0] * 512 + hf * T4r: rb_[0] * 512 + (hf + 1) * T4r],
                                                                             scalar1=1.0 / GDV, scalar2=EPS, op0=ALU.mult, op1=ALU.add), reads=rbr, writes=[rr_])
                    for hf in range(2):
                        rs_, rr_ = RNS[hf]
                        yield
                        S.op("act", lambda e, rs_=rs_: e.activation(out=rs_, in_=rs_, func=AF.Ln), reads=[rr_], writes=[rr_])
                        yield
                        S.op("act", lambda e, rs_=rs_: e.activation(out=rs_, in_=rs_, func=AF.Exp, scale=-0.5), reads=[rr_], writes=[rr_])
                        yield
                        S.op("dve", lambda e, hf=hf, rs_=rs_: e.scalar_tensor_tensor(out=rs_, in0=OG[:, hf * T4r:(hf + 1) * T4r], scalar=ggdn_c[:, 0:1], in1=rs_,
                                                                                    op0=ALU.mult, op1=ALU.mult), reads=["OG", rr_, "ggdn_c"], writes=[rr_])
                        yield
                        S.op("dve", lambda e, hf=hf, rs_=rs_: e.tensor_tensor(out=OGg[:, hf * T4r:(hf + 1) * T4r], in0=rs_, in1=Z[:, hf * T4r:(hf + 1) * T4r], op=ALU.mult),
                             reads=[rr_, "Z"], writes=["OGg"])
                    yield
                def _g_gn():
                    T4 = 4 * TBv
                    if is_sample:
                        GNS, gns_r = GNSs, "GNSs"
                    else:
                        GNS, gns_r = QKVB[:, 8 * TBv:24 * TBv].bitcast(F32), "FA"
                    OBF = FA[:, 0:8 * TBv]
                    OSQ, osq_r = (hnT, "hnT") if is_sample else (VtR, "VtR")
                    yield
                    S.op("act", lambda e: e.activation(out=OBF, in_=OR[:, :], func=AF.Copy), reads=[or_res], writes=["FA"] + FA_AL)
                    yield
                    S.op("act", lambda e: e.activation(out=OSQ[:, :], in_=OR[:, :], func=AF.Square), reads=[or_res], writes=[osq_r])
                    yield
                    while S.bank_i % 4 != 0:
                        S.bank()
                    gb_ = [S.bank() for _ in range(4)]
                    assert gb_[3] == gb_[0] + 3

                    def mm_gn(e):
                        ins = None
                        for h in range(RH):
                            om = PS[:, gb_[0] * 512 + h * TBv: gb_[0] * 512 + (h + 1) * TBv]
                            oq = PS[:, gb_[0] * 512 + T4 + h * TBv: gb_[0] * 512 + T4 + (h + 1) * TBv]
                            for c in range(2):
                                ch = h * 2 + c
                                e.matmul(om, lhsT=onesq_b[:], rhs=OBF[:, ch * TBv:(ch + 1) * TBv], start=(c == 0), stop=(c == 1))
                            for c in range(2):
                                ch = h * 2 + c
                                ins = e.matmul(oq, lhsT=onesq_b[:], rhs=OSQ[:, ch * TBv:(ch + 1) * TBv], start=(c == 0), stop=(c == 1))
                        return ins
                    S.op("pe", mm_gn, reads=["FA", osq_r, "onesq_b"], writes=[f"pb{b}" for b in gb_])
                    pmean = PS[:, gb_[0] * 512: gb_[0] * 512 + T4]
                    pmsq = PS[:, gb_[0] * 512 + T4: gb_[0] * 512 + 2 * T4]
                    gbr = [f"pb{b}" for b in gb_]
                    S.op("act", lambda e: e.activation(out=GNS[:, 0:T4], in_=pmean, func=AF.Copy), reads=gbr, writes=[gns_r])
                    S.op("dve", lambda e: e.tensor_tensor(out=GNS[:, T4:2 * T4], in0=GNS[:, 0:T4], in1=GNS[:, 0:T4], op=ALU.mult), reads=[gns_r], writes=[gns_r])
                    S.op("dve", lambda e: e.scalar_tensor_tensor(out=GNS[:, T4:2 * T4], in0=GNS[:, T4:2 * T4], scalar=-1.0, in1=pmsq, op0=ALU.mult, op1=ALU.add),
                         reads=[gns_r] + gbr, writes=[gns_r])
                    yield
                    S.op("dve", lambda e: e.tensor_scalar(out=GNS[:, T4:2 * T4], in0=GNS[:, T4:2 * T4], scalar1=EPS, scalar2=None, op0=ALU.add), reads=[gns_r], writes=[gns_r])
                    yield
                    S.op("act", lambda e: e.activation(out=GNS[:, T4:2 * T4], in_=GNS[:, T4:2 * T4], func=AF.Ln), reads=[gns_r], writes=[gns_r])
                    yield
                    S.op("act", lambda e: e.activation(out=GNS[:, T4:2 * T4], in_=GNS[:, T4:2 * T4], func=AF.Exp, scale=-0.5), reads=[gns_r], writes=[gns_r])
                    or4 = OR[:, :].rearrange("p (h c t) -> p h c t", c=2, t=TBv)
                    yield
                    S.op("dve", lambda e: e.tensor_tensor(out=or4, in0=or4,
                                                          in1=GNS[:, 0:T4].rearrange("p (h t) -> p h t", t=TBv).unsqueeze(2).to_broadcast([128, RH, 2, TBv]), op=ALU.subtract),
                         reads=[or_res, gns_r], writes=[or_res])
                    yield
                    S.op("dve", lambda e: e.tensor_tensor(out=or4, in0=or4,
                                                          in1=GNS[:, T4:2 * T4].rearrange("p (h t) -> p h t", t=TBv).unsqueeze(2).to_broadcast([128, RH, 2, TBv]), op=ALU.mult),
                         reads=[or_res, gns_r], writes=[or_res])
                    or3 = OR[:, :].rearrange("p (c t) -> p c t", t=TBv)
                    yield
                    S.op("dve", lambda e: e.tensor_tensor(out=or3, in0=or3, in1=ggn_c[:, 0:8].unsqueeze(2).to_broadcast([128, 8, TBv]), op=ALU.mult),
                         reads=[or_res, "ggn_c"], writes=[or_res])
                    yield
                    S.op("dve", lambda e: e.tensor_tensor(out=ORg[:, :], in0=OR[:, :], in1=RG[:, :], op=ALU.mult), reads=[or_res, "RG"], writes=["ORg"])


                    yield
                def late_proj():
                    for (kind, c0, ncols) in late_tiles:
                        base = {"ga": 7184, "gb": 8208}[kind]
                        dstb = {"ga": GA, "gb": GB}[kind]
                        cb = (c0 - base) // 128
                        t, wres = wload(w_in, 0, 8, c0, ncols)
                        for j in range(ncols // 128):
                            yield
                            b = S.bank()

                            def mm(e, j=j, b=b, t=t, ncols=ncols):
                                ins = None
                                for k in range(8):
                                    ins = e.matmul(pbank(b, TBv), lhsT=wv(t, k, ncols, j * 128, 128), rhs=hnT[:, k * TBv:(k + 1) * TBv], start=(k == 0), stop=(k == 7))
                                return ins
                            S.op("pe", mm, reads=[wres, "hnT"], writes=[f"pb{b}"])
                            S.op("act", lambda e, b=b, j=j, dstb=dstb, cb=cb: e.activation(out=dstb[:, (cb + j) * TBv:(cb + j + 1) * TBv], in_=pbank(b, TBv), func=AF.Tanh, scale=0.5),
                                 reads=[f"pb{b}"], writes=[kind.upper()])
                    yield
                _drive([_g_rms(), _g_gn(), late_proj()])

                if blk + 1 < nblk_v:
                    load_x(blk + 1)
                if stage < 4:
                    return
                for half in range(NQ):
                    ta, ra = wload(w_a, 0, 8, half * WC, WC)
                    tb_, rb = wload(w_b, 0, 8, half * WC, WC)
                    def _br(j, half=half, ta=ta, tb_=tb_, ra=ra, rb_w=rb):
                        tmpA, rA = nxt("tmpA"); tmpB, rB = nxt("tmpB")
                        ch = half * (WC // 128) + j
                        b1 = S.bank(); b2 = S.bank()

                        def mm_br(e):
                            ins = None
                            for k in range(8):
                                e.matmul(pbank(b1, TBv), lhsT=wv(ta, k, WC, j * 128, 128), rhs=ORg[:, k * TBv:(k + 1) * TBv], start=(k == 0), stop=(k == 7))
                            for k in range(8):
                                ins = e.matmul(pbank(b2, TBv), lhsT=wv(tb_, k, WC, j * 128, 128), rhs=OGg[:, k * TBv:(k + 1) * TBv], start=(k == 0), stop=(k == 7))
                            return ins
                        S.op("pe", mm_br, reads=[ra, rb_w, "ORg", "OGg"], writes=[f"pb{b1}", f"pb{b2}"])
                        S.op("dve", lambda e: e.scalar_tensor_tensor(out=tmpA[:], in0=GA[:, ch * TBv:(ch + 1) * TBv], scalar=1.0, in1=pbank(b1, TBv),
                                                                     op0=ALU.add, op1=ALU.mult), reads=[f"pb{b1}", "GA"], writes=[rA])
                        S.op("dve", lambda e: e.scalar_tensor_tensor(out=tmpB[:], in0=GB[:, ch * TBv:(ch + 1) * TBv], scalar=1.0, in1=pbank(b2, TBv),
                                                                     op0=ALU.add, op1=ALU.mult), reads=[f"pb{b2}", "GB"], writes=[rB])
                        S.op("dve", lambda e: e.tensor_tensor(out=MRG[:, ch * TBv:(ch + 1) * TBv], in0=tmpA[:], in1=tmpB[:], op=ALU.add),
                             reads=[rA, rB], writes=["MRG"])
                    for j in range(WC // 128):
                        _br(j)

                def resid_cons(half, sc=None):
                    def cons(tau, b):
                        xs_ = Xb[0:tw, tau * D + half * WC: tau * D + half * WC + WC]
                        if sc is None:
                            S.op("dve", lambda e: e.tensor_tensor(out=xs_, in0=xs_, in1=pbank(b, WC)[0:tw, :], op=ALU.add), reads=[f"pb{b}", xres], writes=[xres])
                        else:
                            S.op("dve", lambda e: e.scalar_tensor_tensor(out=xs_, in0=pbank(b, WC)[0:tw, :], scalar=sc, in1=xs_, op0=ALU.mult, op1=ALU.add),
                                 reads=[f"pb{b}", xres], writes=[xres])
                    return cons
                for half in range(NQ):
                    proj_tm(w_out, 0, 8, half * WC, WC, MRG, "MRG", TBv, NTv, resid_cons(half, 0.5), tw=tw)

                if stage < 5:
                    return
                for tau in range(NTv):
                    norm_transpose(Xb[0:tw, tau * D:(tau + 1) * D], xres, gx_c, "gx_c", hnT, "hnT", tw, tau * tw, TBv)
                XQ = ORg
                OX = OGg
                for half in range(NQ):
                    def cons(j, b, half=half):
                        c = half * (WC // 128) + j
                        S.op("act", lambda e: e.activation(out=XQ[:, c * TBv:(c + 1) * TBv], in_=pbank(b, TBv), func=AF.Copy), reads=[f"pb{b}"], writes=["ORg"])
                    proj_fm(w_xq, 0, 8, half * WC, WC, hnT, "hnT", TBv, TBv, cons)
                ET = MRG
                if is_sample:
                    sample_xattn(XQ, OX)
                def _xh(h):
                    tmpB, rB = nxt("tmpB")
                    eo = (h % 2) * 2 * TBv
                    eres = f"MRGs{h % 2}"
                    for mt in range(2):
                        yield
                        b = S.bank()

                        def mm_s(e, b=b, mt=mt):
                            ins = None
                            for c in range(2):
                                ch = h * 2 + c
                                ins = e.matmul(pbank(b, TBv), lhsT=MKT[:, ch * NMEM + mt * 128: ch * NMEM + (mt + 1) * 128],
                                               rhs=XQ[:, ch * TBv:(ch + 1) * TBv], start=(c == 0), stop=(c == 1))
                            return ins
                        S.op("pe", mm_s, reads=["MKT", "ORg"], writes=[f"pb{b}"])
                        S.op("act", lambda e, b=b, mt=mt: e.activation(out=ET[:, eo + mt * TBv: eo + (mt + 1) * TBv], in_=pbank(b, TBv), func=AF.Exp, scale=XHD ** -0.5),
                             reads=[f"pb{b}"], writes=[eres])
                    yield
                    bd = S.bank()

                    def mm_d(e):
                        e.matmul(pbank(bd, TBv), lhsT=ones_b[:], rhs=ET[:, eo:eo + TBv], start=True, stop=False)
                        return e.matmul(pbank(bd, TBv), lhsT=ones_b[:], rhs=ET[:, eo + TBv: eo + 2 * TBv], start=False, stop=True)
                    S.op("pe", mm_d, reads=[eres, "ones_b"], writes=[f"pb{bd}"])
                    S.op("dve", lambda e: e.reciprocal(out=tmpB[:], in_=pbank(bd, TBv)), reads=[f"pb{bd}"], writes=[rB])
                    for c in range(2):
                        ch = h * 2 + c
                        yield
                        b = S.bank()

                        def mm_o(e, b=b, ch=ch):
                            e.matmul(pbank(b, TBv), lhsT=MV[:, ch * 128:(ch + 1) * 128], rhs=ET[:, eo:eo + TBv], start=True, stop=False)
                            return e.matmul(pbank(b, TBv), lhsT=MV[:, D + ch * 128: D + (ch + 1) * 128], rhs=ET[:, eo + TBv: eo + 2 * TBv], start=False, stop=True)
                        S.op("pe", mm_o, reads=["MV", eres], writes=[f"pb{b}"])
                        S.op("dve", lambda e, b=b, ch=ch: e.tensor_tensor(out=OX[:, ch * TBv:(ch + 1) * TBv], in0=pbank(b, TBv), in1=tmpB[:], op=ALU.mult),
                             reads=[f"pb{b}", rB], writes=["OGg"])
                if not is_sample:
                    _drive([_xh(0), _xh(1)])
                    _drive([_xh(2), _xh(3)])
                for half in range(NQ):
                    proj_tm(w_xo, 0, 8, half * WC, WC, OX, "OGg", TBv, NTv, resid_cons(half), tw=tw)

                if stage < 6:
                    return
                for tau in range(NTv):
                    norm_transpose(Xb[0:tw, tau * D:(tau + 1) * D], xres, gffn_c, "gffn_c", hnT, "hnT", tw, tau * tw, TBv)
                for c0 in range(0, DFF, WC):
                    ncols = min(WC, DFF - c0)
                    tg, rg_ = wload(w_gate, 0, 8, c0, ncols)
                    tu, ru = wload(w_up, 0, 8, c0, ncols)
                    def _ff(j, c0=c0, ncols=ncols, tg=tg, tu=tu, rg_=rg_, ru=ru):
                        tmpA, rA = nxt("tmpA")
                        ch = c0 // 128 + j
                        b1 = S.bank(); b2 = S.bank()

                        def mm_f(e):
                            ins = None
                            for k in range(8):
                                e.matmul(pbank(b1, TBv), lhsT=wv(tg, k, ncols, j * 128, 128), rhs=hnT[:, k * TBv:(k + 1) * TBv], start=(k == 0), stop=(k == 7))
                            for k in range(8):
                                ins = e.matmul(pbank(b2, TBv), lhsT=wv(tu, k, ncols, j * 128, 128), rhs=hnT[:, k * TBv:(k + 1) * TBv], start=(k == 0), stop=(k == 7))
                            return ins
                        S.op("pe", mm_f, reads=[rg_, ru, "hnT"], writes=[f"pb{b1}", f"pb{b2}"])
                        S.op("act", lambda e: e.activation(out=tmpA[:], in_=pbank(b1, TBv), func=AF.Silu), reads=[f"pb{b1}"], writes=[rA])
                        S.op("dve", lambda e: e.tensor_tensor(out=FA[:, ch * TBv:(ch + 1) * TBv], in0=tmpA[:], in1=pbank(b2, TBv), op=ALU.mult),
                             reads=[rA, f"pb{b2}"], writes=["FA"] + FA_AL)
                    for j in range(ncols // 128):
                        _ff(j)
                if blk + 1 < nblk_v:
                    pre_norm(blk + 1)
                for half in range(NQ):
                    tiles = [wload(w_down, g * 1024, (8 if g < 2 else 6), half * WC, WC) for g in range(3)]
                    for tau in range(NTv):
                        b = S.bank()

                        def mm_d(e, b=b, tau=tau, tiles=tiles):
                            ins = None
                            for g in range(3):
                                nk = 8 if g < 2 else 6
                                for k in range(nk):
                                    kk = g * 8 + k
                                    ins = e.matmul(pbank(b, WC)[0:tw, :], lhsT=FA[:, kk * TBv + tau * tw: kk * TBv + (tau + 1) * tw], rhs=wv(tiles[g][0], k, WC),
                                                   start=(kk == 0), stop=(kk == 21))
                            return ins
                        S.op("pe", mm_d, reads=[t[1] for t in tiles] + ["FA"], writes=[f"pb{b}"])
                        resid_cons(half)(tau, b)
                if is_sample:
                    gfin_bc, gfin_r = gfin_s, "gfin_s"
                else:
                    gfin_bc, gfin_r = D1, "D1"
                S.op("sp", lambda e, gfin_bc=gfin_bc: e.dma_start(out=gfin_bc[:, 0:D], in_=g_fin.rearrange("(o d) -> o d", o=1).partition_broadcast(128)),
                     writes=[gfin_r], dma="gfin_ld")
                for tau in range(NTv):
                    xt = Xb[0:tw, tau * D:(tau + 1) * D]
                    S.op("act", lambda e, xt=xt: e.activation(out=sqj[0:tw, :], in_=xt, func=AF.Square, accum_out=sscol[0:tw, 0:1]), reads=[xres], writes=["xn", "sscol"])
                    rstd_from_ss(sscol[0:tw, 0:1], D, EPS, "sscol")
                    S.op("dve", lambda e, xt=xt: e.scalar_tensor_tensor(out=xt, in0=xt, scalar=sscol[0:tw, 0:1], in1=gfin_bc[0:tw, 0:D], op0=ALU.mult, op1=ALU.mult),
                         reads=[xres, "sscol", gfin_r], writes=[xres])
                if is_sample:
                    S.op("sp", lambda e, Xb=Xb: e.dma_start(out=ys, in_=Xb[0:NS, 0:D]), reads=[xres], dma=xres + "o")
                else:
                    S.op("sp", lambda e, Xb=Xb, t0=t0: e.dma_start(out=yp[t0:t0 + TBv, :].rearrange("(a p) d -> p a d", p=128),
                                                                 in_=Xb[:, :].rearrange("p (a d) -> p a d", d=D)),
                         reads=[xres], dma=xres + "o")
            for blk in range(nblk_v):
                do_block(blk)
            S.emit()
            esp.close()
            cur_es[0] = es

        run_phase(False)
        if stage >= 7:
            run_phase(True)
    return nc


_CACHE = {}


def _prep_inputs(inputs):
    f = lambda a: np.ascontiguousarray(np.asarray(a, dtype=np.float32))
    common = dict(
        w_in=f(inputs["w_in"][0]), w_a=f(inputs["w_branch_a"][0]), w_b=f(inputs["w_branch_b"][0]), w_out=f(inputs["w_out"][0]),
        w_xq=f(inputs["w_xq"][0]), w_xk=f(inputs["w_xk"][0]), w_xv=f(inputs["w_xv"][0]), w_xo=f(inputs["w_xo"][0]),
        w_gate=f(inputs["w_gate"][0]), w_up=f(inputs["w_up"][0]), w_down=f(inputs["w_down"][0]),
        g_mix=f(inputs["norm_mix_g"][0]), g_x=f(inputs["norm_x_g"][0]), g_mem=f(inputs["mem_norm_g"][0]),
        g_ffn=f(inputs["norm_ffn_g"][0]), g_fin=f(inputs["norm_final_g"]), g_gn=f(inputs["ret_gn_g"][0]),
        g_gdn=f(inputs["gdn_norm_g"][0]), convw=f(inputs["gdn_conv_w"][0]), a_log=f(inputs["gdn_a_log"][0]),
        dt_bias=f(inputs["gdn_dt_bias"][0]))
    common.update(_CONSTS)
    maps = []
    for c in range(NCORES):
        m = dict(common)
        sl = slice(c * NS, (c + 1) * NS)
        m["xp"] = f(inputs["x_prompt"][c]); m["memp"] = f(inputs["mem_prompt"][c])
        m["xs"] = f(inputs["x_sample"][sl, 0]); m["sret"] = f(inputs["state_ret"][0, sl]); m["sgdn"] = f(inputs["state_gdn"][0, sl])
        m["sconv"] = f(inputs["state_conv"][0, sl])
        m["cmk"] = f(inputs["cache_mem_k"][0, sl]).reshape(NS, NMEM, D); m["cmv"] = f(inputs["cache_mem_v"][0, sl]).reshape(NS, NMEM, D)
        maps.append(m)
    return maps


def kernel(**inputs):
    if "nc" not in _CACHE:
        _CACHE["nc"] = build_program()
    nc = _CACHE["nc"]
    maps = _prep_inputs(inputs)
    res = run_bass_kernel_spmd(nc, maps, core_ids=list(range(NCORES)))
    R = res.results
    st = lambda k: np.stack([np.asarray(r[k], dtype=np.float32) for r in R])
    cat = lambda k: np.concatenate([np.asarray(r[k], dtype=np.float32) for r in R], axis=0)
    y_prompt = st("yp")
    y_sample = cat("ys").reshape(NCORES * NS, 1, D)
    return (y_prompt, y_sample, st("srp")[None], st("sgp")[None], st("scp")[None],
            st("mkp").reshape(1, NCORES, NMEM, XH, XHD), st("mvp").reshape(1, NCORES, NMEM, XH, XHD),
            cat("srs")[None], cat("sgs")[None], cat("scs")[None])
```

```python
from contextlib import ExitStack
import math
import numpy as np
import concourse.bass as bass
import concourse.mybir as mybir
from concourse.bass_utils import run_bass_kernel_spmd

F32 = mybir.dt.float32
BF16 = mybir.dt.bfloat16
AF = mybir.ActivationFunctionType
ALU = mybir.AluOpType

NCORES = 8
D = 1024
SEQ = 2048
NS = 16
PAST = 16384
RH, RDK, RDV = 4, 128, 256
GH, GDK, GDV = 8, 128, 128
CONV_CH = 3072
NMEM = 256
XH, XHD = 4, 256
DFF = 2816
DIN = 9232
EPS = 1e-6
TB = 256
NT = TB // 128
NBLK = SEQ // TB
ENGS = ("pe", "act", "dve", "pool", "sp")
EPOCH = 3000
NEG = -30000.0


class Sched:
    def __init__(self, nc, es):
        self.nc, self.es = nc, es
        self.eng_sems = {e: [] for e in ENGS}
        self.nsem = 0
        self.prev_final = []
        self.reset()

    def reset(self):
        self.ops = []
        self.last_w = {}
        self.readers = {}
        self.eng_count = {e: 0 for e in ENGS}
        self.eng_base = {e: len(self.eng_sems[e]) for e in ENGS}
        self.dma_sems = {}
        self.dma_cnt = {}
        self.bank_i = 0
        self.reserved = set()

    def _newsem(self, name):
        self.nsem += 1
        return self.es.enter_context(self.nc.semaphore(f"{name}_{self.nsem}"))

    def _token_compute(self, eng):
        i = self.eng_count[eng]
        self.eng_count[eng] += 1
        ep, k = divmod(i, EPOCH)
        ep += self.eng_base[eng]
        while len(self.eng_sems[eng]) <= ep:
            self.eng_sems[eng].append(self._newsem(f"s{eng}"))
        return (self.eng_sems[eng][ep], k + 1, 1)

    def _token_dma(self, key):
        if key not in self.dma_sems or self.dma_cnt[key] > 60000:
            self.dma_sems[key] = self._newsem("d")
            self.dma_cnt[key] = 0
        self.dma_cnt[key] += 16
        return (self.dma_sems[key], self.dma_cnt[key], 16)

    def bank(self):
        while True:
            b = self.bank_i
            self.bank_i = (self.bank_i + 1) % 8
            if b not in self.reserved:
                return b

    def op(self, eng, fn, reads=(), writes=(), dma=None):
        writes = list(writes) + [r for r in reads if r.startswith("pb") and r not in writes]
        deps = set()
        for r in reads:
            if r in self.last_w:
                deps.add(self.last_w[r])
        for w in writes:
            if w in self.last_w:
                deps.add(self.last_w[w])
            for rd in self.readers.get(w, ()):
                deps.add(rd)
        idx = len(self.ops)
        tok = self._token_dma(dma) if dma is not None else self._token_compute(eng)
        self.ops.append(dict(eng=eng, fn=fn, deps=deps, tok=tok, dma=dma is not None))
        for r in reads:
            self.readers.setdefault(r, []).append(idx)
        for w in writes:
            self.last_w[w] = idx
            self.readers[w] = []
        return idx

    def emit(self):
        nc, ops = self.nc, self.ops
        per = {e: [] for e in ENGS}
        for i, o in enumerate(ops):
            per[o["eng"]].append(i)
        final = {}
        for e in ENGS:
            for ep in range(self.eng_base[e], len(self.eng_sems[e])):
                n = min(EPOCH, self.eng_count[e] - (ep - self.eng_base[e]) * EPOCH)
                s = self.eng_sems[e][ep]
                final[id(s)] = (s, n)
        for o in ops:
            sem, val, _ = o["tok"]
            if final.get(id(sem), (None, 0))[1] < val:
                final[id(sem)] = (sem, val)
        prev_final = self.prev_final

        def run(eng_name, engine):
            waited = {}
            for (sem, val) in prev_final:
                engine.wait_ge(sem, val)
                waited[id(sem)] = val
            for i in per[eng_name]:
                o = ops[i]
                need = {}
                for d in o["deps"]:
                    od = ops[d]
                    if od["eng"] == "pe" and eng_name == "pe" and not od["dma"]:
                        continue
                    sem, val, _ = od["tok"]
                    k = id(sem)
                    if waited.get(k, 0) >= val:
                        continue
                    if k not in need or need[k][1] < val:
                        need[k] = (sem, val)
                for k, (sem, val) in need.items():
                    engine.wait_ge(sem, val)
                    waited[k] = val
                ins = o["fn"](engine)
                sem, val, inc = o["tok"]
                ins.then_inc(sem, inc)
            if eng_name == "sp":
                for k, (sem, val) in final.items():
                    if waited.get(k, 0) >= val:
                        continue
                    engine.wait_ge(sem, val)

        with nc.Block() as block:
            @block.tensor
            def _(e):
                run("pe", e)

            @block.scalar
            def _(e):
                run("act", e)

            @block.vector
            def _(e):
                run("dve", e)

            @block.gpsimd
            def _(e):
                run("pool", e)

            @block.sync
            def _(e):
                run("sp", e)
        self.prev_final = list(final.values())
        self.reset()


def _consts():
    c = {}
    idx = np.arange(128)
    c["c_ident"] = np.eye(128, dtype=np.float32)
    c["c_tri"] = (idx[:, None] <= idx[None, :]).astype(np.float32)
    c["c_mincl"] = np.where(idx[None, :] >= idx[:, None], 0.0, NEG).astype(np.float32)
    c["c_strict"] = (idx[None, :] > idx[:, None]).astype(np.float32)
    c["c_ones"] = np.ones((128, 128), np.float32)
    blk = lambda b: (idx[:, None] // b) == (idx[None, :] // b)
    c["c_m16"] = blk(16).astype(np.float32)
    c["c_moff"] = np.concatenate([-(blk(2 * b) & ~blk(b)).astype(np.float32) for b in (16, 32, 64)], axis=1)
    perm = np.zeros((128, 128), np.float32)
    perm[(idx + 64) % 128, idx] = 1.0
    c["c_perm"] = perm
    h = np.arange(RH, dtype=np.float64)
    log_g = np.log1p(-np.exp2(-5.0 - h))
    diff = idx[None, :] - idx[:, None]
    dt = np.where(diff[:, None, :] >= 0, np.exp(np.maximum(diff, 0)[:, None, :] * log_g[None, :, None]), 0.0)
    c["c_rdt"] = (dt * RDK ** -0.5).astype(np.float32).reshape(128, RH * 128)
    qd = np.exp((idx[None, :] + 1.0) * log_g[:, None])
    c["c_rqd"] = np.broadcast_to(qd.reshape(1, RH * 128), (128, RH * 128)).astype(np.float32).copy()
    kd = np.exp((127.0 - idx)[:, None] * log_g[None, :]) * RDK ** -0.5
    c["c_rkd"] = kd.astype(np.float32)
    gam = np.exp(log_g)
    half = RDK // 2
    inv = 10000.0 ** (-np.arange(half, dtype=np.float64) / half)
    pos = np.arange(SEQ, dtype=np.float64)
    ang = inv[:, None] * pos[None, :]
    ang = (inv.astype(np.float32)[:, None] * pos.astype(np.float32)[None, :]).astype(np.float64)
    cos, sin = np.cos(ang), np.sin(ang)
    c["c_cos"] = np.concatenate([cos, cos], 0).astype(np.float32)
    c["c_sin"] = np.concatenate([-sin, sin], 0).astype(np.float32)
    angs = (inv.astype(np.float32) * np.float32(PAST)).astype(np.float64)
    c["c_cs_s"] = np.stack([np.concatenate([np.cos(angs), np.cos(angs)]),
                            np.concatenate([-np.sin(angs), np.sin(angs)])]).astype(np.float32)
    return c, gam


_CONSTS, _GAM = _consts()

PJ_OFF = {"rq": (0, 0), "rk": (512, 512), "rv": (1024, 1024), "qkv": (3072, 2048), "ba": (7168, 5120)}
WC = 256
NQ = 1024 // WC
W_IN_TILES = []
for _k, _c0, _n in (("rq", 0, 512), ("rk", 512, 512), ("rv", 1024, 1024), ("rg", 2048, 1024), ("qkv", 3072, 3072),
                    ("z", 6144, 1024), ("ba", 7168, 16), ("ga", 7184, 1024), ("gb", 8208, 1024)):
    for _o in range(0, _n, WC):
        W_IN_TILES.append((_k, _c0 + _o, min(WC, _n - _o)))


def build_program(debug=False, nblk=NBLK, stage=99, kinds=None):
    nc = bass.Bass("TRN2", target_bir_lowering=False)
    di = lambda name, shape: nc.dram_tensor(name, list(shape), F32, kind="ExternalInput").ap()
    do = lambda name, shape: nc.dram_tensor(name, list(shape), F32, kind="ExternalOutput").ap()
    xp = di("xp", [SEQ, D]); memp = di("memp", [NMEM, D])
    xs = di("xs", [NS, D]); sret = di("sret", [NS, RH, RDK, RDV]); sgdn = di("sgdn", [NS, GH, GDK, GDV])
    sconv = di("sconv", [NS, 3, CONV_CH]); cmk = di("cmk", [NS, NMEM, D]); cmv = di("cmv", [NS, NMEM, D])
    w_in = di("w_in", [D, DIN]); w_a = di("w_a", [D, D]); w_b = di("w_b", [D, D]); w_out = di("w_out", [D, D])
    w_xq = di("w_xq", [D, D]); w_xk = di("w_xk", [D, D]); w_xv = di("w_xv", [D, D]); w_xo = di("w_xo", [D, D])
    w_gate = di("w_gate", [D, DFF]); w_up = di("w_up", [D, DFF]); w_down = di("w_down", [DFF, D])
    g_mix = di("g_mix", [D]); g_x = di("g_x", [D]); g_mem = di("g_mem", [D]); g_ffn = di("g_ffn", [D])
    g_fin = di("g_fin", [D]); g_gn = di("g_gn", [D]); g_gdn = di("g_gdn", [GDV])
    convw = di("convw", [4, CONV_CH]); a_log = di("a_log", [GH]); dt_bias = di("dt_bias", [GH])
    cst = {k: di(k, v.shape) for k, v in _CONSTS.items()}
    yp = do("yp", [SEQ, D]); ys = do("ys", [NS, D])
    srp = do("srp", [RH, RDK, RDV]); sgp = do("sgp", [GH, GDK, GDV]); scp = do("scp", [3, CONV_CH])
    mkp = do("mkp", [NMEM, D]); mvp = do("mvp", [NMEM, D])
    srs = do("srs", [NS, RH, RDK, RDV]); sgs = do("sgs", [NS, GH, GDK, GDV]); scs = do("scs", [NS, 3, CONV_CH])

    with ExitStack() as es:
        S = Sched(nc, es)
        sfx = [""]
        sbt = lambda name, shape, dt=BF16: cur_es[0].enter_context(nc.sbuf_tensor(name + sfx[0], list(shape), dt))
        cur_es = [es]
        PS = es.enter_context(nc.psum_tensor("PS", [128, 4096], F32))
        PSB = PS[:, :].bitcast(BF16)

        def pbank(b, n=512, off=0):
            return PS[:, b * 512 + off: b * 512 + off + n]

        def pbank_bf(b, n=1024, off=0):
            return PSB[:, b * 1024 + off: b * 1024 + off + n]

        def load_const(name, shape, src, dt=F32, eng="sp"):
            t = sbt(name, shape, dt)
            S.op(eng, lambda e: e.dma_start(out=t[:], in_=src), writes=[name], dma=name)
            return t
        ident_f = load_const("ident_f", [128, 128], cst["c_ident"])
        ident_b = load_const("ident_b", [128, 128], cst["c_ident"], BF16, "pool")
        ones_f = load_const("ones_f", [128, 128], cst["c_ones"])
        ones_b = load_const("ones_b", [128, 128], cst["c_ones"], BF16, "pool")
        perm_b = load_const("perm_b", [128, 128], cst["c_perm"], BF16, "pool")
        tri_f = load_const("tri_f", [128, 128], cst["c_tri"])
        mincl = load_const("mincl", [128, 128], cst["c_mincl"])
        strict = load_const("strict", [128, 128], cst["c_strict"])
        m16 = load_const("m16", [128, 128], cst["c_m16"], BF16, "pool")
        moff = load_const("moff", [128, 384], cst["c_moff"], BF16, "pool")
        rdt = load_const("rdt", [128, 512], cst["c_rdt"], BF16, "pool")
        rqd = load_const("rqd", [128, 512], cst["c_rqd"], BF16, "pool")
        rkd = load_const("rkd", [128, RH], cst["c_rkd"])
        with nc.allow_non_contiguous_dma(reason="small param column loads"):
            def col_load(name, src, n):
                t = sbt(name, [128, n], F32)
                S.op("sp", lambda e: e.dma_start(out=t[:], in_=src.rearrange("(k p) -> p k", p=128), allow_slow_non_contiguous=True),
                     writes=[name], dma=name)
                return t
            gmix_c = col_load("gmix_c", g_mix, 8)
            gx_c = col_load("gx_c", g_x, 8)
            gmem_c = col_load("gmem_c", g_mem, 8)
            gffn_c = col_load("gffn_c", g_ffn, 8)
            ggn_c = col_load("ggn_c", g_gn, 8)
            ggdn_c = col_load("ggdn_c", g_gdn, 1)
            cw = sbt("cw", [128, 4 * 24], F32)
            for i in range(4):
                S.op("sp", lambda e, i=i: e.dma_start(out=cw[:, i * 24:(i + 1) * 24], in_=convw[i].rearrange("(c p) -> p c", p=128),
                                                       allow_slow_non_contiguous=True),
                     writes=["cw"], dma="cw")
        alog_bc = sbt("alog_bc", [128, GH], F32)
        dtb_bc = sbt("dtb_bc", [128, GH], F32)
        S.op("sp", lambda e: e.dma_start(out=alog_bc[:], in_=a_log.rearrange("(o d) -> o d", o=1).partition_broadcast(128)),
             writes=["alog_bc"], dma="alog_bc")
        S.op("sp", lambda e: e.dma_start(out=dtb_bc[:], in_=dt_bias.rearrange("(o d) -> o d", o=1).partition_broadcast(128)),
             writes=["dtb_bc"], dma="dtb_bc")
        nega = sbt("nega", [128, GH], F32)
        S.op("act", lambda e: e.activation(out=nega[:], in_=alog_bc[:], func=AF.Exp), reads=["alog_bc"], writes=["nega"])
        S.op("dve", lambda e: e.tensor_scalar(out=nega[:], in0=nega[:], scalar1=-1.0, scalar2=None, op0=ALU.mult),
             reads=["nega"], writes=["nega"])
        ones128_b = sbt("ones128_b", [128, 128], BF16)
        S.op("dve", lambda e: e.tensor_scalar(out=ones128_b[:], in0=ones_f[:], scalar1=128.0, scalar2=None, op0=ALU.mult),
             reads=["ones_f"], writes=["ones128_b"])
        onesq_b = sbt("onesq_b", [128, 128], BF16)
        S.op("dve", lambda e: e.tensor_scalar(out=onesq_b[:], in0=ones_f[:], scalar1=1.0 / 256, scalar2=None, op0=ALU.mult),
             reads=["ones_f"], writes=["onesq_b"])

        NSLOT = 6
        wslots = [sbt(f"wslot{i}", [128, 8 * WC], BF16) for i in range(NSLOT)]
        wstate = dict(i=0)

        scratch = {}

        def wload(W, r0, nk, c0, ncols):
            s = wstate["i"] % NSLOT
            wstate["i"] += 1
            t = wslots[s]
            dst = t[:, 0:nk * ncols].rearrange("p (k c) -> p k c", c=ncols)
            key = (W.tensor.name, r0, nk, c0, ncols)
            if key not in scratch:
                sc = nc.dram_tensor(f"scr{len(scratch)}", [128, nk * ncols], BF16, kind="Internal").ap()
                scratch[key] = (sc, f"scr{len(scratch)}")
                sc, sres = scratch[key]
                src = W[r0:r0 + nk * 128, c0:c0 + ncols].rearrange("(k p) c -> p k c", p=128)
                S.op("pool", lambda e: e.dma_start(out=dst, in_=src), writes=[f"wslot{s}"], dma=f"wslot{s}")
                S.op("sp", lambda e: e.dma_start(out=sc, in_=t[:, 0:nk * ncols]), reads=[f"wslot{s}"], writes=[sres], dma=f"wst{s}")
            else:
                sc, sres = scratch[key]
                S.op("pool", lambda e: e.dma_start(out=t[:, 0:nk * ncols], in_=sc), reads=[sres], writes=[f"wslot{s}"], dma=f"wslot{s}")
            return t, f"wslot{s}"

        def wv(t, k, ncols, c0=0, n=None):
            n = ncols if n is None else n
            return t[:, k * ncols + c0: k * ncols + c0 + n]

        def rstd_from_ss(ss, n, eps, res):
            S.op("dve", lambda e: e.tensor_scalar(out=ss, in0=ss, scalar1=1.0 / n, scalar2=eps, op0=ALU.mult, op1=ALU.add),
                 reads=[res], writes=[res])
            S.op("act", lambda e: e.activation(out=ss, in_=ss, func=AF.Ln), reads=[res], writes=[res])
            S.op("act", lambda e: e.activation(out=ss, in_=ss, func=AF.Exp, scale=-0.5), reads=[res], writes=[res])

        xn = sbt("xn", [128, D], BF16)
        sqj = xn
        sscol = sbt("sscol", [128, 4], F32)

        def norm_transpose(xt, xres, gcol, gres, dstT, dres, ntok, tcol, ncols_total):
            S.op("act", lambda e: e.activation(out=sqj[0:ntok, :], in_=xt, func=AF.Square, accum_out=sscol[0:ntok, 0:1]),
                 reads=[xres], writes=["xn", "sscol"])
            rstd_from_ss(sscol[0:ntok, 0:1], D, EPS, "sscol")
            S.op("dve", lambda e: e.tensor_scalar(out=xn[0:ntok, :], in0=xt, scalar1=sscol[0:ntok, 0:1], scalar2=None, op0=ALU.mult),
                 reads=[xres, "sscol"], writes=["xn"])
            b = S.bank()

            def tr(e):
                ins = None
                for k in range(8):
                    ins = e.transpose(pbank_bf(b, ntok, k * 128)[:, :], xn[0:ntok, k * 128:(k + 1) * 128], ident_b[0:ntok, 0:ntok])
                return ins
            S.op("pe", tr, reads=["xn", "ident_b"], writes=[f"pb{b}"])
            src = pbank_bf(b, 1024).rearrange("p (k i) -> p k i", i=128)[:, :, 0:ntok]
            dst = dstT[:, :].rearrange("p (k t) -> p k t", t=ncols_total)[:, :, tcol:tcol + ntok]
            S.op("dve", lambda e: e.tensor_tensor(out=dst, in0=src, in1=gcol[:, 0:8].unsqueeze(2).to_broadcast([128, 8, ntok]), op=ALU.mult),
                 reads=[f"pb{b}", gres], writes=[dres])

        def proj_fm(W, r0, nk, c0, ncols, srcT, sres, src_cols, ntok, consume):
            t, wres = wload(W, r0, nk, c0, ncols)
            for j in range(ncols // 128):
                b = S.bank()

                def mm(e, j=j, b=b):
                    ins = None
                    for k in range(nk):
                        ins = e.matmul(pbank(b, ntok), lhsT=wv(t, k, ncols, j * 128, 128),
                                       rhs=srcT[:, k * src_cols: k * src_cols + ntok], start=(k == 0), stop=(k == nk - 1))
                    return ins
                S.op("pe", mm, reads=[wres, sres], writes=[f"pb{b}"])
                consume(j, b)

        def proj_tm(W, r0, nk, c0, ncols, srcT, sres, src_cols, ntiles, consume, tw=128):
            t, wres = wload(W, r0, nk, c0, ncols)
            for tau in range(ntiles):
                b = S.bank()

                def mm(e, tau=tau, b=b):
                    ins = None
                    for k in range(nk):
                        ins = e.matmul(pbank(b, ncols)[0:tw, :], lhsT=srcT[:, k * src_cols + tau * tw: k * src_cols + (tau + 1) * tw],
                                       rhs=wv(t, k, ncols), start=(k == 0), stop=(k == nk - 1))
                    return ins
                S.op("pe", mm, reads=[wres, sres], writes=[f"pb{b}"])
                consume(tau, b)


        def run_phase(is_sample):
            TBv = NS if is_sample else TB
            tw = NS if is_sample else 128
            NTv = TBv // tw
            nblk_v = 1 if is_sample else nblk
            esp = ExitStack()
            cur_es[0] = esp
            sfx[0] = "_s" if is_sample else "_p"
            XOR = [sbt(f"XOR{i_}", [128, NTv * D], F32) for i_ in range(2)]
            hnT = sbt("hnT", [128, 8 * TBv])
            RG = sbt("RG", [128, 8 * TBv])
            Z = sbt("Z", [128, 8 * TBv]); GA = sbt("GA", [128, 8 * TBv]); GB = sbt("GB", [128, 8 * TBv])
            NROT = 4
            _rot = {"tmpA": [sbt(f"tmpA{i}", [128, TBv], F32) for i in range(NROT)],
                    "tmpB": [sbt(f"tmpB{i}", [128, TBv], F32) for i in range(NROT)],
                    "tmpb": [sbt(f"tmpb{i}", [128, TBv], BF16) for i in range(NROT)],
                    "CACC": [sbt(f"CACC{i}", [128, TBv], F32) for i in range(NROT)]}
            _roti = {}

            def nxt(name):
                i = _roti.get(name, 0)
                _roti[name] = i + 1
                return _rot[name][i % NROT], f"{name}{i % NROT}"
            OG = sbt("OG", [128, 8 * TBv], F32)
            OR = XOR[1][:, 0:8 * TBv]
            ORg = sbt("ORg", [128, 8 * TBv]); OGg = sbt("OGg", [128, 8 * TBv])
            MRG = sbt("MRG", [128, 8 * TBv])
            if is_sample:
                FA = sbt("FA", [128, 22 * TBv])
                FA_AL = []
            else:
                QKVB = sbt("QKVB", [128, 24 * TBv])
                FA = QKVB[:, 0:22 * TBv]
                FA_AL = ["GQ", "GK", "GV"]
            if is_sample:
                PJ = sbt("PJ", [NS, 5136], F32)
                GNSs = sbt("GNSs", [128, 8 * TBv], F32)
                RNSs = sbt("RNSs", [128, 8 * TBv], F32)
                RNS = [(RNSs[:, 0:4 * TBv], "RNSa"), (RNSs[:, 4 * TBv:8 * TBv], "RNSb")]
                gfin_s = sbt("gfin_s", [128, D], F32)
            if not is_sample:
                RQ = sbt("RQ", [128, 4 * TBv]); RK = sbt("RK", [128, 4 * TBv]); RQd = sbt("RQd", [128, 4 * TBv])
                VtR = sbt("VtR", [128, NTv * 1024])
                GQ = QKVB[:, 0:8 * TBv]; GK = QKVB[:, 8 * TBv:16 * TBv]; GV = QKVB[:, 16 * TBv:24 * TBv]
                BA = sbt("BA", [128, NTv * 16], F32)
                _rot["CIN"] = [sbt(f"CIN{i}", [128, 3 + TBv], F32) for i in range(NROT)]
                HALO = sbt("HALO", [128, 24 * 3], F32)
                cosb = sbt("cosb", [128, TBv], F32); sinb = sbt("sinb", [128, TBv], F32)
                SR = sbt("SR", [128, RH * RDV], F32); SRb = sbt("SRb", [128, RH * RDV])
                SG = sbt("SG", [128, GH * GDV], F32); SGb = sbt("SGb", [128, GH * GDV])
                MKT = sbt("MKT", [128, 8 * NMEM]); MV = sbt("MV", [128, 2 * D])
                memx = OR[:, 0:D]
                mnT = OGg
                mo = OG[:, 0:512]

                S.op("dve", lambda e: e.memset(SR[:], 0.0), writes=["SR"])
                S.op("dve", lambda e: e.memset(SRb[:], 0.0), writes=["SRb"])
                S.op("dve", lambda e: e.memset(SG[:], 0.0), writes=["SG0", "SG1"])
                S.op("dve", lambda e: e.memset(SGb[:], 0.0), writes=["SGb0", "SGb1"])
                S.op("dve", lambda e: e.memset(HALO[:], 0.0), writes=["HALO"])

                for mt in range(2):
                    S.op("sp", lambda e, mt=mt: e.dma_start(out=memx[:], in_=memp[mt * 128:(mt + 1) * 128, :]), writes=["XOR1"], dma="memx")
                    norm_transpose(memx, "XOR1", gmem_c, "gmem_c", mnT, "OGg", 128, mt * 128, NMEM)
                for (W, outd, isk) in ((w_xk, mkp, True), (w_xv, mvp, False)):
                    for half in range(NQ):
                        def cons(tau, b, half=half, outd=outd, isk=isk):
                            S.op("act", lambda e: e.activation(out=mo[:, 0:WC], in_=pbank(b, WC), func=AF.Copy), reads=[f"pb{b}"], writes=["OG"])
                            if not isk:
                                S.op("dve", lambda e: e.tensor_copy(out=MV[:, tau * D + half * WC: tau * D + half * WC + WC], in_=pbank(b, WC)),
                                     reads=[f"pb{b}"], writes=["MV"])
                            S.op("sp", lambda e: e.dma_start(out=outd[tau * 128:(tau + 1) * 128, half * WC:(half + 1) * WC], in_=mo[:, 0:WC]),
                                 reads=["OG"], dma="mo_out")
                        proj_tm(W, 0, 8, half * WC, WC, mnT, "OGg", NMEM, 2, cons)
                for half in range(NQ):
                    def cons(j, b, half=half):
                        c = half * (WC // 128) + j
                        S.op("act", lambda e: e.activation(out=MKT[:, c * NMEM:(c + 1) * NMEM], in_=pbank(b, NMEM), func=AF.Copy),
                             reads=[f"pb{b}"], writes=["MKT"])
                    proj_fm(w_xk, 0, 8, half * WC, WC, mnT, "OGg", NMEM, NMEM, cons)

                gsmT = [sbt(f"gsm{t_}", [128, 64], F32) for t_ in range(NTv)]
                D1 = sbt("D1", [128, 1024], F32)
                GBm = sbt("GBm", [128, 1024], F32)
                G2 = GBm
                EGC = GBm
                GCTX = []
                for g_ in range(2):
                    GCTX.append((sbt(f"N0_{g_}", [128, 512], BF16), sbt(f"N0T_{g_}", [128, 512], BF16),
                                 sbt(f"WA_{g_}", [128, 1024], BF16), sbt(f"WB_{g_}", [128, 1024], BF16),
                                 [sbt(f"MO{l}_{g_}", [128, 512], BF16) for l in range(3)],
                                 [sbt(f"MOT{l}_{g_}", [128, 512], BF16) for l in range(2)],
                                 sbt(f"V1s_{g_}", [128, 512], BF16), sbt(f"V2s_{g_}", [128, 512], BF16)))
                TTf = sbt("TTf", [128, 1024], BF16)
                RNS = [(D1[:, 0:4 * TBv], "D1"), (GBm[:, 0:4 * TBv], "GBm")]
                PTg = sbt("PTg", [128, 1024], BF16)
                QTdT = [sbt(f"QTd{i_}", [128, 1024], BF16) for i_ in range(2)]
                KtG = sbt("KtG", [128, 1024], BF16)
                VtG = sbt("VtG", [128, 1024], BF16)
                rtl = sbt("rtl", [128, 1024], BF16)
                vnw = sbt("vnw", [128, 1024], BF16)
                PTr = sbt("PTr", [128, 512], BF16)
                KtR = sbt("KtR", [128, 512], BF16)
                identb8 = sbt("identb8", [128, 512], BF16)
                S.op("dve", lambda e: e.tensor_copy(out=identb8[:, :].rearrange("p (h i) -> p h i", i=128),
                                                     in_=ident_f[:, :].unsqueeze(1).to_broadcast([128, 4, 128])),
                     reads=["ident_f"], writes=["identb8"])

            if is_sample:
                AXX = mybir.AxisListType.X
                RES = 7
                cs = sbt("cs_s", [NS, 256], F32)
                S.op("sp", lambda e: e.dma_start(out=cs[:], in_=cst["c_cs_s"].rearrange("(o a) d -> o (a d)", o=1).partition_broadcast(NS)),
                     writes=["cs_s"], dma="cs_s")
                QR = sbt("QR", [NS, 512], F32); KR = sbt("KR", [NS, 512], F32); T1s = sbt("T1s", [NS, 512], F32)
                QKVs = sbt("QKVs", [NS, 3072], F32); QKn = sbt("QKn", [NS, 2048], F32)
                SCc = [sbt(f"SCc{i}", [NS, 3 * 512], F32) for i in range(2)]; CWc = [sbt(f"CWc{i}", [NS, 4 * 512], F32) for i in range(2)]
                ACC = sbt("ACC", [NS, 512], F32); TMPs = sbt("TMPs", [NS, 512], F32)
                gs = sbt("gs", [NS, 64], F32)
                BV = sbt("BV", [NS, 1024], F32); Rr = sbt("Rr", [NS, 1024], F32)
                KMr = sbt("KMr", [NS, 512], F32); KMg = sbt("KMg", [NS, 1024], F32)
                EGd = sbt("EGd", [NS, 128], F32); EGB = sbt("EGB", [128, 128], F32)
                qTr = sbt("qTr", [128, 64], BF16); qTg = sbt("qTg", [128, 128], BF16); kTg = sbt("kTg", [128, 128], F32)
                SRob = sbt("SRob", [128, 1024], BF16); SGob = sbt("SGob", [128, 1024], BF16); Vcb = [sbt("Vcb0", [128, 2048], BF16)] * 2
                ARENA = sbt("ARENA", [128, 8192], F32)
                NPAR = 4
                SRin = [ARENA[:, i * 1024:(i + 1) * 1024] for i in range(NPAR)]
                SGin = [ARENA[:, (4 + i) * 1024:(5 + i) * 1024] for i in range(NPAR)]
                SRo = SRin
                SGo = SGin
                Kc = [ARENA[:, i * 2048:(i + 1) * 2048] for i in range(2)]
                Vc = [ARENA[:, (2 + i) * 2048:(3 + i) * 2048] for i in range(2)]
                ARENA_RES = [f"{n}{i}" for n in ("SRin", "SGin") for i in range(4)]
                QD = [sbt("QD0", [128, 1024], F32)] * 2; PR = sbt("PR", [128, 1024], F32)
                SCs = sbt("SCs", [128, 128], F32); Pm = sbt("Pm", [128, 128], BF16); RD = sbt("RD", [128, 64], F32)

                def rope(src0, dst, dres, scale):
                    x = PJ[0:NS, src0:src0 + 512].rearrange("p (h d) -> p h d", d=128)
                    d3 = dst[:, :].rearrange("p (h d) -> p h d", d=128)
                    t3 = T1s[:, :].rearrange("p (h d) -> p h d", d=128)
                    C = cs[:, 0:128].unsqueeze(1).to_broadcast([NS, 4, 128])
                    S.op("dve", lambda e: e.tensor_tensor(out=t3, in0=x, in1=C, op=ALU.mult), reads=["PJ", "cs_s"], writes=["T1s"])
                    S.op("dve", lambda e: e.tensor_tensor(out=d3[:, :, 0:64], in0=x[:, :, 64:128],
                                                          in1=cs[:, 128:192].unsqueeze(1).to_broadcast([NS, 4, 64]), op=ALU.mult),
                         reads=["PJ", "cs_s"], writes=[dres])
                    S.op("dve", lambda e: e.tensor_tensor(out=d3[:, :, 64:128], in0=x[:, :, 0:64],
                                                          in1=cs[:, 192:256].unsqueeze(1).to_broadcast([NS, 4, 64]), op=ALU.mult),
                         reads=["PJ", "cs_s"], writes=[dres])
                    S.op("dve", lambda e: e.scalar_tensor_tensor(out=dst[:, :], in0=dst[:, :], scalar=1.0, in1=T1s[:, :], op0=ALU.mult, op1=ALU.add),
                         reads=[dres, "T1s"], writes=[dres])
                    if scale != 1.0:
                        S.op("dve", lambda e: e.tensor_scalar(out=dst[:, :], in0=dst[:, :], scalar1=scale, scalar2=None, op0=ALU.mult),
                             reads=[dres], writes=[dres])

                def to_fm(src, sres, nh, dst, dres):
                    b = S.bank()

                    def tr(e):
                        ins = None
                        for h in range(nh):
                            ins = e.transpose(pbank(b, NS, h * NS), src[0:NS, h * 128:(h + 1) * 128], ident_f[0:NS, 0:NS])
                        return ins
                    S.op("pe", tr, reads=[sres, "ident_f"], writes=[f"pb{b}"])
                    S.op("act", lambda e: e.activation(out=dst[:, 0:nh * NS], in_=pbank(b, nh * NS), func=AF.Copy), reads=[f"pb{b}"], writes=[dres])

                def sample_mixers():
                    rope(0, QR, "QR", 1.0)
                    rope(512, KR, "KR", RDK ** -0.5)
                    to_fm(QR, "QR", 4, qTr, "qTr")
                    S.op("sp", lambda e: e.dma_start(out=scs[:, 0:2, :], in_=sconv[:, 1:3, :]), dma="scs_a")
                    S.op("sp", lambda e: e.dma_start(out=scs[:, 2, :], in_=PJ[0:NS, 2048:5120]), reads=["PJ"], dma="scs_b")
                    def conv_loads(cchunk):
                        c0 = cchunk * 512
                        pp = cchunk % 2
                        S.op("sp", lambda e: e.dma_start(out=SCc[pp][:, :].rearrange("p (i c) -> p i c", c=512), in_=sconv[:, :, c0:c0 + 512]),
                             writes=[f"SCc{pp}"], dma=f"SCc{pp}")
                        for i in range(4):
                            S.op("sp", lambda e, i=i: e.dma_start(out=CWc[pp][:, i * 512:(i + 1) * 512], in_=convw[i:i + 1, c0:c0 + 512].partition_broadcast(NS)),
                                 writes=[f"CWc{pp}"], dma=f"CWc{pp}")
                    conv_loads(0)
                    for cchunk in range(6):
                        c0 = cchunk * 512
                        pp = cchunk % 2
                        if cchunk + 1 < 6:
                            conv_loads(cchunk + 1)
                        SC_, CW_, rsc, rcw = SCc[pp], CWc[pp], f"SCc{pp}", f"CWc{pp}"
                        S.op("dve", lambda e, SC_=SC_, CW_=CW_: e.tensor_tensor(out=ACC[:, :], in0=SC_[:, 0:512], in1=CW_[:, 0:512], op=ALU.mult),
                             reads=[rsc, rcw], writes=["ACC"])
                        for i in range(1, 4):
                            src = SC_[:, i * 512:(i + 1) * 512] if i < 3 else PJ[0:NS, 2048 + c0: 2048 + c0 + 512]
                            S.op("dve", lambda e, src=src, i=i, CW_=CW_: e.tensor_tensor(out=TMPs[:, 0:512], in0=src, in1=CW_[:, i * 512:(i + 1) * 512], op=ALU.mult),
                                 reads=[rsc, rcw, "PJ"], writes=["TMPs"])
                            S.op("dve", lambda e: e.tensor_tensor(out=ACC[:, :], in0=ACC[:, :], in1=TMPs[:, 0:512], op=ALU.add), reads=["ACC", "TMPs"], writes=["ACC"])
                        S.op("act", lambda e, c0=c0: e.activation(out=QKVs[:, c0:c0 + 512], in_=ACC[:, :], func=AF.Silu), reads=["ACC"], writes=["QKVs"])
                    S.op("dve", lambda e: e.tensor_tensor(out=QKn[:, :], in0=QKVs[:, 0:2048], in1=QKVs[:, 0:2048], op=ALU.mult), reads=["QKVs"], writes=["QKn"])
                    S.op("dve", lambda e: e.tensor_reduce(out=gs[:, 32:48], in_=QKn[:, :].rearrange("p (h d) -> p h d", d=128), axis=AXX, op=ALU.add),
                         reads=["QKn"], writes=["gs"])
                    S.op("dve", lambda e: e.tensor_scalar(out=gs[:, 32:48], in0=gs[:, 32:48], scalar1=EPS, scalar2=None, op0=ALU.add), reads=["gs"], writes=["gs"])
                    S.op("act", lambda e: e.activation(out=gs[:, 32:48], in_=gs[:, 32:48], func=AF.Ln), reads=["gs"], writes=["gs"])
                    S.op("act", lambda e: e.activation(out=gs[:, 32:48], in_=gs[:, 32:48], func=AF.Exp, scale=-0.5), reads=["gs"], writes=["gs"])
                    S.op("dve", lambda e: e.tensor_scalar(out=gs[:, 32:40], in0=gs[:, 32:40], scalar1=GDK ** -0.5, scalar2=None, op0=ALU.mult), reads=["gs"], writes=["gs"])
                    S.op("dve", lambda e: e.tensor_tensor(out=QKn[:, :].rearrange("p (h d) -> p h d", d=128),
                                                          in0=QKVs[:, 0:2048].rearrange("p (h d) -> p h d", d=128),
                                                          in1=gs[:, 32:48].unsqueeze(2).to_broadcast([NS, 16, 128]), op=ALU.mult),
                         reads=["QKVs", "gs"], writes=["QKn"])
                    to_fm(QKn[:, 0:1024], "QKn", 8, qTg, "qTg")
                    to_fm(QKn[:, 1024:2048], "QKn", 8, kTg, "kTg")
                    ba = PJ[0:NS, 5120:5136]
                    S.op("act", lambda e: e.activation(out=gs[:, 0:8], in_=ba[:, 0:8], func=AF.Sigmoid), reads=["PJ", "gs"], writes=["gs"])
                    S.op("dve", lambda e: e.tensor_tensor(out=gs[:, 8:16], in0=ba[:, 8:16], in1=dtb_bc[0:NS, :], op=ALU.add), reads=["PJ", "dtb_bc", "gs"], writes=["gs"])
                    S.op("act", lambda e: e.activation(out=gs[:, 8:16], in_=gs[:, 8:16], func=AF.Exp), reads=["gs"], writes=["gs"])
                    S.op("dve", lambda e: e.tensor_scalar(out=gs[:, 8:16], in0=gs[:, 8:16], scalar1=1.0, scalar2=None, op0=ALU.add), reads=["gs"], writes=["gs"])
                    S.op("act", lambda e: e.activation(out=gs[:, 8:16], in_=gs[:, 8:16], func=AF.Ln), reads=["gs"], writes=["gs"])
                    S.op("dve", lambda e: e.tensor_tensor(out=gs[:, 8:16], in0=gs[:, 8:16], in1=nega[0:NS, :], op=ALU.mult), reads=["gs", "nega"], writes=["gs"])
                    S.op("act", lambda e: e.activation(out=gs[:, 16:24], in_=gs[:, 8:16], func=AF.Exp), reads=["gs"], writes=["gs"])
                    S.op("dve", lambda e: e.scalar_tensor_tensor(out=gs[:, 24:32], in0=gs[:, 16:24], scalar=-1.0, in1=gs[:, 0:8], op0=ALU.mult, op1=ALU.mult),
                         reads=["gs"], writes=["gs"])
                    S.op("dve", lambda e: e.tensor_tensor(out=BV[:, :].rearrange("p (h d) -> p h d", d=128),
                                                          in0=QKVs[:, 2048:3072].rearrange("p (h d) -> p h d", d=128),
                                                          in1=gs[:, 0:8].unsqueeze(2).to_broadcast([NS, 8, 128]), op=ALU.mult),
                         reads=["QKVs", "gs"], writes=["BV"])
                    S.op("dve", lambda e: e.tensor_tensor(out=EGd[:, :].rearrange("p (b h) -> p b h", h=8),
                                                          in0=ident_f[0:NS, 0:NS].unsqueeze(2).to_broadcast([NS, NS, 8]),
                                                          in1=gs[:, 16:24].unsqueeze(1).to_broadcast([NS, NS, 8]), op=ALU.mult),
                         reads=["ident_f", "gs"], writes=["EGd"])
                    bq = S.bank()
                    S.op("pe", lambda e: e.matmul(pbank(bq, 128), lhsT=ones_f[0:NS, :], rhs=EGd[:, :], start=True, stop=True),
                         reads=["ones_f", "EGd"], writes=[f"pb{bq}"])
                    S.op("act", lambda e: e.activation(out=EGB[:, :], in_=pbank(bq, 128), func=AF.Copy), reads=[f"pb{bq}"], writes=["EGB"])
                    S.reserved = {RES}

                    def loads(b):
                        p = b % NPAR
                        S.op("sp", lambda e: e.dma_start(out=SRin[p][:, :].rearrange("d (h v) -> d h v", v=RDV), in_=sret[b].rearrange("h d v -> d h v")),
                             writes=[f"SRin{p}"], dma=f"SRin{p}")
                        S.op("sp", lambda e: e.dma_start(out=SGin[p][:, :].rearrange("d (h v) -> d h v", v=GDV), in_=sgdn[b].rearrange("h d v -> d h v")),
                             writes=[f"SGin{p}"], dma=f"SGin{p}")
                    for b_ in range(NPAR - 1):
                        loads(b_)
                    for b in range(NS):
                        p = b % NPAR
                        if b + NPAR - 1 < NS:
                            loads(b + NPAR - 1)
                        eb = ident_f[0:NS, b:b + 1]
                        S.op("dve", lambda e, eb=eb: e.tensor_scalar(out=KMr[:, :], in0=KR[:, :], scalar1=eb, scalar2=None, op0=ALU.mult), reads=["KR", "ident_f"], writes=["KMr"])
                        S.op("dve", lambda e, eb=eb: e.tensor_scalar(out=KMg[:, :], in0=QKn[:, 1024:2048], scalar1=eb, scalar2=None, op0=ALU.mult),
                             reads=["QKn", "ident_f"], writes=["KMg"])
                        def ret_chain(b=b, p=p):
                            yield
                            while S.bank_i % 2 != 0:
                                S.bank()
                            b0 = S.bank(); b1 = S.bank()
                            if b1 != b0 + 1:
                                while S.bank_i % 2 != 0:
                                    S.bank()
                                b0 = S.bank(); b1 = S.bank()

                            def mm_a(e, b0=b0):
                                ins = None
                                for h in range(RH):
                                    ins = e.matmul(PS[:, b0 * 512 + h * 256: b0 * 512 + (h + 1) * 256], lhsT=KMr[0:NS, h * 128:(h + 1) * 128],
                                                   rhs=PJ[0:NS, 1024 + h * 256: 1024 + (h + 1) * 256], start=True, stop=True)
                                return ins
                            S.op("pe", mm_a, reads=["KMr", "PJ"], writes=[f"pb{b0}", f"pb{b1}"])
                            for h in range(RH):
                                S.op("dve", lambda e, h=h, b0=b0, p=p: e.scalar_tensor_tensor(
                                    out=SRo[p][:, h * 256:(h + 1) * 256], in0=SRin[p][:, h * 256:(h + 1) * 256], scalar=float(_GAM[h]),
                                    in1=PS[:, b0 * 512 + h * 256: b0 * 512 + (h + 1) * 256], op0=ALU.mult, op1=ALU.add),
                                    reads=[f"SRin{p}", f"pb{b0}", f"pb{b1}"], writes=[f"SRin{p}"])

                            yield
                            S.op("act", lambda e, p=p: e.activation(out=SRob[:, :], in_=SRo[p][:, :], func=AF.Copy), reads=[f"SRin{p}"], writes=["SRob"])

                            def mm_b(e, b=b, p=p):
                                ins = None
                                for h in range(RH):
                                    for c in range(2):
                                        ins = e.matmul(pbank(RES, 1, (h * 2 + c) * NS + b), lhsT=SRob[:, h * 256 + c * 128: h * 256 + (c + 1) * 128],
                                                       rhs=qTr[:, h * NS + b: h * NS + b + 1], start=True, stop=True)
                                return ins
                            S.op("pe", mm_b, reads=["SRob", "qTr"], writes=[f"pb{RES}"])
                            S.op("act", lambda e, b=b, p=p: e.dma_start(out=srs[b].rearrange("h d v -> d h v"), in_=SRo[p][:, :].rearrange("d (h v) -> d h v", v=RDV)),
                                 reads=[f"SRin{p}"], dma=f"SRo{p}o")
                            yield
                        def gdn_chain(b=b, p=p):
                            yield
                            while S.bank_i % 2 != 0:
                                S.bank()
                            c0_ = S.bank(); c1_ = S.bank()
                            if c1_ != c0_ + 1:
                                while S.bank_i % 2 != 0:
                                    S.bank()
                                c0_ = S.bank(); c1_ = S.bank()

                            def mm_1(e, c0_=c0_, p=p):
                                ins = None
                                for h in range(GH):
                                    ins = e.matmul(PS[0:NS, c0_ * 512 + h * 128: c0_ * 512 + (h + 1) * 128], lhsT=kTg[:, h * NS:(h + 1) * NS],
                                                   rhs=SGin[p][:, h * 128:(h + 1) * 128], start=True, stop=True)
                                return ins
                            S.op("pe", mm_1, reads=["kTg", f"SGin{p}"], writes=[f"pb{c0_}", f"pb{c1_}"])
                            S.op("dve", lambda e, c0_=c0_: e.tensor_tensor(out=Rr[:, :].rearrange("p (h d) -> p h d", d=128),
                                                                           in0=PS[0:NS, c0_ * 512: c0_ * 512 + 1024].rearrange("p (h d) -> p h d", d=128),
                                                                           in1=gs[:, 24:32].unsqueeze(2).to_broadcast([NS, 8, 128]), op=ALU.mult),
                                 reads=[f"pb{c0_}", f"pb{c1_}", "gs"], writes=["Rr"])
                            S.op("dve", lambda e: e.tensor_tensor(out=Rr[:, :], in0=Rr[:, :], in1=BV[:, :], op=ALU.add), reads=["Rr", "BV"], writes=["Rr"])
                            yield
                            while S.bank_i % 2 != 0:
                                S.bank()
                            d0_ = S.bank(); d1_ = S.bank()
                            if d1_ != d0_ + 1:
                                while S.bank_i % 2 != 0:
                                    S.bank()
                                d0_ = S.bank(); d1_ = S.bank()

                            def mm_2(e, d0_=d0_):
                                ins = None
                                for h in range(GH):
                                    ins = e.matmul(PS[:, d0_ * 512 + h * 128: d0_ * 512 + (h + 1) * 128], lhsT=KMg[0:NS, h * 128:(h + 1) * 128],
                                                   rhs=Rr[0:NS, h * 128:(h + 1) * 128], start=True, stop=True)
                                return ins
                            S.op("pe", mm_2, reads=["KMg", "Rr"], writes=[f"pb{d0_}", f"pb{d1_}"])
                            S.op("dve", lambda e, b=b, p=p: e.tensor_tensor(out=SGo[p][:, :].rearrange("p (h d) -> p h d", d=128),
                                                                          in0=SGin[p][:, :].rearrange("p (h d) -> p h d", d=128),
                                                                          in1=EGB[:, b * 8:(b + 1) * 8].unsqueeze(2).to_broadcast([128, 8, 128]), op=ALU.mult),
                                 reads=[f"SGin{p}", "EGB"], writes=[f"SGin{p}"])
                            S.op("dve", lambda e, d0_=d0_, p=p: e.tensor_tensor(out=SGo[p][:, :], in0=SGo[p][:, :], in1=PS[:, d0_ * 512: d0_ * 512 + 1024], op=ALU.add),
                                 reads=[f"SGin{p}", f"pb{d0_}", f"pb{d1_}"], writes=[f"SGin{p}"])

                            yield
                            S.op("act", lambda e, p=p: e.activation(out=SGob[:, :], in_=SGo[p][:, :], func=AF.Copy), reads=[f"SGin{p}"], writes=["SGob"])

                            def mm_3(e, b=b, p=p):
                                ins = None
                                for h in range(GH):
                                    ins = e.matmul(pbank(RES, 1, 128 + h * NS + b), lhsT=SGob[:, h * 128:(h + 1) * 128],
                                                   rhs=qTg[:, h * NS + b: h * NS + b + 1], start=True, stop=True)
                                return ins
                            S.op("pe", mm_3, reads=["SGob", "qTg"], writes=[f"pb{RES}"])
                            S.op("act", lambda e, b=b, p=p: e.dma_start(out=sgs[b].rearrange("h d v -> d h v"), in_=SGo[p][:, :].rearrange("d (h v) -> d h v", v=GDV)),
                                 reads=[f"SGin{p}"], dma=f"SGo{p}o")
                            yield
                        _gens = [ret_chain(), gdn_chain()]
                        while _gens:
                            for _g in list(_gens):
                                try:
                                    next(_g)
                                except StopIteration:
                                    _gens.remove(_g)
                    S.op("act", lambda e: e.activation(out=OR[:, 0:128], in_=pbank(RES, 128), func=AF.Copy), reads=[f"pb{RES}"], writes=["XOR1"])
                    S.op("act", lambda e: e.activation(out=OG[:, 0:128], in_=pbank(RES, 128, 128), func=AF.Copy), reads=[f"pb{RES}"], writes=["OG"])
                    S.reserved = set()

                def sample_xattn(XQ, OX):
                    S.reserved = {RES}

                    def loads(b):
                        p = b % 2
                        S.op("sp", lambda e: e.dma_start(out=Kc[p][:, :].rearrange("m (t d) -> m t d", d=D), in_=cmk[b].rearrange("(t m) d -> m t d", m=128)),
                             writes=[f"Kc{p}"] + ARENA_RES, dma=f"Kc{p}")
                        S.op("sp", lambda e: e.dma_start(out=Vc[p][:, :].rearrange("m (t d) -> m t d", d=D), in_=cmv[b].rearrange("(t m) d -> m t d", m=128)),
                             writes=[f"Vc{p}"] + ARENA_RES, dma=f"Vc{p}")

                    qinfo = {}

                    def qstage(b):
                        p = b % 2
                        S.op("pool", lambda e: e.tensor_tensor(out=QD[p][:, :].rearrange("p (c d) -> p c d", d=128),
                                                               in0=ident_f[:, :].unsqueeze(1).to_broadcast([128, 8, 128]),
                                                               in1=XQ[:, :].rearrange("p (c t) -> p c t", t=NS)[:, :, b:b + 1].to_broadcast([128, 8, 128]), op=ALU.mult),
                             reads=["ident_f", "ORg"], writes=["QD0"])
                        while S.bank_i % 2 != 0:
                            S.bank()
                        b0 = S.bank(); b1 = S.bank()
                        if b1 != b0 + 1:
                            while S.bank_i % 2 != 0:
                                S.bank()
                            b0 = S.bank(); b1 = S.bank()

                        def mm_q(e):
                            e.matmul(pbank(b0), lhsT=ones_f[:], rhs=QD[p][:, 0:512], start=True, stop=True)
                            return e.matmul(pbank(b1), lhsT=ones_f[:], rhs=QD[p][:, 512:1024], start=True, stop=True)
                        S.op("pe", mm_q, reads=["ones_f", "QD0"], writes=[f"pb{b0}", f"pb{b1}"])
                        qinfo[b] = (b0, b1)
                    loads(0)
                    qstage(0)
                    for b in range(NS):
                        p = b % 2
                        if b + 1 < NS:
                            loads(b + 1)
                            qstage(b + 1)
                        b0, b1 = qinfo[b]
                        S.op("act", lambda e, p=p: e.activation(out=Vcb[p][:, :], in_=Vc[p][:, :], func=AF.Copy), reads=[f"Vc{p}"], writes=["Vcb0"])
                        for mt in range(2):
                            S.op("dve", lambda e, mt=mt, p=p, b0=b0: e.tensor_tensor(out=PR[:, :], in0=PS[:, b0 * 512: b0 * 512 + 1024],
                                                                                   in1=Kc[p][:, mt * D:(mt + 1) * D], op=ALU.mult),
                                 reads=[f"pb{b0}", f"pb{b1}", f"Kc{p}"], writes=["PR"])
                            S.op("dve", lambda e, mt=mt, b=b: e.tensor_reduce(out=SCs[:, b * 8 + mt * 4: b * 8 + mt * 4 + 4],
                                                                            in_=PR[:, :].rearrange("p (h d) -> p h d", d=XHD), axis=AXX, op=ALU.add),
                                 reads=["PR"], writes=["SCs"])
                        S.op("act", lambda e, b=b: e.activation(out=Pm[:, b * 8:(b + 1) * 8], in_=SCs[:, b * 8:(b + 1) * 8], func=AF.Exp, scale=XHD ** -0.5),
                             reads=["SCs"], writes=["Pm"])

                        def mm_o(e, b=b, p=p):
                            ins = None
                            for ch in range(8):
                                h = ch // 2
                                for mt in range(2):
                                    ins = e.matmul(pbank(RES, 1, ch * NS + b), lhsT=Vcb[p][:, mt * D + ch * 128: mt * D + (ch + 1) * 128],
                                                   rhs=Pm[:, b * 8 + mt * 4 + h: b * 8 + mt * 4 + h + 1], start=(mt == 0), stop=(mt == 1))
                            for mt in range(2):
                                ins = e.matmul(pbank(RES, 4, 128 + b * 4), lhsT=ones_b[:], rhs=Pm[:, b * 8 + mt * 4: b * 8 + mt * 4 + 4],
                                               start=(mt == 0), stop=(mt == 1))
                            return ins
                        S.op("pe", mm_o, reads=["Vcb0", "Pm", "ones_b"], writes=[f"pb{RES}"])
                    S.op("dve", lambda e: e.reciprocal(out=RD[:, :], in_=pbank(RES, 64, 128)), reads=[f"pb{RES}"], writes=["RD"])
                    for ch in range(8):
                        h = ch // 2
                        S.op("dve", lambda e, ch=ch, h=h: e.tensor_tensor(out=OX[:, ch * NS:(ch + 1) * NS], in0=pbank(RES, NS, ch * NS),
                                                                        in1=RD[:, :].rearrange("p (b h) -> p b h", h=4)[:, :, h], op=ALU.mult),
                             reads=[f"pb{RES}", "RD"], writes=["OGg"])
                    S.reserved = set()

            def load_x(blk):
                Xn = XOR[blk % 2]; xr = f"XOR{blk % 2}"
                t0 = blk * TBv
                if is_sample:
                    S.op("sp", lambda e: e.dma_start(out=Xn[0:NS, 0:D], in_=xs), writes=[xr], dma=xr)
                else:
                    S.op("sp", lambda e: e.dma_start(out=Xn[:, :].rearrange("p (a d) -> p a d", d=D),
                                                     in_=xp[t0:t0 + TBv, :].rearrange("(a p) d -> p a d", p=128)),
                         writes=[xr], dma=xr)

            def pre_norm(blk):
                Xn = XOR[blk % 2]; xr = f"XOR{blk % 2}"
                for tau in range(NTv):
                    norm_transpose(Xn[0:tw, tau * D:(tau + 1) * D], xr, gmix_c, "gmix_c", hnT, "hnT", tw, tau * tw, TBv)

            def load_tabs(blk):
                t0 = blk * TBv
                S.op("sp", lambda e: e.dma_start(out=cosb[:], in_=cst["c_cos"][:, t0:t0 + TBv]), writes=["cosb"], dma="cosb")
                S.op("sp", lambda e: e.dma_start(out=sinb[:], in_=cst["c_sin"][:, t0:t0 + TBv]), writes=["sinb"], dma="sinb")

            def do_block(blk):
                if stage < 1:
                    return
                Xb = XOR[blk % 2]; xres = f"XOR{blk % 2}"
                OR = XOR[(blk + 1) % 2][:, 0:8 * TBv]; or_res = f"XOR{(blk + 1) % 2}"
                t0 = blk * TBv
                if blk == 0:
                    load_x(0)
                if not is_sample:
                    load_tabs(blk)
                if blk == 0:
                    pre_norm(0)

                pend = []
                qkv_done = [False]
                LATE = ("ga", "gb")
                main_tiles = W_IN_TILES if is_sample else [t_ for t_ in W_IN_TILES if t_[0] not in LATE]
                late_tiles = [] if is_sample else [t_ for t_ in W_IN_TILES if t_[0] in LATE]
                for (kind, c0, ncols) in main_tiles:
                    if kinds is not None and kind not in kinds:
                        continue
                    if kind == "z" and not is_sample and pend is not None and (pend or not qkv_done[0]):
                        for p_ in pend:
                            p_()
                        del pend[:]
                        qkv_done[0] = True
                        for (ssb, sres, dstb, dres) in ((OR, or_res, GQ, "GQ"), (OG, "OG", GK, "GK")):
                            S.op("act", lambda e, ssb=ssb: e.activation(out=ssb[:, :], in_=ssb[:, :], func=AF.Ln), reads=[sres], writes=[sres])
                            S.op("act", lambda e, ssb=ssb: e.activation(out=ssb[:, :], in_=ssb[:, :], func=AF.Exp, scale=-0.5), reads=[sres], writes=[sres])
                            S.op("dve", lambda e, ssb=ssb, dstb=dstb: e.tensor_tensor(out=dstb[:, :], in0=dstb[:, :], in1=ssb[:, :], op=ALU.mult),
                                 reads=[sres, dres], writes=[dres, "FA"])
                    if is_sample and kind in PJ_OFF:
                        pj0 = PJ_OFF[kind][1] + (c0 - PJ_OFF[kind][0])

                        def cons(tau, b, pj0=pj0, ncols=ncols):
                            S.op("act", lambda e: e.activation(out=PJ[0:NS, pj0:pj0 + ncols], in_=pbank(b, ncols)[0:NS, :], func=AF.Copy),
                                 reads=[f"pb{b}"], writes=["PJ"])
                        proj_tm(w_in, 0, 8, c0, ncols, hnT, "hnT", TBv, 1, cons, tw=tw)
                        continue
                    if kind in ("rq", "rk"):
                        dstb = RQ if kind == "rq" else RK

                        def cons(j0, b, dstb=dstb, kind=kind, cb=(c0 - (0 if kind == "rq" else 512)) // 128):
                            j = cb + j0
                            tmpA, rA = nxt("tmpA"); tmpB, rB = nxt("tmpB"); tmpb, rb = nxt("tmpb")
                            S.op("act", lambda e: e.activation(out=tmpb[:], in_=pbank(b, TBv), func=AF.Copy), reads=[f"pb{b}"], writes=[rb])
                            b2 = S.bank()
                            S.op("pe", lambda e: e.matmul(pbank(b2, TBv), lhsT=perm_b[:], rhs=tmpb[:], start=True, stop=True),
                                 reads=[rb, "perm_b"], writes=[f"pb{b2}"])
                            S.op("dve", lambda e: e.tensor_tensor(out=tmpA[:], in0=pbank(b, TBv), in1=cosb[:], op=ALU.mult),
                                 reads=[f"pb{b}", "cosb"], writes=[rA])
                            S.op("dve", lambda e: e.tensor_tensor(out=tmpB[:], in0=pbank(b2, TBv), in1=sinb[:], op=ALU.mult),
                                 reads=[f"pb{b2}", "sinb"], writes=[rB])
                            S.op("dve", lambda e: e.tensor_tensor(out=dstb[:, j * TBv:(j + 1) * TBv], in0=tmpA[:], in1=tmpB[:], op=ALU.add),
                                 reads=[rA, rB], writes=[kind.upper()])
                            if kind == "rq":
                                S.op("dve", lambda e: e.tensor_tensor(
                                    out=RQd[:, j * TBv:(j + 1) * TBv].rearrange("p (a i) -> p a i", i=128),
                                    in0=RQ[:, j * TBv:(j + 1) * TBv].rearrange("p (a i) -> p a i", i=128),
                                    in1=rqd[:, j * 128:(j + 1) * 128].unsqueeze(1).to_broadcast([128, NTv, 128]), op=ALU.mult),
                                    reads=["RQ", "rqd"], writes=["RQd"])
                        proj_fm(w_in, 0, 8, c0, ncols, hnT, "hnT", TBv, TBv, cons)
                    elif kind == "rv":
                        hv = c0 - 1024

                        def cons(tau, b, hv=hv, ncols=ncols):
                            S.op("act", lambda e: e.activation(out=VtR[:, tau * 1024 + hv: tau * 1024 + hv + ncols], in_=pbank(b, ncols), func=AF.Copy),
                                 reads=[f"pb{b}"], writes=["VtR"])
                        proj_tm(w_in, 0, 8, c0, ncols, hnT, "hnT", TBv, NTv, cons)
                    elif kind in ("rg", "z", "ga", "gb"):
                        base = {"rg": 2048, "z": 6144, "ga": 7184, "gb": 8208}[kind]
                        dstb = {"rg": RG, "z": Z, "ga": GA, "gb": GB}[kind]
                        fn = AF.Silu if kind in ("rg", "z") else AF.Tanh
                        fsc = 1.0 if kind in ("rg", "z") else 0.5
                        cb = (c0 - base) // 128

                        def cons(j, b, dstb=dstb, fn=fn, cb=cb, kind=kind, fsc=fsc):
                            S.op("act", lambda e: e.activation(out=dstb[:, (cb + j) * TBv:(cb + j + 1) * TBv], in_=pbank(b, TBv), func=fn, scale=fsc),
                                 reads=[f"pb{b}"], writes=[kind.upper()])
                        proj_fm(w_in, 0, 8, c0, ncols, hnT, "hnT", TBv, TBv, cons)
                    elif kind == "ba":
                        def cons(tau, b):
                            S.op("dve", lambda e: e.tensor_copy(out=BA[:, tau * 16:(tau + 1) * 16], in_=pbank(b, 16)), reads=[f"pb{b}"], writes=["BA"])
                        proj_tm(w_in, 0, 8, c0, ncols, hnT, "hnT", TBv, NTv, cons)
                    else:
                        cb = (c0 - 3072) // 128

                        def cons(j, b, cb=cb, blk=blk):
                            cc = cb + j
                            tmpb, rb = nxt("tmpb")
                            CACC, rCA = nxt("CACC"); CIN, rCI = nxt("CIN")
                            while len(pend) > 1:
                                pend.pop(0)()
                            S.op("act", lambda e: e.activation(out=CIN[:, 0:3], in_=HALO[:, cc * 3:cc * 3 + 3], func=AF.Copy), reads=["HALO"], writes=[rCI])
                            S.op("act", lambda e: e.activation(out=CIN[:, 3:3 + TBv], in_=pbank(b, TBv), func=AF.Copy), reads=[f"pb{b}", rCI], writes=[rCI])
                            S.op("act", lambda e: e.activation(out=HALO[:, cc * 3:cc * 3 + 3], in_=CIN[:, TBv:TBv + 3], func=AF.Copy), reads=[rCI], writes=["HALO"])
                            S.op("dve", lambda e: e.tensor_scalar(out=CACC[:], in0=CIN[:, 0:TBv], scalar1=cw[:, cc:cc + 1], scalar2=None, op0=ALU.mult),
                                 reads=[rCI, "cw"], writes=[rCA])
                            for i in range(1, 4):
                                S.op("dve", lambda e, i=i: e.scalar_tensor_tensor(out=CACC[:], in0=CIN[:, i:i + TBv], scalar=cw[:, i * 24 + cc:i * 24 + cc + 1],
                                                                                 in1=CACC[:], op0=ALU.mult, op1=ALU.add),
                                     reads=[rCI, "cw", rCA], writes=[rCA])

                            def stage_b():
                                if cc >= 16:
                                    S.op("act", lambda e: e.activation(out=GV[:, (cc - 16) * TBv:(cc - 15) * TBv], in_=CACC[:], func=AF.Silu), reads=[rCA], writes=["GV", "FA"])
                                    return
                                dstb, dres, hh = (GQ, "GQ", cc) if cc < 8 else (GK, "GK", cc - 8)
                                ssb, sres = (OR, or_res) if cc < 8 else (OG, "OG")
                                dsl = dstb[:, hh * TBv:(hh + 1) * TBv]
                                S.op("act", lambda e: e.activation(out=dsl, in_=CACC[:], func=AF.Silu), reads=[rCA], writes=[dres, "FA"])
                                S.op("act", lambda e: e.activation(out=tmpb[:], in_=dsl, func=AF.Square), reads=[dres], writes=[rb])
                                b2 = S.bank()
                                lh = ones128_b if cc < 8 else ones_b
                                S.op("pe", lambda e: e.matmul(pbank(b2, TBv), lhsT=lh[:], rhs=tmpb[:], start=True, stop=True),
                                     reads=[rb, "ones_b", "ones128_b"], writes=[f"pb{b2}"])
                                eps = EPS * 128 if cc < 8 else EPS
                                S.op("dve", lambda e: e.tensor_scalar(out=ssb[:, hh * TBv:(hh + 1) * TBv], in0=pbank(b2, TBv), scalar1=eps, scalar2=None, op0=ALU.add),
                                     reads=[f"pb{b2}"], writes=[sres])
                            pend.append(stage_b)
                        proj_fm(w_in, 0, 8, c0, ncols, hnT, "hnT", TBv, TBv, cons)
                if blk == nblk_v - 1 and not is_sample:
                    for i in range(3):
                        S.op("sp", lambda e, i=i: e.dma_start(out=scp[i].rearrange("(c p) -> p c", p=128),
                                                               in_=HALO[:, :].rearrange("p (c i) -> p c i", i=3)[:, :, i],
                                                               allow_slow_non_contiguous=True),
                             reads=["HALO"], dma="scp")

                def gdn_small(tau):
                    gsm = gsmT[tau]; gres = f"gsm{tau}"
                    ba = BA[:, tau * 16:(tau + 1) * 16]
                    S.op("act", lambda e, ba=ba: e.activation(out=gsm[:, 0:8], in_=ba[:, 0:8], func=AF.Sigmoid), reads=["BA"], writes=[gres])
                    S.op("dve", lambda e, ba=ba: e.tensor_tensor(out=gsm[:, 8:16], in0=ba[:, 8:16], in1=dtb_bc[:], op=ALU.add), reads=["BA", "dtb_bc", gres], writes=[gres])
                    S.op("act", lambda e: e.activation(out=gsm[:, 8:16], in_=gsm[:, 8:16], func=AF.Exp), reads=[gres], writes=[gres])
                    S.op("dve", lambda e: e.tensor_scalar(out=gsm[:, 8:16], in0=gsm[:, 8:16], scalar1=1.0, scalar2=None, op0=ALU.add), reads=[gres], writes=[gres])
                    S.op("act", lambda e: e.activation(out=gsm[:, 8:16], in_=gsm[:, 8:16], func=AF.Ln), reads=[gres], writes=[gres])
                    S.op("dve", lambda e: e.tensor_tensor(out=gsm[:, 8:16], in0=gsm[:, 8:16], in1=nega[:], op=ALU.mult), reads=[gres, "nega"], writes=[gres])
                    bq = S.bank()
                    S.op("pe", lambda e, bq=bq: e.matmul(pbank(bq, 8), lhsT=tri_f[:], rhs=gsm[:, 8:16], start=True, stop=True),
                         reads=["tri_f", gres], writes=[f"pb{bq}"])
                    S.op("dve", lambda e, bq=bq: e.tensor_copy(out=gsm[:, 16:24], in_=pbank(bq, 8)), reads=[f"pb{bq}", gres], writes=[gres])
                    S.op("act", lambda e: e.activation(out=gsm[:, 24:32], in_=gsm[:, 16:24], func=AF.Exp), reads=[gres], writes=[gres])
                    S.op("dve", lambda e: e.tensor_scalar(out=gsm[:, 24:32], in0=gsm[:, 24:32], scalar1=-1.0, scalar2=None, op0=ALU.mult), reads=[gres], writes=[gres])
                for tau in (range(NTv) if not is_sample else []):
                    gdn_small(tau)

                if stage < 2:
                    return
                if is_sample:
                    sample_mixers()
                def ret_gen():
                    for tau in (range(NTv) if not is_sample else []):
                        tc0 = tau * 128
                        yield
                        b = S.bank()

                        def mm_sc(e, b=b, tc0=tc0):
                            ins = None
                            for h in range(RH):
                                ins = e.matmul(pbank(b, 128, h * 128), lhsT=RK[:, h * TBv + tc0: h * TBv + tc0 + 128],
                                               rhs=RQ[:, h * TBv + tc0: h * TBv + tc0 + 128], start=True, stop=True)
                            return ins
                        S.op("pe", mm_sc, reads=["RK", "RQ"], writes=[f"pb{b}"])
                        S.op("dve", lambda e, b=b: e.tensor_tensor(out=PTr[:], in0=pbank(b), in1=rdt[:], op=ALU.mult),
                             reads=[f"pb{b}", "rdt"], writes=["PTr"])
                        yield
                        b3 = S.bank()

                        def tr_k(e, b3=b3, tc0=tc0):
                            ins = None
                            for h in range(RH):
                                ins = e.transpose(pbank_bf(b3, 128, h * 128), RK[:, h * TBv + tc0: h * TBv + tc0 + 128], ident_b[:])
                            return ins
                        S.op("pe", tr_k, reads=["RK", "ident_b"], writes=[f"pb{b3}"])
                        S.op("dve", lambda e, b3=b3: e.tensor_tensor(out=KtR[:, :].rearrange("p (h d) -> p h d", d=128),
                                                                   in0=pbank_bf(b3, 512).rearrange("p (h d) -> p h d", d=128),
                                                                   in1=rkd[:, :].unsqueeze(2).to_broadcast([128, RH, 128]), op=ALU.mult),
                             reads=[f"pb{b3}", "rkd"], writes=["KtR"])
                        for hp in range(2):
                            yield
                            b2 = S.bank()

                            def mm_o(e, b2=b2, hp=hp, tau=tau, tc0=tc0):
                                ins = None
                                for hh in range(2):
                                    h = hp * 2 + hh
                                    for c in range(2):
                                        o = pbank(b2, 128, (hh * 2 + c) * 128)
                                        e.matmul(o, lhsT=SRb[:, h * RDV + c * 128: h * RDV + (c + 1) * 128],
                                                 rhs=RQd[:, h * TBv + tc0: h * TBv + tc0 + 128], start=True, stop=False)
                                        ins = e.matmul(o, lhsT=VtR[:, tau * 1024 + h * RDV + c * 128: tau * 1024 + h * RDV + (c + 1) * 128],
                                                       rhs=PTr[:, h * 128:(h + 1) * 128], start=False, stop=True)
                                return ins
                            S.op("pe", mm_o, reads=["SRb", "RQd", "VtR", "PTr"], writes=[f"pb{b2}"])
                            dst = OR[:, hp * 4 * TBv:(hp + 1) * 4 * TBv].rearrange("p (c t) -> p c t", t=TBv)[:, :, tc0:tc0 + 128]
                            S.op("act", lambda e, b2=b2, dst=dst: e.activation(out=dst, in_=pbank(b2).rearrange("p (c t) -> p c t", t=128), func=AF.Copy),
                                 reads=[f"pb{b2}"], writes=[or_res])
                        for hp in range(2):
                            yield
                            b4 = S.bank()

                            def mm_s(e, b4=b4, hp=hp, tau=tau):
                                ins = None
                                for hh in range(2):
                                    h = hp * 2 + hh
                                    ins = e.matmul(pbank(b4, 256, hh * 256), lhsT=KtR[:, h * 128:(h + 1) * 128],
                                                   rhs=VtR[:, tau * 1024 + h * RDV: tau * 1024 + (h + 1) * RDV], start=True, stop=True)
                                return ins
                            S.op("pe", mm_s, reads=["KtR", "VtR"], writes=[f"pb{b4}"])
                            for hh in range(2):
                                h = hp * 2 + hh
                                S.op("dve", lambda e, b4=b4, hh=hh, h=h: e.scalar_tensor_tensor(
                                    out=SR[:, h * RDV:(h + 1) * RDV], in0=SR[:, h * RDV:(h + 1) * RDV], scalar=float(_GAM[h] ** 128),
                                    in1=pbank(b4, 256, hh * 256), op0=ALU.mult, op1=ALU.add),
                                    reads=[f"pb{b4}", "SR"], writes=["SR"])
                            S.op("act", lambda e, hp=hp: e.activation(out=SRb[:, hp * 512:(hp + 1) * 512], in_=SR[:, hp * 512:(hp + 1) * 512], func=AF.Copy),
                                 reads=["SR"], writes=["SRb"])
                    yield
                _retg = ret_gen()

                if stage < 3:
                    return
                def gdn_tile(tau):
                    gsm = gsmT[tau]; gres = f"gsm{tau}"
                    tc0 = tau * 128
                    QTd = QTdT[tau % 2]; qres = f"QTd{tau % 2}"
                    def setup():
                        yield
                        S.op("dve", lambda e: e.tensor_tensor(out=G2[:, :].rearrange("p (h i) -> p h i", i=128),
                                                              in0=tri_f[:, :].unsqueeze(1).to_broadcast([128, 8, 128]),
                                                              in1=gsm[:, 8:16].unsqueeze(2).to_broadcast([128, 8, 128]), op=ALU.mult),
                             reads=["tri_f", gres], writes=["GBm"])
                        yield
                        while S.bank_i % 2 != 0:
                            S.bank()
                        bg = S.bank(); bg2 = S.bank()

                        def mm_g(e, bg=bg, bg2=bg2):
                            e.matmul(pbank(bg), lhsT=ones_f[:], rhs=G2[:, 0:512], start=True, stop=True)
                            return e.matmul(pbank(bg2), lhsT=ones_f[:], rhs=G2[:, 512:1024], start=True, stop=True)
                        S.op("pe", mm_g, reads=["ones_f", "GBm"], writes=[f"pb{bg}", f"pb{bg2}"])
                        gbc = PS[:, bg * 512: bg * 512 + 1024]
                        S.op("dve", lambda e, gbc=gbc: e.tensor_tensor(out=D1[:, :].rearrange("p (h i) -> p h i", i=128),
                                                                       in0=gbc.rearrange("p (h i) -> p h i", i=128),
                                                                       in1=gsm[:, 16:24].unsqueeze(2).to_broadcast([128, 8, 128]), op=ALU.subtract),
                             reads=[f"pb{bg}", f"pb{bg2}", gres], writes=["D1"])
                        S.op("act", lambda e, gbc=gbc: e.activation(out=EGC[:], in_=gbc, func=AF.Exp), reads=[f"pb{bg}", f"pb{bg2}"], writes=["GBm"])
                        yield
                        S.op("dve", lambda e, tc0=tc0: e.tensor_tensor(out=QTd[:, :].rearrange("p (h i) -> p h i", i=128),
                                                                       in0=GQ[:, :].rearrange("p (h t) -> p h t", t=TBv)[:, :, tc0:tc0 + 128],
                                                                       in1=EGC[:, :].rearrange("p (h i) -> p h i", i=128), op=ALU.mult),
                             reads=["GQ", "GBm"], writes=[qres])
                        yield
                        S.op("act", lambda e: e.activation(out=gsm[:, 40:48], in_=EGC[:, :].rearrange("p (h i) -> p h i", i=128)[:, :, 127], func=AF.Copy),
                             reads=["GBm", gres], writes=[gres])
                        yield
                        S.op("act", lambda e: e.activation(out=gsm[:, 32:40], in_=D1[:, :].rearrange("p (h i) -> p h i", i=128)[:, :, 127], func=AF.Exp),
                             reads=["D1", gres], writes=[gres])
                        yield
                        S.op("dve", lambda e: e.tensor_tensor(out=D1[:, :].rearrange("p (h i) -> p h i", i=128),
                                                              in0=D1[:, :].rearrange("p (h i) -> p h i", i=128),
                                                              in1=mincl[:, :].unsqueeze(1).to_broadcast([128, 8, 128]), op=ALU.add),
                             reads=["D1", "mincl"], writes=["D1"])
                        yield
                        S.op("act", lambda e: e.activation(out=D1[:], in_=D1[:], func=AF.Exp), reads=["D1"], writes=["D1"])
                        yield
                        S.op("dve", lambda e: e.tensor_tensor(out=GBm[:, :].rearrange("p (h i) -> p h i", i=128),
                                                              in0=D1[:, :].rearrange("p (h i) -> p h i", i=128),
                                                              in1=strict[:, :].unsqueeze(1).to_broadcast([128, 8, 128]), op=ALU.mult),
                             reads=["D1", "strict"], writes=["GBm"])
                        yield
                        S.op("dve", lambda e: e.scalar_tensor_tensor(out=GBm[:, :].rearrange("p (h i) -> p h i", i=128),
                                                                     in0=GBm[:, :].rearrange("p (h i) -> p h i", i=128), scalar=-1.0,
                                                                     in1=gsm[:, 0:8].unsqueeze(2).to_broadcast([128, 8, 128]), op0=ALU.mult, op1=ALU.mult),
                             reads=["GBm", gres], writes=["GBm"])
                        yield
                    def chain(hg):
                        N0, N0T, WA, WB, MO, MOT, V1s, V2s = GCTX[hg]
                        rN0, rN0T, rWA, rWB, rV1, rV2 = (f"{n}_{hg}" for n in ("N0", "N0T", "WA", "WB", "V1s", "V2s"))
                        yield
                        bk = S.bank(); bqk = S.bank(); bt = S.bank()

                        def mm_kk(e, hg=hg, bk=bk, bqk=bqk, bt=bt, tc0=tc0):
                            ins = None
                            for hh in range(4):
                                h = hg * 4 + hh
                                ks = GK[:, h * TBv + tc0: h * TBv + tc0 + 128]
                                e.matmul(pbank(bk, 128, hh * 128), lhsT=ks, rhs=ks, start=True, stop=True)
                                e.matmul(pbank(bqk, 128, hh * 128), lhsT=ks, rhs=GQ[:, h * TBv + tc0: h * TBv + tc0 + 128], start=True, stop=True)
                                e.transpose(pbank_bf(bt, 128, hh * 128), ks, ident_b[:])
                                ins = e.transpose(pbank_bf(bt, 128, 512 + hh * 128), GV[:, h * TBv + tc0: h * TBv + tc0 + 128], ident_b[:])
                            return ins
                        S.op("pe", mm_kk, reads=["GK", "GQ", "GV", "ident_b"], writes=[f"pb{bk}", f"pb{bqk}", f"pb{bt}"])
                        sl = slice(hg * 512, (hg + 1) * 512)
                        S.op("dve", lambda e, bk=bk, sl=sl: e.tensor_tensor(out=N0[:, :], in0=pbank(bk), in1=GBm[:, sl], op=ALU.mult),
                             reads=[f"pb{bk}", "GBm"], writes=[rN0])
                        S.op("dve", lambda e, bqk=bqk, sl=sl: e.tensor_tensor(out=PTg[:, sl], in0=pbank(bqk), in1=D1[:, sl], op=ALU.mult),
                             reads=[f"pb{bqk}", "D1"], writes=[f"PTg{hg}"])
                        S.op("dve", lambda e, bt=bt, sl=sl, hg=hg: e.tensor_tensor(out=KtG[:, sl].rearrange("p (h d) -> p h d", d=128),
                                                                                 in0=pbank_bf(bt, 512).rearrange("p (h d) -> p h d", d=128),
                                                                                 in1=gsm[:, 32 + hg * 4:36 + hg * 4].unsqueeze(2).to_broadcast([128, 4, 128]), op=ALU.mult),
                             reads=[f"pb{bt}", gres], writes=[f"KtG{hg}"])
                        S.op("act", lambda e, bt=bt, sl=sl: e.activation(out=VtG[:, sl], in_=pbank_bf(bt, 512, 512), func=AF.Copy),
                             reads=[f"pb{bt}"], writes=[f"VtG{hg}"])
                        yield
                        btr = S.bank()

                        def tr_n(e, btr=btr):
                            ins = None
                            for hh in range(4):
                                ins = e.transpose(pbank_bf(btr, 128, hh * 128), N0[:, hh * 128:(hh + 1) * 128], ident_b[:])
                            return ins
                        S.op("pe", tr_n, reads=[rN0, "ident_b"], writes=[f"pb{btr}"])
                        S.op("act", lambda e, btr=btr: e.activation(out=N0T[:, :], in_=pbank_bf(btr, 512), func=AF.Copy),
                             reads=[f"pb{btr}"], writes=[rN0T])
                        v3 = lambda t: t[:, :].rearrange("p (h i) -> p h i", i=128)
                        wav = WA[:, :].rearrange("p (h a i) -> p h a i", a=2, i=128)
                        wbv = WB[:, :].rearrange("p (h a i) -> p h a i", a=2, i=128)
                        m16b = m16[:, :].unsqueeze(1).to_broadcast([128, 4, 128])
                        idb4 = identb8[:, 0:512].rearrange("p (h i) -> p h i", i=128)
                        S.op("dve", lambda e, wav=wav: e.tensor_tensor(out=wav[:, :, 0, :], in0=v3(N0), in1=m16b, op=ALU.mult), reads=[rN0, "m16"], writes=[rWA])
                        S.op("dve", lambda e, wav=wav: e.tensor_tensor(out=wav[:, :, 1, :], in0=wav[:, :, 0, :], in1=idb4, op=ALU.add), reads=[rWA, "identb8"], writes=[rWA])
                        S.op("dve", lambda e, wbv=wbv: e.tensor_tensor(out=wbv[:, :, 0, :], in0=v3(N0T), in1=m16b, op=ALU.mult), reads=[rN0T, "m16"], writes=[rWB])
                        S.op("dve", lambda e, wbv=wbv: e.tensor_tensor(out=wbv[:, :, 1, :], in0=wbv[:, :, 0, :], in1=idb4, op=ALU.add), reads=[rWB, "identb8"], writes=[rWB])
                        for l in range(3):
                            mk = moff[:, l * 128:(l + 1) * 128].unsqueeze(1).to_broadcast([128, 4, 128])
                            S.op("pool", lambda e, l=l, mk=mk: e.tensor_tensor(out=v3(MO[l]), in0=v3(N0T), in1=mk, op=ALU.mult), reads=[rN0T, "moff"], writes=[f"MO{l}_{hg}"])
                            if l < 2:
                                S.op("pool", lambda e, l=l, mk=mk: e.tensor_tensor(out=v3(MOT[l]), in0=v3(N0), in1=mk, op=ALU.mult), reads=[rN0, "moff"], writes=[f"MOT{l}_{hg}"])
                        for lvl in range(4):
                            yield
                            while S.bank_i % 2 != 0:
                                S.bank()
                            ba0 = S.bank(); ba1 = S.bank(); bb0 = S.bank(); bb1 = S.bank()

                            def mm_l(e, lvl=lvl, ba0=ba0, bb0=bb0):
                                ins = None
                                for hh in range(4):
                                    oa = PS[:, ba0 * 512 + hh * 256: ba0 * 512 + hh * 256 + 256]
                                    ob = PS[:, bb0 * 512 + hh * 256: bb0 * 512 + hh * 256 + 256]
                                    wa = WA[:, hh * 256: hh * 256 + 256]
                                    wb = WB[:, hh * 256: hh * 256 + 256]
                                    lo, hi = (0, 128) if lvl == 0 else ((0, 256) if lvl < 3 else (128, 256))
                                    e.matmul(oa[:, lo:hi], lhsT=wb[:, 0:128], rhs=wa[:, lo:hi], start=True, stop=True)
                                    ins = e.matmul(ob[:, lo:hi], lhsT=wa[:, 0:128], rhs=wb[:, lo:hi], start=True, stop=True)
                                return ins
                            S.op("pe", mm_l, reads=[rWA, rWB], writes=[f"pb{ba0}", f"pb{ba1}", f"pb{bb0}", f"pb{bb1}"])
                            pva = PS[:, ba0 * 512: ba0 * 512 + 1024].rearrange("p (h a i) -> p h a i", a=2, i=128)
                            pvb = PS[:, bb0 * 512: bb0 * 512 + 1024].rearrange("p (h a i) -> p h a i", a=2, i=128)
                            if lvl >= 1:
                                S.op("dve", lambda e, pva=pva, wav=wav: e.tensor_tensor(out=wav[:, :, 1, :], in0=pva[:, :, 1, :], in1=wav[:, :, 1, :], op=ALU.add),
                                     reads=[f"pb{ba0}", f"pb{ba1}", rWA], writes=[rWA])
                                S.op("dve", lambda e, pvb=pvb, wbv=wbv: e.tensor_tensor(out=wbv[:, :, 1, :], in0=pvb[:, :, 1, :], in1=wbv[:, :, 1, :], op=ALU.add),
                                     reads=[f"pb{bb0}", f"pb{bb1}", rWB], writes=[rWB])
                            if lvl < 3:
                                S.op("act", lambda e, pva=pva, wav=wav: e.activation(out=wav[:, :, 0, :], in_=pva[:, :, 0, :], func=AF.Copy),
                                     reads=[f"pb{ba0}", f"pb{ba1}", rWA], writes=[rWA])
                                S.op("act", lambda e, pvb=pvb, wbv=wbv: e.activation(out=wbv[:, :, 0, :], in_=pvb[:, :, 0, :], func=AF.Copy),
                                     reads=[f"pb{bb0}", f"pb{bb1}", rWB], writes=[rWB])
                        for l in range(3):
                            last = (l == 2)
                            yield
                            b1_ = S.bank(); b2_ = None if last else S.bank()

                            def mm_v(e, l=l, b1_=b1_, b2_=b2_, last=last):
                                ins = None
                                for hh in range(4):
                                    ins = e.matmul(pbank(b1_, 128, hh * 128), lhsT=MO[l][:, hh * 128:(hh + 1) * 128], rhs=WA[:, hh * 256 + 128: hh * 256 + 256],
                                                   start=True, stop=True)
                                    if not last:
                                        ins = e.matmul(pbank(b2_, 128, hh * 128), lhsT=MOT[l][:, hh * 128:(hh + 1) * 128], rhs=WB[:, hh * 256 + 128: hh * 256 + 256],
                                                       start=True, stop=True)
                                return ins
                            S.op("pe", mm_v, reads=[rWA, rWB, f"MO{l}_{hg}"] + ([] if last else [f"MOT{l}_{hg}"]),
                                 writes=[f"pb{b1_}"] + ([] if last else [f"pb{b2_}"]))
                            S.op("act", lambda e, b1_=b1_: e.activation(out=V1s[:, :], in_=pbank(b1_), func=AF.Copy), reads=[f"pb{b1_}"], writes=[rV1])
                            if not last:
                                S.op("act", lambda e, b2_=b2_: e.activation(out=V2s[:, :], in_=pbank(b2_), func=AF.Copy), reads=[f"pb{b2_}"], writes=[rV2])
                            yield
                            b3_ = S.bank(); b4_ = None if last else S.bank()

                            def mm_u(e, b3_=b3_, b4_=b4_, last=last):
                                ins = None
                                for hh in range(4):
                                    ins = e.matmul(pbank(b3_, 128, hh * 128), lhsT=WB[:, hh * 256 + 128: hh * 256 + 256], rhs=V1s[:, hh * 128:(hh + 1) * 128],
                                                   start=True, stop=True)
                                    if not last:
                                        ins = e.matmul(pbank(b4_, 128, hh * 128), lhsT=WA[:, hh * 256 + 128: hh * 256 + 256], rhs=V2s[:, hh * 128:(hh + 1) * 128],
                                                       start=True, stop=True)
                                return ins
                            S.op("pe", mm_u, reads=[rWA, rWB, rV1] + ([] if last else [rV2]),
                                 writes=[f"pb{b3_}"] + ([] if last else [f"pb{b4_}"]))
                            if last:
                                S.op("dve", lambda e, b3_=b3_, wav=wav, sl=sl: e.tensor_tensor(out=TTf[:, sl].rearrange("p (h i) -> p h i", i=128),
                                                                                           in0=wav[:, :, 1, :], in1=pbank(b3_).rearrange("p (h i) -> p h i", i=128), op=ALU.subtract),
                                     reads=[f"pb{b3_}", rWA], writes=[f"TTf{hg}"])
                            else:
                                S.op("dve", lambda e, b3_=b3_, wav=wav: e.tensor_tensor(out=wav[:, :, 1, :], in0=wav[:, :, 1, :],
                                                                                      in1=pbank(b3_).rearrange("p (h i) -> p h i", i=128), op=ALU.subtract),
                                     reads=[f"pb{b3_}", rWA], writes=[rWA])
                                S.op("dve", lambda e, b4_=b4_, wbv=wbv: e.tensor_tensor(out=wbv[:, :, 1, :], in0=wbv[:, :, 1, :],
                                                                                      in1=pbank(b4_).rearrange("p (h i) -> p h i", i=128), op=ALU.subtract),
                                     reads=[f"pb{b4_}", rWB], writes=[rWB])
                    def rec(hg):
                        sl = slice(hg * 512, (hg + 1) * 512)
                        yield
                        b1 = S.bank()

                        def mm_ks(e, b1=b1, hg=hg, tc0=tc0):
                            ins = None
                            for hh in range(4):
                                h = hg * 4 + hh
                                ins = e.matmul(pbank(b1, 128, hh * 128), lhsT=GK[:, h * TBv + tc0: h * TBv + tc0 + 128],
                                               rhs=SGb[:, h * 128:(h + 1) * 128], start=True, stop=True)
                            return ins
                        S.op("pe", mm_ks, reads=["GK", f"SGb{hg}"], writes=[f"pb{b1}"])
                        S.op("dve", lambda e, b1=b1, sl=sl, hg=hg: e.tensor_tensor(out=rtl[:, sl].rearrange("p (h d) -> p h d", d=128),
                                                                                 in0=pbank(b1).rearrange("p (h d) -> p h d", d=128),
                                                                                 in1=gsm[:, 24 + hg * 4:28 + hg * 4].unsqueeze(2).to_broadcast([128, 4, 128]), op=ALU.mult),
                             reads=[f"pb{b1}", gres], writes=[f"rtl{hg}"])
                        S.op("dve", lambda e, sl=sl: e.tensor_tensor(out=rtl[:, sl], in0=rtl[:, sl], in1=VtG[:, sl], op=ALU.add),
                             reads=[f"rtl{hg}", f"VtG{hg}"], writes=[f"rtl{hg}"])
                        yield
                        b2 = S.bank()

                        def mm_vn(e, b2=b2, hg=hg):
                            ins = None
                            for hh in range(4):
                                h = hg * 4 + hh
                                ins = e.matmul(pbank(b2, 128, hh * 128), lhsT=TTf[:, h * 128:(h + 1) * 128],
                                               rhs=rtl[:, h * 128:(h + 1) * 128], start=True, stop=True)
                            return ins
                        S.op("pe", mm_vn, reads=[f"TTf{hg}", f"rtl{hg}"], writes=[f"pb{b2}"])
                        S.op("dve", lambda e, b2=b2, sl=sl, hg=hg: e.tensor_tensor(out=vnw[:, sl].rearrange("p (h d) -> p h d", d=128),
                                                                                 in0=pbank(b2).rearrange("p (h d) -> p h d", d=128),
                                                                                 in1=gsm[:, hg * 4:hg * 4 + 4].unsqueeze(2).to_broadcast([128, 4, 128]), op=ALU.mult),
                             reads=[f"pb{b2}", gres], writes=[f"vnw{hg}"])
                        yield
                        b3 = S.bank(); b4 = S.bank()

                        def mm_o(e, b3=b3, b4=b4, hg=hg):
                            ins = None
                            for hh in range(4):
                                h = hg * 4 + hh
                                o = pbank(b3, 128, hh * 128)
                                e.matmul(o, lhsT=SGb[:, h * 128:(h + 1) * 128], rhs=QTd[:, h * 128:(h + 1) * 128], start=True, stop=False)
                                e.matmul(o, lhsT=vnw[:, h * 128:(h + 1) * 128], rhs=PTg[:, h * 128:(h + 1) * 128], start=False, stop=True)
                                ins = e.matmul(pbank(b4, 128, hh * 128), lhsT=KtG[:, h * 128:(h + 1) * 128], rhs=vnw[:, h * 128:(h + 1) * 128],
                                               start=True, stop=True)
                            return ins
                        S.op("pe", mm_o, reads=[f"SGb{hg}", qres, f"vnw{hg}", f"PTg{hg}", f"KtG{hg}"], writes=[f"pb{b3}", f"pb{b4}"])
                        dst = OG[:, hg * 4 * TBv:(hg + 1) * 4 * TBv].rearrange("p (h t) -> p h t", t=TBv)[:, :, tc0:tc0 + 128]
                        S.op("act", lambda e, b3=b3, dst=dst: e.activation(out=dst, in_=pbank(b3).rearrange("p (h t) -> p h t", t=128), func=AF.Copy),
                             reads=[f"pb{b3}"], writes=["OG"])
                        S.op("dve", lambda e, sl=sl, hg=hg: e.tensor_tensor(out=SG[:, sl].rearrange("p (h d) -> p h d", d=128),
                                                                     in0=SG[:, sl].rearrange("p (h d) -> p h d", d=128),
                                                                     in1=gsm[:, 40 + hg * 4:44 + hg * 4].unsqueeze(2).to_broadcast([128, 4, 128]), op=ALU.mult),
                             reads=[f"SG{hg}", gres], writes=[f"SG{hg}"])
                        S.op("dve", lambda e, b4=b4, sl=sl: e.tensor_tensor(out=SG[:, sl], in0=SG[:, sl], in1=pbank(b4), op=ALU.add),
                             reads=[f"SG{hg}", f"pb{b4}"], writes=[f"SG{hg}"])
                        S.op("act", lambda e, sl=sl: e.activation(out=SGb[:, sl], in_=SG[:, sl], func=AF.Copy), reads=[f"SG{hg}"], writes=[f"SGb{hg}"])
                    return setup, chain, rec
                def _drive(gens):
                    gens = list(gens)
                    while gens:
                        for _g in list(gens):
                            try:
                                next(_g)
                            except StopIteration:
                                gens.remove(_g)
                _tiles = [gdn_tile(tau) for tau in (range(NTv) if not is_sample else [])]
                if _tiles:
                    _drive([_tiles[0][0]()])
                for tau in range(len(_tiles)):
                    _st, _ch, _rc = _tiles[tau]
                    _drive([_ch(0), _ch(1), _retg])
                    _drive(([_tiles[tau + 1][0]()] if tau + 1 < len(_tiles) else []) + [_rc(0), _rc(1)])
                _drive([_retg])
                if blk == nblk_v - 1 and not is_sample:
                    S.op("sp", lambda e: e.dma_start(out=sgp.rearrange("h d v -> d h v"), in_=SG[:, :].rearrange("p (h v) -> p h v", v=GDV)),
                         reads=["SG0", "SG1"], dma="sgp")
                if blk == nblk_v - 1 and not is_sample:
                    S.op("sp", lambda e: e.dma_start(out=srp.rearrange("h d v -> d h v"), in_=SR[:, :].rearrange("p (h v) -> p h v", v=RDV)),
                         reads=["SR"], dma="srp")
                def _g_rms():
                    yield
                    S.op("act", lambda e: e.activation(out=MRG[:, :], in_=OG[:, :], func=AF.Square), reads=["OG"], writes=["MRG"])
                    yield
                    while S.bank_i % 4 != 0:
                        S.bank()
                    rb_ = [S.bank() for _ in range(4)]
                    assert rb_[3] == rb_[0] + 3

                    def mm_rn(e):
                        ins = None
                        for h in range(GH):
                            ins = e.matmul(PS[:, rb_[0] * 512 + h * TBv: rb_[0] * 512 + (h + 1) * TBv], lhsT=ones_b[:], rhs=MRG[:, h * TBv:(h + 1) * TBv],
                                           start=True, stop=True)
                        return ins
                    rbr = [f"pb{b}" for b in rb_]
                    S.op("pe", mm_rn, reads=["MRG", "ones_b"], writes=rbr)
                    T4r = 4 * TBv
                    for hf in range(2):
                        rs_, rr_ = RNS[hf]
                        S.op("dve", lambda e, hf=hf, rs_=rs_: e.tensor_scalar(out=rs_, in0=PS[:, rb_[0] * 512 + hf * T4r: rb_[0] * 512 + (hf + 1) * T4r],
                                                                             scalar1=1.0 / GDV, scalar2=EPS, op0=ALU.mult, op1=ALU.add), reads=rbr, writes=[rr_])
                    for hf in range(2):
                        rs_, rr_ = RNS[hf]
                        yield
                        S.op("act", lambda e, rs_=rs_: e.activation(out=rs_, in_=rs_, func=AF.Ln), reads=[rr_], writes=[rr_])
                        yield
                        S.op("act", lambda e, rs_=rs_: e.activation(out=rs_, in_=rs_, func=AF.Exp, scale=-0.5), reads=[rr_], writes=[rr_])
                        yield
                        S.op("dve", lambda e, hf=hf, rs_=rs_: e.scalar_tensor_tensor(out=rs_, in0=OG[:, hf * T4r:(hf + 1) * T4r], scalar=ggdn_c[:, 0:1], in1=rs_,
                                                                                    op0=ALU.mult, op1=ALU.mult), reads=["OG", rr_, "ggdn_c"], writes=[rr_])
                        yield
                        S.op("dve", lambda e, hf=hf, rs_=rs_: e.tensor_tensor(out=OGg[:, hf * T4r:(hf + 1) * T4r], in0=rs_, in1=Z[:, hf * T4r:(hf + 1) * T4r], op=ALU.mult),
                             reads=[rr_, "Z"], writes=["OGg"])
                    yield
                def _g_gn():
                    T4 = 4 * TBv
                    if is_sample:
                        GNS, gns_r = GNSs, "GNSs"
                    else:
                        GNS, gns_r = QKVB[:, 8 * TBv:24 * TBv].bitcast(F32), "FA"
                    OBF = FA[:, 0:8 * TBv]
                    OSQ, osq_r = (hnT, "hnT") if is_sample else (VtR, "VtR")
                    yield
                    S.op("act", lambda e: e.activation(out=OBF, in_=OR[:, :], func=AF.Copy), reads=[or_res], writes=["FA"] + FA_AL)
                    yield
                    S.op("act", lambda e: e.activation(out=OSQ[:, :], in_=OR[:, :], func=AF.Square), reads=[or_res], writes=[osq_r])
                    yield
                    while S.bank_i % 4 != 0:
                        S.bank()
                    gb_ = [S.bank() for _ in range(4)]
                    assert gb_[3] == gb_[0] + 3

                    def mm_gn(e):
                        ins = None
                        for h in range(RH):
                            om = PS[:, gb_[0] * 512 + h * TBv: gb_[0] * 512 + (h + 1) * TBv]
                            oq = PS[:, gb_[0] * 512 + T4 + h * TBv: gb_[0] * 512 + T4 + (h + 1) * TBv]
                            for c in range(2):
                                ch = h * 2 + c
                                e.matmul(om, lhsT=onesq_b[:], rhs=OBF[:, ch * TBv:(ch + 1) * TBv], start=(c == 0), stop=(c == 1))
                            for c in range(2):
                                ch = h * 2 + c
                                ins = e.matmul(oq, lhsT=onesq_b[:], rhs=OSQ[:, ch * TBv:(ch + 1) * TBv], start=(c == 0), stop=(c == 1))
                        return ins
                    S.op("pe", mm_gn, reads=["FA", osq_r, "onesq_b"], writes=[f"pb{b}" for b in gb_])
                    pmean = PS[:, gb_[0] * 512: gb_[0] * 512 + T4]
                    pmsq = PS[:, gb_[0] * 512 + T4: gb_[0] * 512 + 2 * T4]
                    gbr = [f"pb{b}" for b in gb_]
                    S.op("act", lambda e: e.activation(out=GNS[:, 0:T4], in_=pmean, func=AF.Copy), reads=gbr, writes=[gns_r])
                    S.op("dve", lambda e: e.tensor_tensor(out=GNS[:, T4:2 * T4], in0=GNS[:, 0:T4], in1=GNS[:, 0:T4], op=ALU.mult), reads=[gns_r], writes=[gns_r])
                    S.op("dve", lambda e: e.scalar_tensor_tensor(out=GNS[:, T4:2 * T4], in0=GNS[:, T4:2 * T4], scalar=-1.0, in1=pmsq, op0=ALU.mult, op1=ALU.add),
                         reads=[gns_r] + gbr, writes=[gns_r])
                    yield
                    S.op("dve", lambda e: e.tensor_scalar(out=GNS[:, T4:2 * T4], in0=GNS[:, T4:2 * T4], scalar1=EPS, scalar2=None, op0=ALU.add), reads=[gns_r], writes=[gns_r])
                    yield
                    S.op("act", lambda e: e.activation(out=GNS[:, T4:2 * T4], in_=GNS[:, T4:2 * T4], func=AF.Ln), reads=[gns_r], writes=[gns_r])
                    yield
                    S.op("act", lambda e: e.activation(out=GNS[:, T4:2 * T4], in_=GNS[:, T4:2 * T4], func=AF.Exp, scale=-0.5), reads=[gns_r], writes=[gns_r])
                    or4 = OR[:, :].rearrange("p (h c t) -> p h c t", c=2, t=TBv)
                    yield
                    S.op("dve", lambda e: e.tensor_tensor(out=or4, in0=or4,
                                                          in1=GNS[:, 0:T4].rearrange("p (h t) -> p h t", t=TBv).unsqueeze(2).to_broadcast([128, RH, 2, TBv]), op=ALU.subtract),
                         reads=[or_res, gns_r], writes=[or_res])
                    yield
                    S.op("dve", lambda e: e.tensor_tensor(out=or4, in0=or4,
                                                          in1=GNS[:, T4:2 * T4].rearrange("p (h t) -> p h t", t=TBv).unsqueeze(2).to_broadcast([128, RH, 2, TBv]), op=ALU.mult),
                         reads=[or_res, gns_r], writes=[or_res])
                    or3 = OR[:, :].rearrange("p (c t) -> p c t", t=TBv)
                    yield
                    S.op("dve", lambda e: e.tensor_tensor(out=or3, in0=or3, in1=ggn_c[:, 0:8].unsqueeze(2).to_broadcast([128, 8, TBv]), op=ALU.mult),
                         reads=[or_res, "ggn_c"], writes=[or_res])
                    yield
                    S.op("dve", lambda e: e.tensor_tensor(out=ORg[:, :], in0=OR[:, :], in1=RG[:, :], op=ALU.mult), reads=[or_res, "RG"], writes=["ORg"])


                    yield
                def late_proj():
                    for (kind, c0, ncols) in late_tiles:
                        base = {"ga": 7184, "gb": 8208}[kind]
                        dstb = {"ga": GA, "gb": GB}[kind]
                        cb = (c0 - base) // 128
                        t, wres = wload(w_in, 0, 8, c0, ncols)
                        for j in range(ncols // 128):
                            yield
                            b = S.bank()

                            def mm(e, j=j, b=b, t=t, ncols=ncols):
                                ins = None
                                for k in range(8):
                                    ins = e.matmul(pbank(b, TBv), lhsT=wv(t, k, ncols, j * 128, 128), rhs=hnT[:, k * TBv:(k + 1) * TBv], start=(k == 0), stop=(k == 7))
                                return ins
                            S.op("pe", mm, reads=[wres, "hnT"], writes=[f"pb{b}"])
                            S.op("act", lambda e, b=b, j=j, dstb=dstb, cb=cb: e.activation(out=dstb[:, (cb + j) * TBv:(cb + j + 1) * TBv], in_=pbank(b, TBv), func=AF.Tanh, scale=0.5),
                                 reads=[f"pb{b}"], writes=[kind.upper()])
                    yield
                _drive([_g_rms(), _g_gn(), late_proj()])

                if blk + 1 < nblk_v:
                    load_x(blk + 1)
                if stage < 4:
                    return
                for half in range(NQ):
                    ta, ra = wload(w_a, 0, 8, half * WC, WC)
                    tb_, rb = wload(w_b, 0, 8, half * WC, WC)
                    def _br(j, half=half, ta=ta, tb_=tb_, ra=ra, rb_w=rb):
                        tmpA, rA = nxt("tmpA"); tmpB, rB = nxt("tmpB")
                        ch = half * (WC // 128) + j
                        b1 = S.bank(); b2 = S.bank()

                        def mm_br(e):
                            ins = None
                            for k in range(8):
                                e.matmul(pbank(b1, TBv), lhsT=wv(ta, k, WC, j * 128, 128), rhs=ORg[:, k * TBv:(k + 1) * TBv], start=(k == 0), stop=(k == 7))
                            for k in range(8):
                                ins = e.matmul(pbank(b2, TBv), lhsT=wv(tb_, k, WC, j * 128, 128), rhs=OGg[:, k * TBv:(k + 1) * TBv], start=(k == 0), stop=(k == 7))
                            return ins
                        S.op("pe", mm_br, reads=[ra, rb_w, "ORg", "OGg"], writes=[f"pb{b1}", f"pb{b2}"])
                        S.op("dve", lambda e: e.scalar_tensor_tensor(out=tmpA[:], in0=GA[:, ch * TBv:(ch + 1) * TBv], scalar=1.0, in1=pbank(b1, TBv),
                                                                     op0=ALU.add, op1=ALU.mult), reads=[f"pb{b1}", "GA"], writes=[rA])
                        S.op("dve", lambda e: e.scalar_tensor_tensor(out=tmpB[:], in0=GB[:, ch * TBv:(ch + 1) * TBv], scalar=1.0, in1=pbank(b2, TBv),
                                                                     op0=ALU.add, op1=ALU.mult), reads=[f"pb{b2}", "GB"], writes=[rB])
                        S.op("dve", lambda e: e.tensor_tensor(out=MRG[:, ch * TBv:(ch + 1) * TBv], in0=tmpA[:], in1=tmpB[:], op=ALU.add),
                             reads=[rA, rB], writes=["MRG"])
                    for j in range(WC // 128):
                        _br(j)

                def resid_cons(half, sc=None):
                    def cons(tau, b):
                        xs_ = Xb[0:tw, tau * D + half * WC: tau * D + half * WC + WC]
                        if sc is None:
                            S.op("dve", lambda e: e.tensor_tensor(out=xs_, in0=xs_, in1=pbank(b, WC)[0:tw, :], op=ALU.add), reads=[f"pb{b}", xres], writes=[xres])
                        else:
                            S.op("dve", lambda e: e.scalar_tensor_tensor(out=xs_, in0=pbank(b, WC)[0:tw, :], scalar=sc, in1=xs_, op0=ALU.mult, op1=ALU.add),
                                 reads=[f"pb{b}", xres], writes=[xres])
                    return cons
                for half in range(NQ):
                    proj_tm(w_out, 0, 8, half * WC, WC, MRG, "MRG", TBv, NTv, resid_cons(half, 0.5), tw=tw)

                if stage < 5:
                    return
                for tau in range(NTv):
                    norm_transpose(Xb[0:tw, tau * D:(tau + 1) * D], xres, gx_c, "gx_c", hnT, "hnT", tw, tau * tw, TBv)
                XQ = ORg
                OX = OGg
                for half in range(NQ):
                    def cons(j, b, half=half):
                        c = half * (WC // 128) + j
                        S.op("act", lambda e: e.activation(out=XQ[:, c * TBv:(c + 1) * TBv], in_=pbank(b, TBv), func=AF.Copy), reads=[f"pb{b}"], writes=["ORg"])
                    proj_fm(w_xq, 0, 8, half * WC, WC, hnT, "hnT", TBv, TBv, cons)
                ET = MRG
                if is_sample:
                    sample_xattn(XQ, OX)
                def _xh(h):
                    tmpB, rB = nxt("tmpB")
                    eo = (h % 2) * 2 * TBv
                    eres = f"MRGs{h % 2}"
                    for mt in range(2):
                        yield
                        b = S.bank()

                        def mm_s(e, b=b, mt=mt):
                            ins = None
                            for c in range(2):
                                ch = h * 2 + c
                                ins = e.matmul(pbank(b, TBv), lhsT=MKT[:, ch * NMEM + mt * 128: ch * NMEM + (mt + 1) * 128],
                                               rhs=XQ[:, ch * TBv:(ch + 1) * TBv], start=(c == 0), stop=(c == 1))
                            return ins
                        S.op("pe", mm_s, reads=["MKT", "ORg"], writes=[f"pb{b}"])
                        S.op("act", lambda e, b=b, mt=mt: e.activation(out=ET[:, eo + mt * TBv: eo + (mt + 1) * TBv], in_=pbank(b, TBv), func=AF.Exp, scale=XHD ** -0.5),
                             reads=[f"pb{b}"], writes=[eres])
                    yield
                    bd = S.bank()

                    def mm_d(e):
                        e.matmul(pbank(bd, TBv), lhsT=ones_b[:], rhs=ET[:, eo:eo + TBv], start=True, stop=False)
                        return e.matmul(pbank(bd, TBv), lhsT=ones_b[:], rhs=ET[:, eo + TBv: eo + 2 * TBv], start=False, stop=True)
                    S.op("pe", mm_d, reads=[eres, "ones_b"], writes=[f"pb{bd}"])
                    S.op("dve", lambda e: e.reciprocal(out=tmpB[:], in_=pbank(bd, TBv)), reads=[f"pb{bd}"], writes=[rB])
                    for c in range(2):
                        ch = h * 2 + c
                        yield
                        b = S.bank()

                        def mm_o(e, b=b, ch=ch):
                            e.matmul(pbank(b, TBv), lhsT=MV[:, ch * 128:(ch + 1) * 128], rhs=ET[:, eo:eo + TBv], start=True, stop=False)
                            return e.matmul(pbank(b, TBv), lhsT=MV[:, D + ch * 128: D + (ch + 1) * 128], rhs=ET[:, eo + TBv: eo + 2 * TBv], start=False, stop=True)
                        S.op("pe", mm_o, reads=["MV", eres], writes=[f"pb{b}"])
                        S.op("dve", lambda e, b=b, ch=ch: e.tensor_tensor(out=OX[:, ch * TBv:(ch + 1) * TBv], in0=pbank(b, TBv), in1=tmpB[:], op=ALU.mult),
                             reads=[f"pb{b}", rB], writes=["OGg"])
                if not is_sample:
                    _drive([_xh(0), _xh(1)])
                    _drive([_xh(2), _xh(3)])
                for half in range(NQ):
                    proj_tm(w_xo, 0, 8, half * WC, WC, OX, "OGg", TBv, NTv, resid_cons(half), tw=tw)

                if stage < 6:
                    return
                for tau in range(NTv):
                    norm_transpose(Xb[0:tw, tau * D:(tau + 1) * D], xres, gffn_c, "gffn_c", hnT, "hnT", tw, tau * tw, TBv)
                for c0 in range(0, DFF, WC):
                    ncols = min(WC, DFF - c0)
                    tg, rg_ = wload(w_gate, 0, 8, c0, ncols)
                    tu, ru = wload(w_up, 0, 8, c0, ncols)
                    def _ff(j, c0=c0, ncols=ncols, tg=tg, tu=tu, rg_=rg_, ru=ru):
                        tmpA, rA = nxt("tmpA")
                        ch = c0 // 128 + j
                        b1 = S.bank(); b2 = S.bank()

                        def mm_f(e):
                            ins = None
                            for k in range(8):
                                e.matmul(pbank(b1, TBv), lhsT=wv(tg, k, ncols, j * 128, 128), rhs=hnT[:, k * TBv:(k + 1) * TBv], start=(k == 0), stop=(k == 7))
                            for k in range(8):
                                ins = e.matmul(pbank(b2, TBv), lhsT=wv(tu, k, ncols, j * 128, 128), rhs=hnT[:, k * TBv:(k + 1) * TBv], start=(k == 0), stop=(k == 7))
                            return ins
                        S.op("pe", mm_f, reads=[rg_, ru, "hnT"], writes=[f"pb{b1}", f"pb{b2}"])
                        S.op("act", lambda e: e.activation(out=tmpA[:], in_=pbank(b1, TBv), func=AF.Silu), reads=[f"pb{b1}"], writes=[rA])
                        S.op("dve", lambda e: e.tensor_tensor(out=FA[:, ch * TBv:(ch + 1) * TBv], in0=tmpA[:], in1=pbank(b2, TBv), op=ALU.mult),
                             reads=[rA, f"pb{b2}"], writes=["FA"] + FA_AL)
                    for j in range(ncols // 128):
                        _ff(j)
                if blk + 1 < nblk_v:
                    pre_norm(blk + 1)
                for half in range(NQ):
                    tiles = [wload(w_down, g * 1024, (8 if g < 2 else 6), half * WC, WC) for g in range(3)]
                    for tau in range(NTv):
                        b = S.bank()

                        def mm_d(e, b=b, tau=tau, tiles=tiles):
                            ins = None
                            for g in range(3):
                                nk = 8 if g < 2 else 6
                                for k in range(nk):
                                    kk = g * 8 + k
                                    ins = e.matmul(pbank(b, WC)[0:tw, :], lhsT=FA[:, kk * TBv + tau * tw: kk * TBv + (tau + 1) * tw], rhs=wv(tiles[g][0], k, WC),
                                                   start=(kk == 0), stop=(kk == 21))
                            return ins
                        S.op("pe", mm_d, reads=[t[1] for t in tiles] + ["FA"], writes=[f"pb{b}"])
                        resid_cons(half)(tau, b)
                if is_sample:
                    gfin_bc, gfin_r = gfin_s, "gfin_s"
                else:
                    gfin_bc, gfin_r = D1, "D1"
                S.op("sp", lambda e, gfin_bc=gfin_bc: e.dma_start(out=gfin_bc[:, 0:D], in_=g_fin.rearrange("(o d) -> o d", o=1).partition_broadcast(128)),
                     writes=[gfin_r], dma="gfin_ld")
                for tau in range(NTv):
                    xt = Xb[0:tw, tau * D:(tau + 1) * D]
                    S.op("act", lambda e, xt=xt: e.activation(out=sqj[0:tw, :], in_=xt, func=AF.Square, accum_out=sscol[0:tw, 0:1]), reads=[xres], writes=["xn", "sscol"])
                    rstd_from_ss(sscol[0:tw, 0:1], D, EPS, "sscol")
                    S.op("dve", lambda e, xt=xt: e.scalar_tensor_tensor(out=xt, in0=xt, scalar=sscol[0:tw, 0:1], in1=gfin_bc[0:tw, 0:D], op0=ALU.mult, op1=ALU.mult),
                         reads=[xres, "sscol", gfin_r], writes=[xres])
                if is_sample:
                    S.op("sp", lambda e, Xb=Xb: e.dma_start(out=ys, in_=Xb[0:NS, 0:D]), reads=[xres], dma=xres + "o")
                else:
                    S.op("sp", lambda e, Xb=Xb, t0=t0: e.dma_start(out=yp[t0:t0 + TBv, :].rearrange("(a p) d -> p a d", p=128),
                                                                 in_=Xb[:, :].rearrange("p (a d) -> p a d", d=D)),
                         reads=[xres], dma=xres + "o")
            for blk in range(nblk_v):
                do_block(blk)
            S.emit()
            esp.close()
            cur_es[0] = es

        run_phase(False)
        if stage >= 7:
            run_phase(True)
    return nc


_CACHE = {}


def _prep_inputs(inputs):
    f = lambda a: np.ascontiguousarray(np.asarray(a, dtype=np.float32))
    common = dict(
        w_in=f(inputs["w_in"][0]), w_a=f(inputs["w_branch_a"][0]), w_b=f(inputs["w_branch_b"][0]), w_out=f(inputs["w_out"][0]),
        w_xq=f(inputs["w_xq"][0]), w_xk=f(inputs["w_xk"][0]), w_xv=f(inputs["w_xv"][0]), w_xo=f(inputs["w_xo"][0]),
        w_gate=f(inputs["w_gate"][0]), w_up=f(inputs["w_up"][0]), w_down=f(inputs["w_down"][0]),
        g_mix=f(inputs["norm_mix_g"][0]), g_x=f(inputs["norm_x_g"][0]), g_mem=f(inputs["mem_norm_g"][0]),
        g_ffn=f(inputs["norm_ffn_g"][0]), g_fin=f(inputs["norm_final_g"]), g_gn=f(inputs["ret_gn_g"][0]),
        g_gdn=f(inputs["gdn_norm_g"][0]), convw=f(inputs["gdn_conv_w"][0]), a_log=f(inputs["gdn_a_log"][0]),
        dt_bias=f(inputs["gdn_dt_bias"][0]))
    common.update(_CONSTS)
    maps = []
    for c in range(NCORES):
        m = dict(common)
        sl = slice(c * NS, (c + 1) * NS)
        m["xp"] = f(inputs["x_prompt"][c]); m["memp"] = f(inputs["mem_prompt"][c])
        m["xs"] = f(inputs["x_sample"][sl, 0]); m["sret"] = f(inputs["state_ret"][0, sl]); m["sgdn"] = f(inputs["state_gdn"][0, sl])
        m["sconv"] = f(inputs["state_conv"][0, sl])
        m["cmk"] = f(inputs["cache_mem_k"][0, sl]).reshape(NS, NMEM, D); m["cmv"] = f(inputs["cache_mem_v"][0, sl]).reshape(NS, NMEM, D)
        maps.append(m)
    return maps


def kernel(**inputs):
    if "nc" not in _CACHE:
        _CACHE["nc"] = build_program()
    nc = _CACHE["nc"]
    maps = _prep_inputs(inputs)
    res = run_bass_kernel_spmd(nc, maps, core_ids=list(range(NCORES)))
    R = res.results
    st = lambda k: np.stack([np.asarray(r[k], dtype=np.float32) for r in R])
    cat = lambda k: np.concatenate([np.asarray(r[k], dtype=np.float32) for r in R], axis=0)
    y_prompt = st("yp")
    y_sample = cat("ys").reshape(NCORES * NS, 1, D)
    return (y_prompt, y_sample, st("srp")[None], st("sgp")[None], st("scp")[None],
            st("mkp").reshape(1, NCORES, NMEM, XH, XHD), st("mvp").reshape(1, NCORES, NMEM, XH, XHD),
            cat("srs")[None], cat("sgs")[None], cat("scs")[None])
```
